# Optimizing a Trainium2 kernel written in Bass

```python
import math
import jax, jax.numpy as jnp
from jax import lax
import numpy as np

D_MODEL = 1024
BATCH = 1
SEQ = 16384
DEPTH = 2

N_A = DEPTH // 2
N_B = DEPTH - N_A

HG_HEADS = 8
HG_EXPAND = 128
HG_FDIM = HG_HEADS * HG_EXPAND
HG_HEAD_V = D_MODEL // HG_HEADS
HG_CHUNK = 64

DA_HEADS = 8
DA_HEAD_DIM = D_MODEL // (2 * DA_HEADS)
DA_V_DIM = 2 * DA_HEAD_DIM
DA_QDIM = DA_HEADS * 2 * DA_HEAD_DIM
DA_VDIM_TOTAL = DA_HEADS * DA_V_DIM
Q_BLOCK = 128

REL_BUCKETS = 32
REL_MAX_DIST = 128

D_FF = 2816
CONV_W = 3

EPS = 1e-6

kernel_name = 'yoco_hgrn2_diffattn_convffn'


def rmsnorm(x, g):
    x32 = x.astype(jnp.float32)
    y = x32 * lax.rsqrt(jnp.mean(x32 * x32, axis=-1, keepdims=True) + EPS)
    return (y * g.astype(jnp.float32)).astype(x.dtype)


def hgrn2_mixer(h, w_in, w_out, g_norm, lb):
    B, T, _ = h.shape
    proj = h @ w_in
    q, f, i, g = jnp.split(proj, [HG_FDIM, 2 * HG_FDIM, 3 * HG_FDIM], axis=-1)
    q = jax.nn.silu(q)
    log_f = jnp.logaddexp(jnp.log(lb), jnp.log1p(-lb) + jax.nn.log_sigmoid(f.astype(jnp.float32)))
    k = -jnp.expm1(log_f)
    nc = T // HG_CHUNK

    def chunks(a, hd):
        return a.reshape(B, nc, HG_CHUNK, HG_HEADS, hd).swapaxes(0, 1)

    xs = (chunks(q, HG_EXPAND), chunks(k, HG_EXPAND), chunks(i, HG_HEAD_V), chunks(log_f, HG_EXPAND))
    causal = jnp.tril(jnp.ones((HG_CHUNK, HG_CHUNK), dtype=bool))[None, :, :, None, None]

    def step(S, blk):
        qb, kb, vb, lfb = blk
        b = jnp.cumsum(lfb, axis=1)
        b_last = b[:, -1]
        o_inter = jnp.einsum('bthk,bhkv->bthv', qb * jnp.exp(b), S)
        rel = jnp.where(causal, b[:, :, None] - b[:, None, :], -jnp.inf)
        decay = jnp.exp(rel)
        scores = jnp.einsum('bthk,bshk,btshk->bhts', qb, kb, decay)
        o_intra = jnp.einsum('bhts,bshv->bthv', scores, vb)
        k_dec = kb * jnp.exp(b_last[:, None] - b)
        S_new = jnp.exp(b_last)[..., None] * S + jnp.einsum('bshk,bshv->bhkv', k_dec, vb)
        return S_new, o_inter + o_intra

    S0 = jnp.zeros((B, HG_HEADS, HG_EXPAND, HG_HEAD_V), jnp.float32)
    _, o = lax.scan(step, S0, xs)
    o = o.swapaxes(0, 1).reshape(B, T, HG_HEADS * HG_HEAD_V)
    o = rmsnorm(o, g_norm) * jax.nn.silu(g.astype(jnp.float32))
    return (o @ w_out).astype(h.dtype)


def t5_bucket(rel):
    max_exact = REL_BUCKETS // 2
    n = jnp.maximum(rel, 0)
    log_ratio = jnp.log(jnp.maximum(n, 1).astype(jnp.float32) / max_exact) / math.log(REL_MAX_DIST / max_exact)
    large = jnp.minimum(max_exact + (log_ratio * (REL_BUCKETS - max_exact)).astype(jnp.int32), REL_BUCKETS - 1)
    return jnp.where(n < max_exact, n, large)


def shared_kv(x, kv_norm, kv_w):
    B, T, _ = x.shape
    kv = rmsnorm(x, kv_norm) @ kv_w
    k = kv[..., :DA_QDIM].reshape(B, T, DA_HEADS, 2, DA_HEAD_DIM)
    v = kv[..., DA_QDIM:].reshape(B, T, DA_HEADS, DA_V_DIM)
    return k, v


def diff_attention(h, k, v, w_q, w_o, lam_q1, lam_k1, lam_q2, lam_k2, subln_g, rel_table, lambda_init):
    B, T, _ = h.shape
    q = (h @ w_q).reshape(B, T, DA_HEADS, 2, DA_HEAD_DIM) * (DA_HEAD_DIM ** -0.5)
    f32 = jnp.float32
    lam = (jnp.exp(jnp.sum(lam_q1.astype(f32) * lam_k1.astype(f32)))
           - jnp.exp(jnp.sum(lam_q2.astype(f32) * lam_k2.astype(f32))) + lambda_init)
    nb = T // Q_BLOCK
    q_blocks = q.reshape(B, nb, Q_BLOCK, DA_HEADS, 2, DA_HEAD_DIM).swapaxes(0, 1)
    k_pos = jnp.arange(T, dtype=jnp.int32)
    table = rel_table.astype(f32)

    def block(args):
        qb, bi = args
        q_pos = bi * Q_BLOCK + jnp.arange(Q_BLOCK, dtype=jnp.int32)
        rel = q_pos[:, None] - k_pos[None, :]
        bias = jnp.where((rel >= 0)[..., None], table[t5_bucket(rel)], -jnp.inf)
        bias = jnp.transpose(bias, (2, 0, 1))[None, :, None]
        s = jnp.einsum('bqhcd,bkhcd->bhcqk', qb, k, preferred_element_type=f32) + bias
        p = jax.nn.softmax(s, axis=-1)
        attn = p[:, :, 0] - lam * p[:, :, 1]
        return jnp.einsum('bhqk,bkhv->bqhv', attn, v, preferred_element_type=f32)

    o = lax.map(block, (q_blocks, jnp.arange(nb, dtype=jnp.int32)))
    o = o.swapaxes(0, 1).reshape(B, T, DA_HEADS, DA_V_DIM)
    o = rmsnorm(o, subln_g) * (1.0 - lambda_init)
    return (o.reshape(B, T, DA_VDIM_TOTAL) @ w_o).astype(h.dtype)


def conv_ffn(h, w_up, conv_w, conv_b, w_down):
    u = h @ w_up
    c = u.shape[-1]
    u = lax.conv_general_dilated(
        u, conv_w.reshape(CONV_W, 1, c).astype(u.dtype), window_strides=(1,),
        padding=[(CONV_W - 1, 0)], dimension_numbers=('NWC', 'WIO', 'NWC'),
        feature_group_count=c) + conv_b
    gate, val = jnp.split(u, 2, axis=-1)
    return ((jax.nn.silu(gate) * val) @ w_down).astype(h.dtype)


def setup_inputs(seed: int = 0) -> dict:
    key = jax.random.key(seed)
    ks = jax.random.split(key, 22)

    def nrm(k, shape, scale):
        return jax.random.normal(k, shape, jnp.float32) * scale

    def gain(k, shape):
        return 1.0 + nrm(k, shape, 0.02)

    D = D_MODEL
    return {
        'x': nrm(ks[0], (BATCH, SEQ, D), 1.0),
        'a_w_in': nrm(ks[1], (N_A, D, 3 * HG_FDIM + HG_HEADS * HG_HEAD_V), D ** -0.5),
        'a_w_out': nrm(ks[2], (N_A, HG_HEADS * HG_HEAD_V, D), (HG_HEADS * HG_HEAD_V) ** -0.5),
        'a_gnorm': gain(ks[3], (N_A, HG_HEADS * HG_HEAD_V)),
        'a_lb_logits': nrm(ks[4], (N_A + 1, HG_FDIM), 0.5),
        'b_w_q': nrm(ks[5], (N_B, D, DA_QDIM), D ** -0.5),
        'b_w_o': nrm(ks[6], (N_B, DA_VDIM_TOTAL, D), DA_VDIM_TOTAL ** -0.5),
        'b_lam_q1': nrm(ks[7], (N_B, DA_HEAD_DIM), 0.1),
        'b_lam_k1': nrm(ks[8], (N_B, DA_HEAD_DIM), 0.1),
        'b_lam_q2': nrm(ks[9], (N_B, DA_HEAD_DIM), 0.1),
        'b_lam_k2': nrm(ks[10], (N_B, DA_HEAD_DIM), 0.1),
        'b_subln': gain(ks[11], (N_B, DA_V_DIM)),
        'kv_norm': gain(ks[12], (D,)),
        'kv_w': nrm(ks[13], (D, DA_QDIM + DA_VDIM_TOTAL), D ** -0.5),
        'rel_table': nrm(ks[14], (REL_BUCKETS, DA_HEADS), 0.5),
        'norm_mix': gain(ks[15], (DEPTH, D)),
        'norm_ffn': gain(ks[16], (DEPTH, D)),
        'ffn_w_up': nrm(ks[17], (DEPTH, D, 2 * D_FF), D ** -0.5),
        'ffn_conv_w': nrm(ks[18], (DEPTH, CONV_W, 2 * D_FF), CONV_W ** -0.5),
        'ffn_conv_b': nrm(ks[19], (DEPTH, 2 * D_FF), 0.02),
        'ffn_w_down': nrm(ks[20], (DEPTH, D_FF, D), D_FF ** -0.5),
        'final_norm': gain(ks[21], (D,)),
    }


def reference(x, a_w_in, a_w_out, a_gnorm, a_lb_logits, b_w_q, b_w_o, b_lam_q1, b_lam_k1,
              b_lam_q2, b_lam_k2, b_subln, kv_norm, kv_w, rel_table, norm_mix, norm_ffn,
              ffn_w_up, ffn_conv_w, ffn_conv_b, ffn_w_down, final_norm):
    lbs = jnp.cumsum(jax.nn.softmax(a_lb_logits.astype(jnp.float32), axis=0), axis=0)
    k_shared = None
    v_shared = None
    for li in range(DEPTH):
        if li == N_A:
            k_shared, v_shared = shared_kv(x, kv_norm, kv_w)
        h = rmsnorm(x, norm_mix[li])
        if li < N_A:
            x = x + hgrn2_mixer(h, a_w_in[li], a_w_out[li], a_gnorm[li], lbs[li])
        else:
            j = li - N_A
            lambda_init = 0.8 - 0.6 * math.exp(-0.3 * li)
            x = x + diff_attention(h, k_shared, v_shared, b_w_q[j], b_w_o[j], b_lam_q1[j], b_lam_k1[j],
                                   b_lam_q2[j], b_lam_k2[j], b_subln[j], rel_table, lambda_init)
        x = x + conv_ffn(rmsnorm(x, norm_ffn[li]), ffn_w_up[li], ffn_conv_w[li], ffn_conv_b[li], ffn_w_down[li])
    return rmsnorm(x, final_norm)
```

```python
import contextlib
import numpy as np
import concourse.bass as bass
import concourse.mybir as mybir

F32 = mybir.dt.float32
BF16 = mybir.dt.bfloat16
U8 = mybir.dt.uint8
ALU = mybir.AluOpType
AF = mybir.ActivationFunctionType
AX = mybir.AxisListType
DTSZ = {F32: 4, BF16: 2, U8: 1}

ENGS = ("pe", "act", "dve", "pool", "sp")
SEM_ROLL = 2048
DMA_K = 6


class Tok:
    __slots__ = ("writers", "readers", "psum")

    def __init__(self, psum=False):
        self.writers = []
        self.readers = []
        self.psum = psum


class Buf:
    __slots__ = ("name", "t", "tok")

    def __init__(self, name, t=None, tok=None):
        self.name = name
        self.t = t
        self.tok = tok if tok is not None else Tok()

    def __getitem__(self, idx):
        return self.t[idx]

    def alias(self, ap, name=None):
        return Buf(name or self.name, ap, self.tok)


class Op:
    __slots__ = ("eng", "fn", "deps", "is_dma", "flag", "seq", "dma_slot", "dma_val", "name", "inc")


class Arena:
    def __init__(self, nc, nbytes):
        self.big = nc.alloc_sbuf_tensor("arena", [128, nbytes], U8)
        self.size = nbytes
        self.pers = 0
        self.cur = 0
        self.in_phase = False
        self.peak = 0

    def start_phase(self):
        self.in_phase = True
        self.cur = self.pers

    def alloc(self, name, shape, dt, persistent=False):
        p = shape[0]
        n = int(np.prod(shape[1:])) * DTSZ[dt]
        n = (n + 63) // 64 * 64
        if persistent:
            assert not self.in_phase or self.cur == self.pers, "persistent alloc inside a phase"
            off = self.pers
            self.pers += n
            self.cur = self.pers
        else:
            off = self.cur
            self.cur += n
        self.peak = max(self.peak, self.cur)
        assert self.cur <= self.size, f"SBUF arena overflow allocating {name}: {self.cur} > {self.size}"
        ap = self.big[0:p, off:off + n if False else off + int(np.prod(shape[1:])) * DTSZ[dt]].bitcast(dt)
        if len(shape) == 3:
            ap = ap.rearrange("p (a b) -> p a b", a=shape[1])
        elif len(shape) == 4:
            ap = ap.rearrange("p (a b c) -> p a b c", a=shape[1], b=shape[2])
        return Buf(name, ap)


class Prog:
    def __init__(self, nc):
        self.nc = nc
        self.ops = {e: [] for e in ENGS}
        self.n_dma = {e: 0 for e in ENGS}
        self.all_ops = []

    def _add(self, eng, fn, reads, writes, pwrites, is_dma, name=None, inc=None, extra_deps=()):
        op = Op()
        op.eng, op.fn, op.is_dma, op.flag, op.seq, op.name = eng, fn, is_dma, False, None, name
        op.inc = inc if inc is not None else (16 if is_dma else 1)
        deps = list(extra_deps)
        wr_toks = set()
        for r in reads:
            deps.extend(r.tok.writers)
            if r.tok.psum:
                deps.extend(x for x in r.tok.readers if x.eng != eng)
        for w in list(writes) + list(pwrites):
            wr_toks.add(id(w.tok))
        for w in writes:
            deps.extend(w.tok.writers)
            deps.extend(w.tok.readers)
        for w in pwrites:
            deps.extend(w.tok.readers)
            if w.tok.readers:
                deps.extend(w.tok.writers)
        rw_writers = set()
        for x in list(reads) + list(writes):
            for d in x.tok.writers:
                rw_writers.add(id(d))
        out = []
        seen = set()
        for d in deps:
            if id(d) in seen or d is op:
                continue
            seen.add(id(d))
            if (not d.is_dma) and (not is_dma) and d.eng == eng:
                if eng == "pe":
                    continue
                if id(d) not in rw_writers:
                    continue
            out.append(d)
        op.deps = out
        for r in reads:
            r.tok.readers.append(op)
        for w in writes:
            w.tok.writers = [op]
            w.tok.readers = []
        for w in pwrites:
            if w.tok.readers:
                w.tok.writers = [op]
                w.tok.readers = []
            else:
                w.tok.writers.append(op)
        if is_dma:
            i = self.n_dma[eng]
            self.n_dma[eng] += 1
            op.dma_slot = i % DMA_K
            op.dma_val = i // DMA_K + 1
        self.ops[eng].append(op)
        self.all_ops.append(op)
        return op

    def op(self, eng, fn, reads=(), writes=(), pwrites=(), name=None):
        return self._add(eng, fn, reads, writes, pwrites, False, name)

    def dma(self, eng, out, in_, reads=(), writes=(), pwrites=(), **kw):
        def fn(e):
            return e.dma_start(out=out, in_=in_, **kw)
        return self._add(eng, fn, reads, writes, pwrites, True)

    def async_op(self, eng, fn, reads=(), writes=(), inc=1):
        return self._add(eng, fn, reads, writes, (), True, inc=inc)

    def barrier(self):
        lasts = []
        for e in ENGS:
            for op in reversed(self.ops[e]):
                if not op.is_dma:
                    lasts.append(op)
                    break
            dm = [op for op in self.ops[e] if op.is_dma][-DMA_K:]
            lasts.extend(dm)
        for e in ENGS:
            deps = [d for d in lasts if d.is_dma or d.eng != e]
            self._add(e, lambda en: en.nop(), (), (), (), False, "barrier", extra_deps=deps)

    def pid(self, e):
        k = id(e)
        if k not in self._pids:
            self._pids[k] = e.partition_id()
        return self._pids[k]

    def emit(self, final_bufs=()):
        self._pids = {}
        nc = self.nc
        for op in self.all_ops:
            for d in op.deps:
                d.flag = True
        finals = []
        for b in final_bufs:
            for w in b.tok.writers:
                w.flag = True
                finals.append(w)
        nflag = {}
        for e in ENGS:
            n = 0
            for op in self.ops[e]:
                if op.flag and not op.is_dma:
                    op.seq = n
                    n += 1
            nflag[e] = n
        self.nflag = nflag
        with contextlib.ExitStack() as st:
            csem = {}
            for e in ENGS:
                k = (nflag[e] + SEM_ROLL - 1) // SEM_ROLL
                csem[e] = [st.enter_context(nc.semaphore(f"c_{e}_{i}")) for i in range(max(k, 1))]
            dsem = {}
            for e in ENGS:
                if self.n_dma[e]:
                    dsem[e] = [st.enter_context(nc.semaphore(f"d_{e}_{i}")) for i in range(DMA_K)]
            cum = {e: [0] * DMA_K for e in ENGS}
            for e in ENGS:
                for op in self.ops[e]:
                    if op.is_dma:
                        cum[e][op.dma_slot] += op.inc
                        op.dma_val = cum[e][op.dma_slot]
            block = st.enter_context(nc.Block())

            def target(d):
                if d.is_dma:
                    return dsem[d.eng][d.dma_slot], d.dma_val
                return csem[d.eng][d.seq // SEM_ROLL], d.seq % SEM_ROLL + 1

            def run(ename, e):
                waited = {}
                for op in self.ops[ename]:
                    tg = [target(d) for d in op.deps]
                    if op.is_dma and op.dma_val - op.inc > 0:
                        tg.append((dsem[ename][op.dma_slot], op.dma_val - op.inc))
                    for s, v in tg:
                        key = id(s)
                        if waited.get(key, 0) >= v:
                            continue
                        waited[key] = v
                        e.wait_ge(s, v)
                    ins = op.fn(e)
                    if op.is_dma:
                        ins.then_inc(dsem[ename][op.dma_slot], op.inc)
                    elif op.flag:
                        ins.then_inc(csem[ename][op.seq // SEM_ROLL], 1)
                if ename == "sp":
                    for d in finals:
                        s, v = target(d)
                        e.wait_ge(s, v)

            @block.tensor
            def _(e):
                run("pe", e)

            @block.scalar
            def _(e):
                run("act", e)

            @block.vector
            def _(e):
                run("dve", e)

            @block.gpsimd
            def _(e):
                run("pool", e)

            @block.sync
            def _(e):
                run("sp", e)


NT = 2048
SW = 512
NS = NT // SW
CH = 64
NCH = NT // CH
CPS = SW // CH
EPS = 1e-6


class Ctx:
    pass


class KB:
    def __init__(self, nc, sbuf_bytes=206 * 1024):
        self.nc = nc
        self.P = Prog(nc)
        self.A = Arena(nc, sbuf_bytes)
        self.pb = [Buf(f"pb{i}", nc.alloc_psum_tensor(f"pb{i}", [128, 512], F32), Tok(psum=True)) for i in range(6)]
        self.pb2 = Buf("pb2", nc.alloc_psum_tensor("pbig", [128, 1024], F32), Tok(psum=True))

    def sb(self, name, shape, dt, pers=False):
        return self.A.alloc(name, shape, dt, persistent=pers)

    def ps(self, bank, shape, dt=F32):
        base = self.pb2 if bank == 6 else self.pb[bank]
        p = shape[0]
        n = 1
        for x in shape[1:]:
            n *= x
        nf32 = n * DTSZ[dt] // 4
        ap = base.t[0:p, 0:nf32]
        if dt != F32:
            ap = ap.bitcast(dt)
        if len(shape) == 3:
            ap = ap.rearrange("p (a b) -> p a b", a=shape[1])
        return base.alias(ap)

    def phase(self):
        self.P.barrier()
        self.A.start_phase()


def rmsnorm_fm(P, C, xs, g, out_ap, out_buf, sq, ps, rstd, width=SW):
    P.op("act", lambda e: e.activation(out=sq[:], in_=xs[:], func=AF.Square), reads=[xs], writes=[sq])
    for c in range(8):
        P.op("pe", lambda e, c=c: e.matmul(ps[:], lhsT=C.ones[:], rhs=sq[:, c, :], start=(c == 0), stop=(c == 7)),
             reads=[C.ones, sq], writes=[ps])
    P.op("act", lambda e: e.activation(out=rstd[:], in_=ps[:], func=AF.Sqrt, scale=1.0 / 1024, bias=C.eps[:]),
         reads=[ps, C.eps], writes=[rstd])
    P.op("dve", lambda e: e.reciprocal(out=rstd[:], in_=rstd[:]), reads=[rstd], writes=[rstd])
    for c in range(8):
        P.op("dve", lambda e, c=c: e.scalar_tensor_tensor(out=out_ap[:, c, :], in0=xs[:, c, :], scalar=g[:, c:c + 1],
                                                          in1=rstd[:], op0=ALU.mult, op1=ALU.mult),
             reads=[xs, g, rstd], pwrites=[out_buf])


def hgrn_consts(K, d):
    P = K.P
    C = Ctx()
    sb = lambda n, sh, dt: K.sb(n, sh, dt, pers=True)
    C.ones = sb("ones", [128, 128], BF16)
    P.op("pool", lambda e: e.memset(C.ones[:], 1.0), writes=[C.ones])
    C.eps = sb("eps", [128, 1], F32)
    P.op("pool", lambda e: e.memset(C.eps[:], EPS), writes=[C.eps])
    C.ident = sb("ident", [128, 128], BF16)
    P.dma("pool", C.ident[:], d["ident"], writes=[C.ident])
    C.resetm = sb("resetm", [128, SW], F32)
    P.dma("sp", C.resetm[:], d["resetm"], writes=[C.resetm])
    C.maskc = sb("maskc", [64, 8, 64], F32)
    P.dma("sp", C.maskc[:], d["maskc"], writes=[C.maskc])
    C.gmix = sb("gmix", [128, 8], F32)
    P.dma("sp", C.gmix[:], d["gmix0"], writes=[C.gmix])
    C.gn = sb("gn", [128, 8], F32)
    P.dma("sp", C.gn[:], d["gnorm"], writes=[C.gn])
    lbl = sb("lbl", [128, 2, 8], F32)
    P.dma("sp", lbl[:], d["lbl"], writes=[lbl])
    C.lb = sb("lb", [128, 8], F32)
    C.oml = sb("oml", [128, 8], F32)
    P.op("dve", lambda e: e.tensor_sub(out=C.lb[:], in0=lbl[:, 0, :], in1=lbl[:, 1, :]), reads=[lbl], writes=[C.lb])
    P.op("act", lambda e: e.activation(out=C.oml[:], in_=C.lb[:], func=AF.Sigmoid, scale=-1.0), reads=[C.lb], writes=[C.oml])
    P.op("act", lambda e: e.activation(out=C.lb[:], in_=C.lb[:], func=AF.Sigmoid), reads=[C.lb], writes=[C.lb])
    return C


def hgrn_alloc_T(K):
    T = Ctx()
    sb = lambda n, sh, dt: K.sb(n, sh, dt, pers=True)
    T.bl = sb("bl", [128, 8, NCH], F32)
    T.bmid = sb("bmid", [128, 8, NCH], F32)
    T.e1 = sb("e1", [128, 8, NCH], F32)
    T.e2 = sb("e2", [128, 8, NCH], F32)
    T.em = sb("em", [128, 8, NCH], F32)
    T.D = sb("D", [128, 8], F32)
    return T


def hgrn_P(K, C, d, scr, T):
    P, sb, ps = K.P, K.sb, K.ps
    xT3 = d["xT"].rearrange("(c p) t -> p c t", p=128)
    hT = sb("hT", [128, 8, NT], BF16)
    hTs = [Buf(f"hT{s}", hT.t[:, :, s * SW:(s + 1) * SW]) for s in range(NS)]
    xst = [sb("xst", [128, 8, SW], F32) for _ in range(2)]
    sq = sb("sq", [128, 8, SW], BF16)
    rstd = sb("rstd", [128, SW], F32)
    ps_n = ps(0, [128, SW])
    for s in range(NS):
        xs = xst[s % 2]
        P.dma("sp", xs[:], xT3[:, :, s * SW:(s + 1) * SW], writes=[xs])
        rmsnorm_fm(P, C, xs, C.gmix, hTs[s], hTs[s], sq, ps_n, rstd)

    wi = sb("wi", [128, 8, 1024], BF16)
    for h in range(8):
        P.dma("pool", wi[:, :, h * 128:(h + 1) * 128], d["w_in_r"][16 + h], pwrites=[wi])
    ps_v = [ps(1, [64, 512]), ps(2, [64, 512])]
    vst = [sb("vst", [64, CPS, 1024], BF16) for _ in range(2)]
    v3 = scr["v"].rearrange("(c s) j -> s c j", s=64)
    for s in range(NS):
        vs = vst[s % 2]
        for c in range(CPS):
            cg = s * CPS + c
            for hf in range(2):
                pv = ps_v[hf]
                for m in range(8):
                    P.op("pe", lambda e, m=m, cg=cg, hf=hf, pv=pv: e.matmul(
                        pv[:], lhsT=hT[:, m, cg * CH:(cg + 1) * CH], rhs=wi[:, m, hf * 512:(hf + 1) * 512],
                        start=(m == 0), stop=(m == 7)), reads=[hTs[s], wi], writes=[pv])
                if hf == 0:
                    P.op("act", lambda e, c=c, hf=hf, pv=pv, vs=vs: e.copy(out=vs[:, c, hf * 512:(hf + 1) * 512], in_=pv[:]),
                         reads=[pv], pwrites=[vs])
                else:
                    P.op("dve", lambda e, c=c, hf=hf, pv=pv, vs=vs: e.tensor_copy(out=vs[:, c, hf * 512:(hf + 1) * 512], in_=pv[:]),
                         reads=[pv], pwrites=[vs])
        P.dma("sp", v3[:, s * CPS:(s + 1) * CPS, :], vs[:], reads=[vs], pwrites=[scr["v_tok"]])

    wq = [sb("wq", [128, 8, 128], BF16) for _ in range(2)]
    wf = [sb("wf", [128, 8, 128], BF16) for _ in range(2)]
    wg = [sb("wg", [128, 8, 128], BF16) for _ in range(2)]
    ps_f = ps(3, [128, SW])
    ps_q = ps(4, [128, SW])
    ps_g = ps(5, [128, SW])
    ps_t = ps(0, [64, CPS, 128], BF16)
    sig = sb("sig", [128, SW], F32)
    logf = sb("logf", [128, SW], F32)
    nsig = sb("nsig", [128, SW], F32)
    b3 = sb("b3", [128, CPS, CH], F32)
    bm = sb("bm", [128, CPS, CH], F32)
    Ep = sb("Ep", [128, SW], F32)
    Em = sb("Em", [128, SW], F32)
    sqf = sb("sqf", [128, SW], F32)
    qst = [sb("qst", [128, SW], BF16) for _ in range(2)]
    kst = [sb("kst", [128, SW], BF16) for _ in range(2)]
    gst = [sb("gst", [128, SW], BF16) for _ in range(2)]
    ktst = [sb("ktst", [64, CPS, 128], BF16) for _ in range(2)]
    b_flat = b3.t.rearrange("p c t -> p (c t)")
    bm_flat = bm.t.rearrange("p c t -> p (c t)")
    kt3 = scr["kt"].rearrange("(c s) j -> s c j", s=64)
    it = 0
    for h in range(8):
        wqh, wfh, wgh = wq[h % 2], wf[h % 2], wg[h % 2]
        P.dma("pool", wfh[:], d["w_in_r"][8 + h], writes=[wfh])
        P.dma("pool", wqh[:], d["w_in_r"][0 + h], writes=[wqh])
        P.dma("pool", wgh[:], d["w_in_r"][24 + h], writes=[wgh])
        for s in range(NS):
            hs = hTs[s]
            sl = slice(s * SW, (s + 1) * SW)
            for (pp, ww) in ((ps_f, wfh), (ps_q, wqh), (ps_g, wgh)):
                for m in range(8):
                    P.op("pe", lambda e, m=m, pp=pp, ww=ww, sl=sl: e.matmul(
                        pp[:], lhsT=ww[:, m, :], rhs=hT[:, m, sl], start=(m == 0), stop=(m == 7)),
                        reads=[ww, hs], writes=[pp])
            q_o, k_o, g_o, kt_o = qst[it % 2], kst[it % 2], gst[it % 2], ktst[it % 2]
            it += 1
            P.op("act", lambda e: e.activation(out=sig[:], in_=ps_f[:], func=AF.Sigmoid), reads=[ps_f], writes=[sig])
            P.op("act", lambda e, h=h: e.activation(out=logf[:], in_=sig[:], func=AF.Ln, scale=C.oml[:, h:h + 1],
                                                    bias=C.lb[:, h:h + 1]), reads=[sig, C.oml, C.lb], writes=[logf])
            P.op("dve", lambda e: e.tensor_scalar(out=nsig[:], in0=sig[:], scalar1=-1.0, scalar2=1.0, op0=ALU.mult,
                                                  op1=ALU.add), reads=[sig], writes=[nsig])
            P.op("dve", lambda e: e.tensor_tensor_scan(out=b_flat, data0=C.resetm[:], data1=logf[:], initial=0.0,
                                                       op0=ALU.mult, op1=ALU.add), reads=[C.resetm, logf], writes=[b3])
            P.op("dve", lambda e, h=h, s=s: e.tensor_copy(out=T.bl[:, h, s * CPS:(s + 1) * CPS], in_=b3[:, :, CH - 1]),
                 reads=[b3], pwrites=[T.bl])
            P.op("dve", lambda e, h=h, s=s: e.tensor_copy(out=T.bmid[:, h, s * CPS:(s + 1) * CPS], in_=b3[:, :, CH // 2 - 1]),
                 reads=[b3], pwrites=[T.bmid])
            P.op("dve", lambda e: e.tensor_tensor(out=bm[:], in0=b3[:], in1=b3[:, :, CH // 2 - 1:CH // 2].to_broadcast([128, CPS, CH]),
                                                  op=ALU.subtract), reads=[b3], writes=[bm])
            P.op("act", lambda e: e.activation(out=Ep[:], in_=bm_flat, func=AF.Exp), reads=[bm], writes=[Ep])
            P.op("act", lambda e: e.activation(out=Em[:], in_=bm_flat, func=AF.Exp, scale=-1.0), reads=[bm], writes=[Em])
            P.op("act", lambda e: e.activation(out=sqf[:], in_=ps_q[:], func=AF.Silu), reads=[ps_q], writes=[sqf])
            P.op("act", lambda e, g_o=g_o: e.activation(out=g_o[:], in_=ps_g[:], func=AF.Silu), reads=[ps_g], writes=[g_o])
            P.op("dve", lambda e, q_o=q_o: e.tensor_tensor(out=q_o[:], in0=sqf[:], in1=Ep[:], op=ALU.mult),
                 reads=[sqf, Ep], writes=[q_o])
            P.op("dve", lambda e, k_o=k_o, h=h: e.scalar_tensor_tensor(out=k_o[:], in0=nsig[:], scalar=C.oml[:, h:h + 1],
                                                                       in1=Em[:], op0=ALU.mult, op1=ALU.mult),
                 reads=[nsig, C.oml, Em], writes=[k_o])
            hsl = slice(h * 128, (h + 1) * 128)
            P.dma("sp", scr["qT"][hsl, sl], q_o[:], reads=[q_o], pwrites=[scr["qT_tok"]])
            P.dma("sp", scr["kT"][hsl, sl], k_o[:], reads=[k_o], pwrites=[scr["kT_tok"]])
            P.dma("sp", scr["gT"][hsl, sl], g_o[:], reads=[g_o], pwrites=[scr["gT_tok"]])
            for c in range(CPS):
                P.op("pe", lambda e, c=c, k_o=k_o: e.transpose(out=ps_t[:, c, :], in_=k_o[:, c * CH:(c + 1) * CH],
                                                              identity=C.ident[:]), reads=[k_o, C.ident], writes=[ps_t])
            P.op("act", lambda e, kt_o=kt_o: e.copy(out=kt_o[:], in_=ps_t[:]), reads=[ps_t], writes=[kt_o])
            P.dma("sp", kt3[:, s * CPS:(s + 1) * CPS, hsl], kt_o[:], reads=[kt_o], pwrites=[scr["kt_tok"]])
    P.op("act", lambda e: e.activation(out=T.e1[:], in_=T.bl[:], func=AF.Exp), reads=[T.bl], writes=[T.e1])
    P.op("act", lambda e: e.activation(out=T.em[:], in_=T.bmid[:], func=AF.Exp), reads=[T.bmid], writes=[T.em])
    P.op("dve", lambda e: e.tensor_sub(out=T.e2[:], in0=T.bl[:], in1=T.bmid[:]), reads=[T.bl, T.bmid], writes=[T.e2])
    P.op("act", lambda e: e.activation(out=T.e2[:], in_=T.e2[:], func=AF.Exp), reads=[T.e2], writes=[T.e2])
    P.op("dve", lambda e: e.reduce_sum(out=T.D[:], in_=T.bl[:], axis=AX.X), reads=[T.bl], writes=[T.D])
    P.op("act", lambda e: e.activation(out=T.D[:], in_=T.D[:], func=AF.Exp), reads=[T.D], writes=[T.D])


def hgrn_R(K, C, d, scr, T, full, S):
    P, sb, ps = K.P, K.sb, K.ps
    q3 = scr["qT"].rearrange("(h p) t -> p h t", p=128)
    k3 = scr["kT"].rearrange("(h p) t -> p h t", p=128)
    g3 = scr["gT"].rearrange("(h p) t -> p h t", p=128)
    v3 = scr["v"].rearrange("(c s) j -> s c j", s=64)
    kt3 = scr["kt"].rearrange("(c s) j -> s c j", s=64)
    v_sb = [sb("v_sb", [64, CPS, 1024], BF16) for _ in range(2)]
    kt_sb = [sb("kt_sb", [64, CPS, 1024], BF16) for _ in range(1)]
    ps_dS = ps(6, [128, 8, 128])
    tmp = sb("tmpS", [128, 8, 128], F32)
    if full:
        q_sb = [sb("q_sb", [128, 8, SW], BF16) for _ in range(2)]
        k_sb = [sb("k_sb", [128, 8, SW], BF16) for _ in range(2)]
        g_sb = [sb("g_sb", [128, 8, SW], BF16) for _ in range(1)]
        ps_sc = [ps(0, [64, 8, 64]), ps(1, [64, 8, 64])]
        ps_o = [ps(2, [128, 8, 64]), ps(3, [128, 8, 64])]
        sc_sb = [sb("sc_sb", [64, 8, 64], BF16) for _ in range(2)]
        Sb = [sb("Sb", [128, 8, 128], BF16) for _ in range(2)]
        oT = sb("oT", [128, 8, SW], F32)
        osq = sb("osq", [128, 8, SW], BF16)
        ogT = sb("ogT", [128, 8, SW], BF16)
        t1 = sb("t1", [128, SW], F32)
        rstd = sb("rstd", [128, SW], F32)
        wout = sb("wout", [128, 8, 1024], BF16)
        P.dma("pool", wout[:], d["w_out_r"], writes=[wout])
        ps_y = [ps(4, [128, SW]), ps(5, [128, SW])]
        ps_n = ps(4, [128, SW])
        xT3 = d["xT"].rearrange("(c p) t -> p c t", p=128)
        x1T3 = d["x1T"].rearrange("(c p) t -> p c t", p=128)
        xst = [sb("xst", [128, 8, SW], F32) for _ in range(1)]
    for s in range(NS):
        sl = slice(s * SW, (s + 1) * SW)
        vs, kts = v_sb[s % 2], kt_sb[0]
        P.dma("sp", vs[:], v3[:, s * CPS:(s + 1) * CPS, :], reads=[scr["v_tok"]], writes=[vs])
        P.dma("sp", kts[:], kt3[:, s * CPS:(s + 1) * CPS, :], reads=[scr["kt_tok"]], writes=[kts])
        if full:
            qs, ks, gs = q_sb[s % 2], k_sb[s % 2], g_sb[0]
            P.dma("sp", qs[:], q3[:, :, sl], reads=[scr["qT_tok"]], writes=[qs])
            P.dma("sp", ks[:], k3[:, :, sl], reads=[scr["kT_tok"]], writes=[ks])
            P.dma("sp", gs[:], g3[:, :, sl], reads=[scr["gT_tok"]], writes=[gs])
            xs = xst[0]
            P.dma("sp", xs[:], xT3[:, :, sl], writes=[xs])
        for c in range(CPS):
            cg = s * CPS + c
            csl = slice(c * CH, (c + 1) * CH)
            if full:
                psc, pso, scb, Sbb = ps_sc[cg % 2], ps_o[cg % 2], sc_sb[cg % 2], Sb[cg % 2]
                for h in range(8):
                    P.op("pe", lambda e, h=h, psc=psc, ks=ks, qs=qs, csl=csl: e.matmul(
                        psc[:, h, :], lhsT=ks[:, h, csl], rhs=qs[:, h, csl], start=True, stop=True),
                        reads=[ks, qs], writes=[psc])
                P.op("dve", lambda e, psc=psc, scb=scb: e.tensor_tensor(out=scb[:], in0=psc[:], in1=C.maskc[:], op=ALU.mult),
                     reads=[psc, C.maskc], writes=[scb])
                P.op("pool", lambda e, Sbb=Sbb, cg=cg: e.tensor_tensor(
                    out=Sbb[:], in0=S[:], in1=T.em[:, :, cg:cg + 1].to_broadcast([128, 8, 128]), op=ALU.mult),
                    reads=[S, T.em], writes=[Sbb])
                for h in range(8):
                    hsl = slice(h * 128, (h + 1) * 128)
                    P.op("pe", lambda e, h=h, pso=pso, Sbb=Sbb, qs=qs, csl=csl: e.matmul(
                        pso[:, h, :], lhsT=Sbb[:, h, :], rhs=qs[:, h, csl], start=True, stop=False),
                        reads=[Sbb, qs], writes=[pso])
                    P.op("pe", lambda e, h=h, pso=pso, vs=vs, scb=scb, c=c, hsl=hsl: e.matmul(
                        pso[:, h, :], lhsT=vs[:, c, hsl], rhs=scb[:, h, :], start=False, stop=True),
                        reads=[vs, scb], writes=[pso])
                P.op("act", lambda e, pso=pso, csl=csl: e.copy(out=oT[:, :, csl], in_=pso[:]), reads=[pso], pwrites=[oT])
            for h in range(8):
                hsl = slice(h * 128, (h + 1) * 128)
                P.op("pe", lambda e, h=h, kts=kts, vs=vs, c=c, hsl=hsl: e.matmul(
                    ps_dS[:, h, :], lhsT=kts[:, c, hsl], rhs=vs[:, c, hsl], start=True, stop=True),
                    reads=[kts, vs], writes=[ps_dS])
            P.op("dve", lambda e, cg=cg: e.tensor_tensor(
                out=tmp[:], in0=ps_dS[:], in1=T.e2[:, :, cg:cg + 1].to_broadcast([128, 8, 128]), op=ALU.mult),
                reads=[ps_dS, T.e2], writes=[tmp])
            P.op("pool", lambda e, cg=cg: e.tensor_tensor(
                out=S[:], in0=S[:], in1=T.e1[:, :, cg:cg + 1].to_broadcast([128, 8, 128]), op=ALU.mult),
                reads=[S, T.e1], writes=[S])
            P.op("dve", lambda e: e.tensor_tensor(out=S[:], in0=S[:], in1=tmp[:], op=ALU.add), reads=[S, tmp], writes=[S])
        if full:
            P.op("act", lambda e: e.activation(out=osq[:], in_=oT[:], func=AF.Square), reads=[oT], writes=[osq])
            for h in range(8):
                P.op("pe", lambda e, h=h: e.matmul(ps_n[:], lhsT=C.ones[:], rhs=osq[:, h, :], start=(h == 0), stop=(h == 7)),
                     reads=[C.ones, osq], writes=[ps_n])
            P.op("act", lambda e: e.activation(out=rstd[:], in_=ps_n[:], func=AF.Sqrt, scale=1.0 / 1024, bias=C.eps[:]),
                 reads=[ps_n, C.eps], writes=[rstd])
            P.op("dve", lambda e: e.reciprocal(out=rstd[:], in_=rstd[:]), reads=[rstd], writes=[rstd])
            for h in range(8):
                P.op("dve", lambda e, h=h: e.scalar_tensor_tensor(out=t1[:], in0=oT[:, h, :], scalar=C.gn[:, h:h + 1],
                                                                  in1=rstd[:], op0=ALU.mult, op1=ALU.mult),
                     reads=[oT, C.gn, rstd], writes=[t1])
                P.op("dve", lambda e, h=h, gs=gs: e.tensor_tensor(out=ogT[:, h, :], in0=t1[:], in1=gs[:, h, :], op=ALU.mult),
                     reads=[t1, gs], pwrites=[ogT])
            for j in range(8):
                py = ps_y[j % 2]
                for h in range(8):
                    P.op("pe", lambda e, h=h, j=j, py=py: e.matmul(py[:], lhsT=wout[:, h, j * 128:(j + 1) * 128],
                                                                  rhs=ogT[:, h, :], start=(h == 0), stop=(h == 7)),
                         reads=[wout, ogT], writes=[py])
                P.op("dve", lambda e, j=j, py=py, xs=xs: e.tensor_tensor(out=xs[:, j, :], in0=py[:], in1=xs[:, j, :], op=ALU.add),
                     reads=[py, xs], pwrites=[xs])
            P.dma("sp", x1T3[:, :, sl], xs[:], reads=[xs], pwrites=[d["x1T_tok"]])
            if s == NS - 1 and "halo_out" in d:
                P.dma("sp", d["halo_out"].rearrange("p (c t) -> p c t", c=8), xs[:, :, SW - 2:SW], reads=[xs],
                      writes=[d["halo_out_tok"]])


NFF = 22


def ffn_layer(K, C, d, li, xin, xin_tok, xout, xout_tok, halo_all, halo_tok, final_norm=None):
    P, sb, ps = K.P, K.sb, K.ps
    K.phase()
    xT3 = xin.rearrange("(c p) t -> p c t", p=128)
    gf = sb("gf", [128, 8], F32)
    P.dma("sp", gf[:], d[f"gffn{li}"], writes=[gf])
    convp = sb("convp", [128, 2 * NFF, 4], F32)
    P.dma("sp", convp[:], d[f"convp{li}"], writes=[convp])
    nf = sb("nf", [128, 1], F32)
    P.dma("sp", nf[:], d["notfirst"], writes=[nf])
    aT = sb("aT", [128, NFF, NT], BF16)
    aTs = [Buf(f"aT{s}", aT.t[:, :, s * SW:(s + 1) * SW]) for s in range(NS)]
    mark = K.A.cur
    hT = sb("h2T", [128, 8, NT], BF16)
    hTs = [Buf(f"h2T{s}", hT.t[:, :, s * SW:(s + 1) * SW]) for s in range(NS)]
    xst = [sb("xst", [128, 8, SW], F32) for _ in range(1)]
    sq = sb("sq", [128, 8, SW], BF16)
    rstd = sb("rstd", [128, SW], F32)
    ps_n = ps(0, [128, SW])
    for s in range(NS):
        xs = xst[0]
        P.dma("sp", xs[:], xT3[:, :, s * SW:(s + 1) * SW], reads=[xin_tok], writes=[xs])
        rmsnorm_fm(P, C, xs, gf, hTs[s], hTs[s], sq, ps_n, rstd)
    xh = sb("xh", [128, 8, 2], F32)

    def dyn(e):
        pid = P.pid(e)
        prev = (pid + 7) % 8
        return e.dma_start(out=xh[:], in_=halo_all[bass.ds(prev * 128, 128), :].rearrange("p (c t) -> p c t", c=8))
    P._add("sp", dyn, [halo_tok], [xh], (), True)
    sqh = sb("sqh", [128, 8, 2], BF16)
    rsh = sb("rsh", [128, 2], F32)
    hh = sb("hh", [128, 8, 2], BF16)
    ps_h = ps(1, [128, 2])
    P.op("act", lambda e: e.activation(out=sqh[:], in_=xh[:], func=AF.Square), reads=[xh], writes=[sqh])
    for c in range(8):
        P.op("pe", lambda e, c=c: e.matmul(ps_h[:], lhsT=C.ones[:], rhs=sqh[:, c, :], start=(c == 0), stop=(c == 7)),
             reads=[C.ones, sqh], writes=[ps_h])
    P.op("act", lambda e: e.activation(out=rsh[:], in_=ps_h[:], func=AF.Sqrt, scale=1.0 / 1024, bias=C.eps[:]),
         reads=[ps_h, C.eps], writes=[rsh])
    P.op("dve", lambda e: e.reciprocal(out=rsh[:], in_=rsh[:]), reads=[rsh], writes=[rsh])
    P.op("dve", lambda e: e.tensor_scalar(out=rsh[:], in0=rsh[:], scalar1=nf[:, 0:1], scalar2=None, op0=ALU.mult),
         reads=[rsh, nf], writes=[rsh])
    for c in range(8):
        P.op("dve", lambda e, c=c: e.scalar_tensor_tensor(out=hh[:, c, :], in0=xh[:, c, :], scalar=gf[:, c:c + 1],
                                                          in1=rsh[:], op0=ALU.mult, op1=ALU.mult),
             reads=[xh, gf, rsh], pwrites=[hh])

    wu = [[sb("wu", [128, 8, 128], BF16) for _ in range(2)] for _ in range(2)]
    ug = [sb("ug", [128, SW + 2], F32) for _ in range(2)]
    uv = [sb("uv", [128, SW + 2], F32) for _ in range(2)]
    ag = [sb("ag", [128, SW], F32) for _ in range(2)]
    av = [sb("av", [128, SW], F32) for _ in range(2)]
    sg = [sb("sg", [128, SW], F32) for _ in range(2)]
    ps_g = [ps(2, [128, SW]), ps(3, [128, SW])]
    ps_v = [ps(4, [128, SW]), ps(5, [128, SW])]
    ps_hh = ps(1, [128, 2, 2])
    it = 0
    for c in range(NFF):
        wg_, wv_ = wu[c % 2]
        P.dma("pool", wg_[:], d[f"w_up_r{li}"][c], writes=[wg_])
        P.dma("pool", wv_[:], d[f"w_up_r{li}"][c + NFF], writes=[wv_])
        for s in range(NS):
            sl = slice(s * SW, (s + 1) * SW)
            cur, prv = it % 2, (it + 1) % 2
            it += 1
            pg, pv = ps_g[cur], ps_v[cur]
            ugc, uvc, agc, avc, sgc = ug[cur], uv[cur], ag[cur], av[cur], sg[cur]
            for (pp, ww) in ((pg, wg_), (pv, wv_)):
                for m in range(8):
                    P.op("pe", lambda e, m=m, pp=pp, ww=ww, sl=sl: e.matmul(
                        pp[:], lhsT=ww[:, m, :], rhs=hT[:, m, sl], start=(m == 0), stop=(m == 7)),
                        reads=[ww, hTs[s]], writes=[pp])
            if s == 0:
                for gi, ww in ((0, wg_), (1, wv_)):
                    for m in range(8):
                        P.op("pe", lambda e, m=m, gi=gi, ww=ww: e.matmul(
                            ps_hh[:, gi, :], lhsT=ww[:, m, :], rhs=hh[:, m, :], start=(m == 0), stop=(m == 7)),
                            reads=[ww, hh], writes=[ps_hh])
                P.op("dve", lambda e, ugc=ugc: e.tensor_copy(out=ugc[:, 0:2], in_=ps_hh[:, 0, :]), reads=[ps_hh], pwrites=[ugc])
                P.op("dve", lambda e, uvc=uvc: e.tensor_copy(out=uvc[:, 0:2], in_=ps_hh[:, 1, :]), reads=[ps_hh], pwrites=[uvc])
            else:
                P.op("dve", lambda e, ugc=ugc, p_=ug[prv]: e.tensor_copy(out=ugc[:, 0:2], in_=p_[:, SW:SW + 2]),
                     reads=[ug[prv]], pwrites=[ugc])
                P.op("pool", lambda e, uvc=uvc, p_=uv[prv]: e.tensor_copy(out=uvc[:, 0:2], in_=p_[:, SW:SW + 2]),
                     reads=[uv[prv]], pwrites=[uvc])
            cg, cv = c, c + NFF
            P.op("act", lambda e, ugc=ugc, pg=pg: e.copy(out=ugc[:, 2:SW + 2], in_=pg[:]), reads=[pg], pwrites=[ugc])
            P.op("act", lambda e, agc=agc, pg=pg, cg=cg: e.activation(out=agc[:], in_=pg[:], func=AF.Identity,
                                                                      scale=convp[:, cg, 2:3], bias=convp[:, cg, 3:4]),
                 reads=[pg, convp], writes=[agc])
            P.op("act", lambda e, uvc=uvc, pv=pv: e.copy(out=uvc[:, 2:SW + 2], in_=pv[:]), reads=[pv], pwrites=[uvc])
            P.op("act", lambda e, avc=avc, pv=pv, cv=cv: e.activation(out=avc[:], in_=pv[:], func=AF.Identity,
                                                                      scale=convp[:, cv, 2:3], bias=convp[:, cv, 3:4]),
                 reads=[pv, convp], writes=[avc])
            P.op("dve", lambda e, agc=agc, ugc=ugc, cg=cg: e.scalar_tensor_tensor(
                out=agc[:], in0=ugc[:, 1:SW + 1], scalar=convp[:, cg, 1:2], in1=agc[:], op0=ALU.mult, op1=ALU.add),
                reads=[ugc, convp, agc], writes=[agc])
            P.op("dve", lambda e, agc=agc, ugc=ugc, cg=cg: e.scalar_tensor_tensor(
                out=agc[:], in0=ugc[:, 0:SW], scalar=convp[:, cg, 0:1], in1=agc[:], op0=ALU.mult, op1=ALU.add),
                reads=[ugc, convp, agc], writes=[agc])
            P.op("dve", lambda e, avc=avc, uvc=uvc, cv=cv: e.scalar_tensor_tensor(
                out=avc[:], in0=uvc[:, 1:SW + 1], scalar=convp[:, cv, 1:2], in1=avc[:], op0=ALU.mult, op1=ALU.add),
                reads=[uvc, convp, avc], writes=[avc])
            P.op("dve", lambda e, avc=avc, uvc=uvc, cv=cv: e.scalar_tensor_tensor(
                out=avc[:], in0=uvc[:, 0:SW], scalar=convp[:, cv, 0:1], in1=avc[:], op0=ALU.mult, op1=ALU.add),
                reads=[uvc, convp, avc], writes=[avc])
            P.op("act", lambda e, sgc=sgc, agc=agc: e.activation(out=sgc[:], in_=agc[:], func=AF.Silu), reads=[agc], writes=[sgc])
            P.op("dve", lambda e, sgc=sgc, avc=avc, c=c, sl=sl: e.tensor_tensor(out=aT[:, c, sl], in0=sgc[:], in1=avc[:], op=ALU.mult),
                 reads=[sgc, avc], pwrites=[aTs[s]])

    P.barrier()
    K.A.cur = mark
    wd = [sb("wd", [128, NFF, 128], BF16) for _ in range(2)]
    xj = [sb("xj", [128, SW], F32) for _ in range(3)]
    ps_y = [ps(0, [128, SW]), ps(1, [128, SW])]
    xin3 = xin.rearrange("(c p) t -> p c t", p=128)
    xout3 = xout.rearrange("(c p) t -> p c t", p=128)
    it = 0
    for j in range(8):
        wdj = wd[j % 2]
        P.dma("pool", wdj[:], d[f"w_down_r{li}"][j], writes=[wdj])
        for s in range(NS):
            sl = slice(s * SW, (s + 1) * SW)
            py = ps_y[it % 2]
            xs = xj[it % 3]
            it += 1
            P.dma("sp", xs[:], xin3[:, j, sl], reads=[xin_tok], writes=[xs])
            for c in range(NFF):
                P.op("pe", lambda e, c=c, py=py, wdj=wdj, sl=sl: e.matmul(
                    py[:], lhsT=wdj[:, c, :], rhs=aT[:, c, sl], start=(c == 0), stop=(c == NFF - 1)),
                    reads=[wdj, aTs[s]], writes=[py])
            P.op("dve", lambda e, py=py, xs=xs: e.tensor_tensor(out=xs[:], in0=py[:], in1=xs[:], op=ALU.add),
                 reads=[py, xs], writes=[xs])
            P.dma("sp", xout3[:, j, sl], xs[:], reads=[xs], pwrites=[xout_tok])


def final_norm(K, C, d, xin, xin_tok, out, out_tok):
    P, sb, ps = K.P, K.sb, K.ps
    K.phase()
    gfin = sb("gfin", [128, 8], F32)
    P.dma("sp", gfin[:], d["gfinal"], writes=[gfin])
    xT3 = xin.rearrange("(c p) t -> p c t", p=128)
    o3 = out.rearrange("(c p) t -> p c t", p=128)
    xst = [sb("xst", [128, 8, SW], F32) for _ in range(2)]
    ost = [sb("ost", [128, 8, SW], F32) for _ in range(2)]
    sq = sb("sq", [128, 8, SW], BF16)
    rstd = sb("rstd", [128, SW], F32)
    ps_n = ps(0, [128, SW])
    for s in range(NS):
        xs, os_ = xst[s % 2], ost[s % 2]
        P.dma("sp", xs[:], xT3[:, :, s * SW:(s + 1) * SW], reads=[xin_tok], writes=[xs])
        rmsnorm_fm(P, C, xs, gfin, os_, os_, sq, ps_n, rstd)
        P.dma("sp", o3[:, :, s * SW:(s + 1) * SW], os_[:], reads=[os_], pwrites=[out_tok])


import math

T_ALL = 16384
NB = T_ALL // 128
NQS = T_ALL // SW
LAM_INIT = 0.8 - 0.6 * math.exp(-0.3 * 1)
NEG = -30000.0
GLEN = 1151


def kvq_proj(K, C, d, xin, xin_tok, qkv_in, qkv_tok):
    P, sb, ps = K.P, K.sb, K.ps
    K.phase()
    xT3 = xin.rearrange("(c p) t -> p c t", p=128)
    gkv = sb("gkv", [128, 8], F32)
    gq = sb("gq", [128, 8], F32)
    P.dma("sp", gkv[:], d["gkv"], writes=[gkv])
    P.dma("sp", gq[:], d["gmix1"], writes=[gq])
    hk = sb("hk", [128, 8, NT], BF16)
    hq = sb("hq", [128, 8, NT], BF16)
    hks = [Buf(f"hk{s}", hk.t[:, :, s * SW:(s + 1) * SW]) for s in range(NS)]
    hqs = [Buf(f"hq{s}", hq.t[:, :, s * SW:(s + 1) * SW]) for s in range(NS)]
    xst = sb("xst", [128, 8, SW], F32)
    sq = sb("sq", [128, 8, SW], BF16)
    rstd = sb("rstd", [128, SW], F32)
    ps_n = ps(0, [128, SW])
    for s in range(NS):
        P.dma("sp", xst[:], xT3[:, :, s * SW:(s + 1) * SW], reads=[xin_tok], writes=[xst])
        rmsnorm_fm(P, C, xst, gkv, hks[s], hks[s], sq, ps_n, rstd)
        for c in range(8):
            P.op("dve", lambda e, c=c, s=s: e.scalar_tensor_tensor(out=hqs[s][:, c, :], in0=xst[:, c, :], scalar=gq[:, c:c + 1],
                                                                    in1=rstd[:], op0=ALU.mult, op1=ALU.mult),
                 reads=[xst, gq, rstd], pwrites=[hqs[s]])
    wk = [sb("wk", [128, 8, 128], BF16) for _ in range(2)]
    wq = [sb("wq", [128, 8, 128], BF16) for _ in range(2)]
    kst = [sb("kst", [128, NT], BF16) for _ in range(2)]
    qst = [sb("qst", [128, NT], BF16) for _ in range(2)]
    psk = [ps(1, [128, SW]), ps(2, [128, SW])]
    psq = [ps(3, [128, SW]), ps(4, [128, SW])]
    it = 0
    for h in range(8):
        wkh, wqh, ks_, qs_ = wk[h % 2], wq[h % 2], kst[h % 2], qst[h % 2]
        P.dma("pool", wkh[:], d["w_k_r"][h], writes=[wkh])
        P.dma("pool", wqh[:], d["w_q_r"][h], writes=[wqh])
        for s in range(NS):
            sl = slice(s * SW, (s + 1) * SW)
            pk, pq = psk[it % 2], psq[it % 2]
            it += 1
            for m in range(8):
                P.op("pe", lambda e, m=m, pk=pk, wkh=wkh, sl=sl: e.matmul(pk[:], lhsT=wkh[:, m, :], rhs=hk[:, m, sl],
                                                                          start=(m == 0), stop=(m == 7)),
                     reads=[wkh, hks[s]], writes=[pk])
            for m in range(8):
                P.op("pe", lambda e, m=m, pq=pq, wqh=wqh, sl=sl: e.matmul(pq[:], lhsT=wqh[:, m, :], rhs=hq[:, m, sl],
                                                                          start=(m == 0), stop=(m == 7)),
                     reads=[wqh, hqs[s]], writes=[pq])
            P.op("act", lambda e, pk=pk, ks_=ks_, sl=sl: e.copy(out=ks_[:, sl], in_=pk[:]), reads=[pk], pwrites=[ks_])
            P.op("dve", lambda e, pq=pq, qs_=qs_, sl=sl: e.tensor_scalar(out=qs_[:, sl], in0=pq[:], scalar1=0.125, scalar2=None,
                                                                         op0=ALU.mult), reads=[pq], pwrites=[qs_])
        P.dma("sp", qkv_in[h * 384:h * 384 + 128, :], qs_[:], reads=[qs_], pwrites=[qkv_tok])
        P.dma("sp", qkv_in[h * 384 + 128:h * 384 + 256, :], ks_[:], reads=[ks_], pwrites=[qkv_tok])
    wv = sb("wv", [128, 8, 1024], BF16)
    for h in range(8):
        P.dma("pool", wv[:, :, h * 128:(h + 1) * 128], d["w_v_r"][h], pwrites=[wv])
    vstage = sb("vstage", [128, 8, 16, 128], BF16)
    psv = [ps(1, [128, 4, 128]), ps(2, [128, 4, 128])]
    psv_flat = [ps(1, [128, 512]), ps(2, [128, 512])]
    for tb in range(16):
        s = tb // 4
        for hf in range(2):
            pv = psv_flat[hf]
            for m in range(8):
                P.op("pe", lambda e, m=m, pv=pv, tb=tb, hf=hf: e.matmul(
                    pv[:], lhsT=hk[:, m, tb * 128:(tb + 1) * 128], rhs=wv[:, m, hf * 512:(hf + 1) * 512],
                    start=(m == 0), stop=(m == 7)), reads=[hks[s], wv], writes=[pv])
            if hf == 0:
                P.op("act", lambda e, tb=tb, hf=hf: e.copy(out=vstage[:, hf * 4:(hf + 1) * 4, tb, :], in_=psv[hf][:]),
                     reads=[psv[hf]], pwrites=[vstage])
            else:
                P.op("dve", lambda e, tb=tb, hf=hf: e.tensor_copy(out=vstage[:, hf * 4:(hf + 1) * 4, tb, :], in_=psv[hf][:]),
                     reads=[psv[hf]], pwrites=[vstage])
    for h in range(8):
        P.dma("sp", qkv_in[h * 384 + 256:h * 384 + 384, :], vstage.t[:, h, :, :].rearrange("p b v -> p (b v)"),
              reads=[vstage], pwrites=[qkv_tok])


def attn_core(K, C, d, qkv_all, qkv_all_tok, o_in, o_tok, gvec, gvec_tok):
    P, sb, ps = K.P, K.sb, K.ps
    K.phase()
    QKV = sb("QKV", [128, 3, 8, NT], BF16)
    QT = QKV.alias(QKV.t[:, 0, :, :].rearrange("p r t -> p (r t)"))
    KT = QKV.alias(QKV.t[:, 1, :, :].rearrange("p r t -> p (r t)"))
    VA = sb("VA", [128, NB, 129], BF16)
    P.op("pool", lambda e: e.memset(VA[:], 1.0), writes=[VA])
    q4 = qkv_all.rearrange("(r h x) t -> r h x t", r=8, h=8)
    for r in range(8):
        def fn(e, r=r):
            pid = P.pid(e)
            src = q4[r, bass.ds(pid, 1), :, :].rearrange("o (k p) t -> p (o k) t", k=3)
            return e.dma_start(out=QKV[:, :, r, :], in_=src)
        P._add("act", fn, [qkv_all_tok], (), [QKV], True)
    for r in range(8):
        P.op("pool", lambda e, r=r: e.tensor_copy(out=VA[:, r * 16:(r + 1) * 16, 0:128],
                                                  in_=QKV[:, 2, r, :].rearrange("p (b v) -> p b v", b=16)),
             reads=[QKV], pwrites=[VA])
    Vreg = QKV.t[:, 2, :, :].rearrange("p r t -> p (r t)")
    P.op("pool", lambda e: e.tensor_copy(out=Vreg[64:128, :], in_=QT[64:128, :]), reads=[QKV], pwrites=[QKV])
    P.op("pool", lambda e: e.memset(Vreg[0:64, :], 0.0), pwrites=[QKV])
    P.op("pool", lambda e: e.memset(QT[64:128, :], 0.0), reads=[QKV], pwrites=[QKV])
    Qz = [QT, QKV.alias(Vreg)]
    lamv = sb("lamv", [128, 4, 64], F32)
    P.dma("sp", lamv[:], d["lamv"].partition_broadcast(128), writes=[lamv])
    lp = sb("lp", [128, 2, 64], F32)
    ls = sb("ls", [128, 2], F32)
    nlam = sb("nlam", [128, 1], F32)
    P.op("dve", lambda e: e.tensor_tensor(out=lp[:, 0, :], in0=lamv[:, 0, :], in1=lamv[:, 1, :], op=ALU.mult), reads=[lamv], pwrites=[lp])
    P.op("dve", lambda e: e.tensor_tensor(out=lp[:, 1, :], in0=lamv[:, 2, :], in1=lamv[:, 3, :], op=ALU.mult), reads=[lamv], pwrites=[lp])
    P.op("dve", lambda e: e.reduce_sum(out=ls[:], in_=lp[:], axis=AX.X), reads=[lp], writes=[ls])
    P.op("act", lambda e: e.activation(out=ls[:], in_=ls[:], func=AF.Exp), reads=[ls], writes=[ls])
    P.op("dve", lambda e: e.tensor_sub(out=nlam[:], in0=ls[:, 1:2], in1=ls[:, 0:1]), reads=[ls], writes=[nlam])
    P.op("dve", lambda e: e.tensor_scalar(out=nlam[:], in0=nlam[:], scalar1=-LAM_INIT, scalar2=None, op0=ALU.add),
         reads=[nlam], writes=[nlam])
    gsub = sb("gsub", [128, 128], F32)
    P.dma("sp", gsub[:], d["subln"].partition_broadcast(128), writes=[gsub])
    P.op("dve", lambda e: e.tensor_scalar(out=gsub[:], in0=gsub[:], scalar1=1.0 - LAM_INIT, scalar2=None, op0=ALU.mult),
         reads=[gsub], writes=[gsub])
    eps128 = C.eps
    relcol = sb("relcol", [32, 1], F32)
    oh = sb("oh", [32, 128], F32)
    P.dma("sp", relcol[:], d["relcol"], writes=[relcol])
    P.dma("sp", oh[:], d["oh"], writes=[oh])
    ps_g = ps(0, [1, 128])
    gm = sb("gm", [1, 128], F32)
    P.op("pe", lambda e: e.matmul(ps_g[:], lhsT=relcol[:], rhs=oh[:], start=True, stop=True), reads=[relcol, oh], writes=[ps_g])
    P.op("act", lambda e: e.copy(out=gm[:], in_=ps_g[:]), reads=[ps_g], writes=[gm])
    gv = gvec.ap()
    P.dma("sp", gv, d["gconst"], writes=[gvec_tok])
    P.dma("sp", gv[:, 511:639], gm[:], reads=[gm], writes=[gvec_tok])
    btile = sb("btile", [128, 5, SW], F32)
    antiI = sb("antiI", [128, 128], F32)
    P.dma("sp", antiI[:], d["antiI"], writes=[antiI])
    hk_t = [sb("hk_t", [128, SW], F32) for _ in range(2)]
    for i in range(5):
        src = bass.AP(gvec, 512 - 128 * i, [[1, 128], [1, SW]])
        hkt = hk_t[i % 2]
        P.dma("sp", hkt[:], src, reads=[gvec_tok], writes=[hkt])
        pbt = K.pb[i % 2]
        P.op("pe", lambda e, pbt=pbt, hkt=hkt: e.matmul(pbt[:], lhsT=antiI[:], rhs=hkt[:], start=True, stop=True),
             reads=[antiI, hkt], writes=[pbt])
        P.op("act", lambda e, pbt=pbt, i=i: e.copy(out=btile[:, i, :], in_=pbt[:]), reads=[pbt], pwrites=[btile])

    pT = [[sb("pT", [128, SW], BF16) for _ in range(2)] for _ in range(2)]
    stmp = [sb("stmp", [128, SW], F32) for _ in range(2)]
    psS = [[K.pb[0], K.pb[1]], [K.pb[2], K.pb[3]]]
    accb = [K.pb[4], K.pb[5], K.pb2]

    def acc(m, j):
        i = m * 4 + j
        b = accb[i // 3]
        o = (i % 3) * 129
        return b, b.t[:, o:o + 129]
    ps_tr = K.pb2.alias(K.pb2.t[:, 512:768].bitcast(BF16))
    o_sb = sb("o_sb", [128, 128], F32)
    osq = sb("osq", [128, 128], F32)
    on = sb("on", [128, 128], BF16)
    sm = sb("sm", [128, 8], F32)
    oT_st = [sb("oT_st", [128, SW], BF16) for _ in range(2)]
    def emit_qk(qs, kb, maps=(0, 1)):
        i_near = kb - (qs * 4 - 1)
        near = i_near >= 0
        j0 = max(0, kb - qs * 4)
        c0 = j0 * 128
        for m in maps:
            pS = psS[m][kb % 2]
            P.op("pe", lambda e, pS=pS, m=m, kb=kb, qs=qs, c0=c0: e.matmul(
                pS[:, c0:SW], lhsT=KT[:, kb * 128:(kb + 1) * 128], rhs=Qz[m][:, qs * SW + c0:(qs + 1) * SW],
                start=True, stop=True), reads=[KT, QT], writes=[pS])
        for m in maps:
            pS = psS[m][kb % 2]
            pt = pT[m][kb % 2]
            if near:
                st = stmp[m]
                P.op("dve", lambda e, st=st, pS=pS, i_near=i_near, c0=c0: e.tensor_tensor(
                    out=st[:, c0:SW], in0=pS[:, c0:SW], in1=btile[:, i_near, c0:SW], op=ALU.add),
                    reads=[pS, btile], writes=[st])
                P.op("act", lambda e, st=st, pt=pt, c0=c0: e.activation(out=pt[:, c0:SW], in_=st[:, c0:SW], func=AF.Exp),
                     reads=[st], writes=[pt])
            else:
                P.op("act", lambda e, pS=pS, pt=pt: e.activation(out=pt[:], in_=pS[:], func=AF.Exp), reads=[pS], writes=[pt])

    def emit_pv(qs, kb, maps=(0, 1), last=True):
        j0 = max(0, kb - qs * 4)
        for m in maps:
            pt = pT[m][kb % 2]
            for j in range(j0, 4):
                ab, aap = acc(m, j)
                st_ = (kb == 0) and ((m * 4 + j) % 3 == 0)
                P.op("pe", lambda e, aap=aap, pt=pt, j=j, kb=kb, qs=qs, st_=st_: e.matmul(
                    aap, lhsT=pt[:, j * 128:(j + 1) * 128], rhs=VA[:, kb, :], start=st_, stop=(kb == qs * 4 + j)),
                    reads=[pt, VA], pwrites=[ab])
        if last and kb == (qs + 1) * 4 - 1:
            epilogue(qs)

    def epilogue(qs):
        ost = oT_st[qs % 2]
        for j in range(4):
            b0, a0 = acc(0, j)
            b1, a1 = acc(1, j)
            P.op("dve", lambda e, a0=a0: e.reciprocal(out=sm[:, 0:1], in_=a0[:, 128:129]), reads=[b0], pwrites=[sm])
            P.op("dve", lambda e, a1=a1: e.reciprocal(out=sm[:, 1:2], in_=a1[:, 128:129]), reads=[b1], pwrites=[sm])
            P.op("dve", lambda e: e.tensor_tensor(out=sm[:, 2:3], in0=sm[:, 1:2], in1=nlam[:], op=ALU.mult),
                 reads=[sm, nlam], pwrites=[sm])
            P.op("act", lambda e, a0=a0: e.activation(out=o_sb[:], in_=a0[:, 0:128], func=AF.Copy, scale=sm[:, 0:1]),
                 reads=[b0, sm], writes=[o_sb])
            P.op("dve", lambda e, a1=a1: e.scalar_tensor_tensor(out=o_sb[:], in0=a1[:, 0:128], scalar=sm[:, 2:3], in1=o_sb[:],
                                                                op0=ALU.mult, op1=ALU.add), reads=[b1, sm, o_sb], writes=[o_sb])
            P.op("act", lambda e: e.activation(out=osq[:], in_=o_sb[:], func=AF.Square), reads=[o_sb], writes=[osq])
            P.op("dve", lambda e: e.reduce_sum(out=sm[:, 3:4], in_=osq[:], axis=AX.X), reads=[osq], pwrites=[sm])
            P.op("act", lambda e: e.activation(out=sm[:, 4:5], in_=sm[:, 3:4], func=AF.Sqrt, scale=1.0 / 128, bias=eps128[:]),
                 reads=[sm, eps128], pwrites=[sm])
            P.op("dve", lambda e: e.reciprocal(out=sm[:, 5:6], in_=sm[:, 4:5]), reads=[sm], pwrites=[sm])
            P.op("dve", lambda e: e.scalar_tensor_tensor(out=on[:], in0=o_sb[:], scalar=sm[:, 5:6], in1=gsub[:],
                                                         op0=ALU.mult, op1=ALU.mult), reads=[o_sb, sm, gsub], writes=[on])
            P.op("pe", lambda e, j=j: e.transpose(out=ps_tr[:, j * 128:(j + 1) * 128], in_=on[:], identity=C.ident[:]),
                 reads=[on, C.ident], pwrites=[ps_tr])
        P.op("act", lambda e, ost=ost: e.copy(out=ost[:], in_=ps_tr[:]), reads=[ps_tr], writes=[ost])
        P.dma("sp", o_in[:, qs * SW:(qs + 1) * SW], ost[:], reads=[ost], pwrites=[o_tok])


    units = [(qs, kb) for qs in range(NQS) for kb in range((qs + 1) * 4)]
    PAIR_QK = False
    for idx in range(len(units) + 1):
        if PAIR_QK:
            if idx < len(units):
                emit_qk(*units[idx])
            if idx >= 1:
                emit_pv(*units[idx - 1])
        else:
            for m in range(2):
                if idx < len(units):
                    emit_qk(*units[idx], maps=(m,))
                if idx >= 1:
                    emit_pv(*units[idx - 1], maps=(m,), last=(m == 1))


def attn_out(K, C, d, o_all, o_all_tok, xin, xin_tok, xout, xout_tok, halo_in, halo_tok):
    P, sb, ps = K.P, K.sb, K.ps
    K.phase()
    og = sb("og", [128, 8, NT], BF16)

    def fn(e):
        pid = P.pid(e)
        return e.dma_start(out=og[:], in_=o_all.rearrange("(h p) t -> p h t", p=128)[:, :, bass.ds(pid * NT, NT)])
    P._add("sp", fn, [o_all_tok], [og], (), True)
    wo = sb("wo", [128, 8, 1024], BF16)
    P.dma("pool", wo[:], d["w_o_r"], writes=[wo])
    xj = [sb("xj", [128, SW], F32) for _ in range(3)]
    ps_y = [ps(0, [128, SW]), ps(1, [128, SW])]
    xin3 = xin.rearrange("(c p) t -> p c t", p=128)
    xout3 = xout.rearrange("(c p) t -> p c t", p=128)
    it = 0
    for j in range(8):
        for s in range(NS):
            sl = slice(s * SW, (s + 1) * SW)
            py = ps_y[it % 2]
            xs = xj[it % 3]
            it += 1
            P.dma("sp", xs[:], xin3[:, j, sl], reads=[xin_tok], writes=[xs])
            for h in range(8):
                P.op("pe", lambda e, h=h, j=j, py=py, sl=sl: e.matmul(py[:], lhsT=wo[:, h, j * 128:(j + 1) * 128], rhs=og[:, h, sl],
                                                                      start=(h == 0), stop=(h == 7)), reads=[wo, og], writes=[py])
            P.op("dve", lambda e, py=py, xs=xs: e.tensor_tensor(out=xs[:], in0=py[:], in1=xs[:], op=ALU.add),
                 reads=[py, xs], writes=[xs])
            P.dma("sp", xout3[:, j, sl], xs[:], reads=[xs], pwrites=[xout_tok])
            if s == NS - 1:
                P.dma("sp", halo_in[:, j * 2:(j + 1) * 2], xs[:, SW - 2:SW], reads=[xs], pwrites=[halo_tok])


import numpy as np
from concourse.bass_utils import run_bass_kernel_spmd

NCORES = 8


def allgather(P, src_h, dst_h, src_tok, dst_tok, rows=None):
    dst = dst_h.ap() if rows is None else dst_h.ap()[0:rows, :]
    P.async_op("pool", lambda e: e.collective_compute("AllGather", ALU.bypass, replica_groups=[list(range(NCORES))],
                                                      ins=[src_h.ap().opt()], outs=[dst.opt()]),
               reads=[src_tok], writes=[dst_tok], inc=1)


IN_SPECS = [
    ("xT", [1024, NT]), ("w_in_r", [32, 128, 8, 128]), ("w_out_r", [128, 8, 1024]), ("ident", [128, 128]),
    ("resetm", [128, SW]), ("maskc", [64, 8, 64]), ("gmix0", [128, 8]), ("gnorm", [128, 8]), ("lbl", [128, 2, 8]),
    ("sel", [128, 8]), ("notfirst", [128, 1]),
    ("gffn0", [128, 8]), ("w_up_r0", [44, 128, 8, 128]), ("convp0", [128, 44, 4]), ("w_down_r0", [8, 128, 22, 128]),
    ("gffn1", [128, 8]), ("w_up_r1", [44, 128, 8, 128]), ("convp1", [128, 44, 4]), ("w_down_r1", [8, 128, 22, 128]),
    ("gkv", [128, 8]), ("gmix1", [128, 8]), ("w_k_r", [8, 128, 8, 128]), ("w_q_r", [8, 128, 8, 128]),
    ("w_v_r", [8, 128, 8, 128]), ("relcol", [32, 1]), ("oh", [32, 128]), ("gconst", [1, GLEN]), ("antiI", [128, 128]), ("lamv", [4, 64]),
    ("subln", [1, 128]), ("w_o_r", [128, 8, 1024]), ("gfinal", [128, 8]),
]


def build(debug=None):
    nc = bass.Bass("TRN2", target_bir_lowering=False)
    K = KB(nc)
    P = K.P
    d = {}
    for name, shape in IN_SPECS:
        d[name] = nc.dram_tensor(name, shape, F32, kind="ExternalInput").ap()
    outT = nc.dram_tensor("outT", [1024, NT], F32, kind="ExternalOutput").ap()
    out_tok = Buf("out")

    def stream(name):
        kind = "ExternalOutput" if debug == name else "Internal"
        return nc.dram_tensor(name, [1024, NT], F32, kind=kind).ap(), Buf(name)
    x1T, x1_tok = stream("x1T")
    x2T, x2_tok = stream("x2T")
    x3T, x3_tok = stream("x3T")
    x4T, x4_tok = stream("x4T")
    scr = {}
    for n in ("qT", "kT", "gT"):
        scr[n] = nc.dram_tensor(n + "_s", [1024, NT], BF16).ap()
        scr[n + "_tok"] = Buf(n)
    for n in ("v", "kt"):
        scr[n] = nc.dram_tensor(n + "_s", [NT, 1024], BF16).ap()
        scr[n + "_tok"] = Buf(n)
    hx_in = nc.dram_tensor("hx_in", [128, 1032], F32)
    hx_all = nc.dram_tensor("hx_all", [NCORES * 128, 1032], F32)
    hx_in_tok, hx_all_tok = Buf("hx_in"), Buf("hx_all")
    halo_in = [nc.dram_tensor(f"halo_in{i}", [128, 16], F32) for i in range(2)]
    halo_all = [nc.dram_tensor(f"halo_all{i}", [NCORES * 128, 16], F32) for i in range(2)]
    halo_in_tok = [Buf("hi0"), Buf("hi1")]
    halo_all_tok = [Buf("ha0"), Buf("ha1")]
    qkv_in = nc.dram_tensor("qkv_in", [3072, NT], BF16)
    qkv_all = nc.dram_tensor("qkv_all", [NCORES * 3072 + 384, NT], BF16)
    qkv_in_tok, qkv_all_tok = Buf("qkv_in"), Buf("qkv_all")
    gvec = nc.dram_tensor("gvec", [1, GLEN], F32)
    gvec_tok = Buf("gvec")
    o_in = nc.dram_tensor("o_in", [128, T_ALL], BF16)
    o_all = nc.dram_tensor("o_all", [NCORES * 128, T_ALL], BF16)
    o_in_tok, o_all_tok = Buf("o_in"), Buf("o_all")

    C = hgrn_consts(K, d)
    T = hgrn_alloc_T(K)
    S = K.sb("S", [128, 8, 128], F32, pers=True)
    Rr = K.sb("Rr", [128, 8, 128], F32, pers=True)
    sel = K.sb("sel", [128, 8], F32, pers=True)
    P.dma("sp", sel[:], d["sel"], writes=[sel])
    P.op("pool", lambda e: e.memset(S[:], 0.0), writes=[S])
    K.A.start_phase()
    hgrn_P(K, C, d, scr, T)
    K.phase()
    hgrn_R(K, C, d, scr, T, False, S)
    P.dma("sp", hx_in.ap()[:, 0:1024], S.t.rearrange("p h v -> p (h v)"), reads=[S], pwrites=[hx_in_tok])
    P.dma("sp", hx_in.ap()[:, 1024:1032], T.D[:], reads=[T.D], pwrites=[hx_in_tok])
    allgather(P, hx_in, hx_all, hx_in_tok, hx_all_tok)
    K.phase()
    Sj = [K.sb("Sj", [128, 1032], F32) for _ in range(2)]
    P.op("pool", lambda e: e.memset(S[:], 0.0), writes=[S])
    P.op("pool", lambda e: e.memset(Rr[:], 0.0), writes=[Rr])
    for j in range(NCORES):
        sj = Sj[j % 2]
        P.dma("sp", sj[:], hx_all.ap()[j * 128:(j + 1) * 128, :], reads=[hx_all_tok], writes=[sj])
        P.op("dve", lambda e, j=j: e.scalar_tensor_tensor(out=S[:], in0=Rr[:], scalar=sel[:, j:j + 1], in1=S[:],
                                                          op0=ALU.mult, op1=ALU.add), reads=[Rr, sel, S], writes=[S])
        if j < NCORES - 1:
            P.op("dve", lambda e, sj=sj: e.tensor_tensor(out=Rr[:], in0=Rr[:],
                                                         in1=sj[:, 1024:1032].unsqueeze(2).to_broadcast([128, 8, 128]),
                                                         op=ALU.mult), reads=[Rr, sj], writes=[Rr])
            P.op("dve", lambda e, sj=sj: e.tensor_tensor(out=Rr[:], in0=Rr[:],
                                                         in1=sj[:, 0:1024].rearrange("p (h v) -> p h v", h=8),
                                                         op=ALU.add), reads=[Rr, sj], writes=[Rr])
    d["x1T"], d["x1T_tok"] = x1T, x1_tok
    d["halo_out"], d["halo_out_tok"] = halo_in[0].ap(), halo_in_tok[0]
    hgrn_R(K, C, d, scr, T, True, S)
    allgather(P, halo_in[0], halo_all[0], halo_in_tok[0], halo_all_tok[0])
    if debug == "x1T":
        P.emit(final_bufs=[x1_tok, halo_all_tok[0]])
        print("nflag", P.nflag, "ndma", P.n_dma, "peak", K.A.peak)
        return nc
    ffn_layer(K, C, d, 0, x1T, x1_tok, x2T, x2_tok, halo_all[0].ap(), halo_all_tok[0])
    if debug == "x2T":
        P.emit(final_bufs=[x2_tok])
        print("nflag", P.nflag, "ndma", P.n_dma, "peak", K.A.peak)
        return nc
    kvq_proj(K, C, d, x2T, x2_tok, qkv_in.ap(), qkv_in_tok)
    allgather(P, qkv_in, qkv_all, qkv_in_tok, qkv_all_tok, rows=NCORES * 3072)
    attn_core(K, C, d, qkv_all.ap()[0:NCORES * 3072, :], qkv_all_tok, o_in.ap(), o_in_tok, gvec, gvec_tok)
    allgather(P, o_in, o_all, o_in_tok, o_all_tok)
    attn_out(K, C, d, o_all.ap(), o_all_tok, x2T, x2_tok, x3T, x3_tok, halo_in[1].ap(), halo_in_tok[1])
    allgather(P, halo_in[1], halo_all[1], halo_in_tok[1], halo_all_tok[1])
    if debug == "x3T":
        P.emit(final_bufs=[x3_tok, halo_all_tok[1]])
        return nc
    ffn_layer(K, C, d, 1, x3T, x3_tok, x4T, x4_tok, halo_all[1].ap(), halo_all_tok[1])
    final_norm(K, C, d, x4T, x4_tok, outT, out_tok)
    P.emit(final_bufs=[out_tok])
    return nc


def t5_bucket_np(rel):
    max_exact = 16
    n = np.maximum(rel, 0)
    log_ratio = (np.log(np.maximum(n, 1).astype(np.float32) / np.float32(max_exact)) / np.float32(math.log(128 / max_exact))).astype(np.float32)
    large = np.minimum(max_exact + (log_ratio * np.float32(32 - max_exact)).astype(np.int32), 31)
    return np.where(n < max_exact, n, large)


def host_inputs(inp, c):
    f = lambda a: np.ascontiguousarray(np.asarray(a, dtype=np.float32))
    pc = lambda v: f(np.asarray(v).reshape(8, 128).T)
    m = {}
    m["xT"] = f(np.asarray(inp["x"])[0, c * NT:(c + 1) * NT, :].T)
    m["w_in_r"] = f(np.asarray(inp["a_w_in"])[0].reshape(8, 128, 32, 128).transpose(2, 1, 0, 3))
    m["w_out_r"] = f(np.asarray(inp["a_w_out"])[0].reshape(8, 128, 1024).transpose(1, 0, 2))
    m["ident"] = np.eye(128, dtype=np.float32)
    r = np.ones((128, SW), np.float32)
    r[:, ::CH] = 0
    m["resetm"] = r
    mk_ = (np.arange(64)[:, None] <= np.arange(64)[None, :]).astype(np.float32)
    m["maskc"] = f(np.broadcast_to(mk_[:, None, :], (64, 8, 64)))
    m["gmix0"] = pc(inp["norm_mix"][0])
    m["gmix1"] = pc(inp["norm_mix"][1])
    m["gnorm"] = pc(inp["a_gnorm"][0])
    m["lbl"] = f(np.asarray(inp["a_lb_logits"]).reshape(2, 8, 128).transpose(2, 0, 1))
    s = np.zeros((128, 8), np.float32)
    s[:, c] = 1.0
    m["sel"] = s
    m["notfirst"] = np.full((128, 1), 0.0 if c == 0 else 1.0, np.float32)
    for li in range(2):
        m[f"gffn{li}"] = pc(inp["norm_ffn"][li])
        m[f"w_up_r{li}"] = f(np.asarray(inp["ffn_w_up"])[li].reshape(8, 128, 44, 128).transpose(2, 1, 0, 3))
        cw = np.asarray(inp["ffn_conv_w"])[li]
        cb = np.asarray(inp["ffn_conv_b"])[li]
        cp = np.concatenate([cw, cb[None]], 0)
        m[f"convp{li}"] = f(cp.reshape(4, 44, 128).transpose(2, 1, 0))
        m[f"w_down_r{li}"] = f(np.asarray(inp["ffn_w_down"])[li].reshape(22, 128, 8, 128).transpose(2, 1, 0, 3))
    m["gkv"] = pc(inp["kv_norm"])
    kvw = np.asarray(inp["kv_w"])
    m["w_k_r"] = f(kvw[:, :1024].reshape(8, 128, 8, 128).transpose(2, 1, 0, 3))
    m["w_v_r"] = f(kvw[:, 1024:].reshape(8, 128, 8, 128).transpose(2, 1, 0, 3))
    m["w_q_r"] = f(np.asarray(inp["b_w_q"])[0].reshape(8, 128, 8, 128).transpose(2, 1, 0, 3))
    m["w_o_r"] = f(np.asarray(inp["b_w_o"])[0].reshape(8, 128, 1024).transpose(1, 0, 2))
    m["relcol"] = f(np.asarray(inp["rel_table"])[:, c:c + 1])
    bk = t5_bucket_np(np.arange(128))
    oh = np.zeros((32, 128), np.float32)
    oh[bk, np.arange(128)] = 1.0
    oh[31, :] -= 1.0
    m["oh"] = oh
    g = np.zeros((1, GLEN), np.float32)
    g[0, :511] = NEG
    m["gconst"] = g
    m["antiI"] = np.ascontiguousarray(np.eye(128, dtype=np.float32)[::-1])
    m["lamv"] = f(np.stack([np.asarray(inp[k])[0] for k in ("b_lam_q1", "b_lam_k1", "b_lam_q2", "b_lam_k2")]))
    m["subln"] = f(np.asarray(inp["b_subln"])[0][None, :])
    m["gfinal"] = pc(inp["final_norm"])
    return m


_NC_CACHE = {}


def kernel(**inputs):
    if "nc" not in _NC_CACHE:
        _NC_CACHE["nc"] = build()
    nc = _NC_CACHE["nc"]
    in_maps = [host_inputs(inputs, c) for c in range(NCORES)]
    res = run_bass_kernel_spmd(nc, in_maps, core_ids=list(range(NCORES)))
    out = np.empty((1, NCORES * NT, 1024), np.float32)
    for c in range(NCORES):
        out[0, c * NT:(c + 1) * NT, :] = res.results[c]["outT"].T
    return out
```

```python
import contextlib
import numpy as np
import concourse.bass as bass
import concourse.mybir as mybir

F32 = mybir.dt.float32
BF16 = mybir.dt.bfloat16
U8 = mybir.dt.uint8
ALU = mybir.AluOpType
AF = mybir.ActivationFunctionType
AX = mybir.AxisListType
DTSZ = {F32: 4, BF16: 2, U8: 1}

ENGS = ("pe", "act", "dve", "pool", "sp")
SEM_ROLL = 2048
DMA_K = 6


class Tok:
    __slots__ = ("writers", "readers", "psum")

    def __init__(self, psum=False):
        self.writers = []
        self.readers = []
        self.psum = psum


class Buf:
    __slots__ = ("name", "t", "tok")

    def __init__(self, name, t=None, tok=None):
        self.name = name
        self.t = t
        self.tok = tok if tok is not None else Tok()

    def __getitem__(self, idx):
        return self.t[idx]

    def alias(self, ap, name=None):
        return Buf(name or self.name, ap, self.tok)


class Op:
    __slots__ = ("eng", "fn", "deps", "is_dma", "flag", "seq", "dma_slot", "dma_val", "name", "inc")


class Arena:
    def __init__(self, nc, nbytes):
        self.big = nc.alloc_sbuf_tensor("arena", [128, nbytes], U8)
        self.size = nbytes
        self.pers = 0
        self.cur = 0
        self.in_phase = False
        self.peak = 0

    def start_phase(self):
        self.in_phase = True
        self.cur = self.pers

    def alloc(self, name, shape, dt, persistent=False):
        p = shape[0]
        n = int(np.prod(shape[1:])) * DTSZ[dt]
        n = (n + 63) // 64 * 64
        if persistent:
            assert not self.in_phase or self.cur == self.pers, "persistent alloc inside a phase"
            off = self.pers
            self.pers += n
            self.cur = self.pers
        else:
            off = self.cur
            self.cur += n
        self.peak = max(self.peak, self.cur)
        assert self.cur <= self.size, f"SBUF arena overflow allocating {name}: {self.cur} > {self.size}"
        ap = self.big[0:p, off:off + n if False else off + int(np.prod(shape[1:])) * DTSZ[dt]].bitcast(dt)
        if len(shape) == 3:
            ap = ap.rearrange("p (a b) -> p a b", a=shape[1])
        elif len(shape) == 4:
            ap = ap.rearrange("p (a b c) -> p a b c", a=shape[1], b=shape[2])
        return Buf(name, ap)


class Prog:
    def __init__(self, nc):
        self.nc = nc
        self.ops = {e: [] for e in ENGS}
        self.n_dma = {e: 0 for e in ENGS}
        self.all_ops = []

    def _add(self, eng, fn, reads, writes, pwrites, is_dma, name=None, inc=None, extra_deps=()):
        op = Op()
        op.eng, op.fn, op.is_dma, op.flag, op.seq, op.name = eng, fn, is_dma, False, None, name
        op.inc = inc if inc is not None else (16 if is_dma else 1)
        deps = list(extra_deps)
        wr_toks = set()
        for r in reads:
            deps.extend(r.tok.writers)
            if r.tok.psum:
                deps.extend(x for x in r.tok.readers if x.eng != eng)
        for w in list(writes) + list(pwrites):
            wr_toks.add(id(w.tok))
        for w in writes:
            deps.extend(w.tok.writers)
            deps.extend(w.tok.readers)
        for w in pwrites:
            deps.extend(w.tok.readers)
            if w.tok.readers:
                deps.extend(w.tok.writers)
        rw_writers = set()
        for x in list(reads) + list(writes):
            for d in x.tok.writers:
                rw_writers.add(id(d))
        out = []
        seen = set()
        for d in deps:
            if id(d) in seen or d is op:
                continue
            seen.add(id(d))
            if (not d.is_dma) and (not is_dma) and d.eng == eng:
                if eng == "pe":
                    continue
                if id(d) not in rw_writers:
                    continue
            out.append(d)
        op.deps = out
        for r in reads:
            r.tok.readers.append(op)
        for w in writes:
            w.tok.writers = [op]
            w.tok.readers = []
        for w in pwrites:
            if w.tok.readers:
                w.tok.writers = [op]
                w.tok.readers = []
            else:
                w.tok.writers.append(op)
        if is_dma:
            i = self.n_dma[eng]
            self.n_dma[eng] += 1
            op.dma_slot = i % DMA_K
            op.dma_val = i // DMA_K + 1
        self.ops[eng].append(op)
        self.all_ops.append(op)
        return op

    def op(self, eng, fn, reads=(), writes=(), pwrites=(), name=None):
        return self._add(eng, fn, reads, writes, pwrites, False, name)

    def dma(self, eng, out, in_, reads=(), writes=(), pwrites=(), **kw):
        def fn(e):
            return e.dma_start(out=out, in_=in_, **kw)
        return self._add(eng, fn, reads, writes, pwrites, True)

    def async_op(self, eng, fn, reads=(), writes=(), inc=1):
        return self._add(eng, fn, reads, writes, (), True, inc=inc)

    def barrier(self):
        lasts = []
        for e in ENGS:
            for op in reversed(self.ops[e]):
                if not op.is_dma:
                    lasts.append(op)
                    break
            dm = [op for op in self.ops[e] if op.is_dma][-DMA_K:]
            lasts.extend(dm)
        for e in ENGS:
            deps = [d for d in lasts if d.is_dma or d.eng != e]
            self._add(e, lambda en: en.nop(), (), (), (), False, "barrier", extra_deps=deps)

    def pid(self, e):
        k = id(e)
        if k not in self._pids:
            self._pids[k] = e.partition_id()
        return self._pids[k]

    def emit(self, final_bufs=()):
        self._pids = {}
        nc = self.nc
        for op in self.all_ops:
            for d in op.deps:
                d.flag = True
        finals = []
        for b in final_bufs:
            for w in b.tok.writers:
                w.flag = True
                finals.append(w)
        nflag = {}
        for e in ENGS:
            n = 0
            for op in self.ops[e]:
                if op.flag and not op.is_dma:
                    op.seq = n
                    n += 1
            nflag[e] = n
        self.nflag = nflag
        with contextlib.ExitStack() as st:
            csem = {}
            for e in ENGS:
                k = (nflag[e] + SEM_ROLL - 1) // SEM_ROLL
                csem[e] = [st.enter_context(nc.semaphore(f"c_{e}_{i}")) for i in range(max(k, 1))]
            dsem = {}
            for e in ENGS:
                if self.n_dma[e]:
                    dsem[e] = [st.enter_context(nc.semaphore(f"d_{e}_{i}")) for i in range(DMA_K)]
            cum = {e: [0] * DMA_K for e in ENGS}
            for e in ENGS:
                for op in self.ops[e]:
                    if op.is_dma:
                        cum[e][op.dma_slot] += op.inc
                        op.dma_val = cum[e][op.dma_slot]
            block = st.enter_context(nc.Block())

            def target(d):
                if d.is_dma:
                    return dsem[d.eng][d.dma_slot], d.dma_val
                return csem[d.eng][d.seq // SEM_ROLL], d.seq % SEM_ROLL + 1

            def run(ename, e):
                waited = {}
                for op in self.ops[ename]:
                    tg = [target(d) for d in op.deps]
                    if op.is_dma and op.dma_val - op.inc > 0:
                        tg.append((dsem[ename][op.dma_slot], op.dma_val - op.inc))
                    for s, v in tg:
                        key = id(s)
                        if waited.get(key, 0) >= v:
                            continue
                        waited[key] = v
                        e.wait_ge(s, v)
                    ins = op.fn(e)
                    if op.is_dma:
                        ins.then_inc(dsem[ename][op.dma_slot], op.inc)
                    elif op.flag:
                        ins.then_inc(csem[ename][op.seq // SEM_ROLL], 1)
                if ename == "sp":
                    for d in finals:
                        s, v = target(d)
                        e.wait_ge(s, v)

            @block.tensor
            def _(e):
                run("pe", e)

            @block.scalar
            def _(e):
                run("act", e)

            @block.vector
            def _(e):
                run("dve", e)

            @block.gpsimd
            def _(e):
                run("pool", e)

            @block.sync
            def _(e):
                run("sp", e)


NT = 2048
SW = 512
NS = NT // SW
CH = 64
NCH = NT // CH
CPS = SW // CH
EPS = 1e-6


class Ctx:
    pass


class KB:
    def __init__(self, nc, sbuf_bytes=206 * 1024):
        self.nc = nc
        self.P = Prog(nc)
        self.A = Arena(nc, sbuf_bytes)
        self.pb = [Buf(f"pb{i}", nc.alloc_psum_tensor(f"pb{i}", [128, 512], F32), Tok(psum=True)) for i in range(6)]
        self.pb2 = Buf("pb2", nc.alloc_psum_tensor("pbig", [128, 1024], F32), Tok(psum=True))

    def sb(self, name, shape, dt, pers=False):
        return self.A.alloc(name, shape, dt, persistent=pers)

    def ps(self, bank, shape, dt=F32):
        base = self.pb2 if bank == 6 else self.pb[bank]
        p = shape[0]
        n = 1
        for x in shape[1:]:
            n *= x
        nf32 = n * DTSZ[dt] // 4
        ap = base.t[0:p, 0:nf32]
        if dt != F32:
            ap = ap.bitcast(dt)
        if len(shape) == 3:
            ap = ap.rearrange("p (a b) -> p a b", a=shape[1])
        return base.alias(ap)

    def phase(self):
        self.P.barrier()
        self.A.start_phase()


def rmsnorm_fm(P, C, xs, g, out_ap, out_buf, sq, ps, rstd, width=SW):
    P.op("act", lambda e: e.activation(out=sq[:], in_=xs[:], func=AF.Square), reads=[xs], writes=[sq])
    for c in range(8):
        P.op("pe", lambda e, c=c: e.matmul(ps[:], lhsT=C.ones[:], rhs=sq[:, c, :], start=(c == 0), stop=(c == 7)),
             reads=[C.ones, sq], writes=[ps])
    P.op("act", lambda e: e.activation(out=rstd[:], in_=ps[:], func=AF.Sqrt, scale=1.0 / 1024, bias=C.eps[:]),
         reads=[ps, C.eps], writes=[rstd])
    P.op("dve", lambda e: e.reciprocal(out=rstd[:], in_=rstd[:]), reads=[rstd], writes=[rstd])
    for c in range(8):
        P.op("dve", lambda e, c=c: e.scalar_tensor_tensor(out=out_ap[:, c, :], in0=xs[:, c, :], scalar=g[:, c:c + 1],
                                                          in1=rstd[:], op0=ALU.mult, op1=ALU.mult),
             reads=[xs, g, rstd], pwrites=[out_buf])


def hgrn_consts(K, d):
    P = K.P
    C = Ctx()
    sb = lambda n, sh, dt: K.sb(n, sh, dt, pers=True)
    C.ones = sb("ones", [128, 128], BF16)
    P.op("pool", lambda e: e.memset(C.ones[:], 1.0), writes=[C.ones])
    C.eps = sb("eps", [128, 1], F32)
    P.op("pool", lambda e: e.memset(C.eps[:], EPS), writes=[C.eps])
    C.ident = sb("ident", [128, 128], BF16)
    P.dma("pool", C.ident[:], d["ident"], writes=[C.ident])
    C.resetm = sb("resetm", [128, SW], F32)
    P.dma("sp", C.resetm[:], d["resetm"], writes=[C.resetm])
    C.maskc = sb("maskc", [64, 8, 64], F32)
    P.dma("sp", C.maskc[:], d["maskc"], writes=[C.maskc])
    C.gmix = sb("gmix", [128, 8], F32)
    P.dma("sp", C.gmix[:], d["gmix0"], writes=[C.gmix])
    C.gn = sb("gn", [128, 8], F32)
    P.dma("sp", C.gn[:], d["gnorm"], writes=[C.gn])
    lbl = sb("lbl", [128, 2, 8], F32)
    P.dma("sp", lbl[:], d["lbl"], writes=[lbl])
    C.lb = sb("lb", [128, 8], F32)
    C.oml = sb("oml", [128, 8], F32)
    P.op("dve", lambda e: e.tensor_sub(out=C.lb[:], in0=lbl[:, 0, :], in1=lbl[:, 1, :]), reads=[lbl], writes=[C.lb])
    P.op("act", lambda e: e.activation(out=C.oml[:], in_=C.lb[:], func=AF.Sigmoid, scale=-1.0), reads=[C.lb], writes=[C.oml])
    P.op("act", lambda e: e.activation(out=C.lb[:], in_=C.lb[:], func=AF.Sigmoid), reads=[C.lb], writes=[C.lb])
    return C


def hgrn_alloc_T(K):
    T = Ctx()
    sb = lambda n, sh, dt: K.sb(n, sh, dt, pers=True)
    T.bl = sb("bl", [128, 8, NCH], F32)
    T.bmid = sb("bmid", [128, 8, NCH], F32)
    T.e1 = sb("e1", [128, 8, NCH], F32)
    T.e2 = sb("e2", [128, 8, NCH], F32)
    T.em = sb("em", [128, 8, NCH], F32)
    T.D = sb("D", [128, 8], F32)
    return T


def hgrn_P(K, C, d, scr, T):
    P, sb, ps = K.P, K.sb, K.ps
    xT3 = d["xT"].rearrange("(c p) t -> p c t", p=128)
    hT = sb("hT", [128, 8, NT], BF16)
    hTs = [Buf(f"hT{s}", hT.t[:, :, s * SW:(s + 1) * SW]) for s in range(NS)]
    xst = [sb("xst", [128, 8, SW], F32) for _ in range(2)]
    sq = sb("sq", [128, 8, SW], BF16)
    rstd = sb("rstd", [128, SW], F32)
    ps_n = ps(0, [128, SW])
    for s in range(NS):
        xs = xst[s % 2]
        P.dma("sp", xs[:], xT3[:, :, s * SW:(s + 1) * SW], writes=[xs])
        rmsnorm_fm(P, C, xs, C.gmix, hTs[s], hTs[s], sq, ps_n, rstd)

    wi = sb("wi", [128, 8, 1024], BF16)
    for h in range(8):
        P.dma("pool", wi[:, :, h * 128:(h + 1) * 128], d["w_in_r"][16 + h], pwrites=[wi])
    ps_v = [ps(1, [64, 512]), ps(2, [64, 512])]
    vst = [sb("vst", [64, CPS, 1024], BF16) for _ in range(2)]
    v3 = scr["v"].rearrange("(c s) j -> s c j", s=64)
    for s in range(NS):
        vs = vst[s % 2]
        for c in range(CPS):
            cg = s * CPS + c
            for hf in range(2):
                pv = ps_v[hf]
                for m in range(8):
                    P.op("pe", lambda e, m=m, cg=cg, hf=hf, pv=pv: e.matmul(
                        pv[:], lhsT=hT[:, m, cg * CH:(cg + 1) * CH], rhs=wi[:, m, hf * 512:(hf + 1) * 512],
                        start=(m == 0), stop=(m == 7)), reads=[hTs[s], wi], writes=[pv])
                if hf == 0:
                    P.op("act", lambda e, c=c, hf=hf, pv=pv, vs=vs: e.copy(out=vs[:, c, hf * 512:(hf + 1) * 512], in_=pv[:]),
                         reads=[pv], pwrites=[vs])
                else:
                    P.op("dve", lambda e, c=c, hf=hf, pv=pv, vs=vs: e.tensor_copy(out=vs[:, c, hf * 512:(hf + 1) * 512], in_=pv[:]),
                         reads=[pv], pwrites=[vs])
        P.dma("sp", v3[:, s * CPS:(s + 1) * CPS, :], vs[:], reads=[vs], pwrites=[scr["v_tok"]])

    wq = [sb("wq", [128, 8, 128], BF16) for _ in range(2)]
    wf = [sb("wf", [128, 8, 128], BF16) for _ in range(2)]
    wg = [sb("wg", [128, 8, 128], BF16) for _ in range(2)]
    ps_f = ps(3, [128, SW])
    ps_q = ps(4, [128, SW])
    ps_g = ps(5, [128, SW])
    ps_t = ps(0, [64, CPS, 128], BF16)
    sig = sb("sig", [128, SW], F32)
    logf = sb("logf", [128, SW], F32)
    nsig = sb("nsig", [128, SW], F32)
    b3 = sb("b3", [128, CPS, CH], F32)
    bm = sb("bm", [128, CPS, CH], F32)
    Ep = sb("Ep", [128, SW], F32)
    Em = sb("Em", [128, SW], F32)
    sqf = sb("sqf", [128, SW], F32)
    qst = [sb("qst", [128, SW], BF16) for _ in range(2)]
    kst = [sb("kst", [128, SW], BF16) for _ in range(2)]
    gst = [sb("gst", [128, SW], BF16) for _ in range(2)]
    ktst = [sb("ktst", [64, CPS, 128], BF16) for _ in range(2)]
    b_flat = b3.t.rearrange("p c t -> p (c t)")
    bm_flat = bm.t.rearrange("p c t -> p (c t)")
    kt3 = scr["kt"].rearrange("(c s) j -> s c j", s=64)
    it = 0
    for h in range(8):
        wqh, wfh, wgh = wq[h % 2], wf[h % 2], wg[h % 2]
        P.dma("pool", wfh[:], d["w_in_r"][8 + h], writes=[wfh])
        P.dma("pool", wqh[:], d["w_in_r"][0 + h], writes=[wqh])
        P.dma("pool", wgh[:], d["w_in_r"][24 + h], writes=[wgh])
        for s in range(NS):
            hs = hTs[s]
            sl = slice(s * SW, (s + 1) * SW)
            for (pp, ww) in ((ps_f, wfh), (ps_q, wqh), (ps_g, wgh)):
                for m in range(8):
                    P.op("pe", lambda e, m=m, pp=pp, ww=ww, sl=sl: e.matmul(
                        pp[:], lhsT=ww[:, m, :], rhs=hT[:, m, sl], start=(m == 0), stop=(m == 7)),
                        reads=[ww, hs], writes=[pp])
            q_o, k_o, g_o, kt_o = qst[it % 2], kst[it % 2], gst[it % 2], ktst[it % 2]
            it += 1
            P.op("act", lambda e: e.activation(out=sig[:], in_=ps_f[:], func=AF.Sigmoid), reads=[ps_f], writes=[sig])
            P.op("act", lambda e, h=h: e.activation(out=logf[:], in_=sig[:], func=AF.Ln, scale=C.oml[:, h:h + 1],
                                                    bias=C.lb[:, h:h + 1]), reads=[sig, C.oml, C.lb], writes=[logf])
            P.op("dve", lambda e: e.tensor_scalar(out=nsig[:], in0=sig[:], scalar1=-1.0, scalar2=1.0, op0=ALU.mult,
                                                  op1=ALU.add), reads=[sig], writes=[nsig])
            P.op("dve", lambda e: e.tensor_tensor_scan(out=b_flat, data0=C.resetm[:], data1=logf[:], initial=0.0,
                                                       op0=ALU.mult, op1=ALU.add), reads=[C.resetm, logf], writes=[b3])
            P.op("dve", lambda e, h=h, s=s: e.tensor_copy(out=T.bl[:, h, s * CPS:(s + 1) * CPS], in_=b3[:, :, CH - 1]),
                 reads=[b3], pwrites=[T.bl])
            P.op("dve", lambda e, h=h, s=s: e.tensor_copy(out=T.bmid[:, h, s * CPS:(s + 1) * CPS], in_=b3[:, :, CH // 2 - 1]),
                 reads=[b3], pwrites=[T.bmid])
            P.op("dve", lambda e: e.tensor_tensor(out=bm[:], in0=b3[:], in1=b3[:, :, CH // 2 - 1:CH // 2].to_broadcast([128, CPS, CH]),
                                                  op=ALU.subtract), reads=[b3], writes=[bm])
            P.op("act", lambda e: e.activation(out=Ep[:], in_=bm_flat, func=AF.Exp), reads=[bm], writes=[Ep])
            P.op("act", lambda e: e.activation(out=Em[:], in_=bm_flat, func=AF.Exp, scale=-1.0), reads=[bm], writes=[Em])
            P.op("act", lambda e: e.activation(out=sqf[:], in_=ps_q[:], func=AF.Silu), reads=[ps_q], writes=[sqf])
            P.op("act", lambda e, g_o=g_o: e.activation(out=g_o[:], in_=ps_g[:], func=AF.Silu), reads=[ps_g], writes=[g_o])
            P.op("dve", lambda e, q_o=q_o: e.tensor_tensor(out=q_o[:], in0=sqf[:], in1=Ep[:], op=ALU.mult),
                 reads=[sqf, Ep], writes=[q_o])
            P.op("dve", lambda e, k_o=k_o, h=h: e.scalar_tensor_tensor(out=k_o[:], in0=nsig[:], scalar=C.oml[:, h:h + 1],
                                                                       in1=Em[:], op0=ALU.mult, op1=ALU.mult),
                 reads=[nsig, C.oml, Em], writes=[k_o])
            hsl = slice(h * 128, (h + 1) * 128)
            P.dma("sp", scr["qT"][hsl, sl], q_o[:], reads=[q_o], pwrites=[scr["qT_tok"]])
            P.dma("sp", scr["kT"][hsl, sl], k_o[:], reads=[k_o], pwrites=[scr["kT_tok"]])
            P.dma("sp", scr["gT"][hsl, sl], g_o[:], reads=[g_o], pwrites=[scr["gT_tok"]])
            for c in range(CPS):
                P.op("pe", lambda e, c=c, k_o=k_o: e.transpose(out=ps_t[:, c, :], in_=k_o[:, c * CH:(c + 1) * CH],
                                                              identity=C.ident[:]), reads=[k_o, C.ident], writes=[ps_t])
            P.op("act", lambda e, kt_o=kt_o: e.copy(out=kt_o[:], in_=ps_t[:]), reads=[ps_t], writes=[kt_o])
            P.dma("sp", kt3[:, s * CPS:(s + 1) * CPS, hsl], kt_o[:], reads=[kt_o], pwrites=[scr["kt_tok"]])
    P.op("act", lambda e: e.activation(out=T.e1[:], in_=T.bl[:], func=AF.Exp), reads=[T.bl], writes=[T.e1])
    P.op("act", lambda e: e.activation(out=T.em[:], in_=T.bmid[:], func=AF.Exp), reads=[T.bmid], writes=[T.em])
    P.op("dve", lambda e: e.tensor_sub(out=T.e2[:], in0=T.bl[:], in1=T.bmid[:]), reads=[T.bl, T.bmid], writes=[T.e2])
    P.op("act", lambda e: e.activation(out=T.e2[:], in_=T.e2[:], func=AF.Exp), reads=[T.e2], writes=[T.e2])
    P.op("dve", lambda e: e.reduce_sum(out=T.D[:], in_=T.bl[:], axis=AX.X), reads=[T.bl], writes=[T.D])
    P.op("act", lambda e: e.activation(out=T.D[:], in_=T.D[:], func=AF.Exp), reads=[T.D], writes=[T.D])


def hgrn_R(K, C, d, scr, T, full, S):
    P, sb, ps = K.P, K.sb, K.ps
    q3 = scr["qT"].rearrange("(h p) t -> p h t", p=128)
    k3 = scr["kT"].rearrange("(h p) t -> p h t", p=128)
    g3 = scr["gT"].rearrange("(h p) t -> p h t", p=128)
    v3 = scr["v"].rearrange("(c s) j -> s c j", s=64)
    kt3 = scr["kt"].rearrange("(c s) j -> s c j", s=64)
    v_sb = [sb("v_sb", [64, CPS, 1024], BF16) for _ in range(2)]
    kt_sb = [sb("kt_sb", [64, CPS, 1024], BF16) for _ in range(1)]
    ps_dS = ps(6, [128, 8, 128])
    tmp = sb("tmpS", [128, 8, 128], F32)
    if full:
        q_sb = [sb("q_sb", [128, 8, SW], BF16) for _ in range(2)]
        k_sb = [sb("k_sb", [128, 8, SW], BF16) for _ in range(2)]
        g_sb = [sb("g_sb", [128, 8, SW], BF16) for _ in range(1)]
        ps_sc = [ps(0, [64, 8, 64]), ps(1, [64, 8, 64])]
        ps_o = [ps(2, [128, 8, 64]), ps(3, [128, 8, 64])]
        sc_sb = [sb("sc_sb", [64, 8, 64], BF16) for _ in range(2)]
        Sb = [sb("Sb", [128, 8, 128], BF16) for _ in range(2)]
        oT = sb("oT", [128, 8, SW], F32)
        osq = sb("osq", [128, 8, SW], BF16)
        ogT = sb("ogT", [128, 8, SW], BF16)
        t1 = sb("t1", [128, SW], F32)
        rstd = sb("rstd", [128, SW], F32)
        wout = sb("wout", [128, 8, 1024], BF16)
        P.dma("pool", wout[:], d["w_out_r"], writes=[wout])
        ps_y = [ps(4, [128, SW]), ps(5, [128, SW])]
        ps_n = ps(4, [128, SW])
        xT3 = d["xT"].rearrange("(c p) t -> p c t", p=128)
        x1T3 = d["x1T"].rearrange("(c p) t -> p c t", p=128)
        xst = [sb("xst", [128, 8, SW], F32) for _ in range(1)]
    for s in range(NS):
        sl = slice(s * SW, (s + 1) * SW)
        vs, kts = v_sb[s % 2], kt_sb[0]
        P.dma("sp", vs[:], v3[:, s * CPS:(s + 1) * CPS, :], reads=[scr["v_tok"]], writes=[vs])
        P.dma("sp", kts[:], kt3[:, s * CPS:(s + 1) * CPS, :], reads=[scr["kt_tok"]], writes=[kts])
        if full:
            qs, ks, gs = q_sb[s % 2], k_sb[s % 2], g_sb[0]
            P.dma("sp", qs[:], q3[:, :, sl], reads=[scr["qT_tok"]], writes=[qs])
            P.dma("sp", ks[:], k3[:, :, sl], reads=[scr["kT_tok"]], writes=[ks])
            P.dma("sp", gs[:], g3[:, :, sl], reads=[scr["gT_tok"]], writes=[gs])
            xs = xst[0]
            P.dma("sp", xs[:], xT3[:, :, sl], writes=[xs])
        for c in range(CPS):
            cg = s * CPS + c
            csl = slice(c * CH, (c + 1) * CH)
            if full:
                psc, pso, scb, Sbb = ps_sc[cg % 2], ps_o[cg % 2], sc_sb[cg % 2], Sb[cg % 2]
                for h in range(8):
                    P.op("pe", lambda e, h=h, psc=psc, ks=ks, qs=qs, csl=csl: e.matmul(
                        psc[:, h, :], lhsT=ks[:, h, csl], rhs=qs[:, h, csl], start=True, stop=True),
                        reads=[ks, qs], writes=[psc])
                P.op("dve", lambda e, psc=psc, scb=scb: e.tensor_tensor(out=scb[:], in0=psc[:], in1=C.maskc[:], op=ALU.mult),
                     reads=[psc, C.maskc], writes=[scb])
                P.op("pool", lambda e, Sbb=Sbb, cg=cg: e.tensor_tensor(
                    out=Sbb[:], in0=S[:], in1=T.em[:, :, cg:cg + 1].to_broadcast([128, 8, 128]), op=ALU.mult),
                    reads=[S, T.em], writes=[Sbb])
                for h in range(8):
                    hsl = slice(h * 128, (h + 1) * 128)
                    P.op("pe", lambda e, h=h, pso=pso, Sbb=Sbb, qs=qs, csl=csl: e.matmul(
                        pso[:, h, :], lhsT=Sbb[:, h, :], rhs=qs[:, h, csl], start=True, stop=False),
                        reads=[Sbb, qs], writes=[pso])
                    P.op("pe", lambda e, h=h, pso=pso, vs=vs, scb=scb, c=c, hsl=hsl: e.matmul(
                        pso[:, h, :], lhsT=vs[:, c, hsl], rhs=scb[:, h, :], start=False, stop=True),
                        reads=[vs, scb], writes=[pso])
                P.op("act", lambda e, pso=pso, csl=csl: e.copy(out=oT[:, :, csl], in_=pso[:]), reads=[pso], pwrites=[oT])
            for h in range(8):
                hsl = slice(h * 128, (h + 1) * 128)
                P.op("pe", lambda e, h=h, kts=kts, vs=vs, c=c, hsl=hsl: e.matmul(
                    ps_dS[:, h, :], lhsT=kts[:, c, hsl], rhs=vs[:, c, hsl], start=True, stop=True),
                    reads=[kts, vs], writes=[ps_dS])
            P.op("dve", lambda e, cg=cg: e.tensor_tensor(
                out=tmp[:], in0=ps_dS[:], in1=T.e2[:, :, cg:cg + 1].to_broadcast([128, 8, 128]), op=ALU.mult),
                reads=[ps_dS, T.e2], writes=[tmp])
            P.op("pool", lambda e, cg=cg: e.tensor_tensor(
                out=S[:], in0=S[:], in1=T.e1[:, :, cg:cg + 1].to_broadcast([128, 8, 128]), op=ALU.mult),
                reads=[S, T.e1], writes=[S])
            P.op("dve", lambda e: e.tensor_tensor(out=S[:], in0=S[:], in1=tmp[:], op=ALU.add), reads=[S, tmp], writes=[S])
        if full:
            P.op("act", lambda e: e.activation(out=osq[:], in_=oT[:], func=AF.Square), reads=[oT], writes=[osq])
            for h in range(8):
                P.op("pe", lambda e, h=h: e.matmul(ps_n[:], lhsT=C.ones[:], rhs=osq[:, h, :], start=(h == 0), stop=(h == 7)),
                     reads=[C.ones, osq], writes=[ps_n])
            P.op("act", lambda e: e.activation(out=rstd[:], in_=ps_n[:], func=AF.Sqrt, scale=1.0 / 1024, bias=C.eps[:]),
                 reads=[ps_n, C.eps], writes=[rstd])
            P.op("dve", lambda e: e.reciprocal(out=rstd[:], in_=rstd[:]), reads=[rstd], writes=[rstd])
            for h in range(8):
                P.op("dve", lambda e, h=h: e.scalar_tensor_tensor(out=t1[:], in0=oT[:, h, :], scalar=C.gn[:, h:h + 1],
                                                                  in1=rstd[:], op0=ALU.mult, op1=ALU.mult),
                     reads=[oT, C.gn, rstd], writes=[t1])
                P.op("dve", lambda e, h=h, gs=gs: e.tensor_tensor(out=ogT[:, h, :], in0=t1[:], in1=gs[:, h, :], op=ALU.mult),
                     reads=[t1, gs], pwrites=[ogT])
            for j in range(8):
                py = ps_y[j % 2]
                for h in range(8):
                    P.op("pe", lambda e, h=h, j=j, py=py: e.matmul(py[:], lhsT=wout[:, h, j * 128:(j + 1) * 128],
                                                                  rhs=ogT[:, h, :], start=(h == 0), stop=(h == 7)),
                         reads=[wout, ogT], writes=[py])
                P.op("dve", lambda e, j=j, py=py, xs=xs: e.tensor_tensor(out=xs[:, j, :], in0=py[:], in1=xs[:, j, :], op=ALU.add),
                     reads=[py, xs], pwrites=[xs])
            P.dma("sp", x1T3[:, :, sl], xs[:], reads=[xs], pwrites=[d["x1T_tok"]])
            if s == NS - 1 and "halo_out" in d:
                P.dma("sp", d["halo_out"].rearrange("p (c t) -> p c t", c=8), xs[:, :, SW - 2:SW], reads=[xs],
                      writes=[d["halo_out_tok"]])


NFF = 22


def ffn_layer(K, C, d, li, xin, xin_tok, xout, xout_tok, halo_all, halo_tok, final_norm=None):
    P, sb, ps = K.P, K.sb, K.ps
    K.phase()
    xT3 = xin.rearrange("(c p) t -> p c t", p=128)
    gf = sb("gf", [128, 8], F32)
    P.dma("sp", gf[:], d[f"gffn{li}"], writes=[gf])
    convp = sb("convp", [128, 2 * NFF, 4], F32)
    P.dma("sp", convp[:], d[f"convp{li}"], writes=[convp])
    nf = sb("nf", [128, 1], F32)
    P.dma("sp", nf[:], d["notfirst"], writes=[nf])
    aT = sb("aT", [128, NFF, NT], BF16)
    aTs = [Buf(f"aT{s}", aT.t[:, :, s * SW:(s + 1) * SW]) for s in range(NS)]
    mark = K.A.cur
    hT = sb("h2T", [128, 8, NT], BF16)
    hTs = [Buf(f"h2T{s}", hT.t[:, :, s * SW:(s + 1) * SW]) for s in range(NS)]
    xst = [sb("xst", [128, 8, SW], F32) for _ in range(1)]
    sq = sb("sq", [128, 8, SW], BF16)
    rstd = sb("rstd", [128, SW], F32)
    ps_n = ps(0, [128, SW])
    for s in range(NS):
        xs = xst[0]
        P.dma("sp", xs[:], xT3[:, :, s * SW:(s + 1) * SW], reads=[xin_tok], writes=[xs])
        rmsnorm_fm(P, C, xs, gf, hTs[s], hTs[s], sq, ps_n, rstd)
    xh = sb("xh", [128, 8, 2], F32)

    def dyn(e):
        pid = P.pid(e)
        prev = (pid + 7) % 8
        return e.dma_start(out=xh[:], in_=halo_all[bass.ds(prev * 128, 128), :].rearrange("p (c t) -> p c t", c=8))
    P._add("sp", dyn, [halo_tok], [xh], (), True)
    sqh = sb("sqh", [128, 8, 2], BF16)
    rsh = sb("rsh", [128, 2], F32)
    hh = sb("hh", [128, 8, 2], BF16)
    ps_h = ps(1, [128, 2])
    P.op("act", lambda e: e.activation(out=sqh[:], in_=xh[:], func=AF.Square), reads=[xh], writes=[sqh])
    for c in range(8):
        P.op("pe", lambda e, c=c: e.matmul(ps_h[:], lhsT=C.ones[:], rhs=sqh[:, c, :], start=(c == 0), stop=(c == 7)),
             reads=[C.ones, sqh], writes=[ps_h])
    P.op("act", lambda e: e.activation(out=rsh[:], in_=ps_h[:], func=AF.Sqrt, scale=1.0 / 1024, bias=C.eps[:]),
         reads=[ps_h, C.eps], writes=[rsh])
    P.op("dve", lambda e: e.reciprocal(out=rsh[:], in_=rsh[:]), reads=[rsh], writes=[rsh])
    P.op("dve", lambda e: e.tensor_scalar(out=rsh[:], in0=rsh[:], scalar1=nf[:, 0:1], scalar2=None, op0=ALU.mult),
         reads=[rsh, nf], writes=[rsh])
    for c in range(8):
        P.op("dve", lambda e, c=c: e.scalar_tensor_tensor(out=hh[:, c, :], in0=xh[:, c, :], scalar=gf[:, c:c + 1],
                                                          in1=rsh[:], op0=ALU.mult, op1=ALU.mult),
             reads=[xh, gf, rsh], pwrites=[hh])

    wu = [[sb("wu", [128, 8, 128], BF16) for _ in range(2)] for _ in range(2)]
    ug = [sb("ug", [128, SW + 2], F32) for _ in range(2)]
    uv = [sb("uv", [128, SW + 2], F32) for _ in range(2)]
    ag = [sb("ag", [128, SW], F32) for _ in range(2)]
    av = [sb("av", [128, SW], F32) for _ in range(2)]
    sg = [sb("sg", [128, SW], F32) for _ in range(2)]
    ps_g = [ps(2, [128, SW]), ps(3, [128, SW])]
    ps_v = [ps(4, [128, SW]), ps(5, [128, SW])]
    ps_hh = ps(1, [128, 2, 2])
    it = 0
    for c in range(NFF):
        wg_, wv_ = wu[c % 2]
        P.dma("pool", wg_[:], d[f"w_up_r{li}"][c], writes=[wg_])
        P.dma("pool", wv_[:], d[f"w_up_r{li}"][c + NFF], writes=[wv_])
        for s in range(NS):
            sl = slice(s * SW, (s + 1) * SW)
            cur, prv = it % 2, (it + 1) % 2
            it += 1
            pg, pv = ps_g[cur], ps_v[cur]
            ugc, uvc, agc, avc, sgc = ug[cur], uv[cur], ag[cur], av[cur], sg[cur]
            for (pp, ww) in ((pg, wg_), (pv, wv_)):
                for m in range(8):
                    P.op("pe", lambda e, m=m, pp=pp, ww=ww, sl=sl: e.matmul(
                        pp[:], lhsT=ww[:, m, :], rhs=hT[:, m, sl], start=(m == 0), stop=(m == 7)),
                        reads=[ww, hTs[s]], writes=[pp])
            if s == 0:
                for gi, ww in ((0, wg_), (1, wv_)):
                    for m in range(8):
                        P.op("pe", lambda e, m=m, gi=gi, ww=ww: e.matmul(
                            ps_hh[:, gi, :], lhsT=ww[:, m, :], rhs=hh[:, m, :], start=(m == 0), stop=(m == 7)),
                            reads=[ww, hh], writes=[ps_hh])
                P.op("dve", lambda e, ugc=ugc: e.tensor_copy(out=ugc[:, 0:2], in_=ps_hh[:, 0, :]), reads=[ps_hh], pwrites=[ugc])
                P.op("dve", lambda e, uvc=uvc: e.tensor_copy(out=uvc[:, 0:2], in_=ps_hh[:, 1, :]), reads=[ps_hh], pwrites=[uvc])
            else:
                P.op("dve", lambda e, ugc=ugc, p_=ug[prv]: e.tensor_copy(out=ugc[:, 0:2], in_=p_[:, SW:SW + 2]),
                     reads=[ug[prv]], pwrites=[ugc])
                P.op("pool", lambda e, uvc=uvc, p_=uv[prv]: e.tensor_copy(out=uvc[:, 0:2], in_=p_[:, SW:SW + 2]),
                     reads=[uv[prv]], pwrites=[uvc])
            cg, cv = c, c + NFF
            P.op("act", lambda e, ugc=ugc, pg=pg: e.copy(out=ugc[:, 2:SW + 2], in_=pg[:]), reads=[pg], pwrites=[ugc])
            P.op("act", lambda e, agc=agc, pg=pg, cg=cg: e.activation(out=agc[:], in_=pg[:], func=AF.Identity,
                                                                      scale=convp[:, cg, 2:3], bias=convp[:, cg, 3:4]),
                 reads=[pg, convp], writes=[agc])
            P.op("act", lambda e, uvc=uvc, pv=pv: e.copy(out=uvc[:, 2:SW + 2], in_=pv[:]), reads=[pv], pwrites=[uvc])
            P.op("act", lambda e, avc=avc, pv=pv, cv=cv: e.activation(out=avc[:], in_=pv[:], func=AF.Identity,
                                                                      scale=convp[:, cv, 2:3], bias=convp[:, cv, 3:4]),
                 reads=[pv, convp], writes=[avc])
            P.op("dve", lambda e, agc=agc, ugc=ugc, cg=cg: e.scalar_tensor_tensor(
                out=agc[:], in0=ugc[:, 1:SW + 1], scalar=convp[:, cg, 1:2], in1=agc[:], op0=ALU.mult, op1=ALU.add),
                reads=[ugc, convp, agc], writes=[agc])
            P.op("dve", lambda e, agc=agc, ugc=ugc, cg=cg: e.scalar_tensor_tensor(
                out=agc[:], in0=ugc[:, 0:SW], scalar=convp[:, cg, 0:1], in1=agc[:], op0=ALU.mult, op1=ALU.add),
                reads=[ugc, convp, agc], writes=[agc])
            P.op("dve", lambda e, avc=avc, uvc=uvc, cv=cv: e.scalar_tensor_tensor(
                out=avc[:], in0=uvc[:, 1:SW + 1], scalar=convp[:, cv, 1:2], in1=avc[:], op0=ALU.mult, op1=ALU.add),
                reads=[uvc, convp, avc], writes=[avc])
            P.op("dve", lambda e, avc=avc, uvc=uvc, cv=cv: e.scalar_tensor_tensor(
                out=avc[:], in0=uvc[:, 0:SW], scalar=convp[:, cv, 0:1], in1=avc[:], op0=ALU.mult, op1=ALU.add),
                reads=[uvc, convp, avc], writes=[avc])
            P.op("act", lambda e, sgc=sgc, agc=agc: e.activation(out=sgc[:], in_=agc[:], func=AF.Silu), reads=[agc], writes=[sgc])
            P.op("dve", lambda e, sgc=sgc, avc=avc, c=c, sl=sl: e.tensor_tensor(out=aT[:, c, sl], in0=sgc[:], in1=avc[:], op=ALU.mult),
                 reads=[sgc, avc], pwrites=[aTs[s]])

    P.barrier()
    K.A.cur = mark
    wd = [sb("wd", [128, NFF, 128], BF16) for _ in range(2)]
    xj = [sb("xj", [128, SW], F32) for _ in range(3)]
    ps_y = [ps(0, [128, SW]), ps(1, [128, SW])]
    xin3 = xin.rearrange("(c p) t -> p c t", p=128)
    xout3 = xout.rearrange("(c p) t -> p c t", p=128)
    it = 0
    for j in range(8):
        wdj = wd[j % 2]
        P.dma("pool", wdj[:], d[f"w_down_r{li}"][j], writes=[wdj])
        for s in range(NS):
            sl = slice(s * SW, (s + 1) * SW)
            py = ps_y[it % 2]
            xs = xj[it % 3]
            it += 1
            P.dma("sp", xs[:], xin3[:, j, sl], reads=[xin_tok], writes=[xs])
            for c in range(NFF):
                P.op("pe", lambda e, c=c, py=py, wdj=wdj, sl=sl: e.matmul(
                    py[:], lhsT=wdj[:, c, :], rhs=aT[:, c, sl], start=(c == 0), stop=(c == NFF - 1)),
                    reads=[wdj, aTs[s]], writes=[py])
            P.op("dve", lambda e, py=py, xs=xs: e.tensor_tensor(out=xs[:], in0=py[:], in1=xs[:], op=ALU.add),
                 reads=[py, xs], writes=[xs])
            P.dma("sp", xout3[:, j, sl], xs[:], reads=[xs], pwrites=[xout_tok])


def final_norm(K, C, d, xin, xin_tok, out, out_tok):
    P, sb, ps = K.P, K.sb, K.ps
    K.phase()
    gfin = sb("gfin", [128, 8], F32)
    P.dma("sp", gfin[:], d["gfinal"], writes=[gfin])
    xT3 = xin.rearrange("(c p) t -> p c t", p=128)
    o3 = out.rearrange("(c p) t -> p c t", p=128)
    xst = [sb("xst", [128, 8, SW], F32) for _ in range(2)]
    ost = [sb("ost", [128, 8, SW], F32) for _ in range(2)]
    sq = sb("sq", [128, 8, SW], BF16)
    rstd = sb("rstd", [128, SW], F32)
    ps_n = ps(0, [128, SW])
    for s in range(NS):
        xs, os_ = xst[s % 2], ost[s % 2]
        P.dma("sp", xs[:], xT3[:, :, s * SW:(s + 1) * SW], reads=[xin_tok], writes=[xs])
        rmsnorm_fm(P, C, xs, gfin, os_, os_, sq, ps_n, rstd)
        P.dma("sp", o3[:, :, s * SW:(s + 1) * SW], os_[:], reads=[os_], pwrites=[out_tok])


import math

T_ALL = 16384
NB = T_ALL // 128
NQS = T_ALL // SW
LAM_INIT = 0.8 - 0.6 * math.exp(-0.3 * 1)
NEG = -30000.0
GLEN = 1151


def kvq_proj(K, C, d, xin, xin_tok, qkv_in, qkv_tok):
    P, sb, ps = K.P, K.sb, K.ps
    K.phase()
    xT3 = xin.rearrange("(c p) t -> p c t", p=128)
    gkv = sb("gkv", [128, 8], F32)
    gq = sb("gq", [128, 8], F32)
    P.dma("sp", gkv[:], d["gkv"], writes=[gkv])
    P.dma("sp", gq[:], d["gmix1"], writes=[gq])
    hk = sb("hk", [128, 8, NT], BF16)
    hq = sb("hq", [128, 8, NT], BF16)
    hks = [Buf(f"hk{s}", hk.t[:, :, s * SW:(s + 1) * SW]) for s in range(NS)]
    hqs = [Buf(f"hq{s}", hq.t[:, :, s * SW:(s + 1) * SW]) for s in range(NS)]
    xst = sb("xst", [128, 8, SW], F32)
    sq = sb("sq", [128, 8, SW], BF16)
    rstd = sb("rstd", [128, SW], F32)
    ps_n = ps(0, [128, SW])
    for s in range(NS):
        P.dma("sp", xst[:], xT3[:, :, s * SW:(s + 1) * SW], reads=[xin_tok], writes=[xst])
        rmsnorm_fm(P, C, xst, gkv, hks[s], hks[s], sq, ps_n, rstd)
        for c in range(8):
            P.op("dve", lambda e, c=c, s=s: e.scalar_tensor_tensor(out=hqs[s][:, c, :], in0=xst[:, c, :], scalar=gq[:, c:c + 1],
                                                                    in1=rstd[:], op0=ALU.mult, op1=ALU.mult),
                 reads=[xst, gq, rstd], pwrites=[hqs[s]])
    wk = [sb("wk", [128, 8, 128], BF16) for _ in range(2)]
    wq = [sb("wq", [128, 8, 128], BF16) for _ in range(2)]
    kst = [sb("kst", [128, NT], BF16) for _ in range(2)]
    qst = [sb("qst", [128, NT], BF16) for _ in range(2)]
    psk = [ps(1, [128, SW]), ps(2, [128, SW])]
    psq = [ps(3, [128, SW]), ps(4, [128, SW])]
    it = 0
    for h in range(8):
        wkh, wqh, ks_, qs_ = wk[h % 2], wq[h % 2], kst[h % 2], qst[h % 2]
        P.dma("pool", wkh[:], d["w_k_r"][h], writes=[wkh])
        P.dma("pool", wqh[:], d["w_q_r"][h], writes=[wqh])
        for s in range(NS):
            sl = slice(s * SW, (s + 1) * SW)
            pk, pq = psk[it % 2], psq[it % 2]
            it += 1
            for m in range(8):
                P.op("pe", lambda e, m=m, pk=pk, wkh=wkh, sl=sl: e.matmul(pk[:], lhsT=wkh[:, m, :], rhs=hk[:, m, sl],
                                                                          start=(m == 0), stop=(m == 7)),
                     reads=[wkh, hks[s]], writes=[pk])
            for m in range(8):
                P.op("pe", lambda e, m=m, pq=pq, wqh=wqh, sl=sl: e.matmul(pq[:], lhsT=wqh[:, m, :], rhs=hq[:, m, sl],
                                                                          start=(m == 0), stop=(m == 7)),
                     reads=[wqh, hqs[s]], writes=[pq])
            P.op("act", lambda e, pk=pk, ks_=ks_, sl=sl: e.copy(out=ks_[:, sl], in_=pk[:]), reads=[pk], pwrites=[ks_])
            P.op("dve", lambda e, pq=pq, qs_=qs_, sl=sl: e.tensor_scalar(out=qs_[:, sl], in0=pq[:], scalar1=0.125, scalar2=None,
                                                                         op0=ALU.mult), reads=[pq], pwrites=[qs_])
        P.dma("sp", qkv_in[h * 384:h * 384 + 128, :], qs_[:], reads=[qs_], pwrites=[qkv_tok])
        P.dma("sp", qkv_in[h * 384 + 128:h * 384 + 256, :], ks_[:], reads=[ks_], pwrites=[qkv_tok])
    wv = sb("wv", [128, 8, 1024], BF16)
    for h in range(8):
        P.dma("pool", wv[:, :, h * 128:(h + 1) * 128], d["w_v_r"][h], pwrites=[wv])
    vstage = sb("vstage", [128, 8, 16, 128], BF16)
    psv = [ps(1, [128, 4, 128]), ps(2, [128, 4, 128])]
    psv_flat = [ps(1, [128, 512]), ps(2, [128, 512])]
    for tb in range(16):
        s = tb // 4
        for hf in range(2):
            pv = psv_flat[hf]
            for m in range(8):
                P.op("pe", lambda e, m=m, pv=pv, tb=tb, hf=hf: e.matmul(
                    pv[:], lhsT=hk[:, m, tb * 128:(tb + 1) * 128], rhs=wv[:, m, hf * 512:(hf + 1) * 512],
                    start=(m == 0), stop=(m == 7)), reads=[hks[s], wv], writes=[pv])
            if hf == 0:
                P.op("act", lambda e, tb=tb, hf=hf: e.copy(out=vstage[:, hf * 4:(hf + 1) * 4, tb, :], in_=psv[hf][:]),
                     reads=[psv[hf]], pwrites=[vstage])
            else:
                P.op("dve", lambda e, tb=tb, hf=hf: e.tensor_copy(out=vstage[:, hf * 4:(hf + 1) * 4, tb, :], in_=psv[hf][:]),
                     reads=[psv[hf]], pwrites=[vstage])
    for h in range(8):
        P.dma("sp", qkv_in[h * 384 + 256:h * 384 + 384, :], vstage.t[:, h, :, :].rearrange("p b v -> p (b v)"),
              reads=[vstage], pwrites=[qkv_tok])


def attn_core(K, C, d, qkv_all, qkv_all_tok, o_in, o_tok, gvec, gvec_tok):
    P, sb, ps = K.P, K.sb, K.ps
    K.phase()
    QKV = sb("QKV", [128, 3, 8, NT], BF16)
    QT = QKV.alias(QKV.t[:, 0, :, :].rearrange("p r t -> p (r t)"))
    KT = QKV.alias(QKV.t[:, 1, :, :].rearrange("p r t -> p (r t)"))
    VA = sb("VA", [128, NB, 129], BF16)
    P.op("pool", lambda e: e.memset(VA[:], 1.0), writes=[VA])
    q4 = qkv_all.rearrange("(r h x) t -> r h x t", r=8, h=8)
    for r in range(8):
        def fn(e, r=r):
            pid = P.pid(e)
            src = q4[r, bass.ds(pid, 1), :, :].rearrange("o (k p) t -> p (o k) t", k=3)
            return e.dma_start(out=QKV[:, :, r, :], in_=src)
        P._add("act", fn, [qkv_all_tok], (), [QKV], True)
    for r in range(8):
        P.op("pool", lambda e, r=r: e.tensor_copy(out=VA[:, r * 16:(r + 1) * 16, 0:128],
                                                  in_=QKV[:, 2, r, :].rearrange("p (b v) -> p b v", b=16)),
             reads=[QKV], pwrites=[VA])
    Vreg = QKV.t[:, 2, :, :].rearrange("p r t -> p (r t)")
    P.op("pool", lambda e: e.tensor_copy(out=Vreg[64:128, :], in_=QT[64:128, :]), reads=[QKV], pwrites=[QKV])
    P.op("pool", lambda e: e.memset(Vreg[0:64, :], 0.0), pwrites=[QKV])
    P.op("pool", lambda e: e.memset(QT[64:128, :], 0.0), reads=[QKV], pwrites=[QKV])
    Qz = [QT, QKV.alias(Vreg)]
    lamv = sb("lamv", [128, 4, 64], F32)
    P.dma("sp", lamv[:], d["lamv"].partition_broadcast(128), writes=[lamv])
    lp = sb("lp", [128, 2, 64], F32)
    ls = sb("ls", [128, 2], F32)
    nlam = sb("nlam", [128, 1], F32)
    P.op("dve", lambda e: e.tensor_tensor(out=lp[:, 0, :], in0=lamv[:, 0, :], in1=lamv[:, 1, :], op=ALU.mult), reads=[lamv], pwrites=[lp])
    P.op("dve", lambda e: e.tensor_tensor(out=lp[:, 1, :], in0=lamv[:, 2, :], in1=lamv[:, 3, :], op=ALU.mult), reads=[lamv], pwrites=[lp])
    P.op("dve", lambda e: e.reduce_sum(out=ls[:], in_=lp[:], axis=AX.X), reads=[lp], writes=[ls])
    P.op("act", lambda e: e.activation(out=ls[:], in_=ls[:], func=AF.Exp), reads=[ls], writes=[ls])
    P.op("dve", lambda e: e.tensor_sub(out=nlam[:], in0=ls[:, 1:2], in1=ls[:, 0:1]), reads=[ls], writes=[nlam])
    P.op("dve", lambda e: e.tensor_scalar(out=nlam[:], in0=nlam[:], scalar1=-LAM_INIT, scalar2=None, op0=ALU.add),
         reads=[nlam], writes=[nlam])
    gsub = sb("gsub", [128, 128], F32)
    P.dma("sp", gsub[:], d["subln"].partition_broadcast(128), writes=[gsub])
    P.op("dve", lambda e: e.tensor_scalar(out=gsub[:], in0=gsub[:], scalar1=1.0 - LAM_INIT, scalar2=None, op0=ALU.mult),
         reads=[gsub], writes=[gsub])
    eps128 = C.eps
    relcol = sb("relcol", [32, 1], F32)
    oh = sb("oh", [32, 128], F32)
    P.dma("sp", relcol[:], d["relcol"], writes=[relcol])
    P.dma("sp", oh[:], d["oh"], writes=[oh])
    ohb = sb("ohb", [32, 128], BF16)
    rc_hi = sb("rc_hi", [32, 1], BF16)
    rc_lo = sb("rc_lo", [32, 1], BF16)
    P.op("dve", lambda e: e.tensor_copy(out=ohb[:], in_=oh[:]), reads=[oh], writes=[ohb])
    P.op("dve", lambda e: e.tensor_copy(out=rc_hi[:], in_=relcol[:]), reads=[relcol], writes=[rc_hi])
    P.op("dve", lambda e: e.tensor_tensor(out=rc_lo[:], in0=relcol[:], in1=rc_hi[:], op=ALU.subtract),
         reads=[relcol, rc_hi], writes=[rc_lo])
    ps_g = ps(0, [1, 128])
    gm = sb("gm", [1, 128], F32)
    P.op("pe", lambda e: e.matmul(ps_g[:], lhsT=rc_hi[:], rhs=ohb[:], start=True, stop=False), reads=[rc_hi, ohb], writes=[ps_g])
    P.op("pe", lambda e: e.matmul(ps_g[:], lhsT=rc_lo[:], rhs=ohb[:], start=False, stop=True), reads=[rc_lo, ohb], writes=[ps_g])
    P.op("act", lambda e: e.copy(out=gm[:], in_=ps_g[:]), reads=[ps_g], writes=[gm])
    gv = gvec.ap()
    P.dma("sp", gv, d["gconst"], writes=[gvec_tok])
    P.dma("sp", gv[:, 511:639], gm[:], reads=[gm], writes=[gvec_tok])
    btile = sb("btile", [128, 5, SW], F32)
    antiI = sb("antiI", [128, 128], BF16)
    P.dma("pool", antiI[:], d["antiI"], writes=[antiI])
    hk_t = [sb("hk_t", [128, SW], F32) for _ in range(2)]
    hk_hi = [sb("hk_hi", [128, SW], BF16) for _ in range(2)]
    hk_lo = [sb("hk_lo", [128, SW], BF16) for _ in range(2)]
    for i in range(5):
        src = bass.AP(gvec, 512 - 128 * i, [[1, 128], [1, SW]])
        hkt, hhi, hlo = hk_t[i % 2], hk_hi[i % 2], hk_lo[i % 2]
        P.dma("sp", hkt[:], src, reads=[gvec_tok], writes=[hkt])
        P.op("dve", lambda e, hkt=hkt, hhi=hhi: e.tensor_copy(out=hhi[:], in_=hkt[:]), reads=[hkt], writes=[hhi])
        P.op("dve", lambda e, hkt=hkt, hhi=hhi, hlo=hlo: e.tensor_tensor(out=hlo[:], in0=hkt[:], in1=hhi[:], op=ALU.subtract),
             reads=[hkt, hhi], writes=[hlo])
        pbt = K.pb[i % 2]
        P.op("pe", lambda e, pbt=pbt, hhi=hhi: e.matmul(pbt[:], lhsT=antiI[:], rhs=hhi[:], start=True, stop=False),
             reads=[antiI, hhi], writes=[pbt])
        P.op("pe", lambda e, pbt=pbt, hlo=hlo: e.matmul(pbt[:], lhsT=antiI[:], rhs=hlo[:], start=False, stop=True),
             reads=[antiI, hlo], writes=[pbt])
        P.op("act", lambda e, pbt=pbt, i=i: e.copy(out=btile[:, i, :], in_=pbt[:]), reads=[pbt], pwrites=[btile])

    pT = [[sb("pT", [128, SW], BF16) for _ in range(2)] for _ in range(2)]
    stmp = [sb("stmp", [128, SW], F32) for _ in range(2)]
    psS = [[K.pb[0], K.pb[1]], [K.pb[2], K.pb[3]]]
    accb = [K.pb[4], K.pb[5], K.pb2]

    def acc(m, j):
        i = m * 4 + j
        b = accb[i // 3]
        o = (i % 3) * 129
        return b, b.t[:, o:o + 129]
    ps_tr = K.pb2.alias(K.pb2.t[:, 512:768].bitcast(BF16))
    accS = [sb("accS", [128, 8 * 129], F32) for _ in range(2)]
    o_sb = [sb("o_sb", [128, 128], F32) for _ in range(4)]
    osq = [sb("osq", [128, 128], F32) for _ in range(2)]
    on = [sb("on", [128, 128], BF16) for _ in range(4)]
    sm = [sb("sm", [128, 8], F32) for _ in range(4)]
    oT_st = [sb("oT_st", [128, SW], BF16) for _ in range(2)]
    def emit_qk(qs, kb, maps=(0, 1)):
        i_near = kb - (qs * 4 - 1)
        near = i_near >= 0
        j0 = max(0, kb - qs * 4)
        c0 = j0 * 128
        for m in maps:
            pS = psS[m][kb % 2]
            P.op("pe", lambda e, pS=pS, m=m, kb=kb, qs=qs, c0=c0: e.matmul(
                pS[:, c0:SW], lhsT=KT[:, kb * 128:(kb + 1) * 128], rhs=Qz[m][:, qs * SW + c0:(qs + 1) * SW],
                start=True, stop=True), reads=[KT, QT], writes=[pS])
        for m in maps:
            pS = psS[m][kb % 2]
            pt = pT[m][kb % 2]
            if near:
                st = stmp[m]
                P.op("dve", lambda e, st=st, pS=pS, i_near=i_near, c0=c0: e.tensor_tensor(
                    out=st[:, c0:SW], in0=pS[:, c0:SW], in1=btile[:, i_near, c0:SW], op=ALU.add),
                    reads=[pS, btile], writes=[st])
                P.op("act", lambda e, st=st, pt=pt, c0=c0: e.activation(out=pt[:, c0:SW], in_=st[:, c0:SW], func=AF.Exp),
                     reads=[st], writes=[pt])
            else:
                P.op("act", lambda e, pS=pS, pt=pt: e.activation(out=pt[:], in_=pS[:], func=AF.Exp), reads=[pS], writes=[pt])

    def emit_pv(qs, kb, maps=(0, 1), last=True):
        j0 = max(0, kb - qs * 4)
        for m in maps:
            pt = pT[m][kb % 2]
            for j in range(j0, 4):
                ab, aap = acc(m, j)
                st_ = (kb == 0) and ((m * 4 + j) % 3 == 0)
                P.op("pe", lambda e, aap=aap, pt=pt, j=j, kb=kb, qs=qs, st_=st_: e.matmul(
                    aap, lhsT=pt[:, j * 128:(j + 1) * 128], rhs=VA[:, kb, :], start=st_, stop=(kb == qs * 4 + j)),
                    reads=[pt, VA], pwrites=[ab])
        if last and kb == (qs + 1) * 4 - 1:
            epilogue(qs)

    def epilogue(qs):
        aS = accS[qs % 2]
        P.op("act", lambda e, aS=aS: e.copy(out=aS[:, 0:387], in_=accb[0].t[:, 0:387]), reads=[accb[0]], pwrites=[aS])
        P.op("dve", lambda e, aS=aS: e.tensor_copy(out=aS[:, 387:774], in_=accb[1].t[:, 0:387]), reads=[accb[1]], pwrites=[aS])
        P.op("act", lambda e, aS=aS: e.copy(out=aS[:, 774:1032], in_=accb[2].t[:, 0:258]), reads=[accb[2]], pwrites=[aS])
        for j in range(4):
            a0 = aS.t[:, j * 129:(j + 1) * 129]
            a1 = aS.t[:, (4 + j) * 129:(5 + j) * 129]
            s_, o_, q_, n_ = sm[j], o_sb[j], osq[j % 2], on[j]
            P.op("dve", lambda e, a0=a0, s_=s_: e.reciprocal(out=s_[:, 0:1], in_=a0[:, 128:129]), reads=[aS], pwrites=[s_])
            P.op("dve", lambda e, a1=a1, s_=s_: e.reciprocal(out=s_[:, 1:2], in_=a1[:, 128:129]), reads=[aS], pwrites=[s_])
            P.op("dve", lambda e, s_=s_: e.tensor_tensor(out=s_[:, 2:3], in0=s_[:, 1:2], in1=nlam[:], op=ALU.mult),
                 reads=[s_, nlam], pwrites=[s_])
            P.op("dve", lambda e, a0=a0, s_=s_, o_=o_: e.tensor_scalar(out=o_[:], in0=a0[:, 0:128], scalar1=s_[:, 0:1], scalar2=None,
                                                                      op0=ALU.mult), reads=[aS, s_], writes=[o_])
            P.op("dve", lambda e, a1=a1, s_=s_, o_=o_: e.scalar_tensor_tensor(out=o_[:], in0=a1[:, 0:128], scalar=s_[:, 2:3], in1=o_[:],
                                                                             op0=ALU.mult, op1=ALU.add), reads=[aS, s_, o_], writes=[o_])
            P.op("pool", lambda e, o_=o_, q_=q_: e.tensor_tensor(out=q_[:], in0=o_[:], in1=o_[:], op=ALU.mult), reads=[o_], writes=[q_])
            P.op("dve", lambda e, s_=s_, q_=q_: e.reduce_sum(out=s_[:, 3:4], in_=q_[:], axis=AX.X), reads=[q_], pwrites=[s_])
            P.op("act", lambda e, s_=s_: e.activation(out=s_[:, 4:5], in_=s_[:, 3:4], func=AF.Sqrt, scale=1.0 / 128, bias=eps128[:]),
                 reads=[s_, eps128], pwrites=[s_])
            P.op("dve", lambda e, s_=s_: e.reciprocal(out=s_[:, 5:6], in_=s_[:, 4:5]), reads=[s_], pwrites=[s_])
            P.op("dve", lambda e, s_=s_, o_=o_, n_=n_: e.scalar_tensor_tensor(out=n_[:], in0=o_[:], scalar=s_[:, 5:6], in1=gsub[:],
                                                                             op0=ALU.mult, op1=ALU.mult), reads=[o_, s_, gsub], writes=[n_])

    def epilogue_out(qs):
        ost = oT_st[qs % 2]
        for j in range(4):
            P.op("pe", lambda e, j=j: e.transpose(out=ps_tr[:, j * 128:(j + 1) * 128], in_=on[j][:], identity=C.ident[:]),
                 reads=[on[j], C.ident], pwrites=[ps_tr])
        P.op("act", lambda e, ost=ost: e.copy(out=ost[:], in_=ps_tr[:]), reads=[ps_tr], writes=[ost])
        P.dma("sp", o_in[:, qs * SW:(qs + 1) * SW], ost[:], reads=[ost], pwrites=[o_tok])

    units = [(qs, kb) for qs in range(NQS) for kb in range((qs + 1) * 4)]
    pending = []
    for idx in range(len(units) + 1):
        for m in range(2):
            if idx < len(units):
                emit_qk(*units[idx], maps=(m,))
            if idx >= 1:
                emit_pv(*units[idx - 1], maps=(m,), last=(m == 1))
        if idx >= 1:
            qs_, kb_ = units[idx - 1]
            if kb_ == (qs_ + 1) * 4 - 1:
                pending.append((idx + 3, qs_))
        while pending and pending[0][0] <= idx:
            epilogue_out(pending.pop(0)[1])
    for _, qs_ in pending:
        epilogue_out(qs_)


def attn_out(K, C, d, o_all, o_all_tok, xin, xin_tok, xout, xout_tok, halo_in, halo_tok):
    P, sb, ps = K.P, K.sb, K.ps
    K.phase()
    og = sb("og", [128, 8, NT], BF16)

    def fn(e):
        pid = P.pid(e)
        return e.dma_start(out=og[:], in_=o_all.rearrange("(h p) t -> p h t", p=128)[:, :, bass.ds(pid * NT, NT)])
    P._add("sp", fn, [o_all_tok], [og], (), True)
    wo = sb("wo", [128, 8, 1024], BF16)
    P.dma("pool", wo[:], d["w_o_r"], writes=[wo])
    xj = [sb("xj", [128, SW], F32) for _ in range(3)]
    ps_y = [ps(0, [128, SW]), ps(1, [128, SW])]
    xin3 = xin.rearrange("(c p) t -> p c t", p=128)
    xout3 = xout.rearrange("(c p) t -> p c t", p=128)
    it = 0
    for j in range(8):
        for s in range(NS):
            sl = slice(s * SW, (s + 1) * SW)
            py = ps_y[it % 2]
            xs = xj[it % 3]
            it += 1
            P.dma("sp", xs[:], xin3[:, j, sl], reads=[xin_tok], writes=[xs])
            for h in range(8):
                P.op("pe", lambda e, h=h, j=j, py=py, sl=sl: e.matmul(py[:], lhsT=wo[:, h, j * 128:(j + 1) * 128], rhs=og[:, h, sl],
                                                                      start=(h == 0), stop=(h == 7)), reads=[wo, og], writes=[py])
            P.op("dve", lambda e, py=py, xs=xs: e.tensor_tensor(out=xs[:], in0=py[:], in1=xs[:], op=ALU.add),
                 reads=[py, xs], writes=[xs])
            P.dma("sp", xout3[:, j, sl], xs[:], reads=[xs], pwrites=[xout_tok])
            if s == NS - 1:
                P.dma("sp", halo_in[:, j * 2:(j + 1) * 2], xs[:, SW - 2:SW], reads=[xs], pwrites=[halo_tok])


import numpy as np
from concourse.bass_utils import run_bass_kernel_spmd

NCORES = 8


def allgather(P, src_h, dst_h, src_tok, dst_tok, rows=None):
    dst = dst_h.ap() if rows is None else dst_h.ap()[0:rows, :]
    P.async_op("pool", lambda e: e.collective_compute("AllGather", ALU.bypass, replica_groups=[list(range(NCORES))],
                                                      ins=[src_h.ap().opt()], outs=[dst.opt()]),
               reads=[src_tok], writes=[dst_tok], inc=1)


IN_SPECS = [
    ("xT", [1024, NT]), ("w_in_r", [32, 128, 8, 128]), ("w_out_r", [128, 8, 1024]), ("ident", [128, 128]),
    ("resetm", [128, SW]), ("maskc", [64, 8, 64]), ("gmix0", [128, 8]), ("gnorm", [128, 8]), ("lbl", [128, 2, 8]),
    ("sel", [128, 8]), ("notfirst", [128, 1]),
    ("gffn0", [128, 8]), ("w_up_r0", [44, 128, 8, 128]), ("convp0", [128, 44, 4]), ("w_down_r0", [8, 128, 22, 128]),
    ("gffn1", [128, 8]), ("w_up_r1", [44, 128, 8, 128]), ("convp1", [128, 44, 4]), ("w_down_r1", [8, 128, 22, 128]),
    ("gkv", [128, 8]), ("gmix1", [128, 8]), ("w_k_r", [8, 128, 8, 128]), ("w_q_r", [8, 128, 8, 128]),
    ("w_v_r", [8, 128, 8, 128]), ("relcol", [32, 1]), ("oh", [32, 128]), ("gconst", [1, GLEN]), ("antiI", [128, 128]), ("lamv", [4, 64]),
    ("subln", [1, 128]), ("w_o_r", [128, 8, 1024]), ("gfinal", [128, 8]),
]


def build(debug=None):
    nc = bass.Bass("TRN2", target_bir_lowering=False)
    K = KB(nc)
    P = K.P
    d = {}
    for name, shape in IN_SPECS:
        d[name] = nc.dram_tensor(name, shape, F32, kind="ExternalInput").ap()
    outT = nc.dram_tensor("outT", [1024, NT], F32, kind="ExternalOutput").ap()
    out_tok = Buf("out")

    def stream(name):
        kind = "ExternalOutput" if debug == name else "Internal"
        return nc.dram_tensor(name, [1024, NT], F32, kind=kind).ap(), Buf(name)
    x1T, x1_tok = stream("x1T")
    x2T, x2_tok = stream("x2T")
    x3T, x3_tok = stream("x3T")
    x4T, x4_tok = stream("x4T")
    scr = {}
    for n in ("qT", "kT", "gT"):
        scr[n] = nc.dram_tensor(n + "_s", [1024, NT], BF16).ap()
        scr[n + "_tok"] = Buf(n)
    for n in ("v", "kt"):
        scr[n] = nc.dram_tensor(n + "_s", [NT, 1024], BF16).ap()
        scr[n + "_tok"] = Buf(n)
    hx_in = nc.dram_tensor("hx_in", [128, 1032], F32)
    hx_all = nc.dram_tensor("hx_all", [NCORES * 128, 1032], F32)
    hx_in_tok, hx_all_tok = Buf("hx_in"), Buf("hx_all")
    halo_in = [nc.dram_tensor(f"halo_in{i}", [128, 16], F32) for i in range(2)]
    halo_all = [nc.dram_tensor(f"halo_all{i}", [NCORES * 128, 16], F32) for i in range(2)]
    halo_in_tok = [Buf("hi0"), Buf("hi1")]
    halo_all_tok = [Buf("ha0"), Buf("ha1")]
    qkv_in = nc.dram_tensor("qkv_in", [3072, NT], BF16)
    qkv_all = nc.dram_tensor("qkv_all", [NCORES * 3072 + 384, NT], BF16)
    qkv_in_tok, qkv_all_tok = Buf("qkv_in"), Buf("qkv_all")
    gvec = nc.dram_tensor("gvec", [1, GLEN], F32)
    gvec_tok = Buf("gvec")
    o_in = nc.dram_tensor("o_in", [128, T_ALL], BF16)
    o_all = nc.dram_tensor("o_all", [NCORES * 128, T_ALL], BF16)
    o_in_tok, o_all_tok = Buf("o_in"), Buf("o_all")

    C = hgrn_consts(K, d)
    T = hgrn_alloc_T(K)
    S = K.sb("S", [128, 8, 128], F32, pers=True)
    Rr = K.sb("Rr", [128, 8, 128], F32, pers=True)
    sel = K.sb("sel", [128, 8], F32, pers=True)
    P.dma("sp", sel[:], d["sel"], writes=[sel])
    P.op("pool", lambda e: e.memset(S[:], 0.0), writes=[S])
    K.A.start_phase()
    hgrn_P(K, C, d, scr, T)
    K.phase()
    hgrn_R(K, C, d, scr, T, False, S)
    P.dma("sp", hx_in.ap()[:, 0:1024], S.t.rearrange("p h v -> p (h v)"), reads=[S], pwrites=[hx_in_tok])
    P.dma("sp", hx_in.ap()[:, 1024:1032], T.D[:], reads=[T.D], pwrites=[hx_in_tok])
    allgather(P, hx_in, hx_all, hx_in_tok, hx_all_tok)
    K.phase()
    Sj = [K.sb("Sj", [128, 1032], F32) for _ in range(2)]
    P.op("pool", lambda e: e.memset(S[:], 0.0), writes=[S])
    P.op("pool", lambda e: e.memset(Rr[:], 0.0), writes=[Rr])
    for j in range(NCORES):
        sj = Sj[j % 2]
        P.dma("sp", sj[:], hx_all.ap()[j * 128:(j + 1) * 128, :], reads=[hx_all_tok], writes=[sj])
        P.op("dve", lambda e, j=j: e.scalar_tensor_tensor(out=S[:], in0=Rr[:], scalar=sel[:, j:j + 1], in1=S[:],
                                                          op0=ALU.mult, op1=ALU.add), reads=[Rr, sel, S], writes=[S])
        if j < NCORES - 1:
            P.op("dve", lambda e, sj=sj: e.tensor_tensor(out=Rr[:], in0=Rr[:],
                                                         in1=sj[:, 1024:1032].unsqueeze(2).to_broadcast([128, 8, 128]),
                                                         op=ALU.mult), reads=[Rr, sj], writes=[Rr])
            P.op("dve", lambda e, sj=sj: e.tensor_tensor(out=Rr[:], in0=Rr[:],
                                                         in1=sj[:, 0:1024].rearrange("p (h v) -> p h v", h=8),
                                                         op=ALU.add), reads=[Rr, sj], writes=[Rr])
    d["x1T"], d["x1T_tok"] = x1T, x1_tok
    d["halo_out"], d["halo_out_tok"] = halo_in[0].ap(), halo_in_tok[0]
    hgrn_R(K, C, d, scr, T, True, S)
    allgather(P, halo_in[0], halo_all[0], halo_in_tok[0], halo_all_tok[0])
    if debug == "x1T":
        P.emit(final_bufs=[x1_tok, halo_all_tok[0]])
        print("nflag", P.nflag, "ndma", P.n_dma, "peak", K.A.peak)
        return nc
    ffn_layer(K, C, d, 0, x1T, x1_tok, x2T, x2_tok, halo_all[0].ap(), halo_all_tok[0])
    if debug == "x2T":
        P.emit(final_bufs=[x2_tok])
        print("nflag", P.nflag, "ndma", P.n_dma, "peak", K.A.peak)
        return nc
    kvq_proj(K, C, d, x2T, x2_tok, qkv_in.ap(), qkv_in_tok)
    allgather(P, qkv_in, qkv_all, qkv_in_tok, qkv_all_tok, rows=NCORES * 3072)
    attn_core(K, C, d, qkv_all.ap()[0:NCORES * 3072, :], qkv_all_tok, o_in.ap(), o_in_tok, gvec, gvec_tok)
    allgather(P, o_in, o_all, o_in_tok, o_all_tok)
    attn_out(K, C, d, o_all.ap(), o_all_tok, x2T, x2_tok, x3T, x3_tok, halo_in[1].ap(), halo_in_tok[1])
    allgather(P, halo_in[1], halo_all[1], halo_in_tok[1], halo_all_tok[1])
    if debug == "x3T":
        P.emit(final_bufs=[x3_tok, halo_all_tok[1]])
        return nc
    ffn_layer(K, C, d, 1, x3T, x3_tok, x4T, x4_tok, halo_all[1].ap(), halo_all_tok[1])
    final_norm(K, C, d, x4T, x4_tok, outT, out_tok)
    P.emit(final_bufs=[out_tok])
    return nc


def t5_bucket_np(rel):
    max_exact = 16
    n = np.maximum(rel, 0)
    log_ratio = (np.log(np.maximum(n, 1).astype(np.float32) / np.float32(max_exact)) / np.float32(math.log(128 / max_exact))).astype(np.float32)
    large = np.minimum(max_exact + (log_ratio * np.float32(32 - max_exact)).astype(np.int32), 31)
    return np.where(n < max_exact, n, large)


def host_inputs(inp, c):
    f = lambda a: np.ascontiguousarray(np.asarray(a, dtype=np.float32))
    pc = lambda v: f(np.asarray(v).reshape(8, 128).T)
    m = {}
    m["xT"] = f(np.asarray(inp["x"])[0, c * NT:(c + 1) * NT, :].T)
    m["w_in_r"] = f(np.asarray(inp["a_w_in"])[0].reshape(8, 128, 32, 128).transpose(2, 1, 0, 3))
    m["w_out_r"] = f(np.asarray(inp["a_w_out"])[0].reshape(8, 128, 1024).transpose(1, 0, 2))
    m["ident"] = np.eye(128, dtype=np.float32)
    r = np.ones((128, SW), np.float32)
    r[:, ::CH] = 0
    m["resetm"] = r
    mk_ = (np.arange(64)[:, None] <= np.arange(64)[None, :]).astype(np.float32)
    m["maskc"] = f(np.broadcast_to(mk_[:, None, :], (64, 8, 64)))
    m["gmix0"] = pc(inp["norm_mix"][0])
    m["gmix1"] = pc(inp["norm_mix"][1])
    m["gnorm"] = pc(inp["a_gnorm"][0])
    m["lbl"] = f(np.asarray(inp["a_lb_logits"]).reshape(2, 8, 128).transpose(2, 0, 1))
    s = np.zeros((128, 8), np.float32)
    s[:, c] = 1.0
    m["sel"] = s
    m["notfirst"] = np.full((128, 1), 0.0 if c == 0 else 1.0, np.float32)
    for li in range(2):
        m[f"gffn{li}"] = pc(inp["norm_ffn"][li])
        m[f"w_up_r{li}"] = f(np.asarray(inp["ffn_w_up"])[li].reshape(8, 128, 44, 128).transpose(2, 1, 0, 3))
        cw = np.asarray(inp["ffn_conv_w"])[li]
        cb = np.asarray(inp["ffn_conv_b"])[li]
        cp = np.concatenate([cw, cb[None]], 0)
        m[f"convp{li}"] = f(cp.reshape(4, 44, 128).transpose(2, 1, 0))
        m[f"w_down_r{li}"] = f(np.asarray(inp["ffn_w_down"])[li].reshape(22, 128, 8, 128).transpose(2, 1, 0, 3))
    m["gkv"] = pc(inp["kv_norm"])
    kvw = np.asarray(inp["kv_w"])
    m["w_k_r"] = f(kvw[:, :1024].reshape(8, 128, 8, 128).transpose(2, 1, 0, 3))
    m["w_v_r"] = f(kvw[:, 1024:].reshape(8, 128, 8, 128).transpose(2, 1, 0, 3))
    m["w_q_r"] = f(np.asarray(inp["b_w_q"])[0].reshape(8, 128, 8, 128).transpose(2, 1, 0, 3))
    m["w_o_r"] = f(np.asarray(inp["b_w_o"])[0].reshape(8, 128, 1024).transpose(1, 0, 2))
    m["relcol"] = f(np.asarray(inp["rel_table"])[:, c:c + 1])
    bk = t5_bucket_np(np.arange(128))
    oh = np.zeros((32, 128), np.float32)
    oh[bk, np.arange(128)] = 1.0
    oh[31, :] -= 1.0
    m["oh"] = oh
    g = np.zeros((1, GLEN), np.float32)
    g[0, :511] = NEG
    m["gconst"] = g
    m["antiI"] = np.ascontiguousarray(np.eye(128, dtype=np.float32)[::-1])
    m["lamv"] = f(np.stack([np.asarray(inp[k])[0] for k in ("b_lam_q1", "b_lam_k1", "b_lam_q2", "b_lam_k2")]))
    m["subln"] = f(np.asarray(inp["b_subln"])[0][None, :])
    m["gfinal"] = pc(inp["final_norm"])
    return m


_NC_CACHE = {}


def kernel(**inputs):
    if "nc" not in _NC_CACHE:
        _NC_CACHE["nc"] = build()
    nc = _NC_CACHE["nc"]
    in_maps = [host_inputs(inputs, c) for c in range(NCORES)]
    res = run_bass_kernel_spmd(nc, in_maps, core_ids=list(range(NCORES)))
    out = np.empty((1, NCORES * NT, 1024), np.float32)
    for c in range(NCORES):
        out[0, c * NT:(c + 1) * NT, :] = res.results[c]["outT"].T
    return out
```

```python
import contextlib
import numpy as np
import concourse.bass as bass
import concourse.mybir as mybir

F32 = mybir.dt.float32
BF16 = mybir.dt.bfloat16
U8 = mybir.dt.uint8
ALU = mybir.AluOpType
AF = mybir.ActivationFunctionType
AX = mybir.AxisListType
DTSZ = {F32: 4, BF16: 2, U8: 1}

ENGS = ("pe", "act", "dve", "pool", "sp")
SEM_ROLL = 2048
DMA_K = 6


class Tok:
    __slots__ = ("writers", "readers", "psum")

    def __init__(self, psum=False):
        self.writers = []
        self.readers = []
        self.psum = psum


class Buf:
    __slots__ = ("name", "t", "tok")

    def __init__(self, name, t=None, tok=None):
        self.name = name
        self.t = t
        self.tok = tok if tok is not None else Tok()

    def __getitem__(self, idx):
        return self.t[idx]

    def alias(self, ap, name=None):
        return Buf(name or self.name, ap, self.tok)


class Op:
    __slots__ = ("eng", "fn", "deps", "is_dma", "flag", "seq", "dma_slot", "dma_val", "name", "inc", "cc", "gidx")


class Arena:
    def __init__(self, nc, nbytes):
        self.big = nc.alloc_sbuf_tensor("arena", [128, nbytes], U8)
        self.size = nbytes
        self.pers = 0
        self.cur = 0
        self.in_phase = False
        self.peak = 0

    def start_phase(self):
        self.in_phase = True
        self.cur = self.pers

    def alloc(self, name, shape, dt, persistent=False):
        p = shape[0]
        n = int(np.prod(shape[1:])) * DTSZ[dt]
        n = (n + 63) // 64 * 64
        if persistent:
            assert not self.in_phase or self.cur == self.pers, "persistent alloc inside a phase"
            off = self.pers
            self.pers += n
            self.cur = self.pers
        else:
            off = self.cur
            self.cur += n
        self.peak = max(self.peak, self.cur)
        assert self.cur <= self.size, f"SBUF arena overflow allocating {name}: {self.cur} > {self.size}"
        ap = self.big[0:p, off:off + n if False else off + int(np.prod(shape[1:])) * DTSZ[dt]].bitcast(dt)
        if len(shape) == 3:
            ap = ap.rearrange("p (a b) -> p a b", a=shape[1])
        elif len(shape) == 4:
            ap = ap.rearrange("p (a b c) -> p a b c", a=shape[1], b=shape[2])
        return Buf(name, ap)


class Prog:
    def __init__(self, nc):
        self.nc = nc
        self.ops = {e: [] for e in ENGS}
        self.n_dma = {e: 0 for e in ENGS}
        self.all_ops = []

    def _add(self, eng, fn, reads, writes, pwrites, is_dma, name=None, inc=None, extra_deps=(), cc=False):
        op = Op()
        op.eng, op.fn, op.is_dma, op.flag, op.seq, op.name = eng, fn, is_dma, False, None, name
        op.inc = inc if inc is not None else (16 if is_dma else 1)
        op.cc = cc
        op.gidx = len(self.all_ops)
        deps = list(extra_deps)
        wr_toks = set()
        for r in reads:
            deps.extend(r.tok.writers)
            if r.tok.psum:
                deps.extend(x for x in r.tok.readers if x.eng != eng)
        for w in list(writes) + list(pwrites):
            wr_toks.add(id(w.tok))
        for w in writes:
            deps.extend(w.tok.writers)
            deps.extend(w.tok.readers)
        for w in pwrites:
            deps.extend(w.tok.readers)
            if w.tok.readers:
                deps.extend(w.tok.writers)
        rw_writers = set()
        for x in list(reads) + list(writes):
            for d in x.tok.writers:
                rw_writers.add(id(d))
        out = []
        seen = set()
        for d in deps:
            if id(d) in seen or d is op:
                continue
            seen.add(id(d))
            if (not d.is_dma) and (not is_dma) and d.eng == eng:
                if eng == "pe":
                    continue
                if id(d) not in rw_writers:
                    continue
            out.append(d)
        latest = {}
        for d in out:
            if not d.is_dma:
                if d.eng not in latest or d.gidx > latest[d.eng].gidx:
                    latest[d.eng] = d
        out = [d for d in out if d.is_dma or latest[d.eng] is d]
        op.deps = out
        for r in reads:
            r.tok.readers.append(op)
        for w in writes:
            w.tok.writers = [op]
            w.tok.readers = []
        for w in pwrites:
            if w.tok.readers:
                w.tok.writers = [op]
                w.tok.readers = []
            else:
                w.tok.writers.append(op)
        if is_dma and not cc:
            i = self.n_dma[eng]
            self.n_dma[eng] += 1
            op.dma_slot = i % DMA_K
            op.dma_val = i // DMA_K + 1
        self.ops[eng].append(op)
        self.all_ops.append(op)
        return op

    def op(self, eng, fn, reads=(), writes=(), pwrites=(), name=None):
        return self._add(eng, fn, reads, writes, pwrites, False, name)

    def dma(self, eng, out, in_, reads=(), writes=(), pwrites=(), **kw):
        def fn(e):
            return e.dma_start(out=out, in_=in_, **kw)
        return self._add(eng, fn, reads, writes, pwrites, True)

    def async_op(self, eng, fn, reads=(), writes=(), inc=1):
        return self._add(eng, fn, reads, writes, (), True, inc=inc, cc=True)

    def barrier(self):
        lasts = []
        for e in ENGS:
            for op in reversed(self.ops[e]):
                if not op.is_dma:
                    lasts.append(op)
                    break
            dm = [op for op in self.ops[e] if op.is_dma and not op.cc][-DMA_K:]
            lasts.extend(dm)
            lasts.extend(op for op in self.ops[e] if op.cc)
        for e in ENGS:
            deps = [d for d in lasts if d.is_dma or d.eng != e]
            self._add(e, lambda en: en.nop(), (), (), (), False, "barrier", extra_deps=deps)

    def pid(self, e):
        k = id(e)
        if k not in self._pids:
            self._pids[k] = e.partition_id()
        return self._pids[k]

    def emit(self, final_bufs=()):
        self._pids = {}
        nc = self.nc
        for op in self.all_ops:
            for d in op.deps:
                d.flag = True
        finals = []
        for b in final_bufs:
            for w in b.tok.writers:
                w.flag = True
                finals.append(w)
        nflag = {}
        for e in ENGS:
            n = 0
            for op in self.ops[e]:
                if op.flag and not op.is_dma:
                    op.seq = n
                    n += 1
            nflag[e] = n
        self.nflag = nflag
        with contextlib.ExitStack() as st:
            csem = {}
            for e in ENGS:
                k = (nflag[e] + SEM_ROLL - 1) // SEM_ROLL
                csem[e] = [st.enter_context(nc.semaphore(f"c_{e}_{i}")) for i in range(max(k, 1))]
            dsem = {}
            for e in ENGS:
                if self.n_dma[e]:
                    dsem[e] = [st.enter_context(nc.semaphore(f"d_{e}_{i}")) for i in range(DMA_K)]
            ccsem = {}
            for op in self.all_ops:
                if op.cc:
                    ccsem[id(op)] = st.enter_context(nc.semaphore(f"cc_{len(ccsem)}"))
            cum = {e: [0] * DMA_K for e in ENGS}
            for e in ENGS:
                for op in self.ops[e]:
                    if op.is_dma and not op.cc:
                        cum[e][op.dma_slot] += op.inc
                        op.dma_val = cum[e][op.dma_slot]
            block = st.enter_context(nc.Block())

            def target(d):
                if d.cc:
                    return ccsem[id(d)], 1
                if d.is_dma:
                    return dsem[d.eng][d.dma_slot], d.dma_val
                return csem[d.eng][d.seq // SEM_ROLL], d.seq % SEM_ROLL + 1

            def run(ename, e):
                waited = {}
                for op in self.ops[ename]:
                    tg = [target(d) for d in op.deps]
                    if op.is_dma and not op.cc and op.dma_val - op.inc > 0:
                        tg.append((dsem[ename][op.dma_slot], op.dma_val - op.inc))
                    for s, v in tg:
                        key = id(s)
                        if waited.get(key, 0) >= v:
                            continue
                        waited[key] = v
                        e.wait_ge(s, v)
                    ins = op.fn(e)
                    if op.cc:
                        ins.then_inc(ccsem[id(op)], 1)
                    elif op.is_dma:
                        ins.then_inc(dsem[ename][op.dma_slot], op.inc)
                    elif op.flag:
                        ins.then_inc(csem[ename][op.seq // SEM_ROLL], 1)
                if ename == "sp":
                    for d in finals:
                        s, v = target(d)
                        e.wait_ge(s, v)

            @block.tensor
            def _(e):
                run("pe", e)

            @block.scalar
            def _(e):
                run("act", e)

            @block.vector
            def _(e):
                run("dve", e)

            @block.gpsimd
            def _(e):
                run("pool", e)

            @block.sync
            def _(e):
                run("sp", e)


NT = 2048
SW = 512
NS = NT // SW
CH = 64
NCH = NT // CH
CPS = SW // CH
EPS = 1e-6


class Ctx:
    pass


class KB:
    def __init__(self, nc, sbuf_bytes=206 * 1024):
        self.nc = nc
        self.P = Prog(nc)
        self.A = Arena(nc, sbuf_bytes)
        self.pb = [Buf(f"pb{i}", nc.alloc_psum_tensor(f"pb{i}", [128, 512], F32), Tok(psum=True)) for i in range(6)]
        self.pb2 = Buf("pb2", nc.alloc_psum_tensor("pbig", [128, 1024], F32), Tok(psum=True))

    def sb(self, name, shape, dt, pers=False):
        return self.A.alloc(name, shape, dt, persistent=pers)

    def ps(self, bank, shape, dt=F32):
        base = self.pb2 if bank == 6 else self.pb[bank]
        p = shape[0]
        n = 1
        for x in shape[1:]:
            n *= x
        nf32 = n * DTSZ[dt] // 4
        ap = base.t[0:p, 0:nf32]
        if dt != F32:
            ap = ap.bitcast(dt)
        if len(shape) == 3:
            ap = ap.rearrange("p (a b) -> p a b", a=shape[1])
        return base.alias(ap)

    def phase(self):
        self.P.barrier()
        self.A.start_phase()


def rmsnorm_fm(P, C, xs, g, out_ap, out_buf, sq, ps, rstd, width=SW):
    P.op("act", lambda e: e.activation(out=sq[:], in_=xs[:], func=AF.Square), reads=[xs], writes=[sq])
    for c in range(8):
        P.op("pe", lambda e, c=c: e.matmul(ps[:], lhsT=C.ones[:], rhs=sq[:, c, :], start=(c == 0), stop=(c == 7)),
             reads=[C.ones, sq], writes=[ps])
    P.op("act", lambda e: e.activation(out=rstd[:], in_=ps[:], func=AF.Sqrt, scale=1.0 / 1024, bias=C.eps[:]),
         reads=[ps, C.eps], writes=[rstd])
    P.op("dve", lambda e: e.reciprocal(out=rstd[:], in_=rstd[:]), reads=[rstd], writes=[rstd])
    for c in range(8):
        P.op("dve", lambda e, c=c: e.scalar_tensor_tensor(out=out_ap[:, c, :], in0=xs[:, c, :], scalar=g[:, c:c + 1],
                                                          in1=rstd[:], op0=ALU.mult, op1=ALU.mult),
             reads=[xs, g, rstd], pwrites=[out_buf])


def hgrn_consts(K, d):
    P = K.P
    C = Ctx()
    sb = lambda n, sh, dt: K.sb(n, sh, dt, pers=True)
    C.ones = sb("ones", [128, 128], BF16)
    P.op("pool", lambda e: e.memset(C.ones[:], 1.0), writes=[C.ones])
    C.eps = sb("eps", [128, 1], F32)
    P.op("pool", lambda e: e.memset(C.eps[:], EPS), writes=[C.eps])
    C.ident = sb("ident", [128, 128], BF16)
    P.dma("pool", C.ident[:], d["ident"], writes=[C.ident])
    C.resetm = sb("resetm", [128, SW], F32)
    P.dma("sp", C.resetm[:], d["resetm"], writes=[C.resetm])
    C.maskc = sb("maskc", [64, 8, 64], F32)
    P.dma("sp", C.maskc[:], d["maskc"], writes=[C.maskc])
    C.gmix = sb("gmix", [128, 8], F32)
    P.dma("sp", C.gmix[:], d["gmix0"], writes=[C.gmix])
    C.gn = sb("gn", [128, 8], F32)
    P.dma("sp", C.gn[:], d["gnorm"], writes=[C.gn])
    lbl = sb("lbl", [128, 2, 8], F32)
    P.dma("sp", lbl[:], d["lbl"], writes=[lbl])
    C.lb = sb("lb", [128, 8], F32)
    C.oml = sb("oml", [128, 8], F32)
    P.op("dve", lambda e: e.tensor_sub(out=C.lb[:], in0=lbl[:, 0, :], in1=lbl[:, 1, :]), reads=[lbl], writes=[C.lb])
    P.op("act", lambda e: e.activation(out=C.oml[:], in_=C.lb[:], func=AF.Sigmoid, scale=-1.0), reads=[C.lb], writes=[C.oml])
    P.op("act", lambda e: e.activation(out=C.lb[:], in_=C.lb[:], func=AF.Sigmoid), reads=[C.lb], writes=[C.lb])
    return C


def hgrn_alloc_T(K):
    T = Ctx()
    sb = lambda n, sh, dt: K.sb(n, sh, dt, pers=True)
    T.bl = sb("bl", [128, 8, NCH], F32)
    T.bmid = sb("bmid", [128, 8, NCH], F32)
    T.e1 = sb("e1", [128, 8, NCH], F32)
    T.e2 = sb("e2", [128, 8, NCH], F32)
    T.em = sb("em", [128, 8, NCH], F32)
    T.D = sb("D", [128, 8], F32)
    return T


def hgrn_P(K, C, d, scr, T):
    P, sb, ps = K.P, K.sb, K.ps
    xT3 = d["xT"].rearrange("(c p) t -> p c t", p=128)
    hT = sb("hT", [128, 8, NT], BF16)
    hTs = [Buf(f"hT{s}", hT.t[:, :, s * SW:(s + 1) * SW]) for s in range(NS)]
    xst = [sb("xst", [128, 8, SW], F32) for _ in range(2)]
    sq = sb("sq", [128, 8, SW], BF16)
    rstd = sb("rstd", [128, SW], F32)
    ps_n = ps(0, [128, SW])
    for s in range(NS):
        xs = xst[s % 2]
        P.dma("sp", xs[:], xT3[:, :, s * SW:(s + 1) * SW], writes=[xs])
        rmsnorm_fm(P, C, xs, C.gmix, hTs[s], hTs[s], sq, ps_n, rstd)

    wi = sb("wi", [128, 8, 1024], BF16)
    for h in range(8):
        P.dma("pool", wi[:, :, h * 128:(h + 1) * 128], d["w_in_r"][16 + h], pwrites=[wi])
    ps_v = [ps(1, [64, 512]), ps(2, [64, 512])]
    vst = [sb("vst", [64, CPS, 1024], BF16) for _ in range(2)]
    v3 = scr["v"].rearrange("(c s) j -> s c j", s=64)
    for s in range(NS):
        vs = vst[s % 2]
        for c in range(CPS):
            cg = s * CPS + c
            for hf in range(2):
                pv = ps_v[hf]
                for m in range(8):
                    P.op("pe", lambda e, m=m, cg=cg, hf=hf, pv=pv: e.matmul(
                        pv[:], lhsT=hT[:, m, cg * CH:(cg + 1) * CH], rhs=wi[:, m, hf * 512:(hf + 1) * 512],
                        start=(m == 0), stop=(m == 7)), reads=[hTs[s], wi], writes=[pv])
                if hf == 0:
                    P.op("act", lambda e, c=c, hf=hf, pv=pv, vs=vs: e.copy(out=vs[:, c, hf * 512:(hf + 1) * 512], in_=pv[:]),
                         reads=[pv], pwrites=[vs])
                else:
                    P.op("dve", lambda e, c=c, hf=hf, pv=pv, vs=vs: e.tensor_copy(out=vs[:, c, hf * 512:(hf + 1) * 512], in_=pv[:]),
                         reads=[pv], pwrites=[vs])
        P.dma("sp", v3[:, s * CPS:(s + 1) * CPS, :], vs[:], reads=[vs], pwrites=[scr["v_tok"]])

    wq = [sb("wq", [128, 8, 128], BF16) for _ in range(2)]
    wf = [sb("wf", [128, 8, 128], BF16) for _ in range(2)]
    wg = [sb("wg", [128, 8, 128], BF16) for _ in range(2)]
    ps_f = ps(3, [128, SW])
    ps_q = ps(4, [128, SW])
    ps_g = ps(5, [128, SW])
    ps_t = ps(0, [64, CPS, 128], BF16)
    sig = sb("sig", [128, SW], F32)
    logf = sb("logf", [128, SW], F32)
    nsig = sb("nsig", [128, SW], F32)
    b3 = sb("b3", [128, CPS, CH], F32)
    bm = sb("bm", [128, CPS, CH], F32)
    Ep = sb("Ep", [128, SW], F32)
    Em = sb("Em", [128, SW], F32)
    sqf = sb("sqf", [128, SW], F32)
    qst = [sb("qst", [128, SW], BF16) for _ in range(2)]
    kst = [sb("kst", [128, SW], BF16) for _ in range(2)]
    gst = [sb("gst", [128, SW], BF16) for _ in range(2)]
    ktst = [sb("ktst", [64, CPS, 128], BF16) for _ in range(2)]
    b_flat = b3.t.rearrange("p c t -> p (c t)")
    bm_flat = bm.t.rearrange("p c t -> p (c t)")
    kt3 = scr["kt"].rearrange("(c s) j -> s c j", s=64)
    it = 0
    for h in range(8):
        wqh, wfh, wgh = wq[h % 2], wf[h % 2], wg[h % 2]
        P.dma("pool", wfh[:], d["w_in_r"][8 + h], writes=[wfh])
        P.dma("pool", wqh[:], d["w_in_r"][0 + h], writes=[wqh])
        P.dma("pool", wgh[:], d["w_in_r"][24 + h], writes=[wgh])
        for s in range(NS):
            hs = hTs[s]
            sl = slice(s * SW, (s + 1) * SW)
            for (pp, ww) in ((ps_f, wfh), (ps_q, wqh), (ps_g, wgh)):
                for m in range(8):
                    P.op("pe", lambda e, m=m, pp=pp, ww=ww, sl=sl: e.matmul(
                        pp[:], lhsT=ww[:, m, :], rhs=hT[:, m, sl], start=(m == 0), stop=(m == 7)),
                        reads=[ww, hs], writes=[pp])
            q_o, k_o, g_o, kt_o = qst[it % 2], kst[it % 2], gst[it % 2], ktst[it % 2]
            it += 1
            P.op("act", lambda e: e.activation(out=sig[:], in_=ps_f[:], func=AF.Sigmoid), reads=[ps_f], writes=[sig])
            P.op("act", lambda e, h=h: e.activation(out=logf[:], in_=sig[:], func=AF.Ln, scale=C.oml[:, h:h + 1],
                                                    bias=C.lb[:, h:h + 1]), reads=[sig, C.oml, C.lb], writes=[logf])
            P.op("dve", lambda e: e.tensor_scalar(out=nsig[:], in0=sig[:], scalar1=-1.0, scalar2=1.0, op0=ALU.mult,
                                                  op1=ALU.add), reads=[sig], writes=[nsig])
            P.op("dve", lambda e: e.tensor_tensor_scan(out=b_flat, data0=C.resetm[:], data1=logf[:], initial=0.0,
                                                       op0=ALU.mult, op1=ALU.add), reads=[C.resetm, logf], writes=[b3])
            P.op("dve", lambda e, h=h, s=s: e.tensor_copy(out=T.bl[:, h, s * CPS:(s + 1) * CPS], in_=b3[:, :, CH - 1]),
                 reads=[b3], pwrites=[T.bl])
            P.op("dve", lambda e, h=h, s=s: e.tensor_copy(out=T.bmid[:, h, s * CPS:(s + 1) * CPS], in_=b3[:, :, CH // 2 - 1]),
                 reads=[b3], pwrites=[T.bmid])
            P.op("dve", lambda e: e.tensor_tensor(out=bm[:], in0=b3[:], in1=b3[:, :, CH // 2 - 1:CH // 2].to_broadcast([128, CPS, CH]),
                                                  op=ALU.subtract), reads=[b3], writes=[bm])
            P.op("act", lambda e: e.activation(out=Ep[:], in_=bm_flat, func=AF.Exp), reads=[bm], writes=[Ep])
            P.op("act", lambda e: e.activation(out=Em[:], in_=bm_flat, func=AF.Exp, scale=-1.0), reads=[bm], writes=[Em])
            P.op("act", lambda e: e.activation(out=sqf[:], in_=ps_q[:], func=AF.Silu), reads=[ps_q], writes=[sqf])
            P.op("act", lambda e, g_o=g_o: e.activation(out=g_o[:], in_=ps_g[:], func=AF.Silu), reads=[ps_g], writes=[g_o])
            P.op("dve", lambda e, q_o=q_o: e.tensor_tensor(out=q_o[:], in0=sqf[:], in1=Ep[:], op=ALU.mult),
                 reads=[sqf, Ep], writes=[q_o])
            P.op("dve", lambda e, k_o=k_o, h=h: e.scalar_tensor_tensor(out=k_o[:], in0=nsig[:], scalar=C.oml[:, h:h + 1],
                                                                       in1=Em[:], op0=ALU.mult, op1=ALU.mult),
                 reads=[nsig, C.oml, Em], writes=[k_o])
            hsl = slice(h * 128, (h + 1) * 128)
            P.dma("sp", scr["qT"][hsl, sl], q_o[:], reads=[q_o], pwrites=[scr["qT_tok"]])
            P.dma("sp", scr["kT"][hsl, sl], k_o[:], reads=[k_o], pwrites=[scr["kT_tok"]])
            P.dma("sp", scr["gT"][hsl, sl], g_o[:], reads=[g_o], pwrites=[scr["gT_tok"]])
            for c in range(CPS):
                P.op("pe", lambda e, c=c, k_o=k_o: e.transpose(out=ps_t[:, c, :], in_=k_o[:, c * CH:(c + 1) * CH],
                                                              identity=C.ident[:]), reads=[k_o, C.ident], writes=[ps_t])
            P.op("act", lambda e, kt_o=kt_o: e.copy(out=kt_o[:], in_=ps_t[:]), reads=[ps_t], writes=[kt_o])
            P.dma("sp", kt3[:, s * CPS:(s + 1) * CPS, hsl], kt_o[:], reads=[kt_o], pwrites=[scr["kt_tok"]])
    P.op("act", lambda e: e.activation(out=T.e1[:], in_=T.bl[:], func=AF.Exp), reads=[T.bl], writes=[T.e1])
    P.op("act", lambda e: e.activation(out=T.em[:], in_=T.bmid[:], func=AF.Exp), reads=[T.bmid], writes=[T.em])
    P.op("dve", lambda e: e.tensor_sub(out=T.e2[:], in0=T.bl[:], in1=T.bmid[:]), reads=[T.bl, T.bmid], writes=[T.e2])
    P.op("act", lambda e: e.activation(out=T.e2[:], in_=T.e2[:], func=AF.Exp), reads=[T.e2], writes=[T.e2])
    P.op("dve", lambda e: e.reduce_sum(out=T.D[:], in_=T.bl[:], axis=AX.X), reads=[T.bl], writes=[T.D])
    P.op("act", lambda e: e.activation(out=T.D[:], in_=T.D[:], func=AF.Exp), reads=[T.D], writes=[T.D])


def hgrn_R(K, C, d, scr, T, full, S):
    P, sb, ps = K.P, K.sb, K.ps
    q3 = scr["qT"].rearrange("(h p) t -> p h t", p=128)
    k3 = scr["kT"].rearrange("(h p) t -> p h t", p=128)
    g3 = scr["gT"].rearrange("(h p) t -> p h t", p=128)
    v3 = scr["v"].rearrange("(c s) j -> s c j", s=64)
    kt3 = scr["kt"].rearrange("(c s) j -> s c j", s=64)
    v_sb = [sb("v_sb", [64, CPS, 1024], BF16) for _ in range(2)]
    kt_sb = [sb("kt_sb", [64, CPS, 1024], BF16) for _ in range(1)]
    ps_dS = ps(6, [128, 8, 128])
    tmp = sb("tmpS", [128, 8, 128], F32)
    if full:
        q_sb = [sb("q_sb", [128, 8, SW], BF16) for _ in range(2)]
        k_sb = [sb("k_sb", [128, 8, SW], BF16) for _ in range(2)]
        g_sb = [sb("g_sb", [128, 8, SW], BF16) for _ in range(1)]
        ps_sc = [ps(0, [64, 8, 64]), ps(1, [64, 8, 64])]
        ps_o = [ps(2, [128, 8, 64]), ps(3, [128, 8, 64])]
        sc_sb = [sb("sc_sb", [64, 8, 64], BF16) for _ in range(2)]
        Sb = [sb("Sb", [128, 8, 128], BF16) for _ in range(2)]
        oT = sb("oT", [128, 8, SW], F32)
        osq = sb("osq", [128, 8, SW], BF16)
        ogT = sb("ogT", [128, 8, SW], BF16)
        t1 = sb("t1", [128, SW], F32)
        rstd = sb("rstd", [128, SW], F32)
        wout = sb("wout", [128, 8, 1024], BF16)
        P.dma("pool", wout[:], d["w_out_r"], writes=[wout])
        ps_y = [ps(4, [128, SW]), ps(5, [128, SW])]
        ps_n = ps(4, [128, SW])
        xT3 = d["xT"].rearrange("(c p) t -> p c t", p=128)
        x1T3 = d["x1T"].rearrange("(c p) t -> p c t", p=128)
        xst = [sb("xst", [128, 8, SW], F32) for _ in range(1)]
    for s in range(NS):
        sl = slice(s * SW, (s + 1) * SW)
        vs, kts = v_sb[s % 2], kt_sb[0]
        P.dma("sp", vs[:], v3[:, s * CPS:(s + 1) * CPS, :], reads=[scr["v_tok"]], writes=[vs])
        P.dma("sp", kts[:], kt3[:, s * CPS:(s + 1) * CPS, :], reads=[scr["kt_tok"]], writes=[kts])
        if full:
            qs, ks, gs = q_sb[s % 2], k_sb[s % 2], g_sb[0]
            P.dma("sp", qs[:], q3[:, :, sl], reads=[scr["qT_tok"]], writes=[qs])
            P.dma("sp", ks[:], k3[:, :, sl], reads=[scr["kT_tok"]], writes=[ks])
            P.dma("sp", gs[:], g3[:, :, sl], reads=[scr["gT_tok"]], writes=[gs])
            xs = xst[0]
            P.dma("sp", xs[:], xT3[:, :, sl], writes=[xs])
        for c in range(CPS):
            cg = s * CPS + c
            csl = slice(c * CH, (c + 1) * CH)
            if full:
                psc, pso, scb, Sbb = ps_sc[cg % 2], ps_o[cg % 2], sc_sb[cg % 2], Sb[cg % 2]
                for h in range(8):
                    P.op("pe", lambda e, h=h, psc=psc, ks=ks, qs=qs, csl=csl: e.matmul(
                        psc[:, h, :], lhsT=ks[:, h, csl], rhs=qs[:, h, csl], start=True, stop=True),
                        reads=[ks, qs], writes=[psc])
                P.op("dve", lambda e, psc=psc, scb=scb: e.tensor_tensor(out=scb[:], in0=psc[:], in1=C.maskc[:], op=ALU.mult),
                     reads=[psc, C.maskc], writes=[scb])
                P.op("pool", lambda e, Sbb=Sbb, cg=cg: e.tensor_tensor(
                    out=Sbb[:], in0=S[:], in1=T.em[:, :, cg:cg + 1].to_broadcast([128, 8, 128]), op=ALU.mult),
                    reads=[S, T.em], writes=[Sbb])
                for h in range(8):
                    hsl = slice(h * 128, (h + 1) * 128)
                    P.op("pe", lambda e, h=h, pso=pso, Sbb=Sbb, qs=qs, csl=csl: e.matmul(
                        pso[:, h, :], lhsT=Sbb[:, h, :], rhs=qs[:, h, csl], start=True, stop=False),
                        reads=[Sbb, qs], writes=[pso])
                    P.op("pe", lambda e, h=h, pso=pso, vs=vs, scb=scb, c=c, hsl=hsl: e.matmul(
                        pso[:, h, :], lhsT=vs[:, c, hsl], rhs=scb[:, h, :], start=False, stop=True),
                        reads=[vs, scb], writes=[pso])
                P.op("act", lambda e, pso=pso, csl=csl: e.copy(out=oT[:, :, csl], in_=pso[:]), reads=[pso], pwrites=[oT])
            for h in range(8):
                hsl = slice(h * 128, (h + 1) * 128)
                P.op("pe", lambda e, h=h, kts=kts, vs=vs, c=c, hsl=hsl: e.matmul(
                    ps_dS[:, h, :], lhsT=kts[:, c, hsl], rhs=vs[:, c, hsl], start=True, stop=True),
                    reads=[kts, vs], writes=[ps_dS])
            P.op("dve", lambda e, cg=cg: e.tensor_tensor(
                out=tmp[:], in0=ps_dS[:], in1=T.e2[:, :, cg:cg + 1].to_broadcast([128, 8, 128]), op=ALU.mult),
                reads=[ps_dS, T.e2], writes=[tmp])
            P.op("pool", lambda e, cg=cg: e.tensor_tensor(
                out=S[:], in0=S[:], in1=T.e1[:, :, cg:cg + 1].to_broadcast([128, 8, 128]), op=ALU.mult),
                reads=[S, T.e1], writes=[S])
            P.op("dve", lambda e: e.tensor_tensor(out=S[:], in0=S[:], in1=tmp[:], op=ALU.add), reads=[S, tmp], writes=[S])
        if full:
            P.op("act", lambda e: e.activation(out=osq[:], in_=oT[:], func=AF.Square), reads=[oT], writes=[osq])
            for h in range(8):
                P.op("pe", lambda e, h=h: e.matmul(ps_n[:], lhsT=C.ones[:], rhs=osq[:, h, :], start=(h == 0), stop=(h == 7)),
                     reads=[C.ones, osq], writes=[ps_n])
            P.op("act", lambda e: e.activation(out=rstd[:], in_=ps_n[:], func=AF.Sqrt, scale=1.0 / 1024, bias=C.eps[:]),
                 reads=[ps_n, C.eps], writes=[rstd])
            P.op("dve", lambda e: e.reciprocal(out=rstd[:], in_=rstd[:]), reads=[rstd], writes=[rstd])
            for h in range(8):
                P.op("dve", lambda e, h=h: e.scalar_tensor_tensor(out=t1[:], in0=oT[:, h, :], scalar=C.gn[:, h:h + 1],
                                                                  in1=rstd[:], op0=ALU.mult, op1=ALU.mult),
                     reads=[oT, C.gn, rstd], writes=[t1])
                P.op("dve", lambda e, h=h, gs=gs: e.tensor_tensor(out=ogT[:, h, :], in0=t1[:], in1=gs[:, h, :], op=ALU.mult),
                     reads=[t1, gs], pwrites=[ogT])
            for j in range(8):
                py = ps_y[j % 2]
                for h in range(8):
                    P.op("pe", lambda e, h=h, j=j, py=py: e.matmul(py[:], lhsT=wout[:, h, j * 128:(j + 1) * 128],
                                                                  rhs=ogT[:, h, :], start=(h == 0), stop=(h == 7)),
                         reads=[wout, ogT], writes=[py])
                P.op("dve", lambda e, j=j, py=py, xs=xs: e.tensor_tensor(out=xs[:, j, :], in0=py[:], in1=xs[:, j, :], op=ALU.add),
                     reads=[py, xs], pwrites=[xs])
            P.dma("sp", x1T3[:, :, sl], xs[:], reads=[xs], pwrites=[d["x1T_tok"]])
            if s == NS - 1 and "halo_out" in d:
                P.dma("sp", d["halo_out"].rearrange("p (c t) -> p c t", c=8), xs[:, :, SW - 2:SW], reads=[xs],
                      writes=[d["halo_out_tok"]])


NFF = 22


def ffn_layer(K, C, d, li, xin, xin_tok, xout, xout_tok, halo_all, halo_tok, final_norm=None):
    P, sb, ps = K.P, K.sb, K.ps
    K.phase()
    xT3 = xin.rearrange("(c p) t -> p c t", p=128)
    gf = sb("gf", [128, 8], F32)
    P.dma("sp", gf[:], d[f"gffn{li}"], writes=[gf])
    convp = sb("convp", [128, 2 * NFF, 4], F32)
    P.dma("sp", convp[:], d[f"convp{li}"], writes=[convp])
    nf = sb("nf", [128, 1], F32)
    P.dma("sp", nf[:], d["notfirst"], writes=[nf])
    aT = sb("aT", [128, NFF, NT], BF16)
    aTs = [Buf(f"aT{s}", aT.t[:, :, s * SW:(s + 1) * SW]) for s in range(NS)]
    mark = K.A.cur
    hT = sb("h2T", [128, 8, NT], BF16)
    hTs = [Buf(f"h2T{s}", hT.t[:, :, s * SW:(s + 1) * SW]) for s in range(NS)]
    xst = [sb("xst", [128, 8, SW], F32) for _ in range(1)]
    sq = sb("sq", [128, 8, SW], BF16)
    rstd = sb("rstd", [128, SW], F32)
    ps_n = ps(0, [128, SW])
    for s in range(NS):
        xs = xst[0]
        P.dma("sp", xs[:], xT3[:, :, s * SW:(s + 1) * SW], reads=[xin_tok], writes=[xs])
        rmsnorm_fm(P, C, xs, gf, hTs[s], hTs[s], sq, ps_n, rstd)
    xh = sb("xh", [128, 8, 2], F32)

    def dyn(e):
        pid = P.pid(e)
        prev = (pid + 7) % 8
        return e.dma_start(out=xh[:], in_=halo_all[bass.ds(prev * 128, 128), :].rearrange("p (c t) -> p c t", c=8))
    P._add("sp", dyn, [halo_tok], [xh], (), True)
    sqh = sb("sqh", [128, 8, 2], BF16)
    rsh = sb("rsh", [128, 2], F32)
    hh = sb("hh", [128, 8, 2], BF16)
    ps_h = ps(1, [128, 2])
    P.op("act", lambda e: e.activation(out=sqh[:], in_=xh[:], func=AF.Square), reads=[xh], writes=[sqh])
    for c in range(8):
        P.op("pe", lambda e, c=c: e.matmul(ps_h[:], lhsT=C.ones[:], rhs=sqh[:, c, :], start=(c == 0), stop=(c == 7)),
             reads=[C.ones, sqh], writes=[ps_h])
    P.op("act", lambda e: e.activation(out=rsh[:], in_=ps_h[:], func=AF.Sqrt, scale=1.0 / 1024, bias=C.eps[:]),
         reads=[ps_h, C.eps], writes=[rsh])
    P.op("dve", lambda e: e.reciprocal(out=rsh[:], in_=rsh[:]), reads=[rsh], writes=[rsh])
    P.op("dve", lambda e: e.tensor_scalar(out=rsh[:], in0=rsh[:], scalar1=nf[:, 0:1], scalar2=None, op0=ALU.mult),
         reads=[rsh, nf], writes=[rsh])
    for c in range(8):
        P.op("dve", lambda e, c=c: e.scalar_tensor_tensor(out=hh[:, c, :], in0=xh[:, c, :], scalar=gf[:, c:c + 1],
                                                          in1=rsh[:], op0=ALU.mult, op1=ALU.mult),
             reads=[xh, gf, rsh], pwrites=[hh])

    wu = [[sb("wu", [128, 8, 128], BF16) for _ in range(2)] for _ in range(2)]
    ug = [sb("ug", [128, SW + 2], F32) for _ in range(2)]
    uv = [sb("uv", [128, SW + 2], F32) for _ in range(2)]
    ag = [sb("ag", [128, SW], F32) for _ in range(2)]
    av = [sb("av", [128, SW], F32) for _ in range(2)]
    sg = [sb("sg", [128, SW], F32) for _ in range(2)]
    ps_g = [ps(2, [128, SW]), ps(3, [128, SW])]
    ps_v = [ps(4, [128, SW]), ps(5, [128, SW])]
    ps_hh = ps(1, [128, 2, 2])
    it = 0
    for c in range(NFF):
        wg_, wv_ = wu[c % 2]
        P.dma("pool", wg_[:], d[f"w_up_r{li}"][c], writes=[wg_])
        P.dma("pool", wv_[:], d[f"w_up_r{li}"][c + NFF], writes=[wv_])
        for s in range(NS):
            sl = slice(s * SW, (s + 1) * SW)
            cur, prv = it % 2, (it + 1) % 2
            it += 1
            pg, pv = ps_g[cur], ps_v[cur]
            ugc, uvc, agc, avc, sgc = ug[cur], uv[cur], ag[cur], av[cur], sg[cur]
            for (pp, ww) in ((pg, wg_), (pv, wv_)):
                for m in range(8):
                    P.op("pe", lambda e, m=m, pp=pp, ww=ww, sl=sl: e.matmul(
                        pp[:], lhsT=ww[:, m, :], rhs=hT[:, m, sl], start=(m == 0), stop=(m == 7)),
                        reads=[ww, hTs[s]], writes=[pp])
            if s == 0:
                for gi, ww in ((0, wg_), (1, wv_)):
                    for m in range(8):
                        P.op("pe", lambda e, m=m, gi=gi, ww=ww: e.matmul(
                            ps_hh[:, gi, :], lhsT=ww[:, m, :], rhs=hh[:, m, :], start=(m == 0), stop=(m == 7)),
                            reads=[ww, hh], writes=[ps_hh])
                P.op("dve", lambda e, ugc=ugc: e.tensor_copy(out=ugc[:, 0:2], in_=ps_hh[:, 0, :]), reads=[ps_hh], pwrites=[ugc])
                P.op("dve", lambda e, uvc=uvc: e.tensor_copy(out=uvc[:, 0:2], in_=ps_hh[:, 1, :]), reads=[ps_hh], pwrites=[uvc])
            else:
                P.op("dve", lambda e, ugc=ugc, p_=ug[prv]: e.tensor_copy(out=ugc[:, 0:2], in_=p_[:, SW:SW + 2]),
                     reads=[ug[prv]], pwrites=[ugc])
                P.op("pool", lambda e, uvc=uvc, p_=uv[prv]: e.tensor_copy(out=uvc[:, 0:2], in_=p_[:, SW:SW + 2]),
                     reads=[uv[prv]], pwrites=[uvc])
            cg, cv = c, c + NFF
            P.op("act", lambda e, ugc=ugc, pg=pg: e.copy(out=ugc[:, 2:SW + 2], in_=pg[:]), reads=[pg], pwrites=[ugc])
            P.op("act", lambda e, agc=agc, pg=pg, cg=cg: e.activation(out=agc[:], in_=pg[:], func=AF.Identity,
                                                                      scale=convp[:, cg, 2:3], bias=convp[:, cg, 3:4]),
                 reads=[pg, convp], writes=[agc])
            P.op("act", lambda e, uvc=uvc, pv=pv: e.copy(out=uvc[:, 2:SW + 2], in_=pv[:]), reads=[pv], pwrites=[uvc])
            P.op("act", lambda e, avc=avc, pv=pv, cv=cv: e.activation(out=avc[:], in_=pv[:], func=AF.Identity,
                                                                      scale=convp[:, cv, 2:3], bias=convp[:, cv, 3:4]),
                 reads=[pv, convp], writes=[avc])
            P.op("dve", lambda e, agc=agc, ugc=ugc, cg=cg: e.scalar_tensor_tensor(
                out=agc[:], in0=ugc[:, 1:SW + 1], scalar=convp[:, cg, 1:2], in1=agc[:], op0=ALU.mult, op1=ALU.add),
                reads=[ugc, convp, agc], writes=[agc])
            P.op("dve", lambda e, agc=agc, ugc=ugc, cg=cg: e.scalar_tensor_tensor(
                out=agc[:], in0=ugc[:, 0:SW], scalar=convp[:, cg, 0:1], in1=agc[:], op0=ALU.mult, op1=ALU.add),
                reads=[ugc, convp, agc], writes=[agc])
            P.op("dve", lambda e, avc=avc, uvc=uvc, cv=cv: e.scalar_tensor_tensor(
                out=avc[:], in0=uvc[:, 1:SW + 1], scalar=convp[:, cv, 1:2], in1=avc[:], op0=ALU.mult, op1=ALU.add),
                reads=[uvc, convp, avc], writes=[avc])
            P.op("dve", lambda e, avc=avc, uvc=uvc, cv=cv: e.scalar_tensor_tensor(
                out=avc[:], in0=uvc[:, 0:SW], scalar=convp[:, cv, 0:1], in1=avc[:], op0=ALU.mult, op1=ALU.add),
                reads=[uvc, convp, avc], writes=[avc])
            P.op("act", lambda e, sgc=sgc, agc=agc: e.activation(out=sgc[:], in_=agc[:], func=AF.Silu), reads=[agc], writes=[sgc])
            P.op("dve", lambda e, sgc=sgc, avc=avc, c=c, sl=sl: e.tensor_tensor(out=aT[:, c, sl], in0=sgc[:], in1=avc[:], op=ALU.mult),
                 reads=[sgc, avc], pwrites=[aTs[s]])

    P.barrier()
    K.A.cur = mark
    wd = [sb("wd", [128, NFF, 128], BF16) for _ in range(2)]
    xj = [sb("xj", [128, SW], F32) for _ in range(3)]
    ps_y = [ps(0, [128, SW]), ps(1, [128, SW])]
    xin3 = xin.rearrange("(c p) t -> p c t", p=128)
    xout3 = xout.rearrange("(c p) t -> p c t", p=128)
    it = 0
    for j in range(8):
        wdj = wd[j % 2]
        P.dma("pool", wdj[:], d[f"w_down_r{li}"][j], writes=[wdj])
        for s in range(NS):
            sl = slice(s * SW, (s + 1) * SW)
            py = ps_y[it % 2]
            xs = xj[it % 3]
            it += 1
            P.dma("sp", xs[:], xin3[:, j, sl], reads=[xin_tok], writes=[xs])
            for c in range(NFF):
                P.op("pe", lambda e, c=c, py=py, wdj=wdj, sl=sl: e.matmul(
                    py[:], lhsT=wdj[:, c, :], rhs=aT[:, c, sl], start=(c == 0), stop=(c == NFF - 1)),
                    reads=[wdj, aTs[s]], writes=[py])
            P.op("dve", lambda e, py=py, xs=xs: e.tensor_tensor(out=xs[:], in0=py[:], in1=xs[:], op=ALU.add),
                 reads=[py, xs], writes=[xs])
            P.dma("sp", xout3[:, j, sl], xs[:], reads=[xs], pwrites=[xout_tok])


def final_norm(K, C, d, xin, xin_tok, out, out_tok):
    P, sb, ps = K.P, K.sb, K.ps
    K.phase()
    gfin = sb("gfin", [128, 8], F32)
    P.dma("sp", gfin[:], d["gfinal"], writes=[gfin])
    xT3 = xin.rearrange("(c p) t -> p c t", p=128)
    o3 = out.rearrange("(c p) t -> p c t", p=128)
    xst = [sb("xst", [128, 8, SW], F32) for _ in range(2)]
    ost = [sb("ost", [128, 8, SW], F32) for _ in range(2)]
    sq = sb("sq", [128, 8, SW], BF16)
    rstd = sb("rstd", [128, SW], F32)
    ps_n = ps(0, [128, SW])
    for s in range(NS):
        xs, os_ = xst[s % 2], ost[s % 2]
        P.dma("sp", xs[:], xT3[:, :, s * SW:(s + 1) * SW], reads=[xin_tok], writes=[xs])
        rmsnorm_fm(P, C, xs, gfin, os_, os_, sq, ps_n, rstd)
        P.dma("sp", o3[:, :, s * SW:(s + 1) * SW], os_[:], reads=[os_], pwrites=[out_tok])


import math

T_ALL = 16384
NB = T_ALL // 128
NQS = T_ALL // SW
LAM_INIT = 0.8 - 0.6 * math.exp(-0.3 * 1)
NEG = -30000.0
GLEN = 1151


def kvq_proj(K, C, d, xin, xin_tok, qkv_in, qkv_tok):
    P, sb, ps = K.P, K.sb, K.ps
    K.phase()
    xT3 = xin.rearrange("(c p) t -> p c t", p=128)
    gkv = sb("gkv", [128, 8], F32)
    gq = sb("gq", [128, 8], F32)
    P.dma("sp", gkv[:], d["gkv"], writes=[gkv])
    P.dma("sp", gq[:], d["gmix1"], writes=[gq])
    hk = sb("hk", [128, 8, NT], BF16)
    hq = sb("hq", [128, 8, NT], BF16)
    hks = [Buf(f"hk{s}", hk.t[:, :, s * SW:(s + 1) * SW]) for s in range(NS)]
    hqs = [Buf(f"hq{s}", hq.t[:, :, s * SW:(s + 1) * SW]) for s in range(NS)]
    xst = sb("xst", [128, 8, SW], F32)
    sq = sb("sq", [128, 8, SW], BF16)
    rstd = sb("rstd", [128, SW], F32)
    ps_n = ps(0, [128, SW])
    for s in range(NS):
        P.dma("sp", xst[:], xT3[:, :, s * SW:(s + 1) * SW], reads=[xin_tok], writes=[xst])
        rmsnorm_fm(P, C, xst, gkv, hks[s], hks[s], sq, ps_n, rstd)
        for c in range(8):
            P.op("dve", lambda e, c=c, s=s: e.scalar_tensor_tensor(out=hqs[s][:, c, :], in0=xst[:, c, :], scalar=gq[:, c:c + 1],
                                                                    in1=rstd[:], op0=ALU.mult, op1=ALU.mult),
                 reads=[xst, gq, rstd], pwrites=[hqs[s]])
    wk = [sb("wk", [128, 8, 128], BF16) for _ in range(2)]
    wq = [sb("wq", [128, 8, 128], BF16) for _ in range(2)]
    kst = [sb("kst", [128, NT], BF16) for _ in range(2)]
    qst = [sb("qst", [128, NT], BF16) for _ in range(2)]
    psk = [ps(1, [128, SW]), ps(2, [128, SW])]
    psq = [ps(3, [128, SW]), ps(4, [128, SW])]
    it = 0
    for h in range(8):
        wkh, wqh, ks_, qs_ = wk[h % 2], wq[h % 2], kst[h % 2], qst[h % 2]
        P.dma("pool", wkh[:], d["w_k_r"][h], writes=[wkh])
        P.dma("pool", wqh[:], d["w_q_r"][h], writes=[wqh])
        for s in range(NS):
            sl = slice(s * SW, (s + 1) * SW)
            pk, pq = psk[it % 2], psq[it % 2]
            it += 1
            for m in range(8):
                P.op("pe", lambda e, m=m, pk=pk, wkh=wkh, sl=sl: e.matmul(pk[:], lhsT=wkh[:, m, :], rhs=hk[:, m, sl],
                                                                          start=(m == 0), stop=(m == 7)),
                     reads=[wkh, hks[s]], writes=[pk])
            for m in range(8):
                P.op("pe", lambda e, m=m, pq=pq, wqh=wqh, sl=sl: e.matmul(pq[:], lhsT=wqh[:, m, :], rhs=hq[:, m, sl],
                                                                          start=(m == 0), stop=(m == 7)),
                     reads=[wqh, hqs[s]], writes=[pq])
            P.op("act", lambda e, pk=pk, ks_=ks_, sl=sl: e.copy(out=ks_[:, sl], in_=pk[:]), reads=[pk], pwrites=[ks_])
            P.op("dve", lambda e, pq=pq, qs_=qs_, sl=sl: e.tensor_scalar(out=qs_[:, sl], in0=pq[:], scalar1=0.125, scalar2=None,
                                                                         op0=ALU.mult), reads=[pq], pwrites=[qs_])
        P.dma("sp", qkv_in[h * 384:h * 384 + 128, :], qs_[:], reads=[qs_], pwrites=[qkv_tok])
        P.dma("sp", qkv_in[h * 384 + 128:h * 384 + 256, :], ks_[:], reads=[ks_], pwrites=[qkv_tok])
    wv = sb("wv", [128, 8, 1024], BF16)
    for h in range(8):
        P.dma("pool", wv[:, :, h * 128:(h + 1) * 128], d["w_v_r"][h], pwrites=[wv])
    vstage = sb("vstage", [128, 8, 16, 128], BF16)
    psv = [ps(1, [128, 4, 128]), ps(2, [128, 4, 128])]
    psv_flat = [ps(1, [128, 512]), ps(2, [128, 512])]
    for tb in range(16):
        s = tb // 4
        for hf in range(2):
            pv = psv_flat[hf]
            for m in range(8):
                P.op("pe", lambda e, m=m, pv=pv, tb=tb, hf=hf: e.matmul(
                    pv[:], lhsT=hk[:, m, tb * 128:(tb + 1) * 128], rhs=wv[:, m, hf * 512:(hf + 1) * 512],
                    start=(m == 0), stop=(m == 7)), reads=[hks[s], wv], writes=[pv])
            if hf == 0:
                P.op("act", lambda e, tb=tb, hf=hf: e.copy(out=vstage[:, hf * 4:(hf + 1) * 4, tb, :], in_=psv[hf][:]),
                     reads=[psv[hf]], pwrites=[vstage])
            else:
                P.op("dve", lambda e, tb=tb, hf=hf: e.tensor_copy(out=vstage[:, hf * 4:(hf + 1) * 4, tb, :], in_=psv[hf][:]),
                     reads=[psv[hf]], pwrites=[vstage])
    for h in range(8):
        P.dma("sp", qkv_in[h * 384 + 256:h * 384 + 384, :], vstage.t[:, h, :, :].rearrange("p b v -> p (b v)"),
              reads=[vstage], pwrites=[qkv_tok])


def attn_core(K, C, d, qkv_all, qkv_all_tok, o_in, o_tok, gvec, gvec_tok):
    P, sb, ps = K.P, K.sb, K.ps
    K.phase()
    QKV = sb("QKV", [128, 3, 8, NT], BF16)
    QT = QKV.alias(QKV.t[:, 0, :, :].rearrange("p r t -> p (r t)"))
    KT = QKV.alias(QKV.t[:, 1, :, :].rearrange("p r t -> p (r t)"))
    VA = sb("VA", [128, NB, 129], BF16)
    P.op("pool", lambda e: e.memset(VA[:], 1.0), writes=[VA])
    q4 = qkv_all.rearrange("(r h x) t -> r h x t", r=8, h=8)
    for r in range(8):
        def fn(e, r=r):
            pid = P.pid(e)
            src = q4[r, bass.ds(pid, 1), :, :].rearrange("o (k p) t -> p (o k) t", k=3)
            return e.dma_start(out=QKV[:, :, r, :], in_=src)
        P._add("act", fn, [qkv_all_tok], (), [QKV], True)
    for r in range(8):
        P.op("pool", lambda e, r=r: e.tensor_copy(out=VA[:, r * 16:(r + 1) * 16, 0:128],
                                                  in_=QKV[:, 2, r, :].rearrange("p (b v) -> p b v", b=16)),
             reads=[QKV], pwrites=[VA])
    Vreg = QKV.t[:, 2, :, :].rearrange("p r t -> p (r t)")
    P.op("pool", lambda e: e.tensor_copy(out=Vreg[64:128, :], in_=QT[64:128, :]), reads=[QKV], pwrites=[QKV])
    P.op("pool", lambda e: e.memset(Vreg[0:64, :], 0.0), pwrites=[QKV])
    P.op("pool", lambda e: e.memset(QT[64:128, :], 0.0), reads=[QKV], pwrites=[QKV])
    Qz = [QT, QKV.alias(Vreg)]
    lamv = sb("lamv", [128, 4, 64], F32)
    P.dma("sp", lamv[:], d["lamv"].partition_broadcast(128), writes=[lamv])
    lp = sb("lp", [128, 2, 64], F32)
    ls = sb("ls", [128, 2], F32)
    nlam = sb("nlam", [128, 1], F32)
    P.op("dve", lambda e: e.tensor_tensor(out=lp[:, 0, :], in0=lamv[:, 0, :], in1=lamv[:, 1, :], op=ALU.mult), reads=[lamv], pwrites=[lp])
    P.op("dve", lambda e: e.tensor_tensor(out=lp[:, 1, :], in0=lamv[:, 2, :], in1=lamv[:, 3, :], op=ALU.mult), reads=[lamv], pwrites=[lp])
    P.op("dve", lambda e: e.reduce_sum(out=ls[:], in_=lp[:], axis=AX.X), reads=[lp], writes=[ls])
    P.op("act", lambda e: e.activation(out=ls[:], in_=ls[:], func=AF.Exp), reads=[ls], writes=[ls])
    P.op("dve", lambda e: e.tensor_sub(out=nlam[:], in0=ls[:, 1:2], in1=ls[:, 0:1]), reads=[ls], writes=[nlam])
    P.op("dve", lambda e: e.tensor_scalar(out=nlam[:], in0=nlam[:], scalar1=-LAM_INIT, scalar2=None, op0=ALU.add),
         reads=[nlam], writes=[nlam])
    gsub = sb("gsub", [128, 128], F32)
    P.dma("sp", gsub[:], d["subln"].partition_broadcast(128), writes=[gsub])
    P.op("dve", lambda e: e.tensor_scalar(out=gsub[:], in0=gsub[:], scalar1=1.0 - LAM_INIT, scalar2=None, op0=ALU.mult),
         reads=[gsub], writes=[gsub])
    eps128 = C.eps
    relcol = sb("relcol", [32, 1], F32)
    oh = sb("oh", [32, 128], F32)
    P.dma("sp", relcol[:], d["relcol"], writes=[relcol])
    P.dma("sp", oh[:], d["oh"], writes=[oh])
    ohb = sb("ohb", [32, 128], BF16)
    rc_hi = sb("rc_hi", [32, 1], BF16)
    rc_lo = sb("rc_lo", [32, 1], BF16)
    P.op("dve", lambda e: e.tensor_copy(out=ohb[:], in_=oh[:]), reads=[oh], writes=[ohb])
    P.op("dve", lambda e: e.tensor_copy(out=rc_hi[:], in_=relcol[:]), reads=[relcol], writes=[rc_hi])
    P.op("dve", lambda e: e.tensor_tensor(out=rc_lo[:], in0=relcol[:], in1=rc_hi[:], op=ALU.subtract),
         reads=[relcol, rc_hi], writes=[rc_lo])
    ps_g = ps(0, [1, 128])
    gm = sb("gm", [1, 128], F32)
    P.op("pe", lambda e: e.matmul(ps_g[:], lhsT=rc_hi[:], rhs=ohb[:], start=True, stop=False), reads=[rc_hi, ohb], writes=[ps_g])
    P.op("pe", lambda e: e.matmul(ps_g[:], lhsT=rc_lo[:], rhs=ohb[:], start=False, stop=True), reads=[rc_lo, ohb], writes=[ps_g])
    P.op("act", lambda e: e.copy(out=gm[:], in_=ps_g[:]), reads=[ps_g], writes=[gm])
    gv = gvec.ap()
    P.dma("sp", gv, d["gconst"], writes=[gvec_tok])
    P.dma("sp", gv[:, 511:639], gm[:], reads=[gm], writes=[gvec_tok])
    btile = sb("btile", [128, 5, SW], F32)
    antiI = sb("antiI", [128, 128], BF16)
    P.dma("pool", antiI[:], d["antiI"], writes=[antiI])
    hk_t = [sb("hk_t", [128, SW], F32) for _ in range(2)]
    hk_hi = [sb("hk_hi", [128, SW], BF16) for _ in range(2)]
    hk_lo = [sb("hk_lo", [128, SW], BF16) for _ in range(2)]
    for i in range(5):
        src = bass.AP(gvec, 512 - 128 * i, [[1, 128], [1, SW]])
        hkt, hhi, hlo = hk_t[i % 2], hk_hi[i % 2], hk_lo[i % 2]
        P.dma("sp", hkt[:], src, reads=[gvec_tok], writes=[hkt])
        P.op("dve", lambda e, hkt=hkt, hhi=hhi: e.tensor_copy(out=hhi[:], in_=hkt[:]), reads=[hkt], writes=[hhi])
        P.op("dve", lambda e, hkt=hkt, hhi=hhi, hlo=hlo: e.tensor_tensor(out=hlo[:], in0=hkt[:], in1=hhi[:], op=ALU.subtract),
             reads=[hkt, hhi], writes=[hlo])
        pbt = K.pb[i % 2]
        P.op("pe", lambda e, pbt=pbt, hhi=hhi: e.matmul(pbt[:], lhsT=antiI[:], rhs=hhi[:], start=True, stop=False),
             reads=[antiI, hhi], writes=[pbt])
        P.op("pe", lambda e, pbt=pbt, hlo=hlo: e.matmul(pbt[:], lhsT=antiI[:], rhs=hlo[:], start=False, stop=True),
             reads=[antiI, hlo], writes=[pbt])
        P.op("act", lambda e, pbt=pbt, i=i: e.copy(out=btile[:, i, :], in_=pbt[:]), reads=[pbt], pwrites=[btile])

    pT = [[sb("pT", [128, SW], BF16) for _ in range(2)] for _ in range(2)]
    stmp = [sb("stmp", [128, SW], F32) for _ in range(2)]
    psS = [[K.pb[0], K.pb[1]], [K.pb[2], K.pb[3]]]
    accb = [K.pb[4], K.pb[5], K.pb2]

    def acc(m, j):
        i = m * 4 + j
        b = accb[i // 3]
        o = (i % 3) * 129
        return b, b.t[:, o:o + 129]
    ps_tr = K.pb2.alias(K.pb2.t[:, 512:768].bitcast(BF16))
    accS = [sb("accS", [128, 8 * 129], F32) for _ in range(2)]
    o_sb = [sb("o_sb", [128, 128], F32) for _ in range(4)]
    osq = [sb("osq", [128, 128], F32) for _ in range(2)]
    on = [sb("on", [128, 128], BF16) for _ in range(4)]
    sm = [sb("sm", [128, 8], F32) for _ in range(4)]
    oT_st = [sb("oT_st", [128, SW], BF16) for _ in range(2)]
    def emit_qk(qs, kb, maps=(0, 1)):
        i_near = kb - (qs * 4 - 1)
        near = i_near >= 0
        j0 = max(0, kb - qs * 4)
        c0 = j0 * 128
        for m in maps:
            pS = psS[m][kb % 2]
            P.op("pe", lambda e, pS=pS, m=m, kb=kb, qs=qs, c0=c0: e.matmul(
                pS[:, c0:SW], lhsT=KT[:, kb * 128:(kb + 1) * 128], rhs=Qz[m][:, qs * SW + c0:(qs + 1) * SW],
                start=True, stop=True), reads=[KT, QT], writes=[pS])
        for m in maps:
            pS = psS[m][kb % 2]
            pt = pT[m][kb % 2]
            if near:
                st = stmp[m]
                P.op("dve", lambda e, st=st, pS=pS, i_near=i_near, c0=c0: e.tensor_tensor(
                    out=st[:, c0:SW], in0=pS[:, c0:SW], in1=btile[:, i_near, c0:SW], op=ALU.add),
                    reads=[pS, btile], writes=[st])
                P.op("act", lambda e, st=st, pt=pt, c0=c0: e.activation(out=pt[:, c0:SW], in_=st[:, c0:SW], func=AF.Exp),
                     reads=[st], writes=[pt])
            else:
                P.op("act", lambda e, pS=pS, pt=pt: e.activation(out=pt[:], in_=pS[:], func=AF.Exp), reads=[pS], writes=[pt])

    def emit_pv(qs, kb, maps=(0, 1), last=True):
        j0 = max(0, kb - qs * 4)
        for m in maps:
            pt = pT[m][kb % 2]
            for j in range(j0, 4):
                ab, aap = acc(m, j)
                st_ = (kb == 0) and ((m * 4 + j) % 3 == 0)
                P.op("pe", lambda e, aap=aap, pt=pt, j=j, kb=kb, qs=qs, st_=st_: e.matmul(
                    aap, lhsT=pt[:, j * 128:(j + 1) * 128], rhs=VA[:, kb, :], start=st_, stop=(kb == qs * 4 + j)),
                    reads=[pt, VA], pwrites=[ab])
        if last and kb == (qs + 1) * 4 - 1:
            epilogue(qs)

    def epilogue(qs):
        aS = accS[qs % 2]
        P.op("dve", lambda e, aS=aS: e.tensor_copy(out=aS[:, 0:387], in_=accb[0].t[:, 0:387]), reads=[accb[0]], pwrites=[aS])
        P.op("dve", lambda e, aS=aS: e.tensor_copy(out=aS[:, 387:774], in_=accb[1].t[:, 0:387]), reads=[accb[1]], pwrites=[aS])
        P.op("dve", lambda e, aS=aS: e.tensor_copy(out=aS[:, 774:1032], in_=accb[2].t[:, 0:258]), reads=[accb[2]], pwrites=[aS])
        for j in range(4):
            a0 = aS.t[:, j * 129:(j + 1) * 129]
            a1 = aS.t[:, (4 + j) * 129:(5 + j) * 129]
            s_, o_, q_, n_ = sm[j], o_sb[j], osq[j % 2], on[j]
            P.op("dve", lambda e, a0=a0, s_=s_: e.reciprocal(out=s_[:, 0:1], in_=a0[:, 128:129]), reads=[aS], pwrites=[s_])
            P.op("dve", lambda e, a1=a1, s_=s_: e.reciprocal(out=s_[:, 1:2], in_=a1[:, 128:129]), reads=[aS], pwrites=[s_])
            P.op("dve", lambda e, s_=s_: e.tensor_tensor(out=s_[:, 2:3], in0=s_[:, 1:2], in1=nlam[:], op=ALU.mult),
                 reads=[s_, nlam], pwrites=[s_])
            P.op("dve", lambda e, a0=a0, s_=s_, o_=o_: e.tensor_scalar(out=o_[:], in0=a0[:, 0:128], scalar1=s_[:, 0:1], scalar2=None,
                                                                      op0=ALU.mult), reads=[aS, s_], writes=[o_])
            P.op("dve", lambda e, a1=a1, s_=s_, o_=o_: e.scalar_tensor_tensor(out=o_[:], in0=a1[:, 0:128], scalar=s_[:, 2:3], in1=o_[:],
                                                                             op0=ALU.mult, op1=ALU.add), reads=[aS, s_, o_], writes=[o_])
            P.op("pool", lambda e, o_=o_, q_=q_: e.tensor_tensor(out=q_[:], in0=o_[:], in1=o_[:], op=ALU.mult), reads=[o_], writes=[q_])
            P.op("dve", lambda e, s_=s_, q_=q_: e.reduce_sum(out=s_[:, 3:4], in_=q_[:], axis=AX.X), reads=[q_], pwrites=[s_])
            P.op("act", lambda e, s_=s_: e.activation(out=s_[:, 4:5], in_=s_[:, 3:4], func=AF.Sqrt, scale=1.0 / 128, bias=eps128[:]),
                 reads=[s_, eps128], pwrites=[s_])
            P.op("dve", lambda e, s_=s_: e.reciprocal(out=s_[:, 5:6], in_=s_[:, 4:5]), reads=[s_], pwrites=[s_])
            P.op("dve", lambda e, s_=s_, o_=o_, n_=n_: e.scalar_tensor_tensor(out=n_[:], in0=o_[:], scalar=s_[:, 5:6], in1=gsub[:],
                                                                             op0=ALU.mult, op1=ALU.mult), reads=[o_, s_, gsub], writes=[n_])

    def epilogue_out(qs):
        ost = oT_st[qs % 2]
        for j in range(4):
            P.op("pe", lambda e, j=j: e.transpose(out=ps_tr[:, j * 128:(j + 1) * 128], in_=on[j][:], identity=C.ident[:]),
                 reads=[on[j], C.ident], pwrites=[ps_tr])
        P.op("dve", lambda e, ost=ost: e.tensor_copy(out=ost[:], in_=ps_tr[:]), reads=[ps_tr], writes=[ost])
        P.dma("sp", o_in[:, qs * SW:(qs + 1) * SW], ost[:], reads=[ost], pwrites=[o_tok])

    units = [(qs, kb) for qs in range(NQS) for kb in range((qs + 1) * 4)]
    pending = []
    for idx in range(len(units) + 1):
        for m in range(2):
            if idx < len(units):
                emit_qk(*units[idx], maps=(m,))
            if idx >= 1:
                emit_pv(*units[idx - 1], maps=(m,), last=(m == 1))
        if idx >= 1:
            qs_, kb_ = units[idx - 1]
            if kb_ == (qs_ + 1) * 4 - 1:
                pending.append((idx + 3, qs_))
        while pending and pending[0][0] <= idx:
            epilogue_out(pending.pop(0)[1])
    for _, qs_ in pending:
        epilogue_out(qs_)


def attn_out(K, C, d, o_all, o_all_tok, xin, xin_tok, xout, xout_tok, halo_in, halo_tok):
    P, sb, ps = K.P, K.sb, K.ps
    K.phase()
    og = sb("og", [128, 8, NT], BF16)

    def fn(e):
        pid = P.pid(e)
        return e.dma_start(out=og[:], in_=o_all.rearrange("(h p) t -> p h t", p=128)[:, :, bass.ds(pid * NT, NT)])
    P._add("sp", fn, [o_all_tok], [og], (), True)
    wo = sb("wo", [128, 8, 1024], BF16)
    P.dma("pool", wo[:], d["w_o_r"], writes=[wo])
    xj = [sb("xj", [128, SW], F32) for _ in range(3)]
    ps_y = [ps(0, [128, SW]), ps(1, [128, SW])]
    xin3 = xin.rearrange("(c p) t -> p c t", p=128)
    xout3 = xout.rearrange("(c p) t -> p c t", p=128)
    it = 0
    for j in range(8):
        for s in range(NS):
            sl = slice(s * SW, (s + 1) * SW)
            py = ps_y[it % 2]
            xs = xj[it % 3]
            it += 1
            P.dma("sp", xs[:], xin3[:, j, sl], reads=[xin_tok], writes=[xs])
            for h in range(8):
                P.op("pe", lambda e, h=h, j=j, py=py, sl=sl: e.matmul(py[:], lhsT=wo[:, h, j * 128:(j + 1) * 128], rhs=og[:, h, sl],
                                                                      start=(h == 0), stop=(h == 7)), reads=[wo, og], writes=[py])
            P.op("dve", lambda e, py=py, xs=xs: e.tensor_tensor(out=xs[:], in0=py[:], in1=xs[:], op=ALU.add),
                 reads=[py, xs], writes=[xs])
            P.dma("sp", xout3[:, j, sl], xs[:], reads=[xs], pwrites=[xout_tok])
            if s == NS - 1:
                P.dma("sp", halo_in[:, j * 2:(j + 1) * 2], xs[:, SW - 2:SW], reads=[xs], pwrites=[halo_tok])


import numpy as np
from concourse.bass_utils import run_bass_kernel_spmd

NCORES = 8


def allgather(P, src_h, dst_h, src_tok, dst_tok, rows=None):
    dst = dst_h.ap() if rows is None else dst_h.ap()[0:rows, :]
    P.async_op("pool", lambda e: e.collective_compute("AllGather", ALU.bypass, replica_groups=[list(range(NCORES))],
                                                      ins=[src_h.ap().opt()], outs=[dst.opt()]),
               reads=[src_tok], writes=[dst_tok], inc=1)


IN_SPECS = [
    ("xT", [1024, NT]), ("w_in_r", [32, 128, 8, 128]), ("w_out_r", [128, 8, 1024]), ("ident", [128, 128]),
    ("resetm", [128, SW]), ("maskc", [64, 8, 64]), ("gmix0", [128, 8]), ("gnorm", [128, 8]), ("lbl", [128, 2, 8]),
    ("sel", [128, 8]), ("notfirst", [128, 1]),
    ("gffn0", [128, 8]), ("w_up_r0", [44, 128, 8, 128]), ("convp0", [128, 44, 4]), ("w_down_r0", [8, 128, 22, 128]),
    ("gffn1", [128, 8]), ("w_up_r1", [44, 128, 8, 128]), ("convp1", [128, 44, 4]), ("w_down_r1", [8, 128, 22, 128]),
    ("gkv", [128, 8]), ("gmix1", [128, 8]), ("w_k_r", [8, 128, 8, 128]), ("w_q_r", [8, 128, 8, 128]),
    ("w_v_r", [8, 128, 8, 128]), ("relcol", [32, 1]), ("oh", [32, 128]), ("gconst", [1, GLEN]), ("antiI", [128, 128]), ("lamv", [4, 64]),
    ("subln", [1, 128]), ("w_o_r", [128, 8, 1024]), ("gfinal", [128, 8]),
]


def build(debug=None):
    nc = bass.Bass("TRN2", target_bir_lowering=False)
    K = KB(nc)
    P = K.P
    d = {}
    for name, shape in IN_SPECS:
        d[name] = nc.dram_tensor(name, shape, F32, kind="ExternalInput").ap()
    outT = nc.dram_tensor("outT", [1024, NT], F32, kind="ExternalOutput").ap()
    out_tok = Buf("out")

    def stream(name):
        kind = "ExternalOutput" if debug == name else "Internal"
        return nc.dram_tensor(name, [1024, NT], F32, kind=kind).ap(), Buf(name)
    x1T, x1_tok = stream("x1T")
    x2T, x2_tok = stream("x2T")
    x3T, x3_tok = stream("x3T")
    x4T, x4_tok = stream("x4T")
    scr = {}
    for n in ("qT", "kT", "gT"):
        scr[n] = nc.dram_tensor(n + "_s", [1024, NT], BF16).ap()
        scr[n + "_tok"] = Buf(n)
    for n in ("v", "kt"):
        scr[n] = nc.dram_tensor(n + "_s", [NT, 1024], BF16).ap()
        scr[n + "_tok"] = Buf(n)
    hx_in = nc.dram_tensor("hx_in", [128, 1032], F32)
    hx_all = nc.dram_tensor("hx_all", [NCORES * 128, 1032], F32)
    hx_in_tok, hx_all_tok = Buf("hx_in"), Buf("hx_all")
    halo_in = [nc.dram_tensor(f"halo_in{i}", [128, 16], F32) for i in range(2)]
    halo_all = [nc.dram_tensor(f"halo_all{i}", [NCORES * 128, 16], F32) for i in range(2)]
    halo_in_tok = [Buf("hi0"), Buf("hi1")]
    halo_all_tok = [Buf("ha0"), Buf("ha1")]
    qkv_in = nc.dram_tensor("qkv_in", [3072, NT], BF16)
    qkv_all = nc.dram_tensor("qkv_all", [NCORES * 3072 + 384, NT], BF16)
    qkv_in_tok, qkv_all_tok = Buf("qkv_in"), Buf("qkv_all")
    gvec = nc.dram_tensor("gvec", [1, GLEN], F32)
    gvec_tok = Buf("gvec")
    o_in = nc.dram_tensor("o_in", [128, T_ALL], BF16)
    o_all = nc.dram_tensor("o_all", [NCORES * 128, T_ALL], BF16)
    o_in_tok, o_all_tok = Buf("o_in"), Buf("o_all")

    C = hgrn_consts(K, d)
    T = hgrn_alloc_T(K)
    S = K.sb("S", [128, 8, 128], F32, pers=True)
    Rr = K.sb("Rr", [128, 8, 128], F32, pers=True)
    sel = K.sb("sel", [128, 8], F32, pers=True)
    P.dma("sp", sel[:], d["sel"], writes=[sel])
    P.op("pool", lambda e: e.memset(S[:], 0.0), writes=[S])
    K.A.start_phase()
    hgrn_P(K, C, d, scr, T)
    K.phase()
    hgrn_R(K, C, d, scr, T, False, S)
    P.dma("sp", hx_in.ap()[:, 0:1024], S.t.rearrange("p h v -> p (h v)"), reads=[S], pwrites=[hx_in_tok])
    P.dma("sp", hx_in.ap()[:, 1024:1032], T.D[:], reads=[T.D], pwrites=[hx_in_tok])
    allgather(P, hx_in, hx_all, hx_in_tok, hx_all_tok)
    K.phase()
    Sj = [K.sb("Sj", [128, 1032], F32) for _ in range(2)]
    P.op("pool", lambda e: e.memset(S[:], 0.0), writes=[S])
    P.op("pool", lambda e: e.memset(Rr[:], 0.0), writes=[Rr])
    for j in range(NCORES):
        sj = Sj[j % 2]
        P.dma("sp", sj[:], hx_all.ap()[j * 128:(j + 1) * 128, :], reads=[hx_all_tok], writes=[sj])
        P.op("dve", lambda e, j=j: e.scalar_tensor_tensor(out=S[:], in0=Rr[:], scalar=sel[:, j:j + 1], in1=S[:],
                                                          op0=ALU.mult, op1=ALU.add), reads=[Rr, sel, S], writes=[S])
        if j < NCORES - 1:
            P.op("dve", lambda e, sj=sj: e.tensor_tensor(out=Rr[:], in0=Rr[:],
                                                         in1=sj[:, 1024:1032].unsqueeze(2).to_broadcast([128, 8, 128]),
                                                         op=ALU.mult), reads=[Rr, sj], writes=[Rr])
            P.op("dve", lambda e, sj=sj: e.tensor_tensor(out=Rr[:], in0=Rr[:],
                                                         in1=sj[:, 0:1024].rearrange("p (h v) -> p h v", h=8),
                                                         op=ALU.add), reads=[Rr, sj], writes=[Rr])
    d["x1T"], d["x1T_tok"] = x1T, x1_tok
    d["halo_out"], d["halo_out_tok"] = halo_in[0].ap(), halo_in_tok[0]
    hgrn_R(K, C, d, scr, T, True, S)
    allgather(P, halo_in[0], halo_all[0], halo_in_tok[0], halo_all_tok[0])
    if debug == "x1T":
        P.emit(final_bufs=[x1_tok, halo_all_tok[0]])
        print("nflag", P.nflag, "ndma", P.n_dma, "peak", K.A.peak)
        return nc
    ffn_layer(K, C, d, 0, x1T, x1_tok, x2T, x2_tok, halo_all[0].ap(), halo_all_tok[0])
    if debug == "x2T":
        P.emit(final_bufs=[x2_tok])
        print("nflag", P.nflag, "ndma", P.n_dma, "peak", K.A.peak)
        return nc
    kvq_proj(K, C, d, x2T, x2_tok, qkv_in.ap(), qkv_in_tok)
    allgather(P, qkv_in, qkv_all, qkv_in_tok, qkv_all_tok, rows=NCORES * 3072)
    attn_core(K, C, d, qkv_all.ap()[0:NCORES * 3072, :], qkv_all_tok, o_in.ap(), o_in_tok, gvec, gvec_tok)
    allgather(P, o_in, o_all, o_in_tok, o_all_tok)
    attn_out(K, C, d, o_all.ap(), o_all_tok, x2T, x2_tok, x3T, x3_tok, halo_in[1].ap(), halo_in_tok[1])
    allgather(P, halo_in[1], halo_all[1], halo_in_tok[1], halo_all_tok[1])
    if debug == "x3T":
        P.emit(final_bufs=[x3_tok, halo_all_tok[1]])
        return nc
    ffn_layer(K, C, d, 1, x3T, x3_tok, x4T, x4_tok, halo_all[1].ap(), halo_all_tok[1])
    final_norm(K, C, d, x4T, x4_tok, outT, out_tok)
    P.emit(final_bufs=[out_tok])
    return nc


def t5_bucket_np(rel):
    max_exact = 16
    n = np.maximum(rel, 0)
    log_ratio = (np.log(np.maximum(n, 1).astype(np.float32) / np.float32(max_exact)) / np.float32(math.log(128 / max_exact))).astype(np.float32)
    large = np.minimum(max_exact + (log_ratio * np.float32(32 - max_exact)).astype(np.int32), 31)
    return np.where(n < max_exact, n, large)


def host_inputs(inp, c):
    f = lambda a: np.ascontiguousarray(np.asarray(a, dtype=np.float32))
    pc = lambda v: f(np.asarray(v).reshape(8, 128).T)
    m = {}
    m["xT"] = f(np.asarray(inp["x"])[0, c * NT:(c + 1) * NT, :].T)
    m["w_in_r"] = f(np.asarray(inp["a_w_in"])[0].reshape(8, 128, 32, 128).transpose(2, 1, 0, 3))
    m["w_out_r"] = f(np.asarray(inp["a_w_out"])[0].reshape(8, 128, 1024).transpose(1, 0, 2))
    m["ident"] = np.eye(128, dtype=np.float32)
    r = np.ones((128, SW), np.float32)
    r[:, ::CH] = 0
    m["resetm"] = r
    mk_ = (np.arange(64)[:, None] <= np.arange(64)[None, :]).astype(np.float32)
    m["maskc"] = f(np.broadcast_to(mk_[:, None, :], (64, 8, 64)))
    m["gmix0"] = pc(inp["norm_mix"][0])
    m["gmix1"] = pc(inp["norm_mix"][1])
    m["gnorm"] = pc(inp["a_gnorm"][0])
    m["lbl"] = f(np.asarray(inp["a_lb_logits"]).reshape(2, 8, 128).transpose(2, 0, 1))
    s = np.zeros((128, 8), np.float32)
    s[:, c] = 1.0
    m["sel"] = s
    m["notfirst"] = np.full((128, 1), 0.0 if c == 0 else 1.0, np.float32)
    for li in range(2):
        m[f"gffn{li}"] = pc(inp["norm_ffn"][li])
        m[f"w_up_r{li}"] = f(np.asarray(inp["ffn_w_up"])[li].reshape(8, 128, 44, 128).transpose(2, 1, 0, 3))
        cw = np.asarray(inp["ffn_conv_w"])[li]
        cb = np.asarray(inp["ffn_conv_b"])[li]
        cp = np.concatenate([cw, cb[None]], 0)
        m[f"convp{li}"] = f(cp.reshape(4, 44, 128).transpose(2, 1, 0))
        m[f"w_down_r{li}"] = f(np.asarray(inp["ffn_w_down"])[li].reshape(22, 128, 8, 128).transpose(2, 1, 0, 3))
    m["gkv"] = pc(inp["kv_norm"])
    kvw = np.asarray(inp["kv_w"])
    m["w_k_r"] = f(kvw[:, :1024].reshape(8, 128, 8, 128).transpose(2, 1, 0, 3))
    m["w_v_r"] = f(kvw[:, 1024:].reshape(8, 128, 8, 128).transpose(2, 1, 0, 3))
    m["w_q_r"] = f(np.asarray(inp["b_w_q"])[0].reshape(8, 128, 8, 128).transpose(2, 1, 0, 3))
    m["w_o_r"] = f(np.asarray(inp["b_w_o"])[0].reshape(8, 128, 1024).transpose(1, 0, 2))
    m["relcol"] = f(np.asarray(inp["rel_table"])[:, c:c + 1])
    bk = t5_bucket_np(np.arange(128))
    oh = np.zeros((32, 128), np.float32)
    oh[bk, np.arange(128)] = 1.0
    oh[31, :] -= 1.0
    m["oh"] = oh
    g = np.zeros((1, GLEN), np.float32)
    g[0, :511] = NEG
    m["gconst"] = g
    m["antiI"] = np.ascontiguousarray(np.eye(128, dtype=np.float32)[::-1])
    m["lamv"] = f(np.stack([np.asarray(inp[k])[0] for k in ("b_lam_q1", "b_lam_k1", "b_lam_q2", "b_lam_k2")]))
    m["subln"] = f(np.asarray(inp["b_subln"])[0][None, :])
    m["gfinal"] = pc(inp["final_norm"])
    return m


_NC_CACHE = {}


def kernel(**inputs):
    if "nc" not in _NC_CACHE:
        _NC_CACHE["nc"] = build()
    nc = _NC_CACHE["nc"]
    in_maps = [host_inputs(inputs, c) for c in range(NCORES)]
    res = run_bass_kernel_spmd(nc, in_maps, core_ids=list(range(NCORES)))
    out = np.empty((1, NCORES * NT, 1024), np.float32)
    for c in range(NCORES):
        out[0, c * NT:(c + 1) * NT, :] = res.results[c]["outT"].T
    return out
```

```python
import contextlib
import numpy as np
import concourse.bass as bass
import concourse.mybir as mybir

F32 = mybir.dt.float32
BF16 = mybir.dt.bfloat16
U8 = mybir.dt.uint8
ALU = mybir.AluOpType
AF = mybir.ActivationFunctionType
AX = mybir.AxisListType
DTSZ = {F32: 4, BF16: 2, U8: 1}

ENGS = ("pe", "act", "dve", "pool", "sp")
SEM_ROLL = 2048
DMA_K = 6


class Tok:
    __slots__ = ("writers", "readers", "psum")

    def __init__(self, psum=False):
        self.writers = []
        self.readers = []
        self.psum = psum


class Buf:
    __slots__ = ("name", "t", "tok")

    def __init__(self, name, t=None, tok=None):
        self.name = name
        self.t = t
        self.tok = tok if tok is not None else Tok()

    def __getitem__(self, idx):
        return self.t[idx]

    def alias(self, ap, name=None):
        return Buf(name or self.name, ap, self.tok)


class Op:
    __slots__ = ("eng", "fn", "deps", "is_dma", "flag", "seq", "dma_slot", "dma_val", "name", "inc", "cc", "gidx")


class Arena:
    def __init__(self, nc, nbytes):
        self.big = nc.alloc_sbuf_tensor("arena", [128, nbytes], U8)
        self.size = nbytes
        self.pers = 0
        self.cur = 0
        self.in_phase = False
        self.peak = 0

    def start_phase(self):
        self.in_phase = True
        self.cur = self.pers

    def alloc(self, name, shape, dt, persistent=False):
        p = shape[0]
        n = int(np.prod(shape[1:])) * DTSZ[dt]
        n = (n + 63) // 64 * 64
        if persistent:
            assert not self.in_phase or self.cur == self.pers, "persistent alloc inside a phase"
            off = self.pers
            self.pers += n
            self.cur = self.pers
        else:
            off = self.cur
            self.cur += n
        self.peak = max(self.peak, self.cur)
        assert self.cur <= self.size, f"SBUF arena overflow allocating {name}: {self.cur} > {self.size}"
        ap = self.big[0:p, off:off + n if False else off + int(np.prod(shape[1:])) * DTSZ[dt]].bitcast(dt)
        if len(shape) == 3:
            ap = ap.rearrange("p (a b) -> p a b", a=shape[1])
        elif len(shape) == 4:
            ap = ap.rearrange("p (a b c) -> p a b c", a=shape[1], b=shape[2])
        return Buf(name, ap)


class Prog:
    def __init__(self, nc):
        self.nc = nc
        self.ops = {e: [] for e in ENGS}
        self.n_dma = {e: 0 for e in ENGS}
        self.all_ops = []

    def _add(self, eng, fn, reads, writes, pwrites, is_dma, name=None, inc=None, extra_deps=(), cc=False):
        op = Op()
        op.eng, op.fn, op.is_dma, op.flag, op.seq, op.name = eng, fn, is_dma, False, None, name
        op.inc = inc if inc is not None else (16 if is_dma else 1)
        op.cc = cc
        op.gidx = len(self.all_ops)
        deps = list(extra_deps)
        wr_toks = set()
        for r in reads:
            deps.extend(r.tok.writers)
            if r.tok.psum:
                deps.extend(x for x in r.tok.readers if x.eng != eng)
        for w in list(writes) + list(pwrites):
            wr_toks.add(id(w.tok))
        for w in writes:
            deps.extend(w.tok.writers)
            deps.extend(w.tok.readers)
        for w in pwrites:
            deps.extend(w.tok.readers)
            if w.tok.readers:
                deps.extend(w.tok.writers)
        rw_writers = set()
        for x in list(reads) + list(writes):
            for d in x.tok.writers:
                rw_writers.add(id(d))
        out = []
        seen = set()
        for d in deps:
            if id(d) in seen or d is op:
                continue
            seen.add(id(d))
            if (not d.is_dma) and (not is_dma) and d.eng == eng:
                if eng == "pe":
                    continue
                if id(d) not in rw_writers:
                    continue
            out.append(d)
        latest = {}
        for d in out:
            if not d.is_dma:
                if d.eng not in latest or d.gidx > latest[d.eng].gidx:
                    latest[d.eng] = d
        out = [d for d in out if d.is_dma or latest[d.eng] is d]
        op.deps = out
        for r in reads:
            r.tok.readers.append(op)
        for w in writes:
            w.tok.writers = [op]
            w.tok.readers = []
        for w in pwrites:
            if w.tok.readers:
                w.tok.writers = [op]
                w.tok.readers = []
            else:
                w.tok.writers.append(op)
        if is_dma and not cc:
            i = self.n_dma[eng]
            self.n_dma[eng] += 1
            op.dma_slot = i % DMA_K
            op.dma_val = i // DMA_K + 1
        self.ops[eng].append(op)
        self.all_ops.append(op)
        return op

    def op(self, eng, fn, reads=(), writes=(), pwrites=(), name=None):
        return self._add(eng, fn, reads, writes, pwrites, False, name)

    def dma(self, eng, out, in_, reads=(), writes=(), pwrites=(), **kw):
        def fn(e):
            return e.dma_start(out=out, in_=in_, **kw)
        return self._add(eng, fn, reads, writes, pwrites, True)

    def async_op(self, eng, fn, reads=(), writes=(), inc=1):
        return self._add(eng, fn, reads, writes, (), True, inc=inc, cc=True)

    def barrier(self):
        lasts = []
        for e in ENGS:
            for op in reversed(self.ops[e]):
                if not op.is_dma:
                    lasts.append(op)
                    break
            dm = [op for op in self.ops[e] if op.is_dma and not op.cc][-DMA_K:]
            lasts.extend(dm)
            lasts.extend(op for op in self.ops[e] if op.cc)
        for e in ENGS:
            deps = [d for d in lasts if d.is_dma or d.eng != e]
            self._add(e, lambda en: en.nop(), (), (), (), False, "barrier", extra_deps=deps)

    def pid(self, e):
        k = id(e)
        if k not in self._pids:
            self._pids[k] = e.partition_id()
        return self._pids[k]

    def emit(self, final_bufs=()):
        self._pids = {}
        nc = self.nc
        for op in self.all_ops:
            for d in op.deps:
                d.flag = True
        finals = []
        for b in final_bufs:
            for w in b.tok.writers:
                w.flag = True
                finals.append(w)
        nflag = {}
        for e in ENGS:
            n = 0
            for op in self.ops[e]:
                if op.flag and not op.is_dma:
                    op.seq = n
                    n += 1
            nflag[e] = n
        self.nflag = nflag
        with contextlib.ExitStack() as st:
            csem = {}
            for e in ENGS:
                k = (nflag[e] + SEM_ROLL - 1) // SEM_ROLL
                csem[e] = [st.enter_context(nc.semaphore(f"c_{e}_{i}")) for i in range(max(k, 1))]
            dsem = {}
            for e in ENGS:
                if self.n_dma[e]:
                    dsem[e] = [st.enter_context(nc.semaphore(f"d_{e}_{i}")) for i in range(DMA_K)]
            ccsem = {}
            for op in self.all_ops:
                if op.cc:
                    ccsem[id(op)] = st.enter_context(nc.semaphore(f"cc_{len(ccsem)}"))
            cum = {e: [0] * DMA_K for e in ENGS}
            for e in ENGS:
                for op in self.ops[e]:
                    if op.is_dma and not op.cc:
                        cum[e][op.dma_slot] += op.inc
                        op.dma_val = cum[e][op.dma_slot]
            block = st.enter_context(nc.Block())

            def target(d):
                if d.cc:
                    return ccsem[id(d)], 1
                if d.is_dma:
                    return dsem[d.eng][d.dma_slot], d.dma_val
                return csem[d.eng][d.seq // SEM_ROLL], d.seq % SEM_ROLL + 1

            def run(ename, e):
                waited = {}
                for op in self.ops[ename]:
                    tg = [target(d) for d in op.deps]
                    if op.is_dma and not op.cc and op.dma_val - op.inc > 0:
                        tg.append((dsem[ename][op.dma_slot], op.dma_val - op.inc))
                    for s, v in tg:
                        key = id(s)
                        if waited.get(key, 0) >= v:
                            continue
                        waited[key] = v
                        e.wait_ge(s, v)
                    ins = op.fn(e)
                    if op.cc:
                        ins.then_inc(ccsem[id(op)], 1)
                    elif op.is_dma:
                        ins.then_inc(dsem[ename][op.dma_slot], op.inc)
                    elif op.flag:
                        ins.then_inc(csem[ename][op.seq // SEM_ROLL], 1)
                if ename == "sp":
                    for d in finals:
                        s, v = target(d)
                        e.wait_ge(s, v)

            @block.tensor
            def _(e):
                run("pe", e)

            @block.scalar
            def _(e):
                run("act", e)

            @block.vector
            def _(e):
                run("dve", e)

            @block.gpsimd
            def _(e):
                run("pool", e)

            @block.sync
            def _(e):
                run("sp", e)


NT = 2048
SW = 512
NS = NT // SW
CH = 64
NCH = NT // CH
CPS = SW // CH
EPS = 1e-6


class Ctx:
    pass


class KB:
    def __init__(self, nc, sbuf_bytes=206 * 1024):
        self.nc = nc
        self.P = Prog(nc)
        self.A = Arena(nc, sbuf_bytes)
        self.pb = [Buf(f"pb{i}", nc.alloc_psum_tensor(f"pb{i}", [128, 512], F32), Tok(psum=True)) for i in range(6)]
        self.pb2 = Buf("pb2", nc.alloc_psum_tensor("pbig", [128, 1024], F32), Tok(psum=True))

    def sb(self, name, shape, dt, pers=False):
        return self.A.alloc(name, shape, dt, persistent=pers)

    def ps(self, bank, shape, dt=F32):
        base = self.pb2 if bank == 6 else self.pb[bank]
        p = shape[0]
        n = 1
        for x in shape[1:]:
            n *= x
        nf32 = n * DTSZ[dt] // 4
        ap = base.t[0:p, 0:nf32]
        if dt != F32:
            ap = ap.bitcast(dt)
        if len(shape) == 3:
            ap = ap.rearrange("p (a b) -> p a b", a=shape[1])
        return base.alias(ap)

    def phase(self):
        self.P.barrier()
        self.A.start_phase()


def rmsnorm_fm(P, C, xs, g, out_ap, out_buf, sq, ps, rstd, width=SW):
    P.op("act", lambda e: e.activation(out=sq[:], in_=xs[:], func=AF.Square), reads=[xs], writes=[sq])
    for c in range(8):
        P.op("pe", lambda e, c=c: e.matmul(ps[:], lhsT=C.ones[:], rhs=sq[:, c, :], start=(c == 0), stop=(c == 7)),
             reads=[C.ones, sq], writes=[ps])
    P.op("act", lambda e: e.activation(out=rstd[:], in_=ps[:], func=AF.Sqrt, scale=1.0 / 1024, bias=C.eps[:]),
         reads=[ps, C.eps], writes=[rstd])
    P.op("dve", lambda e: e.reciprocal(out=rstd[:], in_=rstd[:]), reads=[rstd], writes=[rstd])
    for c in range(8):
        P.op("dve", lambda e, c=c: e.scalar_tensor_tensor(out=out_ap[:, c, :], in0=xs[:, c, :], scalar=g[:, c:c + 1],
                                                          in1=rstd[:], op0=ALU.mult, op1=ALU.mult),
             reads=[xs, g, rstd], pwrites=[out_buf])


def hgrn_consts(K, d):
    P = K.P
    C = Ctx()
    sb = lambda n, sh, dt: K.sb(n, sh, dt, pers=True)
    C.ones = sb("ones", [128, 128], BF16)
    P.op("pool", lambda e: e.memset(C.ones[:], 1.0), writes=[C.ones])
    C.eps = sb("eps", [128, 1], F32)
    P.op("pool", lambda e: e.memset(C.eps[:], EPS), writes=[C.eps])
    C.ident = sb("ident", [128, 128], BF16)
    P.dma("pool", C.ident[:], d["ident"], writes=[C.ident])
    C.resetm = sb("resetm", [128, SW], F32)
    P.dma("sp", C.resetm[:], d["resetm"], writes=[C.resetm])
    C.maskc = sb("maskc", [64, 8, 64], F32)
    P.dma("sp", C.maskc[:], d["maskc"], writes=[C.maskc])
    C.gmix = sb("gmix", [128, 8], F32)
    P.dma("sp", C.gmix[:], d["gmix0"], writes=[C.gmix])
    C.gn = sb("gn", [128, 8], F32)
    P.dma("sp", C.gn[:], d["gnorm"], writes=[C.gn])
    lbl = sb("lbl", [128, 2, 8], F32)
    P.dma("sp", lbl[:], d["lbl"], writes=[lbl])
    C.lb = sb("lb", [128, 8], F32)
    C.oml = sb("oml", [128, 8], F32)
    P.op("dve", lambda e: e.tensor_sub(out=C.lb[:], in0=lbl[:, 0, :], in1=lbl[:, 1, :]), reads=[lbl], writes=[C.lb])
    P.op("act", lambda e: e.activation(out=C.oml[:], in_=C.lb[:], func=AF.Sigmoid, scale=-1.0), reads=[C.lb], writes=[C.oml])
    P.op("act", lambda e: e.activation(out=C.lb[:], in_=C.lb[:], func=AF.Sigmoid), reads=[C.lb], writes=[C.lb])
    return C


def hgrn_alloc_T(K):
    T = Ctx()
    sb = lambda n, sh, dt: K.sb(n, sh, dt, pers=True)
    T.bl = sb("bl", [128, 8, NCH], F32)
    T.bmid = sb("bmid", [128, 8, NCH], F32)
    T.e1 = sb("e1", [128, 8, NCH], F32)
    T.e2 = sb("e2", [128, 8, NCH], F32)
    T.em = sb("em", [128, 8, NCH], F32)
    T.D = sb("D", [128, 8], F32)
    return T


def hgrn_P(K, C, d, scr, T):
    P, sb, ps = K.P, K.sb, K.ps
    xT3 = d["xT"].rearrange("(c p) t -> p c t", p=128)
    hT = sb("hT", [128, 8, NT], BF16)
    hTs = [Buf(f"hT{s}", hT.t[:, :, s * SW:(s + 1) * SW]) for s in range(NS)]
    xst = [sb("xst", [128, 8, SW], F32) for _ in range(2)]
    sq = sb("sq", [128, 8, SW], BF16)
    rstd = sb("rstd", [128, SW], F32)
    ps_n = ps(0, [128, SW])
    for s in range(NS):
        xs = xst[s % 2]
        P.dma("sp", xs[:], xT3[:, :, s * SW:(s + 1) * SW], writes=[xs])
        rmsnorm_fm(P, C, xs, C.gmix, hTs[s], hTs[s], sq, ps_n, rstd)

    wi = sb("wi", [128, 8, 1024], BF16)
    for h in range(8):
        P.dma("pool", wi[:, :, h * 128:(h + 1) * 128], d["w_in_r"][16 + h], pwrites=[wi])
    ps_v = [ps(1, [64, 512]), ps(2, [64, 512])]
    vst = [sb("vst", [64, CPS, 1024], BF16) for _ in range(2)]
    v3 = scr["v"].rearrange("(c s) j -> s c j", s=64)
    for s in range(NS):
        vs = vst[s % 2]
        for c in range(CPS):
            cg = s * CPS + c
            for hf in range(2):
                pv = ps_v[hf]
                for m in range(8):
                    P.op("pe", lambda e, m=m, cg=cg, hf=hf, pv=pv: e.matmul(
                        pv[:], lhsT=hT[:, m, cg * CH:(cg + 1) * CH], rhs=wi[:, m, hf * 512:(hf + 1) * 512],
                        start=(m == 0), stop=(m == 7)), reads=[hTs[s], wi], writes=[pv])
                if hf == 0:
                    P.op("act", lambda e, c=c, hf=hf, pv=pv, vs=vs: e.copy(out=vs[:, c, hf * 512:(hf + 1) * 512], in_=pv[:]),
                         reads=[pv], pwrites=[vs])
                else:
                    P.op("dve", lambda e, c=c, hf=hf, pv=pv, vs=vs: e.tensor_copy(out=vs[:, c, hf * 512:(hf + 1) * 512], in_=pv[:]),
                         reads=[pv], pwrites=[vs])
        P.dma("sp", v3[:, s * CPS:(s + 1) * CPS, :], vs[:], reads=[vs], pwrites=[scr["v_tok"]])

    wq = [sb("wq", [128, 8, 128], BF16) for _ in range(2)]
    wf = [sb("wf", [128, 8, 128], BF16) for _ in range(2)]
    wg = [sb("wg", [128, 8, 128], BF16) for _ in range(2)]
    ps_f = ps(3, [128, SW])
    ps_q = ps(4, [128, SW])
    ps_g = ps(5, [128, SW])
    ps_t = ps(0, [64, CPS, 128], BF16)
    sig = sb("sig", [128, SW], F32)
    logf = sb("logf", [128, SW], F32)
    nsig = sb("nsig", [128, SW], F32)
    b3 = sb("b3", [128, CPS, CH], F32)
    bm = sb("bm", [128, CPS, CH], F32)
    Ep = sb("Ep", [128, SW], F32)
    Em = sb("Em", [128, SW], F32)
    sqf = sb("sqf", [128, SW], F32)
    qst = [sb("qst", [128, SW], BF16) for _ in range(2)]
    kst = [sb("kst", [128, SW], BF16) for _ in range(2)]
    gst = [sb("gst", [128, SW], BF16) for _ in range(2)]
    ktst = [sb("ktst", [64, CPS, 128], BF16) for _ in range(2)]
    b_flat = b3.t.rearrange("p c t -> p (c t)")
    bm_flat = bm.t.rearrange("p c t -> p (c t)")
    kt3 = scr["kt"].rearrange("(c s) j -> s c j", s=64)
    it = 0
    for h in range(8):
        wqh, wfh, wgh = wq[h % 2], wf[h % 2], wg[h % 2]
        P.dma("pool", wfh[:], d["w_in_r"][8 + h], writes=[wfh])
        P.dma("pool", wqh[:], d["w_in_r"][0 + h], writes=[wqh])
        P.dma("pool", wgh[:], d["w_in_r"][24 + h], writes=[wgh])
        for s in range(NS):
            hs = hTs[s]
            sl = slice(s * SW, (s + 1) * SW)
            for (pp, ww) in ((ps_f, wfh), (ps_q, wqh), (ps_g, wgh)):
                for m in range(8):
                    P.op("pe", lambda e, m=m, pp=pp, ww=ww, sl=sl: e.matmul(
                        pp[:], lhsT=ww[:, m, :], rhs=hT[:, m, sl], start=(m == 0), stop=(m == 7)),
                        reads=[ww, hs], writes=[pp])
            q_o, k_o, g_o, kt_o = qst[it % 2], kst[it % 2], gst[it % 2], ktst[it % 2]
            it += 1
            P.op("act", lambda e: e.activation(out=sig[:], in_=ps_f[:], func=AF.Sigmoid), reads=[ps_f], writes=[sig])
            P.op("act", lambda e, h=h: e.activation(out=logf[:], in_=sig[:], func=AF.Ln, scale=C.oml[:, h:h + 1],
                                                    bias=C.lb[:, h:h + 1]), reads=[sig, C.oml, C.lb], writes=[logf])
            P.op("dve", lambda e: e.tensor_scalar(out=nsig[:], in0=sig[:], scalar1=-1.0, scalar2=1.0, op0=ALU.mult,
                                                  op1=ALU.add), reads=[sig], writes=[nsig])
            P.op("dve", lambda e: e.tensor_tensor_scan(out=b_flat, data0=C.resetm[:], data1=logf[:], initial=0.0,
                                                       op0=ALU.mult, op1=ALU.add), reads=[C.resetm, logf], writes=[b3])
            P.op("dve", lambda e, h=h, s=s: e.tensor_copy(out=T.bl[:, h, s * CPS:(s + 1) * CPS], in_=b3[:, :, CH - 1]),
                 reads=[b3], pwrites=[T.bl])
            P.op("dve", lambda e, h=h, s=s: e.tensor_copy(out=T.bmid[:, h, s * CPS:(s + 1) * CPS], in_=b3[:, :, CH // 2 - 1]),
                 reads=[b3], pwrites=[T.bmid])
            P.op("dve", lambda e: e.tensor_tensor(out=bm[:], in0=b3[:], in1=b3[:, :, CH // 2 - 1:CH // 2].to_broadcast([128, CPS, CH]),
                                                  op=ALU.subtract), reads=[b3], writes=[bm])
            P.op("act", lambda e: e.activation(out=Ep[:], in_=bm_flat, func=AF.Exp), reads=[bm], writes=[Ep])
            P.op("act", lambda e: e.activation(out=Em[:], in_=bm_flat, func=AF.Exp, scale=-1.0), reads=[bm], writes=[Em])
            P.op("act", lambda e: e.activation(out=sqf[:], in_=ps_q[:], func=AF.Silu), reads=[ps_q], writes=[sqf])
            P.op("act", lambda e, g_o=g_o: e.activation(out=g_o[:], in_=ps_g[:], func=AF.Silu), reads=[ps_g], writes=[g_o])
            P.op("dve", lambda e, q_o=q_o: e.tensor_tensor(out=q_o[:], in0=sqf[:], in1=Ep[:], op=ALU.mult),
                 reads=[sqf, Ep], writes=[q_o])
            P.op("dve", lambda e, k_o=k_o, h=h: e.scalar_tensor_tensor(out=k_o[:], in0=nsig[:], scalar=C.oml[:, h:h + 1],
                                                                       in1=Em[:], op0=ALU.mult, op1=ALU.mult),
                 reads=[nsig, C.oml, Em], writes=[k_o])
            hsl = slice(h * 128, (h + 1) * 128)
            P.dma("sp", scr["qT"][hsl, sl], q_o[:], reads=[q_o], pwrites=[scr["qT_tok"]])
            P.dma("sp", scr["kT"][hsl, sl], k_o[:], reads=[k_o], pwrites=[scr["kT_tok"]])
            P.dma("sp", scr["gT"][hsl, sl], g_o[:], reads=[g_o], pwrites=[scr["gT_tok"]])
            for c in range(CPS):
                P.op("pe", lambda e, c=c, k_o=k_o: e.transpose(out=ps_t[:, c, :], in_=k_o[:, c * CH:(c + 1) * CH],
                                                              identity=C.ident[:]), reads=[k_o, C.ident], writes=[ps_t])
            P.op("act", lambda e, kt_o=kt_o: e.copy(out=kt_o[:], in_=ps_t[:]), reads=[ps_t], writes=[kt_o])
            P.dma("sp", kt3[:, s * CPS:(s + 1) * CPS, hsl], kt_o[:], reads=[kt_o], pwrites=[scr["kt_tok"]])
    P.op("act", lambda e: e.activation(out=T.e1[:], in_=T.bl[:], func=AF.Exp), reads=[T.bl], writes=[T.e1])
    P.op("act", lambda e: e.activation(out=T.em[:], in_=T.bmid[:], func=AF.Exp), reads=[T.bmid], writes=[T.em])
    P.op("dve", lambda e: e.tensor_sub(out=T.e2[:], in0=T.bl[:], in1=T.bmid[:]), reads=[T.bl, T.bmid], writes=[T.e2])
    P.op("act", lambda e: e.activation(out=T.e2[:], in_=T.e2[:], func=AF.Exp), reads=[T.e2], writes=[T.e2])
    P.op("dve", lambda e: e.reduce_sum(out=T.D[:], in_=T.bl[:], axis=AX.X), reads=[T.bl], writes=[T.D])
    P.op("act", lambda e: e.activation(out=T.D[:], in_=T.D[:], func=AF.Exp), reads=[T.D], writes=[T.D])


def hgrn_R(K, C, d, scr, T, full, S):
    P, sb, ps = K.P, K.sb, K.ps
    q3 = scr["qT"].rearrange("(h p) t -> p h t", p=128)
    k3 = scr["kT"].rearrange("(h p) t -> p h t", p=128)
    g3 = scr["gT"].rearrange("(h p) t -> p h t", p=128)
    v3 = scr["v"].rearrange("(c s) j -> s c j", s=64)
    kt3 = scr["kt"].rearrange("(c s) j -> s c j", s=64)
    v_sb = [sb("v_sb", [64, CPS, 1024], BF16) for _ in range(2)]
    kt_sb = [sb("kt_sb", [64, CPS, 1024], BF16) for _ in range(1)]
    ps_dS = ps(6, [128, 8, 128])
    tmp = sb("tmpS", [128, 8, 128], F32)
    if full:
        q_sb = [sb("q_sb", [128, 8, SW], BF16) for _ in range(2)]
        k_sb = [sb("k_sb", [128, 8, SW], BF16) for _ in range(2)]
        g_sb = [sb("g_sb", [128, 8, SW], BF16) for _ in range(1)]
        ps_sc = [ps(0, [64, 8, 64]), ps(1, [64, 8, 64])]
        ps_o = [ps(2, [128, 8, 64]), ps(3, [128, 8, 64])]
        sc_sb = [sb("sc_sb", [64, 8, 64], BF16) for _ in range(2)]
        Sb = [sb("Sb", [128, 8, 128], BF16) for _ in range(2)]
        oT = sb("oT", [128, 8, SW], F32)
        osq = sb("osq", [128, 8, SW], BF16)
        ogT = sb("ogT", [128, 8, SW], BF16)
        t1 = sb("t1", [128, SW], F32)
        rstd = sb("rstd", [128, SW], F32)
        wout = sb("wout", [128, 8, 1024], BF16)
        P.dma("pool", wout[:], d["w_out_r"], writes=[wout])
        ps_y = [ps(4, [128, SW]), ps(5, [128, SW])]
        ps_n = ps(4, [128, SW])
        xT3 = d["xT"].rearrange("(c p) t -> p c t", p=128)
        x1T3 = d["x1T"].rearrange("(c p) t -> p c t", p=128)
        xst = [sb("xst", [128, 8, SW], F32) for _ in range(1)]
    for s in range(NS):
        sl = slice(s * SW, (s + 1) * SW)
        vs, kts = v_sb[s % 2], kt_sb[0]
        P.dma("sp", vs[:], v3[:, s * CPS:(s + 1) * CPS, :], reads=[scr["v_tok"]], writes=[vs])
        P.dma("sp", kts[:], kt3[:, s * CPS:(s + 1) * CPS, :], reads=[scr["kt_tok"]], writes=[kts])
        if full:
            qs, ks, gs = q_sb[s % 2], k_sb[s % 2], g_sb[0]
            P.dma("sp", qs[:], q3[:, :, sl], reads=[scr["qT_tok"]], writes=[qs])
            P.dma("sp", ks[:], k3[:, :, sl], reads=[scr["kT_tok"]], writes=[ks])
            P.dma("sp", gs[:], g3[:, :, sl], reads=[scr["gT_tok"]], writes=[gs])
            xs = xst[0]
            P.dma("sp", xs[:], xT3[:, :, sl], writes=[xs])
        for c in range(CPS):
            cg = s * CPS + c
            csl = slice(c * CH, (c + 1) * CH)
            if full:
                psc, pso, scb, Sbb = ps_sc[cg % 2], ps_o[cg % 2], sc_sb[cg % 2], Sb[cg % 2]
                for h in range(8):
                    P.op("pe", lambda e, h=h, psc=psc, ks=ks, qs=qs, csl=csl: e.matmul(
                        psc[:, h, :], lhsT=ks[:, h, csl], rhs=qs[:, h, csl], start=True, stop=True),
                        reads=[ks, qs], writes=[psc])
            for h in range(8):
                hsl = slice(h * 128, (h + 1) * 128)
                P.op("pe", lambda e, h=h, kts=kts, vs=vs, c=c, hsl=hsl: e.matmul(
                    ps_dS[:, h, :], lhsT=kts[:, c, hsl], rhs=vs[:, c, hsl], start=True, stop=True),
                    reads=[kts, vs], writes=[ps_dS])
            if full:
                P.op("dve", lambda e, psc=psc, scb=scb: e.tensor_tensor(out=scb[:], in0=psc[:], in1=C.maskc[:], op=ALU.mult),
                     reads=[psc, C.maskc], writes=[scb])
                P.op("dve", lambda e, Sbb=Sbb, cg=cg: e.tensor_tensor(
                    out=Sbb[:], in0=S[:], in1=T.em[:, :, cg:cg + 1].to_broadcast([128, 8, 128]), op=ALU.mult),
                    reads=[S, T.em], writes=[Sbb])
                for h in range(8):
                    hsl = slice(h * 128, (h + 1) * 128)
                    P.op("pe", lambda e, h=h, pso=pso, Sbb=Sbb, qs=qs, csl=csl: e.matmul(
                        pso[:, h, :], lhsT=Sbb[:, h, :], rhs=qs[:, h, csl], start=True, stop=False),
                        reads=[Sbb, qs], writes=[pso])
                    P.op("pe", lambda e, h=h, pso=pso, vs=vs, scb=scb, c=c, hsl=hsl: e.matmul(
                        pso[:, h, :], lhsT=vs[:, c, hsl], rhs=scb[:, h, :], start=False, stop=True),
                        reads=[vs, scb], writes=[pso])
                P.op("act", lambda e, pso=pso, csl=csl: e.copy(out=oT[:, :, csl], in_=pso[:]), reads=[pso], pwrites=[oT])
            P.op("dve", lambda e, cg=cg: e.tensor_tensor(
                out=S[:], in0=S[:], in1=T.e1[:, :, cg:cg + 1].to_broadcast([128, 8, 128]), op=ALU.mult),
                reads=[S, T.e1], writes=[S])
            for h in range(8):
                P.op("act", lambda e, h=h, cg=cg: e.activation(out=tmp[:, h, :], in_=ps_dS[:, h, :], func=AF.Copy,
                                                               scale=T.e2[:, h, cg:cg + 1]),
                     reads=[ps_dS, T.e2], pwrites=[tmp])
            P.op("dve", lambda e: e.tensor_tensor(out=S[:], in0=S[:], in1=tmp[:], op=ALU.add), reads=[S, tmp], writes=[S])
        if full:
            P.op("act", lambda e: e.activation(out=osq[:], in_=oT[:], func=AF.Square), reads=[oT], writes=[osq])
            for h in range(8):
                P.op("pe", lambda e, h=h: e.matmul(ps_n[:], lhsT=C.ones[:], rhs=osq[:, h, :], start=(h == 0), stop=(h == 7)),
                     reads=[C.ones, osq], writes=[ps_n])
            P.op("act", lambda e: e.activation(out=rstd[:], in_=ps_n[:], func=AF.Sqrt, scale=1.0 / 1024, bias=C.eps[:]),
                 reads=[ps_n, C.eps], writes=[rstd])
            P.op("dve", lambda e: e.reciprocal(out=rstd[:], in_=rstd[:]), reads=[rstd], writes=[rstd])
            for h in range(8):
                P.op("dve", lambda e, h=h: e.scalar_tensor_tensor(out=t1[:], in0=oT[:, h, :], scalar=C.gn[:, h:h + 1],
                                                                  in1=rstd[:], op0=ALU.mult, op1=ALU.mult),
                     reads=[oT, C.gn, rstd], writes=[t1])
                P.op("dve", lambda e, h=h, gs=gs: e.tensor_tensor(out=ogT[:, h, :], in0=t1[:], in1=gs[:, h, :], op=ALU.mult),
                     reads=[t1, gs], pwrites=[ogT])
            for j in range(8):
                py = ps_y[j % 2]
                for h in range(8):
                    P.op("pe", lambda e, h=h, j=j, py=py: e.matmul(py[:], lhsT=wout[:, h, j * 128:(j + 1) * 128],
                                                                  rhs=ogT[:, h, :], start=(h == 0), stop=(h == 7)),
                         reads=[wout, ogT], writes=[py])
                P.op("dve", lambda e, j=j, py=py, xs=xs: e.tensor_tensor(out=xs[:, j, :], in0=py[:], in1=xs[:, j, :], op=ALU.add),
                     reads=[py, xs], pwrites=[xs])
            P.dma("sp", x1T3[:, :, sl], xs[:], reads=[xs], pwrites=[d["x1T_tok"]])
            if s == NS - 1 and "halo_out" in d:
                P.dma("sp", d["halo_out"].rearrange("p (c t) -> p c t", c=8), xs[:, :, SW - 2:SW], reads=[xs],
                      writes=[d["halo_out_tok"]])


NFF = 22


def ffn_layer(K, C, d, li, xin, xin_tok, xout, xout_tok, halo_all, halo_tok, final_norm=None):
    P, sb, ps = K.P, K.sb, K.ps
    K.phase()
    xT3 = xin.rearrange("(c p) t -> p c t", p=128)
    gf = sb("gf", [128, 8], F32)
    P.dma("sp", gf[:], d[f"gffn{li}"], writes=[gf])
    convp = sb("convp", [128, 2 * NFF, 4], F32)
    P.dma("sp", convp[:], d[f"convp{li}"], writes=[convp])
    nf = sb("nf", [128, 1], F32)
    P.dma("sp", nf[:], d["notfirst"], writes=[nf])
    aT = sb("aT", [128, NFF, NT], BF16)
    aTs = [Buf(f"aT{s}", aT.t[:, :, s * SW:(s + 1) * SW]) for s in range(NS)]
    mark = K.A.cur
    hT = sb("h2T", [128, 8, NT], BF16)
    hTs = [Buf(f"h2T{s}", hT.t[:, :, s * SW:(s + 1) * SW]) for s in range(NS)]
    xst = [sb("xst", [128, 8, SW], F32) for _ in range(1)]
    sq = sb("sq", [128, 8, SW], BF16)
    rstd = sb("rstd", [128, SW], F32)
    ps_n = ps(0, [128, SW])
    for s in range(NS):
        xs = xst[0]
        P.dma("sp", xs[:], xT3[:, :, s * SW:(s + 1) * SW], reads=[xin_tok], writes=[xs])
        rmsnorm_fm(P, C, xs, gf, hTs[s], hTs[s], sq, ps_n, rstd)
    xh = sb("xh", [128, 8, 2], F32)

    def dyn(e):
        pid = P.pid(e)
        prev = (pid + 7) % 8
        return e.dma_start(out=xh[:], in_=halo_all[bass.ds(prev * 128, 128), :].rearrange("p (c t) -> p c t", c=8))
    P._add("sp", dyn, [halo_tok], [xh], (), True)
    sqh = sb("sqh", [128, 8, 2], BF16)
    rsh = sb("rsh", [128, 2], F32)
    hh = sb("hh", [128, 8, 2], BF16)
    ps_h = ps(1, [128, 2])
    P.op("act", lambda e: e.activation(out=sqh[:], in_=xh[:], func=AF.Square), reads=[xh], writes=[sqh])
    for c in range(8):
        P.op("pe", lambda e, c=c: e.matmul(ps_h[:], lhsT=C.ones[:], rhs=sqh[:, c, :], start=(c == 0), stop=(c == 7)),
             reads=[C.ones, sqh], writes=[ps_h])
    P.op("act", lambda e: e.activation(out=rsh[:], in_=ps_h[:], func=AF.Sqrt, scale=1.0 / 1024, bias=C.eps[:]),
         reads=[ps_h, C.eps], writes=[rsh])
    P.op("dve", lambda e: e.reciprocal(out=rsh[:], in_=rsh[:]), reads=[rsh], writes=[rsh])
    P.op("dve", lambda e: e.tensor_scalar(out=rsh[:], in0=rsh[:], scalar1=nf[:, 0:1], scalar2=None, op0=ALU.mult),
         reads=[rsh, nf], writes=[rsh])
    for c in range(8):
        P.op("dve", lambda e, c=c: e.scalar_tensor_tensor(out=hh[:, c, :], in0=xh[:, c, :], scalar=gf[:, c:c + 1],
                                                          in1=rsh[:], op0=ALU.mult, op1=ALU.mult),
             reads=[xh, gf, rsh], pwrites=[hh])

    wu = [[sb("wu", [128, 8, 128], BF16) for _ in range(2)] for _ in range(2)]
    ug = [sb("ug", [128, SW + 2], F32) for _ in range(2)]
    uv = [sb("uv", [128, SW + 2], F32) for _ in range(2)]
    ag = [sb("ag", [128, SW], F32) for _ in range(2)]
    av = [sb("av", [128, SW], F32) for _ in range(2)]
    sg = [sb("sg", [128, SW], F32) for _ in range(2)]
    ps_g = [ps(2, [128, SW]), ps(3, [128, SW])]
    ps_v = [ps(4, [128, SW]), ps(5, [128, SW])]
    ps_hh = ps(1, [128, 2, 2])
    it = 0
    for c in range(NFF):
        wg_, wv_ = wu[c % 2]
        P.dma("pool", wg_[:], d[f"w_up_r{li}"][c], writes=[wg_])
        P.dma("pool", wv_[:], d[f"w_up_r{li}"][c + NFF], writes=[wv_])
        for s in range(NS):
            sl = slice(s * SW, (s + 1) * SW)
            cur, prv = it % 2, (it + 1) % 2
            it += 1
            pg, pv = ps_g[cur], ps_v[cur]
            ugc, uvc, agc, avc, sgc = ug[cur], uv[cur], ag[cur], av[cur], sg[cur]
            for (pp, ww) in ((pg, wg_), (pv, wv_)):
                for m in range(8):
                    P.op("pe", lambda e, m=m, pp=pp, ww=ww, sl=sl: e.matmul(
                        pp[:], lhsT=ww[:, m, :], rhs=hT[:, m, sl], start=(m == 0), stop=(m == 7)),
                        reads=[ww, hTs[s]], writes=[pp])
            if s == 0:
                for gi, ww in ((0, wg_), (1, wv_)):
                    for m in range(8):
                        P.op("pe", lambda e, m=m, gi=gi, ww=ww: e.matmul(
                            ps_hh[:, gi, :], lhsT=ww[:, m, :], rhs=hh[:, m, :], start=(m == 0), stop=(m == 7)),
                            reads=[ww, hh], writes=[ps_hh])
                P.op("dve", lambda e, ugc=ugc: e.tensor_copy(out=ugc[:, 0:2], in_=ps_hh[:, 0, :]), reads=[ps_hh], pwrites=[ugc])
                P.op("dve", lambda e, uvc=uvc: e.tensor_copy(out=uvc[:, 0:2], in_=ps_hh[:, 1, :]), reads=[ps_hh], pwrites=[uvc])
            else:
                P.op("dve", lambda e, ugc=ugc, p_=ug[prv]: e.tensor_copy(out=ugc[:, 0:2], in_=p_[:, SW:SW + 2]),
                     reads=[ug[prv]], pwrites=[ugc])
                P.op("pool", lambda e, uvc=uvc, p_=uv[prv]: e.tensor_copy(out=uvc[:, 0:2], in_=p_[:, SW:SW + 2]),
                     reads=[uv[prv]], pwrites=[uvc])
            cg, cv = c, c + NFF
            P.op("act", lambda e, ugc=ugc, pg=pg: e.copy(out=ugc[:, 2:SW + 2], in_=pg[:]), reads=[pg], pwrites=[ugc])
            P.op("act", lambda e, agc=agc, pg=pg, cg=cg: e.activation(out=agc[:], in_=pg[:], func=AF.Identity,
                                                                      scale=convp[:, cg, 2:3], bias=convp[:, cg, 3:4]),
                 reads=[pg, convp], writes=[agc])
            P.op("act", lambda e, uvc=uvc, pv=pv: e.copy(out=uvc[:, 2:SW + 2], in_=pv[:]), reads=[pv], pwrites=[uvc])
            P.op("act", lambda e, avc=avc, pv=pv, cv=cv: e.activation(out=avc[:], in_=pv[:], func=AF.Identity,
                                                                      scale=convp[:, cv, 2:3], bias=convp[:, cv, 3:4]),
                 reads=[pv, convp], writes=[avc])
            P.op("dve", lambda e, agc=agc, ugc=ugc, cg=cg: e.scalar_tensor_tensor(
                out=agc[:], in0=ugc[:, 1:SW + 1], scalar=convp[:, cg, 1:2], in1=agc[:], op0=ALU.mult, op1=ALU.add),
                reads=[ugc, convp, agc], writes=[agc])
            P.op("dve", lambda e, agc=agc, ugc=ugc, cg=cg: e.scalar_tensor_tensor(
                out=agc[:], in0=ugc[:, 0:SW], scalar=convp[:, cg, 0:1], in1=agc[:], op0=ALU.mult, op1=ALU.add),
                reads=[ugc, convp, agc], writes=[agc])
            P.op("dve", lambda e, avc=avc, uvc=uvc, cv=cv: e.scalar_tensor_tensor(
                out=avc[:], in0=uvc[:, 1:SW + 1], scalar=convp[:, cv, 1:2], in1=avc[:], op0=ALU.mult, op1=ALU.add),
                reads=[uvc, convp, avc], writes=[avc])
            P.op("dve", lambda e, avc=avc, uvc=uvc, cv=cv: e.scalar_tensor_tensor(
                out=avc[:], in0=uvc[:, 0:SW], scalar=convp[:, cv, 0:1], in1=avc[:], op0=ALU.mult, op1=ALU.add),
                reads=[uvc, convp, avc], writes=[avc])
            P.op("act", lambda e, sgc=sgc, agc=agc: e.activation(out=sgc[:], in_=agc[:], func=AF.Silu), reads=[agc], writes=[sgc])
            P.op("dve", lambda e, sgc=sgc, avc=avc, c=c, sl=sl: e.tensor_tensor(out=aT[:, c, sl], in0=sgc[:], in1=avc[:], op=ALU.mult),
                 reads=[sgc, avc], pwrites=[aTs[s]])

    P.barrier()
    K.A.cur = mark
    wd = [sb("wd", [128, NFF, 128], BF16) for _ in range(2)]
    xj = [sb("xj", [128, SW], F32) for _ in range(3)]
    ps_y = [ps(0, [128, SW]), ps(1, [128, SW])]
    xin3 = xin.rearrange("(c p) t -> p c t", p=128)
    xout3 = xout.rearrange("(c p) t -> p c t", p=128)
    it = 0
    for j in range(8):
        wdj = wd[j % 2]
        P.dma("pool", wdj[:], d[f"w_down_r{li}"][j], writes=[wdj])
        for s in range(NS):
            sl = slice(s * SW, (s + 1) * SW)
            py = ps_y[it % 2]
            xs = xj[it % 3]
            it += 1
            P.dma("sp", xs[:], xin3[:, j, sl], reads=[xin_tok], writes=[xs])
            for c in range(NFF):
                P.op("pe", lambda e, c=c, py=py, wdj=wdj, sl=sl: e.matmul(
                    py[:], lhsT=wdj[:, c, :], rhs=aT[:, c, sl], start=(c == 0), stop=(c == NFF - 1)),
                    reads=[wdj, aTs[s]], writes=[py])
            P.op("dve", lambda e, py=py, xs=xs: e.tensor_tensor(out=xs[:], in0=py[:], in1=xs[:], op=ALU.add),
                 reads=[py, xs], writes=[xs])
            P.dma("sp", xout3[:, j, sl], xs[:], reads=[xs], pwrites=[xout_tok])


def final_norm(K, C, d, xin, xin_tok, out, out_tok):
    P, sb, ps = K.P, K.sb, K.ps
    K.phase()
    gfin = sb("gfin", [128, 8], F32)
    P.dma("sp", gfin[:], d["gfinal"], writes=[gfin])
    xT3 = xin.rearrange("(c p) t -> p c t", p=128)
    o3 = out.rearrange("(c p) t -> p c t", p=128)
    xst = [sb("xst", [128, 8, SW], F32) for _ in range(2)]
    ost = [sb("ost", [128, 8, SW], F32) for _ in range(2)]
    sq = sb("sq", [128, 8, SW], BF16)
    rstd = sb("rstd", [128, SW], F32)
    ps_n = ps(0, [128, SW])
    for s in range(NS):
        xs, os_ = xst[s % 2], ost[s % 2]
        P.dma("sp", xs[:], xT3[:, :, s * SW:(s + 1) * SW], reads=[xin_tok], writes=[xs])
        rmsnorm_fm(P, C, xs, gfin, os_, os_, sq, ps_n, rstd)
        P.dma("sp", o3[:, :, s * SW:(s + 1) * SW], os_[:], reads=[os_], pwrites=[out_tok])


import math

T_ALL = 16384
NB = T_ALL // 128
NQS = T_ALL // SW
LAM_INIT = 0.8 - 0.6 * math.exp(-0.3 * 1)
NEG = -30000.0
GLEN = 1151


def kvq_proj(K, C, d, xin, xin_tok, qkv_in, qkv_tok):
    P, sb, ps = K.P, K.sb, K.ps
    K.phase()
    xT3 = xin.rearrange("(c p) t -> p c t", p=128)
    gkv = sb("gkv", [128, 8], F32)
    gq = sb("gq", [128, 8], F32)
    P.dma("sp", gkv[:], d["gkv"], writes=[gkv])
    P.dma("sp", gq[:], d["gmix1"], writes=[gq])
    hk = sb("hk", [128, 8, NT], BF16)
    hq = sb("hq", [128, 8, NT], BF16)
    hks = [Buf(f"hk{s}", hk.t[:, :, s * SW:(s + 1) * SW]) for s in range(NS)]
    hqs = [Buf(f"hq{s}", hq.t[:, :, s * SW:(s + 1) * SW]) for s in range(NS)]
    xst = sb("xst", [128, 8, SW], F32)
    sq = sb("sq", [128, 8, SW], BF16)
    rstd = sb("rstd", [128, SW], F32)
    ps_n = ps(0, [128, SW])
    for s in range(NS):
        P.dma("sp", xst[:], xT3[:, :, s * SW:(s + 1) * SW], reads=[xin_tok], writes=[xst])
        rmsnorm_fm(P, C, xst, gkv, hks[s], hks[s], sq, ps_n, rstd)
        for c in range(8):
            P.op("dve", lambda e, c=c, s=s: e.scalar_tensor_tensor(out=hqs[s][:, c, :], in0=xst[:, c, :], scalar=gq[:, c:c + 1],
                                                                    in1=rstd[:], op0=ALU.mult, op1=ALU.mult),
                 reads=[xst, gq, rstd], pwrites=[hqs[s]])
    wk = [sb("wk", [128, 8, 128], BF16) for _ in range(2)]
    wq = [sb("wq", [128, 8, 128], BF16) for _ in range(2)]
    kst = [sb("kst", [128, NT], BF16) for _ in range(2)]
    qst = [sb("qst", [128, NT], BF16) for _ in range(2)]
    psk = [ps(1, [128, SW]), ps(2, [128, SW])]
    psq = [ps(3, [128, SW]), ps(4, [128, SW])]
    it = 0
    for h in range(8):
        wkh, wqh, ks_, qs_ = wk[h % 2], wq[h % 2], kst[h % 2], qst[h % 2]
        P.dma("pool", wkh[:], d["w_k_r"][h], writes=[wkh])
        P.dma("pool", wqh[:], d["w_q_r"][h], writes=[wqh])
        for s in range(NS):
            sl = slice(s * SW, (s + 1) * SW)
            pk, pq = psk[it % 2], psq[it % 2]
            it += 1
            for m in range(8):
                P.op("pe", lambda e, m=m, pk=pk, wkh=wkh, sl=sl: e.matmul(pk[:], lhsT=wkh[:, m, :], rhs=hk[:, m, sl],
                                                                          start=(m == 0), stop=(m == 7)),
                     reads=[wkh, hks[s]], writes=[pk])
            for m in range(8):
                P.op("pe", lambda e, m=m, pq=pq, wqh=wqh, sl=sl: e.matmul(pq[:], lhsT=wqh[:, m, :], rhs=hq[:, m, sl],
                                                                          start=(m == 0), stop=(m == 7)),
                     reads=[wqh, hqs[s]], writes=[pq])
            P.op("act", lambda e, pk=pk, ks_=ks_, sl=sl: e.copy(out=ks_[:, sl], in_=pk[:]), reads=[pk], pwrites=[ks_])
            P.op("dve", lambda e, pq=pq, qs_=qs_, sl=sl: e.tensor_scalar(out=qs_[:, sl], in0=pq[:], scalar1=0.125, scalar2=None,
                                                                         op0=ALU.mult), reads=[pq], pwrites=[qs_])
        P.dma("sp", qkv_in[h * 384:h * 384 + 128, :], qs_[:], reads=[qs_], pwrites=[qkv_tok])
        P.dma("sp", qkv_in[h * 384 + 128:h * 384 + 256, :], ks_[:], reads=[ks_], pwrites=[qkv_tok])
    wv = sb("wv", [128, 8, 1024], BF16)
    for h in range(8):
        P.dma("pool", wv[:, :, h * 128:(h + 1) * 128], d["w_v_r"][h], pwrites=[wv])
    vstage = sb("vstage", [128, 8, 16, 128], BF16)
    psv = [ps(1, [128, 4, 128]), ps(2, [128, 4, 128])]
    psv_flat = [ps(1, [128, 512]), ps(2, [128, 512])]
    for tb in range(16):
        s = tb // 4
        for hf in range(2):
            pv = psv_flat[hf]
            for m in range(8):
                P.op("pe", lambda e, m=m, pv=pv, tb=tb, hf=hf: e.matmul(
                    pv[:], lhsT=hk[:, m, tb * 128:(tb + 1) * 128], rhs=wv[:, m, hf * 512:(hf + 1) * 512],
                    start=(m == 0), stop=(m == 7)), reads=[hks[s], wv], writes=[pv])
            if hf == 0:
                P.op("act", lambda e, tb=tb, hf=hf: e.copy(out=vstage[:, hf * 4:(hf + 1) * 4, tb, :], in_=psv[hf][:]),
                     reads=[psv[hf]], pwrites=[vstage])
            else:
                P.op("dve", lambda e, tb=tb, hf=hf: e.tensor_copy(out=vstage[:, hf * 4:(hf + 1) * 4, tb, :], in_=psv[hf][:]),
                     reads=[psv[hf]], pwrites=[vstage])
    for h in range(8):
        P.dma("sp", qkv_in[h * 384 + 256:h * 384 + 384, :], vstage.t[:, h, :, :].rearrange("p b v -> p (b v)"),
              reads=[vstage], pwrites=[qkv_tok])


def attn_core(K, C, d, qkv_all, qkv_all_tok, o_in, o_tok, gvec, gvec_tok):
    P, sb, ps = K.P, K.sb, K.ps
    K.phase()
    QKV = sb("QKV", [128, 3, 8, NT], BF16)
    QT = QKV.alias(QKV.t[:, 0, :, :].rearrange("p r t -> p (r t)"))
    KT = QKV.alias(QKV.t[:, 1, :, :].rearrange("p r t -> p (r t)"))
    VA = sb("VA", [128, NB, 129], BF16)
    P.op("pool", lambda e: e.memset(VA[:], 1.0), writes=[VA])
    q4 = qkv_all.rearrange("(r h x) t -> r h x t", r=8, h=8)
    for r in range(8):
        def fn(e, r=r):
            pid = P.pid(e)
            src = q4[r, bass.ds(pid, 1), :, :].rearrange("o (k p) t -> p (o k) t", k=3)
            return e.dma_start(out=QKV[:, :, r, :], in_=src)
        P._add("act", fn, [qkv_all_tok], (), [QKV], True)
    for r in range(8):
        P.op("pool", lambda e, r=r: e.tensor_copy(out=VA[:, r * 16:(r + 1) * 16, 0:128],
                                                  in_=QKV[:, 2, r, :].rearrange("p (b v) -> p b v", b=16)),
             reads=[QKV], pwrites=[VA])
    Vreg = QKV.t[:, 2, :, :].rearrange("p r t -> p (r t)")
    P.op("pool", lambda e: e.tensor_copy(out=Vreg[64:128, :], in_=QT[64:128, :]), reads=[QKV], pwrites=[QKV])
    P.op("pool", lambda e: e.memset(Vreg[0:64, :], 0.0), pwrites=[QKV])
    P.op("pool", lambda e: e.memset(QT[64:128, :], 0.0), reads=[QKV], pwrites=[QKV])
    Qz = [QT, QKV.alias(Vreg)]
    lamv = sb("lamv", [128, 4, 64], F32)
    P.dma("sp", lamv[:], d["lamv"].partition_broadcast(128), writes=[lamv])
    lp = sb("lp", [128, 2, 64], F32)
    ls = sb("ls", [128, 2], F32)
    nlam = sb("nlam", [128, 1], F32)
    P.op("dve", lambda e: e.tensor_tensor(out=lp[:, 0, :], in0=lamv[:, 0, :], in1=lamv[:, 1, :], op=ALU.mult), reads=[lamv], pwrites=[lp])
    P.op("dve", lambda e: e.tensor_tensor(out=lp[:, 1, :], in0=lamv[:, 2, :], in1=lamv[:, 3, :], op=ALU.mult), reads=[lamv], pwrites=[lp])
    P.op("dve", lambda e: e.reduce_sum(out=ls[:], in_=lp[:], axis=AX.X), reads=[lp], writes=[ls])
    P.op("act", lambda e: e.activation(out=ls[:], in_=ls[:], func=AF.Exp), reads=[ls], writes=[ls])
    P.op("dve", lambda e: e.tensor_sub(out=nlam[:], in0=ls[:, 1:2], in1=ls[:, 0:1]), reads=[ls], writes=[nlam])
    P.op("dve", lambda e: e.tensor_scalar(out=nlam[:], in0=nlam[:], scalar1=-LAM_INIT, scalar2=None, op0=ALU.add),
         reads=[nlam], writes=[nlam])
    gsub = sb("gsub", [128, 128], F32)
    P.dma("sp", gsub[:], d["subln"].partition_broadcast(128), writes=[gsub])
    P.op("dve", lambda e: e.tensor_scalar(out=gsub[:], in0=gsub[:], scalar1=1.0 - LAM_INIT, scalar2=None, op0=ALU.mult),
         reads=[gsub], writes=[gsub])
    eps128 = C.eps
    relcol = sb("relcol", [32, 1], F32)
    oh = sb("oh", [32, 128], F32)
    P.dma("sp", relcol[:], d["relcol"], writes=[relcol])
    P.dma("sp", oh[:], d["oh"], writes=[oh])
    ohb = sb("ohb", [32, 128], BF16)
    rc_hi = sb("rc_hi", [32, 1], BF16)
    rc_lo = sb("rc_lo", [32, 1], BF16)
    P.op("dve", lambda e: e.tensor_copy(out=ohb[:], in_=oh[:]), reads=[oh], writes=[ohb])
    P.op("dve", lambda e: e.tensor_copy(out=rc_hi[:], in_=relcol[:]), reads=[relcol], writes=[rc_hi])
    P.op("dve", lambda e: e.tensor_tensor(out=rc_lo[:], in0=relcol[:], in1=rc_hi[:], op=ALU.subtract),
         reads=[relcol, rc_hi], writes=[rc_lo])
    ps_g = ps(0, [1, 128])
    gm = sb("gm", [1, 128], F32)
    P.op("pe", lambda e: e.matmul(ps_g[:], lhsT=rc_hi[:], rhs=ohb[:], start=True, stop=False), reads=[rc_hi, ohb], writes=[ps_g])
    P.op("pe", lambda e: e.matmul(ps_g[:], lhsT=rc_lo[:], rhs=ohb[:], start=False, stop=True), reads=[rc_lo, ohb], writes=[ps_g])
    P.op("act", lambda e: e.copy(out=gm[:], in_=ps_g[:]), reads=[ps_g], writes=[gm])
    gv = gvec.ap()
    P.dma("sp", gv, d["gconst"], writes=[gvec_tok])
    P.dma("sp", gv[:, 511:639], gm[:], reads=[gm], writes=[gvec_tok])
    btile = sb("btile", [128, 5, SW], F32)
    antiI = sb("antiI", [128, 128], BF16)
    P.dma("pool", antiI[:], d["antiI"], writes=[antiI])
    hk_t = [sb("hk_t", [128, SW], F32) for _ in range(2)]
    hk_hi = [sb("hk_hi", [128, SW], BF16) for _ in range(2)]
    hk_lo = [sb("hk_lo", [128, SW], BF16) for _ in range(2)]
    for i in range(5):
        src = bass.AP(gvec, 512 - 128 * i, [[1, 128], [1, SW]])
        hkt, hhi, hlo = hk_t[i % 2], hk_hi[i % 2], hk_lo[i % 2]
        P.dma("sp", hkt[:], src, reads=[gvec_tok], writes=[hkt])
        P.op("dve", lambda e, hkt=hkt, hhi=hhi: e.tensor_copy(out=hhi[:], in_=hkt[:]), reads=[hkt], writes=[hhi])
        P.op("dve", lambda e, hkt=hkt, hhi=hhi, hlo=hlo: e.tensor_tensor(out=hlo[:], in0=hkt[:], in1=hhi[:], op=ALU.subtract),
             reads=[hkt, hhi], writes=[hlo])
        pbt = K.pb[i % 2]
        P.op("pe", lambda e, pbt=pbt, hhi=hhi: e.matmul(pbt[:], lhsT=antiI[:], rhs=hhi[:], start=True, stop=False),
             reads=[antiI, hhi], writes=[pbt])
        P.op("pe", lambda e, pbt=pbt, hlo=hlo: e.matmul(pbt[:], lhsT=antiI[:], rhs=hlo[:], start=False, stop=True),
             reads=[antiI, hlo], writes=[pbt])
        P.op("act", lambda e, pbt=pbt, i=i: e.copy(out=btile[:, i, :], in_=pbt[:]), reads=[pbt], pwrites=[btile])

    pT = [[sb("pT", [128, SW], BF16) for _ in range(2)] for _ in range(2)]
    stmp = [sb("stmp", [128, SW], F32) for _ in range(2)]
    psS = [[K.pb[0], K.pb[1]], [K.pb[2], K.pb[3]]]
    accb = [K.pb[4], K.pb[5], K.pb2]

    def acc(m, j):
        i = m * 4 + j
        b = accb[i // 3]
        o = (i % 3) * 129
        return b, b.t[:, o:o + 129]
    ps_tr = K.pb2.alias(K.pb2.t[:, 512:768].bitcast(BF16))
    accS = [sb("accS", [128, 8 * 129], F32) for _ in range(2)]
    o_sb = [sb("o_sb", [128, 128], F32) for _ in range(4)]
    osq = [sb("osq", [128, 128], F32) for _ in range(2)]
    on = [sb("on", [128, 128], BF16) for _ in range(4)]
    sm = [sb("sm", [128, 8], F32) for _ in range(4)]
    oT_st = [sb("oT_st", [128, SW], BF16) for _ in range(2)]
    def emit_qk(qs, kb, maps=(0, 1)):
        i_near = kb - (qs * 4 - 1)
        near = i_near >= 0
        j0 = max(0, kb - qs * 4)
        c0 = j0 * 128
        for m in maps:
            pS = psS[m][kb % 2]
            P.op("pe", lambda e, pS=pS, m=m, kb=kb, qs=qs, c0=c0: e.matmul(
                pS[:, c0:SW], lhsT=KT[:, kb * 128:(kb + 1) * 128], rhs=Qz[m][:, qs * SW + c0:(qs + 1) * SW],
                start=True, stop=True), reads=[KT, QT], writes=[pS])
        for m in maps:
            pS = psS[m][kb % 2]
            pt = pT[m][kb % 2]
            if near:
                st = stmp[m]
                P.op("dve", lambda e, st=st, pS=pS, i_near=i_near, c0=c0: e.tensor_tensor(
                    out=st[:, c0:SW], in0=pS[:, c0:SW], in1=btile[:, i_near, c0:SW], op=ALU.add),
                    reads=[pS, btile], writes=[st])
                P.op("act", lambda e, st=st, pt=pt, c0=c0: e.activation(out=pt[:, c0:SW], in_=st[:, c0:SW], func=AF.Exp),
                     reads=[st], writes=[pt])
            else:
                P.op("act", lambda e, pS=pS, pt=pt: e.activation(out=pt[:], in_=pS[:], func=AF.Exp), reads=[pS], writes=[pt])

    def emit_pv(qs, kb, maps=(0, 1), last=True):
        j0 = max(0, kb - qs * 4)
        for m in maps:
            pt = pT[m][kb % 2]
            for j in range(j0, 4):
                ab, aap = acc(m, j)
                st_ = (kb == 0) and ((m * 4 + j) % 3 == 0)
                P.op("pe", lambda e, aap=aap, pt=pt, j=j, kb=kb, qs=qs, st_=st_: e.matmul(
                    aap, lhsT=pt[:, j * 128:(j + 1) * 128], rhs=VA[:, kb, :], start=st_, stop=(kb == qs * 4 + j)),
                    reads=[pt, VA], pwrites=[ab])
        if last and kb == (qs + 1) * 4 - 1:
            epilogue(qs)

    def epilogue(qs):
        aS = accS[qs % 2]
        P.op("dve", lambda e, aS=aS: e.tensor_copy(out=aS[:, 0:387], in_=accb[0].t[:, 0:387]), reads=[accb[0]], pwrites=[aS])
        P.op("dve", lambda e, aS=aS: e.tensor_copy(out=aS[:, 387:774], in_=accb[1].t[:, 0:387]), reads=[accb[1]], pwrites=[aS])
        P.op("dve", lambda e, aS=aS: e.tensor_copy(out=aS[:, 774:1032], in_=accb[2].t[:, 0:258]), reads=[accb[2]], pwrites=[aS])
        for j in range(4):
            a0 = aS.t[:, j * 129:(j + 1) * 129]
            a1 = aS.t[:, (4 + j) * 129:(5 + j) * 129]
            s_, o_, q_, n_ = sm[j], o_sb[j], osq[j % 2], on[j]
            P.op("dve", lambda e, a0=a0, s_=s_: e.reciprocal(out=s_[:, 0:1], in_=a0[:, 128:129]), reads=[aS], pwrites=[s_])
            P.op("dve", lambda e, a1=a1, s_=s_: e.reciprocal(out=s_[:, 1:2], in_=a1[:, 128:129]), reads=[aS], pwrites=[s_])
            P.op("dve", lambda e, s_=s_: e.tensor_tensor(out=s_[:, 2:3], in0=s_[:, 1:2], in1=nlam[:], op=ALU.mult),
                 reads=[s_, nlam], pwrites=[s_])
            P.op("dve", lambda e, a0=a0, s_=s_, o_=o_: e.tensor_scalar(out=o_[:], in0=a0[:, 0:128], scalar1=s_[:, 0:1], scalar2=None,
                                                                      op0=ALU.mult), reads=[aS, s_], writes=[o_])
            P.op("dve", lambda e, a1=a1, s_=s_, o_=o_: e.scalar_tensor_tensor(out=o_[:], in0=a1[:, 0:128], scalar=s_[:, 2:3], in1=o_[:],
                                                                             op0=ALU.mult, op1=ALU.add), reads=[aS, s_, o_], writes=[o_])
            P.op("pool", lambda e, o_=o_, q_=q_: e.tensor_tensor(out=q_[:], in0=o_[:], in1=o_[:], op=ALU.mult), reads=[o_], writes=[q_])
            P.op("dve", lambda e, s_=s_, q_=q_: e.reduce_sum(out=s_[:, 3:4], in_=q_[:], axis=AX.X), reads=[q_], pwrites=[s_])
            P.op("act", lambda e, s_=s_: e.activation(out=s_[:, 4:5], in_=s_[:, 3:4], func=AF.Ln, scale=1.0 / 128, bias=eps128[:]),
                 reads=[s_, eps128], pwrites=[s_])
            P.op("act", lambda e, s_=s_: e.activation(out=s_[:, 5:6], in_=s_[:, 4:5], func=AF.Exp, scale=-0.5),
                 reads=[s_], pwrites=[s_])
            P.op("dve", lambda e, s_=s_, o_=o_, n_=n_: e.scalar_tensor_tensor(out=n_[:], in0=o_[:], scalar=s_[:, 5:6], in1=gsub[:],
                                                                             op0=ALU.mult, op1=ALU.mult), reads=[o_, s_, gsub], writes=[n_])

    def epilogue_out(qs):
        ost = oT_st[qs % 2]
        for j in range(4):
            P.op("pe", lambda e, j=j: e.transpose(out=ps_tr[:, j * 128:(j + 1) * 128], in_=on[j][:], identity=C.ident[:]),
                 reads=[on[j], C.ident], pwrites=[ps_tr])
        P.op("dve", lambda e, ost=ost: e.tensor_copy(out=ost[:], in_=ps_tr[:]), reads=[ps_tr], writes=[ost])
        P.dma("sp", o_in[:, qs * SW:(qs + 1) * SW], ost[:], reads=[ost], pwrites=[o_tok])

    units = [(qs, kb) for qs in range(NQS) for kb in range((qs + 1) * 4)]
    pending = []
    for idx in range(len(units) + 1):
        for m in range(2):
            if idx < len(units):
                emit_qk(*units[idx], maps=(m,))
            if idx >= 1:
                emit_pv(*units[idx - 1], maps=(m,), last=(m == 1))
        if idx >= 1:
            qs_, kb_ = units[idx - 1]
            if kb_ == (qs_ + 1) * 4 - 1:
                pending.append((idx + 3, qs_))
        while pending and pending[0][0] <= idx:
            epilogue_out(pending.pop(0)[1])
    for _, qs_ in pending:
        epilogue_out(qs_)


def attn_out(K, C, d, o_all, o_all_tok, xin, xin_tok, xout, xout_tok, halo_in, halo_tok):
    P, sb, ps = K.P, K.sb, K.ps
    K.phase()
    og = sb("og", [128, 8, NT], BF16)

    def fn(e):
        pid = P.pid(e)
        return e.dma_start(out=og[:], in_=o_all.rearrange("(h p) t -> p h t", p=128)[:, :, bass.ds(pid * NT, NT)])
    P._add("sp", fn, [o_all_tok], [og], (), True)
    wo = sb("wo", [128, 8, 1024], BF16)
    P.dma("pool", wo[:], d["w_o_r"], writes=[wo])
    xj = [sb("xj", [128, SW], F32) for _ in range(3)]
    ps_y = [ps(0, [128, SW]), ps(1, [128, SW])]
    xin3 = xin.rearrange("(c p) t -> p c t", p=128)
    xout3 = xout.rearrange("(c p) t -> p c t", p=128)
    it = 0
    for j in range(8):
        for s in range(NS):
            sl = slice(s * SW, (s + 1) * SW)
            py = ps_y[it % 2]
            xs = xj[it % 3]
            it += 1
            P.dma("sp", xs[:], xin3[:, j, sl], reads=[xin_tok], writes=[xs])
            for h in range(8):
                P.op("pe", lambda e, h=h, j=j, py=py, sl=sl: e.matmul(py[:], lhsT=wo[:, h, j * 128:(j + 1) * 128], rhs=og[:, h, sl],
                                                                      start=(h == 0), stop=(h == 7)), reads=[wo, og], writes=[py])
            P.op("dve", lambda e, py=py, xs=xs: e.tensor_tensor(out=xs[:], in0=py[:], in1=xs[:], op=ALU.add),
                 reads=[py, xs], writes=[xs])
            P.dma("sp", xout3[:, j, sl], xs[:], reads=[xs], pwrites=[xout_tok])
            if s == NS - 1:
                P.dma("sp", halo_in[:, j * 2:(j + 1) * 2], xs[:, SW - 2:SW], reads=[xs], pwrites=[halo_tok])


import numpy as np
from concourse.bass_utils import run_bass_kernel_spmd

NCORES = 8


def allgather(P, src_h, dst_h, src_tok, dst_tok, rows=None):
    dst = dst_h.ap() if rows is None else dst_h.ap()[0:rows, :]
    P.async_op("pool", lambda e: e.collective_compute("AllGather", ALU.bypass, replica_groups=[list(range(NCORES))],
                                                      ins=[src_h.ap().opt()], outs=[dst.opt()]),
               reads=[src_tok], writes=[dst_tok], inc=1)


IN_SPECS = [
    ("xT", [1024, NT]), ("w_in_r", [32, 128, 8, 128]), ("w_out_r", [128, 8, 1024]), ("ident", [128, 128]),
    ("resetm", [128, SW]), ("maskc", [64, 8, 64]), ("gmix0", [128, 8]), ("gnorm", [128, 8]), ("lbl", [128, 2, 8]),
    ("sel", [128, 8]), ("notfirst", [128, 1]),
    ("gffn0", [128, 8]), ("w_up_r0", [44, 128, 8, 128]), ("convp0", [128, 44, 4]), ("w_down_r0", [8, 128, 22, 128]),
    ("gffn1", [128, 8]), ("w_up_r1", [44, 128, 8, 128]), ("convp1", [128, 44, 4]), ("w_down_r1", [8, 128, 22, 128]),
    ("gkv", [128, 8]), ("gmix1", [128, 8]), ("w_k_r", [8, 128, 8, 128]), ("w_q_r", [8, 128, 8, 128]),
    ("w_v_r", [8, 128, 8, 128]), ("relcol", [32, 1]), ("oh", [32, 128]), ("gconst", [1, GLEN]), ("antiI", [128, 128]), ("lamv", [4, 64]),
    ("subln", [1, 128]), ("w_o_r", [128, 8, 1024]), ("gfinal", [128, 8]),
]


def build(debug=None):
    nc = bass.Bass("TRN2", target_bir_lowering=False)
    K = KB(nc)
    P = K.P
    d = {}
    for name, shape in IN_SPECS:
        d[name] = nc.dram_tensor(name, shape, F32, kind="ExternalInput").ap()
    outT = nc.dram_tensor("outT", [1024, NT], F32, kind="ExternalOutput").ap()
    out_tok = Buf("out")

    def stream(name):
        kind = "ExternalOutput" if debug == name else "Internal"
        return nc.dram_tensor(name, [1024, NT], F32, kind=kind).ap(), Buf(name)
    x1T, x1_tok = stream("x1T")
    x2T, x2_tok = stream("x2T")
    x3T, x3_tok = stream("x3T")
    x4T, x4_tok = stream("x4T")
    scr = {}
    for n in ("qT", "kT", "gT"):
        scr[n] = nc.dram_tensor(n + "_s", [1024, NT], BF16).ap()
        scr[n + "_tok"] = Buf(n)
    for n in ("v", "kt"):
        scr[n] = nc.dram_tensor(n + "_s", [NT, 1024], BF16).ap()
        scr[n + "_tok"] = Buf(n)
    hx_in = nc.dram_tensor("hx_in", [128, 1032], F32)
    hx_all = nc.dram_tensor("hx_all", [NCORES * 128, 1032], F32)
    hx_in_tok, hx_all_tok = Buf("hx_in"), Buf("hx_all")
    halo_in = [nc.dram_tensor(f"halo_in{i}", [128, 16], F32) for i in range(2)]
    halo_all = [nc.dram_tensor(f"halo_all{i}", [NCORES * 128, 16], F32) for i in range(2)]
    halo_in_tok = [Buf("hi0"), Buf("hi1")]
    halo_all_tok = [Buf("ha0"), Buf("ha1")]
    qkv_in = nc.dram_tensor("qkv_in", [3072, NT], BF16)
    qkv_all = nc.dram_tensor("qkv_all", [NCORES * 3072 + 384, NT], BF16)
    qkv_in_tok, qkv_all_tok = Buf("qkv_in"), Buf("qkv_all")
    gvec = nc.dram_tensor("gvec", [1, GLEN], F32)
    gvec_tok = Buf("gvec")
    o_in = nc.dram_tensor("o_in", [128, T_ALL], BF16)
    o_all = nc.dram_tensor("o_all", [NCORES * 128, T_ALL], BF16)
    o_in_tok, o_all_tok = Buf("o_in"), Buf("o_all")

    C = hgrn_consts(K, d)
    T = hgrn_alloc_T(K)
    S = K.sb("S", [128, 8, 128], F32, pers=True)
    Rr = K.sb("Rr", [128, 8, 128], F32, pers=True)
    sel = K.sb("sel", [128, 8], F32, pers=True)
    P.dma("sp", sel[:], d["sel"], writes=[sel])
    P.op("pool", lambda e: e.memset(S[:], 0.0), writes=[S])
    K.A.start_phase()
    hgrn_P(K, C, d, scr, T)
    K.phase()
    hgrn_R(K, C, d, scr, T, False, S)
    P.dma("sp", hx_in.ap()[:, 0:1024], S.t.rearrange("p h v -> p (h v)"), reads=[S], pwrites=[hx_in_tok])
    P.dma("sp", hx_in.ap()[:, 1024:1032], T.D[:], reads=[T.D], pwrites=[hx_in_tok])
    allgather(P, hx_in, hx_all, hx_in_tok, hx_all_tok)
    K.phase()
    Sj = [K.sb("Sj", [128, 1032], F32) for _ in range(2)]
    P.op("pool", lambda e: e.memset(S[:], 0.0), writes=[S])
    P.op("pool", lambda e: e.memset(Rr[:], 0.0), writes=[Rr])
    for j in range(NCORES):
        sj = Sj[j % 2]
        P.dma("sp", sj[:], hx_all.ap()[j * 128:(j + 1) * 128, :], reads=[hx_all_tok], writes=[sj])
        P.op("dve", lambda e, j=j: e.scalar_tensor_tensor(out=S[:], in0=Rr[:], scalar=sel[:, j:j + 1], in1=S[:],
                                                          op0=ALU.mult, op1=ALU.add), reads=[Rr, sel, S], writes=[S])
        if j < NCORES - 1:
            P.op("dve", lambda e, sj=sj: e.tensor_tensor(out=Rr[:], in0=Rr[:],
                                                         in1=sj[:, 1024:1032].unsqueeze(2).to_broadcast([128, 8, 128]),
                                                         op=ALU.mult), reads=[Rr, sj], writes=[Rr])
            P.op("dve", lambda e, sj=sj: e.tensor_tensor(out=Rr[:], in0=Rr[:],
                                                         in1=sj[:, 0:1024].rearrange("p (h v) -> p h v", h=8),
                                                         op=ALU.add), reads=[Rr, sj], writes=[Rr])
    d["x1T"], d["x1T_tok"] = x1T, x1_tok
    d["halo_out"], d["halo_out_tok"] = halo_in[0].ap(), halo_in_tok[0]
    hgrn_R(K, C, d, scr, T, True, S)
    allgather(P, halo_in[0], halo_all[0], halo_in_tok[0], halo_all_tok[0])
    if debug == "x1T":
        P.emit(final_bufs=[x1_tok, halo_all_tok[0]])
        print("nflag", P.nflag, "ndma", P.n_dma, "peak", K.A.peak)
        return nc
    ffn_layer(K, C, d, 0, x1T, x1_tok, x2T, x2_tok, halo_all[0].ap(), halo_all_tok[0])
    if debug == "x2T":
        P.emit(final_bufs=[x2_tok])
        print("nflag", P.nflag, "ndma", P.n_dma, "peak", K.A.peak)
        return nc
    kvq_proj(K, C, d, x2T, x2_tok, qkv_in.ap(), qkv_in_tok)
    allgather(P, qkv_in, qkv_all, qkv_in_tok, qkv_all_tok, rows=NCORES * 3072)
    attn_core(K, C, d, qkv_all.ap()[0:NCORES * 3072, :], qkv_all_tok, o_in.ap(), o_in_tok, gvec, gvec_tok)
    allgather(P, o_in, o_all, o_in_tok, o_all_tok)
    attn_out(K, C, d, o_all.ap(), o_all_tok, x2T, x2_tok, x3T, x3_tok, halo_in[1].ap(), halo_in_tok[1])
    allgather(P, halo_in[1], halo_all[1], halo_in_tok[1], halo_all_tok[1])
    if debug == "x3T":
        P.emit(final_bufs=[x3_tok, halo_all_tok[1]])
        return nc
    ffn_layer(K, C, d, 1, x3T, x3_tok, x4T, x4_tok, halo_all[1].ap(), halo_all_tok[1])
    final_norm(K, C, d, x4T, x4_tok, outT, out_tok)
    P.emit(final_bufs=[out_tok])
    return nc


def t5_bucket_np(rel):
    max_exact = 16
    n = np.maximum(rel, 0)
    log_ratio = (np.log(np.maximum(n, 1).astype(np.float32) / np.float32(max_exact)) / np.float32(math.log(128 / max_exact))).astype(np.float32)
    large = np.minimum(max_exact + (log_ratio * np.float32(32 - max_exact)).astype(np.int32), 31)
    return np.where(n < max_exact, n, large)


def host_inputs(inp, c):
    f = lambda a: np.ascontiguousarray(np.asarray(a, dtype=np.float32))
    pc = lambda v: f(np.asarray(v).reshape(8, 128).T)
    m = {}
    m["xT"] = f(np.asarray(inp["x"])[0, c * NT:(c + 1) * NT, :].T)
    m["w_in_r"] = f(np.asarray(inp["a_w_in"])[0].reshape(8, 128, 32, 128).transpose(2, 1, 0, 3))
    m["w_out_r"] = f(np.asarray(inp["a_w_out"])[0].reshape(8, 128, 1024).transpose(1, 0, 2))
    m["ident"] = np.eye(128, dtype=np.float32)
    r = np.ones((128, SW), np.float32)
    r[:, ::CH] = 0
    m["resetm"] = r
    mk_ = (np.arange(64)[:, None] <= np.arange(64)[None, :]).astype(np.float32)
    m["maskc"] = f(np.broadcast_to(mk_[:, None, :], (64, 8, 64)))
    m["gmix0"] = pc(inp["norm_mix"][0])
    m["gmix1"] = pc(inp["norm_mix"][1])
    m["gnorm"] = pc(inp["a_gnorm"][0])
    m["lbl"] = f(np.asarray(inp["a_lb_logits"]).reshape(2, 8, 128).transpose(2, 0, 1))
    s = np.zeros((128, 8), np.float32)
    s[:, c] = 1.0
    m["sel"] = s
    m["notfirst"] = np.full((128, 1), 0.0 if c == 0 else 1.0, np.float32)
    for li in range(2):
        m[f"gffn{li}"] = pc(inp["norm_ffn"][li])
        m[f"w_up_r{li}"] = f(np.asarray(inp["ffn_w_up"])[li].reshape(8, 128, 44, 128).transpose(2, 1, 0, 3))
        cw = np.asarray(inp["ffn_conv_w"])[li]
        cb = np.asarray(inp["ffn_conv_b"])[li]
        cp = np.concatenate([cw, cb[None]], 0)
        m[f"convp{li}"] = f(cp.reshape(4, 44, 128).transpose(2, 1, 0))
        m[f"w_down_r{li}"] = f(np.asarray(inp["ffn_w_down"])[li].reshape(22, 128, 8, 128).transpose(2, 1, 0, 3))
    m["gkv"] = pc(inp["kv_norm"])
    kvw = np.asarray(inp["kv_w"])
    m["w_k_r"] = f(kvw[:, :1024].reshape(8, 128, 8, 128).transpose(2, 1, 0, 3))
    m["w_v_r"] = f(kvw[:, 1024:].reshape(8, 128, 8, 128).transpose(2, 1, 0, 3))
    m["w_q_r"] = f(np.asarray(inp["b_w_q"])[0].reshape(8, 128, 8, 128).transpose(2, 1, 0, 3))
    m["w_o_r"] = f(np.asarray(inp["b_w_o"])[0].reshape(8, 128, 1024).transpose(1, 0, 2))
    m["relcol"] = f(np.asarray(inp["rel_table"])[:, c:c + 1])
    bk = t5_bucket_np(np.arange(128))
    oh = np.zeros((32, 128), np.float32)
    oh[bk, np.arange(128)] = 1.0
    oh[31, :] -= 1.0
    m["oh"] = oh
    g = np.zeros((1, GLEN), np.float32)
    g[0, :511] = NEG
    m["gconst"] = g
    m["antiI"] = np.ascontiguousarray(np.eye(128, dtype=np.float32)[::-1])
    m["lamv"] = f(np.stack([np.asarray(inp[k])[0] for k in ("b_lam_q1", "b_lam_k1", "b_lam_q2", "b_lam_k2")]))
    m["subln"] = f(np.asarray(inp["b_subln"])[0][None, :])
    m["gfinal"] = pc(inp["final_norm"])
    return m


_NC_CACHE = {}


def kernel(**inputs):
    if "nc" not in _NC_CACHE:
        _NC_CACHE["nc"] = build()
    nc = _NC_CACHE["nc"]
    in_maps = [host_inputs(inputs, c) for c in range(NCORES)]
    res = run_bass_kernel_spmd(nc, in_maps, core_ids=list(range(NCORES)))
    out = np.empty((1, NCORES * NT, 1024), np.float32)
    for c in range(NCORES):
        out[0, c * NT:(c + 1) * NT, :] = res.results[c]["outT"].T
    return out
```

```python
import contextlib
import numpy as np
import concourse.bass as bass
import concourse.mybir as mybir

F32 = mybir.dt.float32
BF16 = mybir.dt.bfloat16
U8 = mybir.dt.uint8
ALU = mybir.AluOpType
AF = mybir.ActivationFunctionType
AX = mybir.AxisListType
DTSZ = {F32: 4, BF16: 2, U8: 1}

ENGS = ("pe", "act", "dve", "pool", "sp")
SEM_ROLL = 2048
DMA_K = 6


class Tok:
    __slots__ = ("writers", "readers", "psum")

    def __init__(self, psum=False):
        self.writers = []
        self.readers = []
        self.psum = psum


class Buf:
    __slots__ = ("name", "t", "tok")

    def __init__(self, name, t=None, tok=None):
        self.name = name
        self.t = t
        self.tok = tok if tok is not None else Tok()

    def __getitem__(self, idx):
        return self.t[idx]

    def alias(self, ap, name=None):
        return Buf(name or self.name, ap, self.tok)


class Op:
    __slots__ = ("eng", "fn", "deps", "is_dma", "flag", "seq", "dma_slot", "dma_val", "name", "inc", "cc", "gidx")


class Arena:
    def __init__(self, nc, nbytes):
        self.big = nc.alloc_sbuf_tensor("arena", [128, nbytes], U8)
        self.size = nbytes
        self.pers = 0
        self.cur = 0
        self.in_phase = False
        self.peak = 0

    def start_phase(self):
        self.in_phase = True
        self.cur = self.pers

    def alloc(self, name, shape, dt, persistent=False):
        p = shape[0]
        n = int(np.prod(shape[1:])) * DTSZ[dt]
        n = (n + 63) // 64 * 64
        if persistent:
            assert not self.in_phase or self.cur == self.pers, "persistent alloc inside a phase"
            off = self.pers
            self.pers += n
            self.cur = self.pers
        else:
            off = self.cur
            self.cur += n
        self.peak = max(self.peak, self.cur)
        assert self.cur <= self.size, f"SBUF arena overflow allocating {name}: {self.cur} > {self.size}"
        ap = self.big[0:p, off:off + n if False else off + int(np.prod(shape[1:])) * DTSZ[dt]].bitcast(dt)
        if len(shape) == 3:
            ap = ap.rearrange("p (a b) -> p a b", a=shape[1])
        elif len(shape) == 4:
            ap = ap.rearrange("p (a b c) -> p a b c", a=shape[1], b=shape[2])
        return Buf(name, ap)


class Prog:
    def __init__(self, nc):
        self.nc = nc
        self.ops = {e: [] for e in ENGS}
        self.n_dma = {e: 0 for e in ENGS}
        self.all_ops = []

    def _add(self, eng, fn, reads, writes, pwrites, is_dma, name=None, inc=None, extra_deps=(), cc=False):
        op = Op()
        op.eng, op.fn, op.is_dma, op.flag, op.seq, op.name = eng, fn, is_dma, False, None, name
        op.inc = inc if inc is not None else (16 if is_dma else 1)
        op.cc = cc
        op.gidx = len(self.all_ops)
        deps = list(extra_deps)
        wr_toks = set()
        for r in reads:
            deps.extend(r.tok.writers)
            if r.tok.psum:
                deps.extend(x for x in r.tok.readers if x.eng != eng)
        for w in list(writes) + list(pwrites):
            wr_toks.add(id(w.tok))
        for w in writes:
            deps.extend(w.tok.writers)
            deps.extend(w.tok.readers)
        for w in pwrites:
            deps.extend(w.tok.readers)
            if w.tok.readers:
                deps.extend(w.tok.writers)
        rw_writers = set()
        for x in list(reads) + list(writes):
            for d in x.tok.writers:
                rw_writers.add(id(d))
        out = []
        seen = set()
        for d in deps:
            if id(d) in seen or d is op:
                continue
            seen.add(id(d))
            if (not d.is_dma) and (not is_dma) and d.eng == eng:
                if eng == "pe":
                    continue
                if id(d) not in rw_writers:
                    continue
            out.append(d)
        latest = {}
        for d in out:
            if not d.is_dma:
                if d.eng not in latest or d.gidx > latest[d.eng].gidx:
                    latest[d.eng] = d
        out = [d for d in out if d.is_dma or latest[d.eng] is d]
        op.deps = out
        for r in reads:
            r.tok.readers.append(op)
        for w in writes:
            w.tok.writers = [op]
            w.tok.readers = []
        for w in pwrites:
            if w.tok.readers:
                w.tok.writers = [op]
                w.tok.readers = []
            else:
                w.tok.writers.append(op)
        if is_dma and not cc:
            i = self.n_dma[eng]
            self.n_dma[eng] += 1
            op.dma_slot = i % DMA_K
            op.dma_val = i // DMA_K + 1
        self.ops[eng].append(op)
        self.all_ops.append(op)
        return op

    def op(self, eng, fn, reads=(), writes=(), pwrites=(), name=None):
        return self._add(eng, fn, reads, writes, pwrites, False, name)

    def dma(self, eng, out, in_, reads=(), writes=(), pwrites=(), **kw):
        def fn(e):
            return e.dma_start(out=out, in_=in_, **kw)
        return self._add(eng, fn, reads, writes, pwrites, True)

    def async_op(self, eng, fn, reads=(), writes=(), inc=1):
        return self._add(eng, fn, reads, writes, (), True, inc=inc, cc=True)

    def barrier(self):
        lasts = []
        for e in ENGS:
            for op in reversed(self.ops[e]):
                if not op.is_dma:
                    lasts.append(op)
                    break
            dm = [op for op in self.ops[e] if op.is_dma and not op.cc][-DMA_K:]
            lasts.extend(dm)
            lasts.extend(op for op in self.ops[e] if op.cc)
        for e in ENGS:
            deps = [d for d in lasts if d.is_dma or d.eng != e]
            self._add(e, lambda en: en.nop(), (), (), (), False, "barrier", extra_deps=deps)

    def pid(self, e):
        k = id(e)
        if k not in self._pids:
            self._pids[k] = e.partition_id()
        return self._pids[k]

    def emit(self, final_bufs=()):
        self._pids = {}
        nc = self.nc
        for op in self.all_ops:
            for d in op.deps:
                d.flag = True
        finals = []
        for b in final_bufs:
            for w in b.tok.writers:
                w.flag = True
                finals.append(w)
        nflag = {}
        for e in ENGS:
            n = 0
            for op in self.ops[e]:
                if op.flag and not op.is_dma:
                    op.seq = n
                    n += 1
            nflag[e] = n
        self.nflag = nflag
        with contextlib.ExitStack() as st:
            csem = {}
            for e in ENGS:
                k = (nflag[e] + SEM_ROLL - 1) // SEM_ROLL
                csem[e] = [st.enter_context(nc.semaphore(f"c_{e}_{i}")) for i in range(max(k, 1))]
            dsem = {}
            for e in ENGS:
                if self.n_dma[e]:
                    dsem[e] = [st.enter_context(nc.semaphore(f"d_{e}_{i}")) for i in range(DMA_K)]
            ccsem = {}
            for op in self.all_ops:
                if op.cc:
                    ccsem[id(op)] = st.enter_context(nc.semaphore(f"cc_{len(ccsem)}"))
            cum = {e: [0] * DMA_K for e in ENGS}
            for e in ENGS:
                for op in self.ops[e]:
                    if op.is_dma and not op.cc:
                        cum[e][op.dma_slot] += op.inc
                        op.dma_val = cum[e][op.dma_slot]
            block = st.enter_context(nc.Block())

            def target(d):
                if d.cc:
                    return ccsem[id(d)], 1
                if d.is_dma:
                    return dsem[d.eng][d.dma_slot], d.dma_val
                return csem[d.eng][d.seq // SEM_ROLL], d.seq % SEM_ROLL + 1

            def run(ename, e):
                waited = {}
                for op in self.ops[ename]:
                    tg = [target(d) for d in op.deps]
                    if op.is_dma and not op.cc and op.dma_val - op.inc > 0:
                        tg.append((dsem[ename][op.dma_slot], op.dma_val - op.inc))
                    for s, v in tg:
                        key = id(s)
                        if waited.get(key, 0) >= v:
                            continue
                        waited[key] = v
                        e.wait_ge(s, v)
                    ins = op.fn(e)
                    if op.cc:
                        ins.then_inc(ccsem[id(op)], 1)
                    elif op.is_dma:
                        ins.then_inc(dsem[ename][op.dma_slot], op.inc)
                    elif op.flag:
                        ins.then_inc(csem[ename][op.seq // SEM_ROLL], 1)
                if ename == "sp":
                    for d in finals:
                        s, v = target(d)
                        e.wait_ge(s, v)

            @block.tensor
            def _(e):
                run("pe", e)

            @block.scalar
            def _(e):
                run("act", e)

            @block.vector
            def _(e):
                run("dve", e)

            @block.gpsimd
            def _(e):
                run("pool", e)

            @block.sync
            def _(e):
                run("sp", e)


NT = 2048
SW = 512
NS = NT // SW
CH = 64
NCH = NT // CH
CPS = SW // CH
EPS = 1e-6


class Ctx:
    pass


class KB:
    def __init__(self, nc, sbuf_bytes=206 * 1024):
        self.nc = nc
        self.P = Prog(nc)
        self.A = Arena(nc, sbuf_bytes)
        self.pb = [Buf(f"pb{i}", nc.alloc_psum_tensor(f"pb{i}", [128, 512], F32), Tok(psum=True)) for i in range(6)]
        self.pb2 = Buf("pb2", nc.alloc_psum_tensor("pbig", [128, 1024], F32), Tok(psum=True))

    def sb(self, name, shape, dt, pers=False):
        return self.A.alloc(name, shape, dt, persistent=pers)

    def ps(self, bank, shape, dt=F32):
        base = self.pb2 if bank == 6 else self.pb[bank]
        p = shape[0]
        n = 1
        for x in shape[1:]:
            n *= x
        nf32 = n * DTSZ[dt] // 4
        ap = base.t[0:p, 0:nf32]
        if dt != F32:
            ap = ap.bitcast(dt)
        if len(shape) == 3:
            ap = ap.rearrange("p (a b) -> p a b", a=shape[1])
        return base.alias(ap)

    def phase(self):
        self.P.barrier()
        self.A.start_phase()


def rmsnorm_fm(P, C, xs, g, out_ap, out_buf, sq, ps, rstd, width=SW):
    P.op("act", lambda e: e.activation(out=sq[:], in_=xs[:], func=AF.Square), reads=[xs], writes=[sq])
    for c in range(8):
        P.op("pe", lambda e, c=c: e.matmul(ps[:], lhsT=C.ones[:], rhs=sq[:, c, :], start=(c == 0), stop=(c == 7)),
             reads=[C.ones, sq], writes=[ps])
    P.op("act", lambda e: e.activation(out=rstd[:], in_=ps[:], func=AF.Sqrt, scale=1.0 / 1024, bias=C.eps[:]),
         reads=[ps, C.eps], writes=[rstd])
    P.op("dve", lambda e: e.reciprocal(out=rstd[:], in_=rstd[:]), reads=[rstd], writes=[rstd])
    for c in range(8):
        P.op("dve", lambda e, c=c: e.scalar_tensor_tensor(out=out_ap[:, c, :], in0=xs[:, c, :], scalar=g[:, c:c + 1],
                                                          in1=rstd[:], op0=ALU.mult, op1=ALU.mult),
             reads=[xs, g, rstd], pwrites=[out_buf])


def hgrn_consts(K, d):
    P = K.P
    C = Ctx()
    sb = lambda n, sh, dt: K.sb(n, sh, dt, pers=True)
    C.ones = sb("ones", [128, 128], BF16)
    P.op("pool", lambda e: e.memset(C.ones[:], 1.0), writes=[C.ones])
    C.eps = sb("eps", [128, 1], F32)
    P.op("pool", lambda e: e.memset(C.eps[:], EPS), writes=[C.eps])
    C.ident = sb("ident", [128, 128], BF16)
    P.dma("pool", C.ident[:], d["ident"], writes=[C.ident])
    C.resetm = sb("resetm", [128, SW], F32)
    P.dma("sp", C.resetm[:], d["resetm"], writes=[C.resetm])
    C.maskc = sb("maskc", [64, 8, 64], F32)
    P.dma("sp", C.maskc[:], d["maskc"], writes=[C.maskc])
    C.gmix = sb("gmix", [128, 8], F32)
    P.dma("sp", C.gmix[:], d["gmix0"], writes=[C.gmix])
    C.gn = sb("gn", [128, 8], F32)
    P.dma("sp", C.gn[:], d["gnorm"], writes=[C.gn])
    lbl = sb("lbl", [128, 2, 8], F32)
    P.dma("sp", lbl[:], d["lbl"], writes=[lbl])
    C.lb = sb("lb", [128, 8], F32)
    C.oml = sb("oml", [128, 8], F32)
    P.op("dve", lambda e: e.tensor_sub(out=C.lb[:], in0=lbl[:, 0, :], in1=lbl[:, 1, :]), reads=[lbl], writes=[C.lb])
    P.op("act", lambda e: e.activation(out=C.oml[:], in_=C.lb[:], func=AF.Sigmoid, scale=-1.0), reads=[C.lb], writes=[C.oml])
    P.op("act", lambda e: e.activation(out=C.lb[:], in_=C.lb[:], func=AF.Sigmoid), reads=[C.lb], writes=[C.lb])
    return C


def hgrn_alloc_T(K):
    T = Ctx()
    sb = lambda n, sh, dt: K.sb(n, sh, dt, pers=True)
    T.bl = sb("bl", [128, 8, NCH], F32)
    T.bmid = sb("bmid", [128, 8, NCH], F32)
    T.e1 = sb("e1", [128, 8, NCH], F32)
    T.e2 = sb("e2", [128, 8, NCH], F32)
    T.em = sb("em", [128, 8, NCH], F32)
    T.D = sb("D", [128, 8], F32)
    return T


def hgrn_P(K, C, d, scr, T):
    P, sb, ps = K.P, K.sb, K.ps
    xT3 = d["xT"].rearrange("(c p) t -> p c t", p=128)
    hT = sb("hT", [128, 8, NT], BF16)
    hTs = [Buf(f"hT{s}", hT.t[:, :, s * SW:(s + 1) * SW]) for s in range(NS)]
    xst = [sb("xst", [128, 8, SW], F32) for _ in range(2)]
    sq = sb("sq", [128, 8, SW], BF16)
    rstd = sb("rstd", [128, SW], F32)
    ps_n = ps(0, [128, SW])
    for s in range(NS):
        xs = xst[s % 2]
        P.dma("sp", xs[:], xT3[:, :, s * SW:(s + 1) * SW], writes=[xs])
        rmsnorm_fm(P, C, xs, C.gmix, hTs[s], hTs[s], sq, ps_n, rstd)

    wi = sb("wi", [128, 8, 1024], BF16)
    for h in range(8):
        P.dma("pool", wi[:, :, h * 128:(h + 1) * 128], d["w_in_r"][16 + h], pwrites=[wi])
    ps_v = [ps(1, [64, 512]), ps(2, [64, 512])]
    vst = [sb("vst", [64, CPS, 1024], BF16) for _ in range(2)]
    v3 = scr["v"].rearrange("(c s) j -> s c j", s=64)
    for s in range(NS):
        vs = vst[s % 2]
        for c in range(CPS):
            cg = s * CPS + c
            for hf in range(2):
                pv = ps_v[hf]
                for m in range(8):
                    P.op("pe", lambda e, m=m, cg=cg, hf=hf, pv=pv: e.matmul(
                        pv[:], lhsT=hT[:, m, cg * CH:(cg + 1) * CH], rhs=wi[:, m, hf * 512:(hf + 1) * 512],
                        start=(m == 0), stop=(m == 7)), reads=[hTs[s], wi], writes=[pv])
                if hf == 0:
                    P.op("act", lambda e, c=c, hf=hf, pv=pv, vs=vs: e.copy(out=vs[:, c, hf * 512:(hf + 1) * 512], in_=pv[:]),
                         reads=[pv], pwrites=[vs])
                else:
                    P.op("dve", lambda e, c=c, hf=hf, pv=pv, vs=vs: e.tensor_copy(out=vs[:, c, hf * 512:(hf + 1) * 512], in_=pv[:]),
                         reads=[pv], pwrites=[vs])
        P.dma("sp", v3[:, s * CPS:(s + 1) * CPS, :], vs[:], reads=[vs], pwrites=[scr["v_tok"]])

    wq = [sb("wq", [128, 8, 128], BF16) for _ in range(2)]
    wf = [sb("wf", [128, 8, 128], BF16) for _ in range(2)]
    wg = [sb("wg", [128, 8, 128], BF16) for _ in range(2)]
    ps_f = ps(3, [128, SW])
    ps_q = ps(4, [128, SW])
    ps_g = ps(5, [128, SW])
    ps_t = ps(0, [64, CPS, 128], BF16)
    sig = sb("sig", [128, SW], F32)
    logf = sb("logf", [128, SW], F32)
    nsig = sb("nsig", [128, SW], F32)
    b3 = sb("b3", [128, CPS, CH], F32)
    bm = sb("bm", [128, CPS, CH], F32)
    Ep = sb("Ep", [128, SW], F32)
    Em = sb("Em", [128, SW], F32)
    sqf = sb("sqf", [128, SW], F32)
    qst = [sb("qst", [128, SW], BF16) for _ in range(2)]
    kst = [sb("kst", [128, SW], BF16) for _ in range(2)]
    gst = [sb("gst", [128, SW], BF16) for _ in range(2)]
    ktst = [sb("ktst", [64, CPS, 128], BF16) for _ in range(2)]
    b_flat = b3.t.rearrange("p c t -> p (c t)")
    bm_flat = bm.t.rearrange("p c t -> p (c t)")
    kt3 = scr["kt"].rearrange("(c s) j -> s c j", s=64)
    it = 0
    for h in range(8):
        wqh, wfh, wgh = wq[h % 2], wf[h % 2], wg[h % 2]
        P.dma("pool", wfh[:], d["w_in_r"][8 + h], writes=[wfh])
        P.dma("pool", wqh[:], d["w_in_r"][0 + h], writes=[wqh])
        P.dma("pool", wgh[:], d["w_in_r"][24 + h], writes=[wgh])
        for s in range(NS):
            hs = hTs[s]
            sl = slice(s * SW, (s + 1) * SW)
            for (pp, ww) in ((ps_f, wfh), (ps_q, wqh), (ps_g, wgh)):
                for m in range(8):
                    P.op("pe", lambda e, m=m, pp=pp, ww=ww, sl=sl: e.matmul(
                        pp[:], lhsT=ww[:, m, :], rhs=hT[:, m, sl], start=(m == 0), stop=(m == 7)),
                        reads=[ww, hs], writes=[pp])
            q_o, k_o, g_o, kt_o = qst[it % 2], kst[it % 2], gst[it % 2], ktst[it % 2]
            it += 1
            P.op("act", lambda e: e.activation(out=sig[:], in_=ps_f[:], func=AF.Sigmoid), reads=[ps_f], writes=[sig])
            P.op("act", lambda e, h=h: e.activation(out=logf[:], in_=sig[:], func=AF.Ln, scale=C.oml[:, h:h + 1],
                                                    bias=C.lb[:, h:h + 1]), reads=[sig, C.oml, C.lb], writes=[logf])
            P.op("dve", lambda e: e.tensor_scalar(out=nsig[:], in0=sig[:], scalar1=-1.0, scalar2=1.0, op0=ALU.mult,
                                                  op1=ALU.add), reads=[sig], writes=[nsig])
            P.op("dve", lambda e: e.tensor_tensor_scan(out=b_flat, data0=C.resetm[:], data1=logf[:], initial=0.0,
                                                       op0=ALU.mult, op1=ALU.add), reads=[C.resetm, logf], writes=[b3])
            P.op("dve", lambda e, h=h, s=s: e.tensor_copy(out=T.bl[:, h, s * CPS:(s + 1) * CPS], in_=b3[:, :, CH - 1]),
                 reads=[b3], pwrites=[T.bl])
            P.op("dve", lambda e, h=h, s=s: e.tensor_copy(out=T.bmid[:, h, s * CPS:(s + 1) * CPS], in_=b3[:, :, CH // 2 - 1]),
                 reads=[b3], pwrites=[T.bmid])
            P.op("dve", lambda e: e.tensor_tensor(out=bm[:], in0=b3[:], in1=b3[:, :, CH // 2 - 1:CH // 2].to_broadcast([128, CPS, CH]),
                                                  op=ALU.subtract), reads=[b3], writes=[bm])
            P.op("act", lambda e: e.activation(out=Ep[:], in_=bm_flat, func=AF.Exp), reads=[bm], writes=[Ep])
            P.op("act", lambda e: e.activation(out=Em[:], in_=bm_flat, func=AF.Exp, scale=-1.0), reads=[bm], writes=[Em])
            P.op("act", lambda e: e.activation(out=sqf[:], in_=ps_q[:], func=AF.Silu), reads=[ps_q], writes=[sqf])
            P.op("act", lambda e, g_o=g_o: e.activation(out=g_o[:], in_=ps_g[:], func=AF.Silu), reads=[ps_g], writes=[g_o])
            P.op("dve", lambda e, q_o=q_o: e.tensor_tensor(out=q_o[:], in0=sqf[:], in1=Ep[:], op=ALU.mult),
                 reads=[sqf, Ep], writes=[q_o])
            P.op("dve", lambda e, k_o=k_o, h=h: e.scalar_tensor_tensor(out=k_o[:], in0=nsig[:], scalar=C.oml[:, h:h + 1],
                                                                       in1=Em[:], op0=ALU.mult, op1=ALU.mult),
                 reads=[nsig, C.oml, Em], writes=[k_o])
            hsl = slice(h * 128, (h + 1) * 128)
            P.dma("sp", scr["qT"][hsl, sl], q_o[:], reads=[q_o], pwrites=[scr["qT_tok"]])
            P.dma("sp", scr["kT"][hsl, sl], k_o[:], reads=[k_o], pwrites=[scr["kT_tok"]])
            P.dma("sp", scr["gT"][hsl, sl], g_o[:], reads=[g_o], pwrites=[scr["gT_tok"]])
            for c in range(CPS):
                P.op("pe", lambda e, c=c, k_o=k_o: e.transpose(out=ps_t[:, c, :], in_=k_o[:, c * CH:(c + 1) * CH],
                                                              identity=C.ident[:]), reads=[k_o, C.ident], writes=[ps_t])
            P.op("act", lambda e, kt_o=kt_o: e.copy(out=kt_o[:], in_=ps_t[:]), reads=[ps_t], writes=[kt_o])
            P.dma("sp", kt3[:, s * CPS:(s + 1) * CPS, hsl], kt_o[:], reads=[kt_o], pwrites=[scr["kt_tok"]])
    P.op("act", lambda e: e.activation(out=T.e1[:], in_=T.bl[:], func=AF.Exp), reads=[T.bl], writes=[T.e1])
    P.op("act", lambda e: e.activation(out=T.em[:], in_=T.bmid[:], func=AF.Exp), reads=[T.bmid], writes=[T.em])
    P.op("dve", lambda e: e.tensor_sub(out=T.e2[:], in0=T.bl[:], in1=T.bmid[:]), reads=[T.bl, T.bmid], writes=[T.e2])
    P.op("act", lambda e: e.activation(out=T.e2[:], in_=T.e2[:], func=AF.Exp), reads=[T.e2], writes=[T.e2])
    P.op("dve", lambda e: e.reduce_sum(out=T.D[:], in_=T.bl[:], axis=AX.X), reads=[T.bl], writes=[T.D])
    P.op("act", lambda e: e.activation(out=T.D[:], in_=T.D[:], func=AF.Exp), reads=[T.D], writes=[T.D])


def hgrn_R(K, C, d, scr, T, full, S):
    P, sb, ps = K.P, K.sb, K.ps
    q3 = scr["qT"].rearrange("(h p) t -> p h t", p=128)
    k3 = scr["kT"].rearrange("(h p) t -> p h t", p=128)
    g3 = scr["gT"].rearrange("(h p) t -> p h t", p=128)
    v3 = scr["v"].rearrange("(c s) j -> s c j", s=64)
    kt3 = scr["kt"].rearrange("(c s) j -> s c j", s=64)
    v_sb = [sb("v_sb", [64, CPS, 1024], BF16) for _ in range(2)]
    kt_sb = [sb("kt_sb", [64, CPS, 1024], BF16) for _ in range(1)]
    ps_dS = ps(6, [128, 8, 128])
    tmp = sb("tmpS", [128, 8, 128], F32)
    if full:
        q_sb = [sb("q_sb", [128, 8, SW], BF16) for _ in range(2)]
        k_sb = [sb("k_sb", [128, 8, SW], BF16) for _ in range(2)]
        g_sb = [sb("g_sb", [128, 8, SW], BF16) for _ in range(1)]
        ps_sc = [ps(0, [64, 8, 64]), ps(1, [64, 8, 64])]
        ps_o = [ps(2, [128, 8, 64]), ps(3, [128, 8, 64])]
        sc_sb = [sb("sc_sb", [64, 8, 64], BF16) for _ in range(2)]
        Sb = [sb("Sb", [128, 8, 128], BF16) for _ in range(2)]
        oT = sb("oT", [128, 8, SW], F32)
        osq = sb("osq", [128, 8, SW], BF16)
        ogT = sb("ogT", [128, 8, SW], BF16)
        t1 = sb("t1", [128, SW], F32)
        rstd = sb("rstd", [128, SW], F32)
        wout = sb("wout", [128, 8, 1024], BF16)
        P.dma("pool", wout[:], d["w_out_r"], writes=[wout])
        ps_y = [ps(4, [128, SW]), ps(5, [128, SW])]
        ps_n = ps(4, [128, SW])
        xT3 = d["xT"].rearrange("(c p) t -> p c t", p=128)
        x1T3 = d["x1T"].rearrange("(c p) t -> p c t", p=128)
        xst = [sb("xst", [128, 8, SW], F32) for _ in range(1)]
    for s in range(NS):
        sl = slice(s * SW, (s + 1) * SW)
        vs, kts = v_sb[s % 2], kt_sb[0]
        P.dma("sp", vs[:], v3[:, s * CPS:(s + 1) * CPS, :], reads=[scr["v_tok"]], writes=[vs])
        P.dma("sp", kts[:], kt3[:, s * CPS:(s + 1) * CPS, :], reads=[scr["kt_tok"]], writes=[kts])
        if full:
            qs, ks, gs = q_sb[s % 2], k_sb[s % 2], g_sb[0]
            P.dma("sp", qs[:], q3[:, :, sl], reads=[scr["qT_tok"]], writes=[qs])
            P.dma("sp", ks[:], k3[:, :, sl], reads=[scr["kT_tok"]], writes=[ks])
            P.dma("sp", gs[:], g3[:, :, sl], reads=[scr["gT_tok"]], writes=[gs])
            xs = xst[0]
            P.dma("sp", xs[:], xT3[:, :, sl], writes=[xs])
        for c in range(CPS):
            cg = s * CPS + c
            csl = slice(c * CH, (c + 1) * CH)
            if full:
                psc, pso, scb, Sbb = ps_sc[cg % 2], ps_o[cg % 2], sc_sb[cg % 2], Sb[cg % 2]
                for h in range(8):
                    P.op("pe", lambda e, h=h, psc=psc, ks=ks, qs=qs, csl=csl: e.matmul(
                        psc[:, h, :], lhsT=ks[:, h, csl], rhs=qs[:, h, csl], start=True, stop=True),
                        reads=[ks, qs], writes=[psc])
                P.op("dve", lambda e, psc=psc, scb=scb: e.tensor_tensor(out=scb[:], in0=psc[:], in1=C.maskc[:], op=ALU.mult),
                     reads=[psc, C.maskc], writes=[scb])
                P.op("pool", lambda e, Sbb=Sbb, cg=cg: e.tensor_tensor(
                    out=Sbb[:], in0=S[:], in1=T.em[:, :, cg:cg + 1].to_broadcast([128, 8, 128]), op=ALU.mult),
                    reads=[S, T.em], writes=[Sbb])
                for h in range(8):
                    hsl = slice(h * 128, (h + 1) * 128)
                    P.op("pe", lambda e, h=h, pso=pso, Sbb=Sbb, qs=qs, csl=csl: e.matmul(
                        pso[:, h, :], lhsT=Sbb[:, h, :], rhs=qs[:, h, csl], start=True, stop=False),
                        reads=[Sbb, qs], writes=[pso])
                    P.op("pe", lambda e, h=h, pso=pso, vs=vs, scb=scb, c=c, hsl=hsl: e.matmul(
                        pso[:, h, :], lhsT=vs[:, c, hsl], rhs=scb[:, h, :], start=False, stop=True),
                        reads=[vs, scb], writes=[pso])
                P.op("act", lambda e, pso=pso, csl=csl: e.copy(out=oT[:, :, csl], in_=pso[:]), reads=[pso], pwrites=[oT])
            for h in range(8):
                hsl = slice(h * 128, (h + 1) * 128)
                P.op("pe", lambda e, h=h, kts=kts, vs=vs, c=c, hsl=hsl: e.matmul(
                    ps_dS[:, h, :], lhsT=kts[:, c, hsl], rhs=vs[:, c, hsl], start=True, stop=True),
                    reads=[kts, vs], writes=[ps_dS])
            P.op("dve", lambda e, cg=cg: e.tensor_tensor(
                out=tmp[:], in0=ps_dS[:], in1=T.e2[:, :, cg:cg + 1].to_broadcast([128, 8, 128]), op=ALU.mult),
                reads=[ps_dS, T.e2], writes=[tmp])
            P.op("pool", lambda e, cg=cg: e.tensor_tensor(
                out=S[:], in0=S[:], in1=T.e1[:, :, cg:cg + 1].to_broadcast([128, 8, 128]), op=ALU.mult),
                reads=[S, T.e1], writes=[S])
            P.op("dve", lambda e: e.tensor_tensor(out=S[:], in0=S[:], in1=tmp[:], op=ALU.add), reads=[S, tmp], writes=[S])
        if full:
            P.op("act", lambda e: e.activation(out=osq[:], in_=oT[:], func=AF.Square), reads=[oT], writes=[osq])
            for h in range(8):
                P.op("pe", lambda e, h=h: e.matmul(ps_n[:], lhsT=C.ones[:], rhs=osq[:, h, :], start=(h == 0), stop=(h == 7)),
                     reads=[C.ones, osq], writes=[ps_n])
            P.op("act", lambda e: e.activation(out=rstd[:], in_=ps_n[:], func=AF.Sqrt, scale=1.0 / 1024, bias=C.eps[:]),
                 reads=[ps_n, C.eps], writes=[rstd])
            P.op("dve", lambda e: e.reciprocal(out=rstd[:], in_=rstd[:]), reads=[rstd], writes=[rstd])
            for h in range(8):
                P.op("dve", lambda e, h=h: e.scalar_tensor_tensor(out=t1[:], in0=oT[:, h, :], scalar=C.gn[:, h:h + 1],
                                                                  in1=rstd[:], op0=ALU.mult, op1=ALU.mult),
                     reads=[oT, C.gn, rstd], writes=[t1])
                P.op("dve", lambda e, h=h, gs=gs: e.tensor_tensor(out=ogT[:, h, :], in0=t1[:], in1=gs[:, h, :], op=ALU.mult),
                     reads=[t1, gs], pwrites=[ogT])
            for j in range(8):
                py = ps_y[j % 2]
                for h in range(8):
                    P.op("pe", lambda e, h=h, j=j, py=py: e.matmul(py[:], lhsT=wout[:, h, j * 128:(j + 1) * 128],
                                                                  rhs=ogT[:, h, :], start=(h == 0), stop=(h == 7)),
                         reads=[wout, ogT], writes=[py])
                P.op("dve", lambda e, j=j, py=py, xs=xs: e.tensor_tensor(out=xs[:, j, :], in0=py[:], in1=xs[:, j, :], op=ALU.add),
                     reads=[py, xs], pwrites=[xs])
            P.dma("sp", x1T3[:, :, sl], xs[:], reads=[xs], pwrites=[d["x1T_tok"]])
            if s == NS - 1 and "halo_out" in d:
                P.dma("sp", d["halo_out"].rearrange("p (c t) -> p c t", c=8), xs[:, :, SW - 2:SW], reads=[xs],
                      writes=[d["halo_out_tok"]])


NFF = 22


def ffn_layer(K, C, d, li, xin, xin_tok, xout, xout_tok, halo_all, halo_tok, final_norm=None):
    P, sb, ps = K.P, K.sb, K.ps
    K.phase()
    xT3 = xin.rearrange("(c p) t -> p c t", p=128)
    gf = sb("gf", [128, 8], F32)
    P.dma("sp", gf[:], d[f"gffn{li}"], writes=[gf])
    convp = sb("convp", [128, 2 * NFF, 4], F32)
    P.dma("sp", convp[:], d[f"convp{li}"], writes=[convp])
    nf = sb("nf", [128, 1], F32)
    P.dma("sp", nf[:], d["notfirst"], writes=[nf])
    aT = sb("aT", [128, NFF, NT], BF16)
    aTs = [Buf(f"aT{s}", aT.t[:, :, s * SW:(s + 1) * SW]) for s in range(NS)]
    mark = K.A.cur
    hT = sb("h2T", [128, 8, NT], BF16)
    hTs = [Buf(f"h2T{s}", hT.t[:, :, s * SW:(s + 1) * SW]) for s in range(NS)]
    xst = [sb("xst", [128, 8, SW], F32) for _ in range(1)]
    sq = sb("sq", [128, 8, SW], BF16)
    rstd = sb("rstd", [128, SW], F32)
    ps_n = ps(0, [128, SW])
    for s in range(NS):
        xs = xst[0]
        P.dma("sp", xs[:], xT3[:, :, s * SW:(s + 1) * SW], reads=[xin_tok], writes=[xs])
        rmsnorm_fm(P, C, xs, gf, hTs[s], hTs[s], sq, ps_n, rstd)
    xh = sb("xh", [128, 8, 2], F32)

    def dyn(e):
        pid = P.pid(e)
        prev = (pid + 7) % 8
        return e.dma_start(out=xh[:], in_=halo_all[bass.ds(prev * 128, 128), :].rearrange("p (c t) -> p c t", c=8))
    P._add("sp", dyn, [halo_tok], [xh], (), True)
    sqh = sb("sqh", [128, 8, 2], BF16)
    rsh = sb("rsh", [128, 2], F32)
    hh = sb("hh", [128, 8, 2], BF16)
    ps_h = ps(1, [128, 2])
    P.op("act", lambda e: e.activation(out=sqh[:], in_=xh[:], func=AF.Square), reads=[xh], writes=[sqh])
    for c in range(8):
        P.op("pe", lambda e, c=c: e.matmul(ps_h[:], lhsT=C.ones[:], rhs=sqh[:, c, :], start=(c == 0), stop=(c == 7)),
             reads=[C.ones, sqh], writes=[ps_h])
    P.op("act", lambda e: e.activation(out=rsh[:], in_=ps_h[:], func=AF.Sqrt, scale=1.0 / 1024, bias=C.eps[:]),
         reads=[ps_h, C.eps], writes=[rsh])
    P.op("dve", lambda e: e.reciprocal(out=rsh[:], in_=rsh[:]), reads=[rsh], writes=[rsh])
    P.op("dve", lambda e: e.tensor_scalar(out=rsh[:], in0=rsh[:], scalar1=nf[:, 0:1], scalar2=None, op0=ALU.mult),
         reads=[rsh, nf], writes=[rsh])
    for c in range(8):
        P.op("dve", lambda e, c=c: e.scalar_tensor_tensor(out=hh[:, c, :], in0=xh[:, c, :], scalar=gf[:, c:c + 1],
                                                          in1=rsh[:], op0=ALU.mult, op1=ALU.mult),
             reads=[xh, gf, rsh], pwrites=[hh])

    wu = [[sb("wu", [128, 8, 128], BF16) for _ in range(2)] for _ in range(2)]
    ug = [sb("ug", [128, SW + 2], F32) for _ in range(2)]
    uv = [sb("uv", [128, SW + 2], F32) for _ in range(2)]
    ag = [sb("ag", [128, SW], F32) for _ in range(2)]
    av = [sb("av", [128, SW], F32) for _ in range(2)]
    sg = [sb("sg", [128, SW], F32) for _ in range(2)]
    ps_g = [ps(2, [128, SW]), ps(3, [128, SW])]
    ps_v = [ps(4, [128, SW]), ps(5, [128, SW])]
    ps_hh = ps(1, [128, 2, 2])
    it = 0
    for c in range(NFF):
        wg_, wv_ = wu[c % 2]
        P.dma("pool", wg_[:], d[f"w_up_r{li}"][c], writes=[wg_])
        P.dma("pool", wv_[:], d[f"w_up_r{li}"][c + NFF], writes=[wv_])
        for s in range(NS):
            sl = slice(s * SW, (s + 1) * SW)
            cur, prv = it % 2, (it + 1) % 2
            it += 1
            pg, pv = ps_g[cur], ps_v[cur]
            ugc, uvc, agc, avc, sgc = ug[cur], uv[cur], ag[cur], av[cur], sg[cur]
            for (pp, ww) in ((pg, wg_), (pv, wv_)):
                for m in range(8):
                    P.op("pe", lambda e, m=m, pp=pp, ww=ww, sl=sl: e.matmul(
                        pp[:], lhsT=ww[:, m, :], rhs=hT[:, m, sl], start=(m == 0), stop=(m == 7)),
                        reads=[ww, hTs[s]], writes=[pp])
            if s == 0:
                for gi, ww in ((0, wg_), (1, wv_)):
                    for m in range(8):
                        P.op("pe", lambda e, m=m, gi=gi, ww=ww: e.matmul(
                            ps_hh[:, gi, :], lhsT=ww[:, m, :], rhs=hh[:, m, :], start=(m == 0), stop=(m == 7)),
                            reads=[ww, hh], writes=[ps_hh])
                P.op("dve", lambda e, ugc=ugc: e.tensor_copy(out=ugc[:, 0:2], in_=ps_hh[:, 0, :]), reads=[ps_hh], pwrites=[ugc])
                P.op("dve", lambda e, uvc=uvc: e.tensor_copy(out=uvc[:, 0:2], in_=ps_hh[:, 1, :]), reads=[ps_hh], pwrites=[uvc])
            else:
                P.op("dve", lambda e, ugc=ugc, p_=ug[prv]: e.tensor_copy(out=ugc[:, 0:2], in_=p_[:, SW:SW + 2]),
                     reads=[ug[prv]], pwrites=[ugc])
                P.op("pool", lambda e, uvc=uvc, p_=uv[prv]: e.tensor_copy(out=uvc[:, 0:2], in_=p_[:, SW:SW + 2]),
                     reads=[uv[prv]], pwrites=[uvc])
            cg, cv = c, c + NFF
            P.op("act", lambda e, ugc=ugc, pg=pg: e.copy(out=ugc[:, 2:SW + 2], in_=pg[:]), reads=[pg], pwrites=[ugc])
            P.op("act", lambda e, agc=agc, pg=pg, cg=cg: e.activation(out=agc[:], in_=pg[:], func=AF.Identity,
                                                                      scale=convp[:, cg, 2:3], bias=convp[:, cg, 3:4]),
                 reads=[pg, convp], writes=[agc])
            P.op("act", lambda e, uvc=uvc, pv=pv: e.copy(out=uvc[:, 2:SW + 2], in_=pv[:]), reads=[pv], pwrites=[uvc])
            P.op("act", lambda e, avc=avc, pv=pv, cv=cv: e.activation(out=avc[:], in_=pv[:], func=AF.Identity,
                                                                      scale=convp[:, cv, 2:3], bias=convp[:, cv, 3:4]),
                 reads=[pv, convp], writes=[avc])
            P.op("dve", lambda e, agc=agc, ugc=ugc, cg=cg: e.scalar_tensor_tensor(
                out=agc[:], in0=ugc[:, 1:SW + 1], scalar=convp[:, cg, 1:2], in1=agc[:], op0=ALU.mult, op1=ALU.add),
                reads=[ugc, convp, agc], writes=[agc])
            P.op("dve", lambda e, agc=agc, ugc=ugc, cg=cg: e.scalar_tensor_tensor(
                out=agc[:], in0=ugc[:, 0:SW], scalar=convp[:, cg, 0:1], in1=agc[:], op0=ALU.mult, op1=ALU.add),
                reads=[ugc, convp, agc], writes=[agc])
            P.op("dve", lambda e, avc=avc, uvc=uvc, cv=cv: e.scalar_tensor_tensor(
                out=avc[:], in0=uvc[:, 1:SW + 1], scalar=convp[:, cv, 1:2], in1=avc[:], op0=ALU.mult, op1=ALU.add),
                reads=[uvc, convp, avc], writes=[avc])
            P.op("dve", lambda e, avc=avc, uvc=uvc, cv=cv: e.scalar_tensor_tensor(
                out=avc[:], in0=uvc[:, 0:SW], scalar=convp[:, cv, 0:1], in1=avc[:], op0=ALU.mult, op1=ALU.add),
                reads=[uvc, convp, avc], writes=[avc])
            P.op("act", lambda e, sgc=sgc, agc=agc: e.activation(out=sgc[:], in_=agc[:], func=AF.Silu), reads=[agc], writes=[sgc])
            P.op("dve", lambda e, sgc=sgc, avc=avc, c=c, sl=sl: e.tensor_tensor(out=aT[:, c, sl], in0=sgc[:], in1=avc[:], op=ALU.mult),
                 reads=[sgc, avc], pwrites=[aTs[s]])

    P.barrier()
    K.A.cur = mark
    wd = [sb("wd", [128, NFF, 128], BF16) for _ in range(2)]
    xj = [sb("xj", [128, SW], F32) for _ in range(3)]
    ps_y = [ps(0, [128, SW]), ps(1, [128, SW])]
    xin3 = xin.rearrange("(c p) t -> p c t", p=128)
    xout3 = xout.rearrange("(c p) t -> p c t", p=128)
    it = 0
    for j in range(8):
        wdj = wd[j % 2]
        P.dma("pool", wdj[:], d[f"w_down_r{li}"][j], writes=[wdj])
        for s in range(NS):
            sl = slice(s * SW, (s + 1) * SW)
            py = ps_y[it % 2]
            xs = xj[it % 3]
            it += 1
            P.dma("sp", xs[:], xin3[:, j, sl], reads=[xin_tok], writes=[xs])
            for c in range(NFF):
                P.op("pe", lambda e, c=c, py=py, wdj=wdj, sl=sl: e.matmul(
                    py[:], lhsT=wdj[:, c, :], rhs=aT[:, c, sl], start=(c == 0), stop=(c == NFF - 1)),
                    reads=[wdj, aTs[s]], writes=[py])
            P.op("dve", lambda e, py=py, xs=xs: e.tensor_tensor(out=xs[:], in0=py[:], in1=xs[:], op=ALU.add),
                 reads=[py, xs], writes=[xs])
            P.dma("sp", xout3[:, j, sl], xs[:], reads=[xs], pwrites=[xout_tok])


def final_norm(K, C, d, xin, xin_tok, out, out_tok):
    P, sb, ps = K.P, K.sb, K.ps
    K.phase()
    gfin = sb("gfin", [128, 8], F32)
    P.dma("sp", gfin[:], d["gfinal"], writes=[gfin])
    xT3 = xin.rearrange("(c p) t -> p c t", p=128)
    o3 = out.rearrange("(c p) t -> p c t", p=128)
    xst = [sb("xst", [128, 8, SW], F32) for _ in range(2)]
    ost = [sb("ost", [128, 8, SW], F32) for _ in range(2)]
    sq = sb("sq", [128, 8, SW], BF16)
    rstd = sb("rstd", [128, SW], F32)
    ps_n = ps(0, [128, SW])
    for s in range(NS):
        xs, os_ = xst[s % 2], ost[s % 2]
        P.dma("sp", xs[:], xT3[:, :, s * SW:(s + 1) * SW], reads=[xin_tok], writes=[xs])
        rmsnorm_fm(P, C, xs, gfin, os_, os_, sq, ps_n, rstd)
        P.dma("sp", o3[:, :, s * SW:(s + 1) * SW], os_[:], reads=[os_], pwrites=[out_tok])


import math

T_ALL = 16384
NB = T_ALL // 128
NQS = T_ALL // SW
LAM_INIT = 0.8 - 0.6 * math.exp(-0.3 * 1)
NEG = -30000.0
GLEN = 1151


def kvq_proj(K, C, d, xin, xin_tok, qkv_in, qkv_tok):
    P, sb, ps = K.P, K.sb, K.ps
    K.phase()
    xT3 = xin.rearrange("(c p) t -> p c t", p=128)
    gkv = sb("gkv", [128, 8], F32)
    gq = sb("gq", [128, 8], F32)
    P.dma("sp", gkv[:], d["gkv"], writes=[gkv])
    P.dma("sp", gq[:], d["gmix1"], writes=[gq])
    hk = sb("hk", [128, 8, NT], BF16)
    hq = sb("hq", [128, 8, NT], BF16)
    hks = [Buf(f"hk{s}", hk.t[:, :, s * SW:(s + 1) * SW]) for s in range(NS)]
    hqs = [Buf(f"hq{s}", hq.t[:, :, s * SW:(s + 1) * SW]) for s in range(NS)]
    xst = sb("xst", [128, 8, SW], F32)
    sq = sb("sq", [128, 8, SW], BF16)
    rstd = sb("rstd", [128, SW], F32)
    ps_n = ps(0, [128, SW])
    for s in range(NS):
        P.dma("sp", xst[:], xT3[:, :, s * SW:(s + 1) * SW], reads=[xin_tok], writes=[xst])
        rmsnorm_fm(P, C, xst, gkv, hks[s], hks[s], sq, ps_n, rstd)
        for c in range(8):
            P.op("dve", lambda e, c=c, s=s: e.scalar_tensor_tensor(out=hqs[s][:, c, :], in0=xst[:, c, :], scalar=gq[:, c:c + 1],
                                                                    in1=rstd[:], op0=ALU.mult, op1=ALU.mult),
                 reads=[xst, gq, rstd], pwrites=[hqs[s]])
    wk = [sb("wk", [128, 8, 128], BF16) for _ in range(2)]
    wq = [sb("wq", [128, 8, 128], BF16) for _ in range(2)]
    kst = [sb("kst", [128, NT], BF16) for _ in range(2)]
    qst = [sb("qst", [128, NT], BF16) for _ in range(2)]
    psk = [ps(1, [128, SW]), ps(2, [128, SW])]
    psq = [ps(3, [128, SW]), ps(4, [128, SW])]
    it = 0
    for h in range(8):
        wkh, wqh, ks_, qs_ = wk[h % 2], wq[h % 2], kst[h % 2], qst[h % 2]
        P.dma("pool", wkh[:], d["w_k_r"][h], writes=[wkh])
        P.dma("pool", wqh[:], d["w_q_r"][h], writes=[wqh])
        for s in range(NS):
            sl = slice(s * SW, (s + 1) * SW)
            pk, pq = psk[it % 2], psq[it % 2]
            it += 1
            for m in range(8):
                P.op("pe", lambda e, m=m, pk=pk, wkh=wkh, sl=sl: e.matmul(pk[:], lhsT=wkh[:, m, :], rhs=hk[:, m, sl],
                                                                          start=(m == 0), stop=(m == 7)),
                     reads=[wkh, hks[s]], writes=[pk])
            for m in range(8):
                P.op("pe", lambda e, m=m, pq=pq, wqh=wqh, sl=sl: e.matmul(pq[:], lhsT=wqh[:, m, :], rhs=hq[:, m, sl],
                                                                          start=(m == 0), stop=(m == 7)),
                     reads=[wqh, hqs[s]], writes=[pq])
            P.op("act", lambda e, pk=pk, ks_=ks_, sl=sl: e.copy(out=ks_[:, sl], in_=pk[:]), reads=[pk], pwrites=[ks_])
            P.op("dve", lambda e, pq=pq, qs_=qs_, sl=sl: e.tensor_scalar(out=qs_[:, sl], in0=pq[:], scalar1=0.125, scalar2=None,
                                                                         op0=ALU.mult), reads=[pq], pwrites=[qs_])
        P.dma("sp", qkv_in[h * 384:h * 384 + 128, :], qs_[:], reads=[qs_], pwrites=[qkv_tok])
        P.dma("sp", qkv_in[h * 384 + 128:h * 384 + 256, :], ks_[:], reads=[ks_], pwrites=[qkv_tok])
    wv = sb("wv", [128, 8, 1024], BF16)
    for h in range(8):
        P.dma("pool", wv[:, :, h * 128:(h + 1) * 128], d["w_v_r"][h], pwrites=[wv])
    vstage = sb("vstage", [128, 8, 16, 128], BF16)
    psv = [ps(1, [128, 4, 128]), ps(2, [128, 4, 128])]
    psv_flat = [ps(1, [128, 512]), ps(2, [128, 512])]
    for tb in range(16):
        s = tb // 4
        for hf in range(2):
            pv = psv_flat[hf]
            for m in range(8):
                P.op("pe", lambda e, m=m, pv=pv, tb=tb, hf=hf: e.matmul(
                    pv[:], lhsT=hk[:, m, tb * 128:(tb + 1) * 128], rhs=wv[:, m, hf * 512:(hf + 1) * 512],
                    start=(m == 0), stop=(m == 7)), reads=[hks[s], wv], writes=[pv])
            if hf == 0:
                P.op("act", lambda e, tb=tb, hf=hf: e.copy(out=vstage[:, hf * 4:(hf + 1) * 4, tb, :], in_=psv[hf][:]),
                     reads=[psv[hf]], pwrites=[vstage])
            else:
                P.op("dve", lambda e, tb=tb, hf=hf: e.tensor_copy(out=vstage[:, hf * 4:(hf + 1) * 4, tb, :], in_=psv[hf][:]),
                     reads=[psv[hf]], pwrites=[vstage])
    for h in range(8):
        P.dma("sp", qkv_in[h * 384 + 256:h * 384 + 384, :], vstage.t[:, h, :, :].rearrange("p b v -> p (b v)"),
              reads=[vstage], pwrites=[qkv_tok])


def attn_core(K, C, d, qkv_all, qkv_all_tok, o_in, o_tok, gvec, gvec_tok):
    P, sb, ps = K.P, K.sb, K.ps
    K.phase()
    QKV = sb("QKV", [128, 3, 8, NT], BF16)
    QT = QKV.alias(QKV.t[:, 0, :, :].rearrange("p r t -> p (r t)"))
    KT = QKV.alias(QKV.t[:, 1, :, :].rearrange("p r t -> p (r t)"))
    VA = sb("VA", [128, NB, 129], BF16)
    P.op("pool", lambda e: e.memset(VA[:], 1.0), writes=[VA])
    q4 = qkv_all.rearrange("(r h x) t -> r h x t", r=8, h=8)
    for r in range(8):
        def fn(e, r=r):
            pid = P.pid(e)
            src = q4[r, bass.ds(pid, 1), :, :].rearrange("o (k p) t -> p (o k) t", k=3)
            return e.dma_start(out=QKV[:, :, r, :], in_=src)
        P._add("act", fn, [qkv_all_tok], (), [QKV], True)
    for r in range(8):
        P.op("dve" if r % 2 == 0 else "pool",
             lambda e, r=r: e.tensor_copy(out=VA[:, r * 16:(r + 1) * 16, 0:128],
                                          in_=QKV[:, 2, r, :].rearrange("p (b v) -> p b v", b=16)),
             reads=[QKV], pwrites=[VA])
    Vreg = QKV.t[:, 2, :, :].rearrange("p r t -> p (r t)")
    P.op("dve", lambda e: e.tensor_copy(out=Vreg[64:128, :], in_=QT[64:128, :]), reads=[QKV], pwrites=[QKV])
    P.op("pool", lambda e: e.memset(Vreg[0:64, :], 0.0), pwrites=[QKV])
    P.op("dve", lambda e: e.memset(QT[64:128, :], 0.0), reads=[QKV], pwrites=[QKV])
    Qz = [QT, QKV.alias(Vreg)]
    lamv = sb("lamv", [128, 4, 64], F32)
    P.dma("sp", lamv[:], d["lamv"].partition_broadcast(128), writes=[lamv])
    lp = sb("lp", [128, 2, 64], F32)
    ls = sb("ls", [128, 2], F32)
    nlam = sb("nlam", [128, 1], F32)
    P.op("dve", lambda e: e.tensor_tensor(out=lp[:, 0, :], in0=lamv[:, 0, :], in1=lamv[:, 1, :], op=ALU.mult), reads=[lamv], pwrites=[lp])
    P.op("dve", lambda e: e.tensor_tensor(out=lp[:, 1, :], in0=lamv[:, 2, :], in1=lamv[:, 3, :], op=ALU.mult), reads=[lamv], pwrites=[lp])
    P.op("dve", lambda e: e.reduce_sum(out=ls[:], in_=lp[:], axis=AX.X), reads=[lp], writes=[ls])
    P.op("act", lambda e: e.activation(out=ls[:], in_=ls[:], func=AF.Exp), reads=[ls], writes=[ls])
    P.op("dve", lambda e: e.tensor_sub(out=nlam[:], in0=ls[:, 1:2], in1=ls[:, 0:1]), reads=[ls], writes=[nlam])
    P.op("dve", lambda e: e.tensor_scalar(out=nlam[:], in0=nlam[:], scalar1=-LAM_INIT, scalar2=None, op0=ALU.add),
         reads=[nlam], writes=[nlam])
    gsub = sb("gsub", [128, 128], F32)
    P.dma("sp", gsub[:], d["subln"].partition_broadcast(128), writes=[gsub])
    P.op("dve", lambda e: e.tensor_scalar(out=gsub[:], in0=gsub[:], scalar1=1.0 - LAM_INIT, scalar2=None, op0=ALU.mult),
         reads=[gsub], writes=[gsub])
    eps128 = C.eps
    relcol = sb("relcol", [32, 1], F32)
    oh = sb("oh", [32, 128], F32)
    P.dma("sp", relcol[:], d["relcol"], writes=[relcol])
    P.dma("sp", oh[:], d["oh"], writes=[oh])
    ohb = sb("ohb", [32, 128], BF16)
    rc_hi = sb("rc_hi", [32, 1], BF16)
    rc_lo = sb("rc_lo", [32, 1], BF16)
    P.op("dve", lambda e: e.tensor_copy(out=ohb[:], in_=oh[:]), reads=[oh], writes=[ohb])
    P.op("dve", lambda e: e.tensor_copy(out=rc_hi[:], in_=relcol[:]), reads=[relcol], writes=[rc_hi])
    P.op("dve", lambda e: e.tensor_tensor(out=rc_lo[:], in0=relcol[:], in1=rc_hi[:], op=ALU.subtract),
         reads=[relcol, rc_hi], writes=[rc_lo])
    ps_g = ps(0, [1, 128])
    gm = sb("gm", [1, 128], F32)
    P.op("pe", lambda e: e.matmul(ps_g[:], lhsT=rc_hi[:], rhs=ohb[:], start=True, stop=False), reads=[rc_hi, ohb], writes=[ps_g])
    P.op("pe", lambda e: e.matmul(ps_g[:], lhsT=rc_lo[:], rhs=ohb[:], start=False, stop=True), reads=[rc_lo, ohb], writes=[ps_g])
    P.op("act", lambda e: e.copy(out=gm[:], in_=ps_g[:]), reads=[ps_g], writes=[gm])
    gv = gvec.ap()
    P.dma("sp", gv, d["gconst"], writes=[gvec_tok])
    P.dma("sp", gv[:, 511:639], gm[:], reads=[gm], writes=[gvec_tok])
    btile = sb("btile", [128, 5, SW], F32)
    antiI = sb("antiI", [128, 128], BF16)
    P.dma("pool", antiI[:], d["antiI"], writes=[antiI])
    hk_t = [sb("hk_t", [128, SW], F32) for _ in range(2)]
    hk_hi = [sb("hk_hi", [128, SW], BF16) for _ in range(2)]
    hk_lo = [sb("hk_lo", [128, SW], BF16) for _ in range(2)]
    for i in range(5):
        src = bass.AP(gvec, 512 - 128 * i, [[1, 128], [1, SW]])
        hkt, hhi, hlo = hk_t[i % 2], hk_hi[i % 2], hk_lo[i % 2]
        P.dma("sp", hkt[:], src, reads=[gvec_tok], writes=[hkt])
        P.op("dve", lambda e, hkt=hkt, hhi=hhi: e.tensor_copy(out=hhi[:], in_=hkt[:]), reads=[hkt], writes=[hhi])
        P.op("dve", lambda e, hkt=hkt, hhi=hhi, hlo=hlo: e.tensor_tensor(out=hlo[:], in0=hkt[:], in1=hhi[:], op=ALU.subtract),
             reads=[hkt, hhi], writes=[hlo])
        pbt = K.pb[i % 2]
        P.op("pe", lambda e, pbt=pbt, hhi=hhi: e.matmul(pbt[:], lhsT=antiI[:], rhs=hhi[:], start=True, stop=False),
             reads=[antiI, hhi], writes=[pbt])
        P.op("pe", lambda e, pbt=pbt, hlo=hlo: e.matmul(pbt[:], lhsT=antiI[:], rhs=hlo[:], start=False, stop=True),
             reads=[antiI, hlo], writes=[pbt])
        P.op("act", lambda e, pbt=pbt, i=i: e.copy(out=btile[:, i, :], in_=pbt[:]), reads=[pbt], pwrites=[btile])

    pT = [[sb("pT", [128, SW], BF16) for _ in range(2)] for _ in range(2)]
    stmp = [sb("stmp", [128, SW], F32) for _ in range(2)]
    psS = [[K.pb[0], K.pb[1]], [K.pb[2], K.pb[3]]]
    accb = [K.pb[4], K.pb[5], K.pb2]

    def acc(m, j):
        i = m * 4 + j
        b = accb[i // 3]
        o = (i % 3) * 129
        return b, b.t[:, o:o + 129]
    ps_tr = K.pb2.alias(K.pb2.t[:, 512:768].bitcast(BF16))
    accS = [sb("accS", [128, 8 * 129], F32) for _ in range(2)]
    o_sb = [sb("o_sb", [128, 128], F32) for _ in range(4)]
    osq = [sb("osq", [128, 128], F32) for _ in range(2)]
    on = [sb("on", [128, 128], BF16) for _ in range(4)]
    sm = [sb("sm", [128, 8], F32) for _ in range(4)]
    oT_st = [sb("oT_st", [128, SW], BF16) for _ in range(2)]
    def emit_qk(qs, kb, maps=(0, 1)):
        i_near = kb - (qs * 4 - 1)
        near = i_near >= 0
        j0 = max(0, kb - qs * 4)
        c0 = j0 * 128
        for m in maps:
            pS = psS[m][kb % 2]
            P.op("pe", lambda e, pS=pS, m=m, kb=kb, qs=qs, c0=c0: e.matmul(
                pS[:, c0:SW], lhsT=KT[:, kb * 128:(kb + 1) * 128], rhs=Qz[m][:, qs * SW + c0:(qs + 1) * SW],
                start=True, stop=True), reads=[KT, QT], writes=[pS])
        for m in maps:
            pS = psS[m][kb % 2]
            pt = pT[m][kb % 2]
            if near:
                st = stmp[m]
                P.op("dve", lambda e, st=st, pS=pS, i_near=i_near, c0=c0: e.tensor_tensor(
                    out=st[:, c0:SW], in0=pS[:, c0:SW], in1=btile[:, i_near, c0:SW], op=ALU.add),
                    reads=[pS, btile], writes=[st])
                P.op("act", lambda e, st=st, pt=pt, c0=c0: e.activation(out=pt[:, c0:SW], in_=st[:, c0:SW], func=AF.Exp),
                     reads=[st], writes=[pt])
            else:
                P.op("act", lambda e, pS=pS, pt=pt: e.activation(out=pt[:], in_=pS[:], func=AF.Exp), reads=[pS], writes=[pt])

    def emit_pv(qs, kb, maps=(0, 1), last=True):
        j0 = max(0, kb - qs * 4)
        for m in maps:
            pt = pT[m][kb % 2]
            for j in range(j0, 4):
                ab, aap = acc(m, j)
                st_ = (kb == 0) and ((m * 4 + j) % 3 == 0)
                P.op("pe", lambda e, aap=aap, pt=pt, j=j, kb=kb, qs=qs, st_=st_: e.matmul(
                    aap, lhsT=pt[:, j * 128:(j + 1) * 128], rhs=VA[:, kb, :], start=st_, stop=(kb == qs * 4 + j)),
                    reads=[pt, VA], pwrites=[ab])
        if last and kb == (qs + 1) * 4 - 1:
            epilogue(qs)

    def epilogue(qs):
        aS = accS[qs % 2]
        P.op("dve", lambda e, aS=aS: e.tensor_copy(out=aS[:, 0:387], in_=accb[0].t[:, 0:387]), reads=[accb[0]], pwrites=[aS])
        P.op("dve", lambda e, aS=aS: e.tensor_copy(out=aS[:, 387:774], in_=accb[1].t[:, 0:387]), reads=[accb[1]], pwrites=[aS])
        P.op("dve", lambda e, aS=aS: e.tensor_copy(out=aS[:, 774:1032], in_=accb[2].t[:, 0:258]), reads=[accb[2]], pwrites=[aS])
        for j in range(4):
            a0 = aS.t[:, j * 129:(j + 1) * 129]
            a1 = aS.t[:, (4 + j) * 129:(5 + j) * 129]
            s_, o_, q_, n_ = sm[j], o_sb[j], osq[j % 2], on[j]
            P.op("dve", lambda e, a0=a0, s_=s_: e.reciprocal(out=s_[:, 0:1], in_=a0[:, 128:129]), reads=[aS], pwrites=[s_])
            P.op("dve", lambda e, a1=a1, s_=s_: e.reciprocal(out=s_[:, 1:2], in_=a1[:, 128:129]), reads=[aS], pwrites=[s_])
            P.op("dve", lambda e, s_=s_: e.tensor_tensor(out=s_[:, 2:3], in0=s_[:, 1:2], in1=nlam[:], op=ALU.mult),
                 reads=[s_, nlam], pwrites=[s_])
            P.op("dve", lambda e, a0=a0, s_=s_, o_=o_: e.tensor_scalar(out=o_[:], in0=a0[:, 0:128], scalar1=s_[:, 0:1], scalar2=None,
                                                                      op0=ALU.mult), reads=[aS, s_], writes=[o_])
            P.op("dve", lambda e, a1=a1, s_=s_, o_=o_: e.scalar_tensor_tensor(out=o_[:], in0=a1[:, 0:128], scalar=s_[:, 2:3], in1=o_[:],
                                                                             op0=ALU.mult, op1=ALU.add), reads=[aS, s_, o_], writes=[o_])
            P.op("pool", lambda e, o_=o_, q_=q_: e.tensor_tensor(out=q_[:], in0=o_[:], in1=o_[:], op=ALU.mult), reads=[o_], writes=[q_])
            P.op("dve", lambda e, s_=s_, q_=q_: e.reduce_sum(out=s_[:, 3:4], in_=q_[:], axis=AX.X), reads=[q_], pwrites=[s_])
            P.op("act", lambda e, s_=s_: e.activation(out=s_[:, 4:5], in_=s_[:, 3:4], func=AF.Ln, scale=1.0 / 128, bias=eps128[:]),
                 reads=[s_, eps128], pwrites=[s_])
            P.op("act", lambda e, s_=s_: e.activation(out=s_[:, 5:6], in_=s_[:, 4:5], func=AF.Exp, scale=-0.5),
                 reads=[s_], pwrites=[s_])
            P.op("dve", lambda e, s_=s_, o_=o_, n_=n_: e.scalar_tensor_tensor(out=n_[:], in0=o_[:], scalar=s_[:, 5:6], in1=gsub[:],
                                                                             op0=ALU.mult, op1=ALU.mult), reads=[o_, s_, gsub], writes=[n_])

    def epilogue_out(qs):
        ost = oT_st[qs % 2]
        for j in range(4):
            P.op("pe", lambda e, j=j: e.transpose(out=ps_tr[:, j * 128:(j + 1) * 128], in_=on[j][:], identity=C.ident[:]),
                 reads=[on[j], C.ident], pwrites=[ps_tr])
        P.op("dve", lambda e, ost=ost: e.tensor_copy(out=ost[:], in_=ps_tr[:]), reads=[ps_tr], writes=[ost])
        P.dma("sp", o_in[:, qs * SW:(qs + 1) * SW], ost[:], reads=[ost], pwrites=[o_tok])

    units = [(qs, kb) for qs in range(NQS) for kb in range((qs + 1) * 4)]
    pending = []
    for idx in range(len(units) + 1):
        for m in range(2):
            if idx < len(units):
                emit_qk(*units[idx], maps=(m,))
            if idx >= 1:
                emit_pv(*units[idx - 1], maps=(m,), last=(m == 1))
        if idx >= 1:
            qs_, kb_ = units[idx - 1]
            if kb_ == (qs_ + 1) * 4 - 1:
                pending.append((idx + 3, qs_))
        while pending and pending[0][0] <= idx:
            epilogue_out(pending.pop(0)[1])
    for _, qs_ in pending:
        epilogue_out(qs_)


def attn_out(K, C, d, o_all, o_all_tok, xin, xin_tok, xout, xout_tok, halo_in, halo_tok):
    P, sb, ps = K.P, K.sb, K.ps
    K.phase()
    og = sb("og", [128, 8, NT], BF16)

    def fn(e):
        pid = P.pid(e)
        return e.dma_start(out=og[:], in_=o_all.rearrange("(h p) t -> p h t", p=128)[:, :, bass.ds(pid * NT, NT)])
    P._add("sp", fn, [o_all_tok], [og], (), True)
    wo = sb("wo", [128, 8, 1024], BF16)
    P.dma("pool", wo[:], d["w_o_r"], writes=[wo])
    xj = [sb("xj", [128, SW], F32) for _ in range(3)]
    ps_y = [ps(0, [128, SW]), ps(1, [128, SW])]
    xin3 = xin.rearrange("(c p) t -> p c t", p=128)
    xout3 = xout.rearrange("(c p) t -> p c t", p=128)
    it = 0
    for j in range(8):
        for s in range(NS):
            sl = slice(s * SW, (s + 1) * SW)
            py = ps_y[it % 2]
            xs = xj[it % 3]
            it += 1
            P.dma("sp", xs[:], xin3[:, j, sl], reads=[xin_tok], writes=[xs])
            for h in range(8):
                P.op("pe", lambda e, h=h, j=j, py=py, sl=sl: e.matmul(py[:], lhsT=wo[:, h, j * 128:(j + 1) * 128], rhs=og[:, h, sl],
                                                                      start=(h == 0), stop=(h == 7)), reads=[wo, og], writes=[py])
            P.op("dve", lambda e, py=py, xs=xs: e.tensor_tensor(out=xs[:], in0=py[:], in1=xs[:], op=ALU.add),
                 reads=[py, xs], writes=[xs])
            P.dma("sp", xout3[:, j, sl], xs[:], reads=[xs], pwrites=[xout_tok])
            if s == NS - 1:
                P.dma("sp", halo_in[:, j * 2:(j + 1) * 2], xs[:, SW - 2:SW], reads=[xs], pwrites=[halo_tok])


import numpy as np
from concourse.bass_utils import run_bass_kernel_spmd

NCORES = 8


def allgather(P, src_h, dst_h, src_tok, dst_tok, rows=None):
    dst = dst_h.ap() if rows is None else dst_h.ap()[0:rows, :]
    P.async_op("pool", lambda e: e.collective_compute("AllGather", ALU.bypass, replica_groups=[list(range(NCORES))],
                                                      ins=[src_h.ap().opt()], outs=[dst.opt()]),
               reads=[src_tok], writes=[dst_tok], inc=1)


IN_SPECS = [
    ("xT", [1024, NT]), ("w_in_r", [32, 128, 8, 128]), ("w_out_r", [128, 8, 1024]), ("ident", [128, 128]),
    ("resetm", [128, SW]), ("maskc", [64, 8, 64]), ("gmix0", [128, 8]), ("gnorm", [128, 8]), ("lbl", [128, 2, 8]),
    ("sel", [128, 8]), ("notfirst", [128, 1]),
    ("gffn0", [128, 8]), ("w_up_r0", [44, 128, 8, 128]), ("convp0", [128, 44, 4]), ("w_down_r0", [8, 128, 22, 128]),
    ("gffn1", [128, 8]), ("w_up_r1", [44, 128, 8, 128]), ("convp1", [128, 44, 4]), ("w_down_r1", [8, 128, 22, 128]),
    ("gkv", [128, 8]), ("gmix1", [128, 8]), ("w_k_r", [8, 128, 8, 128]), ("w_q_r", [8, 128, 8, 128]),
    ("w_v_r", [8, 128, 8, 128]), ("relcol", [32, 1]), ("oh", [32, 128]), ("gconst", [1, GLEN]), ("antiI", [128, 128]), ("lamv", [4, 64]),
    ("subln", [1, 128]), ("w_o_r", [128, 8, 1024]), ("gfinal", [128, 8]),
]


def build(debug=None):
    nc = bass.Bass("TRN2", target_bir_lowering=False)
    K = KB(nc)
    P = K.P
    d = {}
    for name, shape in IN_SPECS:
        d[name] = nc.dram_tensor(name, shape, F32, kind="ExternalInput").ap()
    outT = nc.dram_tensor("outT", [1024, NT], F32, kind="ExternalOutput").ap()
    out_tok = Buf("out")

    def stream(name):
        kind = "ExternalOutput" if debug == name else "Internal"
        return nc.dram_tensor(name, [1024, NT], F32, kind=kind).ap(), Buf(name)
    x1T, x1_tok = stream("x1T")
    x2T, x2_tok = stream("x2T")
    x3T, x3_tok = stream("x3T")
    x4T, x4_tok = stream("x4T")
    scr = {}
    for n in ("qT", "kT", "gT"):
        scr[n] = nc.dram_tensor(n + "_s", [1024, NT], BF16).ap()
        scr[n + "_tok"] = Buf(n)
    for n in ("v", "kt"):
        scr[n] = nc.dram_tensor(n + "_s", [NT, 1024], BF16).ap()
        scr[n + "_tok"] = Buf(n)
    hx_in = nc.dram_tensor("hx_in", [128, 1032], F32)
    hx_all = nc.dram_tensor("hx_all", [NCORES * 128, 1032], F32)
    hx_in_tok, hx_all_tok = Buf("hx_in"), Buf("hx_all")
    halo_in = [nc.dram_tensor(f"halo_in{i}", [128, 16], F32) for i in range(2)]
    halo_all = [nc.dram_tensor(f"halo_all{i}", [NCORES * 128, 16], F32) for i in range(2)]
    halo_in_tok = [Buf("hi0"), Buf("hi1")]
    halo_all_tok = [Buf("ha0"), Buf("ha1")]
    qkv_in = nc.dram_tensor("qkv_in", [3072, NT], BF16)
    qkv_all = nc.dram_tensor("qkv_all", [NCORES * 3072 + 384, NT], BF16)
    qkv_in_tok, qkv_all_tok = Buf("qkv_in"), Buf("qkv_all")
    gvec = nc.dram_tensor("gvec", [1, GLEN], F32)
    gvec_tok = Buf("gvec")
    o_in = nc.dram_tensor("o_in", [128, T_ALL], BF16)
    o_all = nc.dram_tensor("o_all", [NCORES * 128, T_ALL], BF16)
    o_in_tok, o_all_tok = Buf("o_in"), Buf("o_all")

    C = hgrn_consts(K, d)
    T = hgrn_alloc_T(K)
    S = K.sb("S", [128, 8, 128], F32, pers=True)
    Rr = K.sb("Rr", [128, 8, 128], F32, pers=True)
    sel = K.sb("sel", [128, 8], F32, pers=True)
    P.dma("sp", sel[:], d["sel"], writes=[sel])
    P.op("pool", lambda e: e.memset(S[:], 0.0), writes=[S])
    K.A.start_phase()
    hgrn_P(K, C, d, scr, T)
    K.phase()
    hgrn_R(K, C, d, scr, T, False, S)
    P.dma("sp", hx_in.ap()[:, 0:1024], S.t.rearrange("p h v -> p (h v)"), reads=[S], pwrites=[hx_in_tok])
    P.dma("sp", hx_in.ap()[:, 1024:1032], T.D[:], reads=[T.D], pwrites=[hx_in_tok])
    allgather(P, hx_in, hx_all, hx_in_tok, hx_all_tok)
    K.phase()
    Sj = [K.sb("Sj", [128, 1032], F32) for _ in range(2)]
    P.op("pool", lambda e: e.memset(S[:], 0.0), writes=[S])
    P.op("pool", lambda e: e.memset(Rr[:], 0.0), writes=[Rr])
    for j in range(NCORES):
        sj = Sj[j % 2]
        P.dma("sp", sj[:], hx_all.ap()[j * 128:(j + 1) * 128, :], reads=[hx_all_tok], writes=[sj])
        P.op("dve", lambda e, j=j: e.scalar_tensor_tensor(out=S[:], in0=Rr[:], scalar=sel[:, j:j + 1], in1=S[:],
                                                          op0=ALU.mult, op1=ALU.add), reads=[Rr, sel, S], writes=[S])
        if j < NCORES - 1:
            P.op("dve", lambda e, sj=sj: e.tensor_tensor(out=Rr[:], in0=Rr[:],
                                                         in1=sj[:, 1024:1032].unsqueeze(2).to_broadcast([128, 8, 128]),
                                                         op=ALU.mult), reads=[Rr, sj], writes=[Rr])
            P.op("dve", lambda e, sj=sj: e.tensor_tensor(out=Rr[:], in0=Rr[:],
                                                         in1=sj[:, 0:1024].rearrange("p (h v) -> p h v", h=8),
                                                         op=ALU.add), reads=[Rr, sj], writes=[Rr])
    d["x1T"], d["x1T_tok"] = x1T, x1_tok
    d["halo_out"], d["halo_out_tok"] = halo_in[0].ap(), halo_in_tok[0]
    hgrn_R(K, C, d, scr, T, True, S)
    allgather(P, halo_in[0], halo_all[0], halo_in_tok[0], halo_all_tok[0])
    if debug == "x1T":
        P.emit(final_bufs=[x1_tok, halo_all_tok[0]])
        print("nflag", P.nflag, "ndma", P.n_dma, "peak", K.A.peak)
        return nc
    ffn_layer(K, C, d, 0, x1T, x1_tok, x2T, x2_tok, halo_all[0].ap(), halo_all_tok[0])
    if debug == "x2T":
        P.emit(final_bufs=[x2_tok])
        print("nflag", P.nflag, "ndma", P.n_dma, "peak", K.A.peak)
        return nc
    kvq_proj(K, C, d, x2T, x2_tok, qkv_in.ap(), qkv_in_tok)
    allgather(P, qkv_in, qkv_all, qkv_in_tok, qkv_all_tok, rows=NCORES * 3072)
    attn_core(K, C, d, qkv_all.ap()[0:NCORES * 3072, :], qkv_all_tok, o_in.ap(), o_in_tok, gvec, gvec_tok)
    allgather(P, o_in, o_all, o_in_tok, o_all_tok)
    attn_out(K, C, d, o_all.ap(), o_all_tok, x2T, x2_tok, x3T, x3_tok, halo_in[1].ap(), halo_in_tok[1])
    allgather(P, halo_in[1], halo_all[1], halo_in_tok[1], halo_all_tok[1])
    if debug == "x3T":
        P.emit(final_bufs=[x3_tok, halo_all_tok[1]])
        return nc
    ffn_layer(K, C, d, 1, x3T, x3_tok, x4T, x4_tok, halo_all[1].ap(), halo_all_tok[1])
    final_norm(K, C, d, x4T, x4_tok, outT, out_tok)
    P.emit(final_bufs=[out_tok])
    return nc


def t5_bucket_np(rel):
    max_exact = 16
    n = np.maximum(rel, 0)
    log_ratio = (np.log(np.maximum(n, 1).astype(np.float32) / np.float32(max_exact)) / np.float32(math.log(128 / max_exact))).astype(np.float32)
    large = np.minimum(max_exact + (log_ratio * np.float32(32 - max_exact)).astype(np.int32), 31)
    return np.where(n < max_exact, n, large)


def host_inputs(inp, c):
    f = lambda a: np.ascontiguousarray(np.asarray(a, dtype=np.float32))
    pc = lambda v: f(np.asarray(v).reshape(8, 128).T)
    m = {}
    m["xT"] = f(np.asarray(inp["x"])[0, c * NT:(c + 1) * NT, :].T)
    m["w_in_r"] = f(np.asarray(inp["a_w_in"])[0].reshape(8, 128, 32, 128).transpose(2, 1, 0, 3))
    m["w_out_r"] = f(np.asarray(inp["a_w_out"])[0].reshape(8, 128, 1024).transpose(1, 0, 2))
    m["ident"] = np.eye(128, dtype=np.float32)
    r = np.ones((128, SW), np.float32)
    r[:, ::CH] = 0
    m["resetm"] = r
    mk_ = (np.arange(64)[:, None] <= np.arange(64)[None, :]).astype(np.float32)
    m["maskc"] = f(np.broadcast_to(mk_[:, None, :], (64, 8, 64)))
    m["gmix0"] = pc(inp["norm_mix"][0])
    m["gmix1"] = pc(inp["norm_mix"][1])
    m["gnorm"] = pc(inp["a_gnorm"][0])
    m["lbl"] = f(np.asarray(inp["a_lb_logits"]).reshape(2, 8, 128).transpose(2, 0, 1))
    s = np.zeros((128, 8), np.float32)
    s[:, c] = 1.0
    m["sel"] = s
    m["notfirst"] = np.full((128, 1), 0.0 if c == 0 else 1.0, np.float32)
    for li in range(2):
        m[f"gffn{li}"] = pc(inp["norm_ffn"][li])
        m[f"w_up_r{li}"] = f(np.asarray(inp["ffn_w_up"])[li].reshape(8, 128, 44, 128).transpose(2, 1, 0, 3))
        cw = np.asarray(inp["ffn_conv_w"])[li]
        cb = np.asarray(inp["ffn_conv_b"])[li]
        cp = np.concatenate([cw, cb[None]], 0)
        m[f"convp{li}"] = f(cp.reshape(4, 44, 128).transpose(2, 1, 0))
        m[f"w_down_r{li}"] = f(np.asarray(inp["ffn_w_down"])[li].reshape(22, 128, 8, 128).transpose(2, 1, 0, 3))
    m["gkv"] = pc(inp["kv_norm"])
    kvw = np.asarray(inp["kv_w"])
    m["w_k_r"] = f(kvw[:, :1024].reshape(8, 128, 8, 128).transpose(2, 1, 0, 3))
    m["w_v_r"] = f(kvw[:, 1024:].reshape(8, 128, 8, 128).transpose(2, 1, 0, 3))
    m["w_q_r"] = f(np.asarray(inp["b_w_q"])[0].reshape(8, 128, 8, 128).transpose(2, 1, 0, 3))
    m["w_o_r"] = f(np.asarray(inp["b_w_o"])[0].reshape(8, 128, 1024).transpose(1, 0, 2))
    m["relcol"] = f(np.asarray(inp["rel_table"])[:, c:c + 1])
    bk = t5_bucket_np(np.arange(128))
    oh = np.zeros((32, 128), np.float32)
    oh[bk, np.arange(128)] = 1.0
    oh[31, :] -= 1.0
    m["oh"] = oh
    g = np.zeros((1, GLEN), np.float32)
    g[0, :511] = NEG
    m["gconst"] = g
    m["antiI"] = np.ascontiguousarray(np.eye(128, dtype=np.float32)[::-1])
    m["lamv"] = f(np.stack([np.asarray(inp[k])[0] for k in ("b_lam_q1", "b_lam_k1", "b_lam_q2", "b_lam_k2")]))
    m["subln"] = f(np.asarray(inp["b_subln"])[0][None, :])
    m["gfinal"] = pc(inp["final_norm"])
    return m


_NC_CACHE = {}


def kernel(**inputs):
    if "nc" not in _NC_CACHE:
        _NC_CACHE["nc"] = build()
    nc = _NC_CACHE["nc"]
    in_maps = [host_inputs(inputs, c) for c in range(NCORES)]
    res = run_bass_kernel_spmd(nc, in_maps, core_ids=list(range(NCORES)))
    out = np.empty((1, NCORES * NT, 1024), np.float32)
    for c in range(NCORES):
        out[0, c * NT:(c + 1) * NT, :] = res.results[c]["outT"].T
    return out
```

```python
import contextlib
import numpy as np
import concourse.bass as bass
import concourse.mybir as mybir

F32 = mybir.dt.float32
BF16 = mybir.dt.bfloat16
U8 = mybir.dt.uint8
ALU = mybir.AluOpType
AF = mybir.ActivationFunctionType
AX = mybir.AxisListType
DTSZ = {F32: 4, BF16: 2, U8: 1}

ENGS = ("pe", "act", "dve", "pool", "sp")
SEM_ROLL = 2048
DMA_K = 6


class Tok:
    __slots__ = ("writers", "readers", "psum")

    def __init__(self, psum=False):
        self.writers = []
        self.readers = []
        self.psum = psum


class Buf:
    __slots__ = ("name", "t", "tok")

    def __init__(self, name, t=None, tok=None):
        self.name = name
        self.t = t
        self.tok = tok if tok is not None else Tok()

    def __getitem__(self, idx):
        return self.t[idx]

    def alias(self, ap, name=None):
        return Buf(name or self.name, ap, self.tok)


class Op:
    __slots__ = ("eng", "fn", "deps", "is_dma", "flag", "seq", "dma_slot", "dma_val", "name", "inc", "cc", "gidx")


class Arena:
    def __init__(self, nc, nbytes):
        self.big = nc.alloc_sbuf_tensor("arena", [128, nbytes], U8)
        self.size = nbytes
        self.pers = 0
        self.cur = 0
        self.in_phase = False
        self.peak = 0

    def start_phase(self):
        self.in_phase = True
        self.cur = self.pers

    def alloc(self, name, shape, dt, persistent=False):
        p = shape[0]
        n = int(np.prod(shape[1:])) * DTSZ[dt]
        n = (n + 63) // 64 * 64
        if persistent:
            assert not self.in_phase or self.cur == self.pers, "persistent alloc inside a phase"
            off = self.pers
            self.pers += n
            self.cur = self.pers
        else:
            off = self.cur
            self.cur += n
        self.peak = max(self.peak, self.cur)
        assert self.cur <= self.size, f"SBUF arena overflow allocating {name}: {self.cur} > {self.size}"
        ap = self.big[0:p, off:off + n if False else off + int(np.prod(shape[1:])) * DTSZ[dt]].bitcast(dt)
        if len(shape) == 3:
            ap = ap.rearrange("p (a b) -> p a b", a=shape[1])
        elif len(shape) == 4:
            ap = ap.rearrange("p (a b c) -> p a b c", a=shape[1], b=shape[2])
        return Buf(name, ap)


class Prog:
    def __init__(self, nc):
        self.nc = nc
        self.ops = {e: [] for e in ENGS}
        self.n_dma = {e: 0 for e in ENGS}
        self.all_ops = []

    def _add(self, eng, fn, reads, writes, pwrites, is_dma, name=None, inc=None, extra_deps=(), cc=False):
        op = Op()
        op.eng, op.fn, op.is_dma, op.flag, op.seq, op.name = eng, fn, is_dma, False, None, name
        op.inc = inc if inc is not None else (16 if is_dma else 1)
        op.cc = cc
        op.gidx = len(self.all_ops)
        deps = list(extra_deps)
        wr_toks = set()
        for r in reads:
            deps.extend(r.tok.writers)
            if r.tok.psum:
                deps.extend(x for x in r.tok.readers if x.eng != eng)
        for w in list(writes) + list(pwrites):
            wr_toks.add(id(w.tok))
        for w in writes:
            deps.extend(w.tok.writers)
            deps.extend(w.tok.readers)
        for w in pwrites:
            deps.extend(w.tok.readers)
            if w.tok.readers:
                deps.extend(w.tok.writers)
        rw_writers = set()
        for x in list(reads) + list(writes):
            for d in x.tok.writers:
                rw_writers.add(id(d))
        out = []
        seen = set()
        for d in deps:
            if id(d) in seen or d is op:
                continue
            seen.add(id(d))
            if (not d.is_dma) and (not is_dma) and d.eng == eng:
                if eng == "pe":
                    continue
                if id(d) not in rw_writers:
                    continue
            out.append(d)
        latest = {}
        for d in out:
            if not d.is_dma:
                if d.eng not in latest or d.gidx > latest[d.eng].gidx:
                    latest[d.eng] = d
        out = [d for d in out if d.is_dma or latest[d.eng] is d]
        op.deps = out
        for r in reads:
            r.tok.readers.append(op)
        for w in writes:
            w.tok.writers = [op]
            w.tok.readers = []
        for w in pwrites:
            if w.tok.readers:
                w.tok.writers = [op]
                w.tok.readers = []
            else:
                w.tok.writers.append(op)
        if is_dma and not cc:
            i = self.n_dma[eng]
            self.n_dma[eng] += 1
            op.dma_slot = i % DMA_K
            op.dma_val = i // DMA_K + 1
        self.ops[eng].append(op)
        self.all_ops.append(op)
        return op

    def op(self, eng, fn, reads=(), writes=(), pwrites=(), name=None):
        return self._add(eng, fn, reads, writes, pwrites, False, name)

    def dma(self, eng, out, in_, reads=(), writes=(), pwrites=(), **kw):
        def fn(e):
            return e.dma_start(out=out, in_=in_, **kw)
        return self._add(eng, fn, reads, writes, pwrites, True)

    def async_op(self, eng, fn, reads=(), writes=(), inc=1):
        return self._add(eng, fn, reads, writes, (), True, inc=inc, cc=True)

    def barrier(self):
        lasts = []
        for e in ENGS:
            for op in reversed(self.ops[e]):
                if not op.is_dma:
                    lasts.append(op)
                    break
            dm = [op for op in self.ops[e] if op.is_dma and not op.cc][-DMA_K:]
            lasts.extend(dm)
            lasts.extend(op for op in self.ops[e] if op.cc)
        for e in ENGS:
            deps = [d for d in lasts if d.is_dma or d.eng != e]
            self._add(e, lambda en: en.nop(), (), (), (), False, "barrier", extra_deps=deps)

    def pid(self, e):
        k = id(e)
        if k not in self._pids:
            self._pids[k] = e.partition_id()
        return self._pids[k]

    def emit(self, final_bufs=()):
        self._pids = {}
        nc = self.nc
        for op in self.all_ops:
            for d in op.deps:
                d.flag = True
        finals = []
        for b in final_bufs:
            for w in b.tok.writers:
                w.flag = True
                finals.append(w)
        nflag = {}
        for e in ENGS:
            n = 0
            for op in self.ops[e]:
                if op.flag and not op.is_dma:
                    op.seq = n
                    n += 1
            nflag[e] = n
        self.nflag = nflag
        with contextlib.ExitStack() as st:
            csem = {}
            for e in ENGS:
                k = (nflag[e] + SEM_ROLL - 1) // SEM_ROLL
                csem[e] = [st.enter_context(nc.semaphore(f"c_{e}_{i}")) for i in range(max(k, 1))]
            dsem = {}
            for e in ENGS:
                if self.n_dma[e]:
                    dsem[e] = [st.enter_context(nc.semaphore(f"d_{e}_{i}")) for i in range(DMA_K)]
            ccsem = {}
            for op in self.all_ops:
                if op.cc:
                    ccsem[id(op)] = st.enter_context(nc.semaphore(f"cc_{len(ccsem)}"))
            cum = {e: [0] * DMA_K for e in ENGS}
            for e in ENGS:
                for op in self.ops[e]:
                    if op.is_dma and not op.cc:
                        cum[e][op.dma_slot] += op.inc
                        op.dma_val = cum[e][op.dma_slot]
            block = st.enter_context(nc.Block())

            def target(d):
                if d.cc:
                    return ccsem[id(d)], 1
                if d.is_dma:
                    return dsem[d.eng][d.dma_slot], d.dma_val
                return csem[d.eng][d.seq // SEM_ROLL], d.seq % SEM_ROLL + 1

            def run(ename, e):
                waited = {}
                for op in self.ops[ename]:
                    tg = [target(d) for d in op.deps]
                    if op.is_dma and not op.cc and op.dma_val - op.inc > 0:
                        tg.append((dsem[ename][op.dma_slot], op.dma_val - op.inc))
                    for s, v in tg:
                        key = id(s)
                        if waited.get(key, 0) >= v:
                            continue
                        waited[key] = v
                        e.wait_ge(s, v)
                    ins = op.fn(e)
                    if op.cc:
                        ins.then_inc(ccsem[id(op)], 1)
                    elif op.is_dma:
                        ins.then_inc(dsem[ename][op.dma_slot], op.inc)
                    elif op.flag:
                        ins.then_inc(csem[ename][op.seq // SEM_ROLL], 1)
                if ename == "sp":
                    for d in finals:
                        s, v = target(d)
                        e.wait_ge(s, v)

            @block.tensor
            def _(e):
                run("pe", e)

            @block.scalar
            def _(e):
                run("act", e)

            @block.vector
            def _(e):
                run("dve", e)

            @block.gpsimd
            def _(e):
                run("pool", e)

            @block.sync
            def _(e):
                run("sp", e)


NT = 2048
SW = 512
NS = NT // SW
CH = 64
NCH = NT // CH
CPS = SW // CH
EPS = 1e-6


class Ctx:
    pass


class KB:
    def __init__(self, nc, sbuf_bytes=206 * 1024):
        self.nc = nc
        self.P = Prog(nc)
        self.A = Arena(nc, sbuf_bytes)
        self.pb = [Buf(f"pb{i}", nc.alloc_psum_tensor(f"pb{i}", [128, 512], F32), Tok(psum=True)) for i in range(6)]
        self.pb2 = Buf("pb2", nc.alloc_psum_tensor("pbig", [128, 1024], F32), Tok(psum=True))

    def sb(self, name, shape, dt, pers=False):
        return self.A.alloc(name, shape, dt, persistent=pers)

    def ps(self, bank, shape, dt=F32):
        base = self.pb2 if bank == 6 else self.pb[bank]
        p = shape[0]
        n = 1
        for x in shape[1:]:
            n *= x
        nf32 = n * DTSZ[dt] // 4
        ap = base.t[0:p, 0:nf32]
        if dt != F32:
            ap = ap.bitcast(dt)
        if len(shape) == 3:
            ap = ap.rearrange("p (a b) -> p a b", a=shape[1])
        return base.alias(ap)

    def phase(self):
        self.P.barrier()
        self.A.start_phase()


def rmsnorm_fm(P, C, xs, g, out_ap, out_buf, sq, ps, rstd, width=SW):
    P.op("act", lambda e: e.activation(out=sq[:], in_=xs[:], func=AF.Square), reads=[xs], writes=[sq])
    for c in range(8):
        P.op("pe", lambda e, c=c: e.matmul(ps[:], lhsT=C.ones[:], rhs=sq[:, c, :], start=(c == 0), stop=(c == 7)),
             reads=[C.ones, sq], writes=[ps])
    P.op("act", lambda e: e.activation(out=rstd[:], in_=ps[:], func=AF.Sqrt, scale=1.0 / 1024, bias=C.eps[:]),
         reads=[ps, C.eps], writes=[rstd])
    P.op("dve", lambda e: e.reciprocal(out=rstd[:], in_=rstd[:]), reads=[rstd], writes=[rstd])
    for c in range(8):
        P.op("dve", lambda e, c=c: e.scalar_tensor_tensor(out=out_ap[:, c, :], in0=xs[:, c, :], scalar=g[:, c:c + 1],
                                                          in1=rstd[:], op0=ALU.mult, op1=ALU.mult),
             reads=[xs, g, rstd], pwrites=[out_buf])


def hgrn_consts(K, d):
    P = K.P
    C = Ctx()
    sb = lambda n, sh, dt: K.sb(n, sh, dt, pers=True)
    C.ones = sb("ones", [128, 128], BF16)
    P.op("pool", lambda e: e.memset(C.ones[:], 1.0), writes=[C.ones])
    C.eps = sb("eps", [128, 1], F32)
    P.op("pool", lambda e: e.memset(C.eps[:], EPS), writes=[C.eps])
    C.ident = sb("ident", [128, 128], BF16)
    P.dma("pool", C.ident[:], d["ident"], writes=[C.ident])
    C.resetm = sb("resetm", [128, SW], F32)
    P.dma("sp", C.resetm[:], d["resetm"], writes=[C.resetm])
    C.maskc = sb("maskc", [64, 8, 64], F32)
    P.dma("sp", C.maskc[:], d["maskc"], writes=[C.maskc])
    C.gmix = sb("gmix", [128, 8], F32)
    P.dma("sp", C.gmix[:], d["gmix0"], writes=[C.gmix])
    C.gn = sb("gn", [128, 8], F32)
    P.dma("sp", C.gn[:], d["gnorm"], writes=[C.gn])
    lbl = sb("lbl", [128, 2, 8], F32)
    P.dma("sp", lbl[:], d["lbl"], writes=[lbl])
    C.lb = sb("lb", [128, 8], F32)
    C.oml = sb("oml", [128, 8], F32)
    P.op("dve", lambda e: e.tensor_sub(out=C.lb[:], in0=lbl[:, 0, :], in1=lbl[:, 1, :]), reads=[lbl], writes=[C.lb])
    P.op("act", lambda e: e.activation(out=C.oml[:], in_=C.lb[:], func=AF.Sigmoid, scale=-1.0), reads=[C.lb], writes=[C.oml])
    P.op("act", lambda e: e.activation(out=C.lb[:], in_=C.lb[:], func=AF.Sigmoid), reads=[C.lb], writes=[C.lb])
    return C


def hgrn_alloc_T(K):
    T = Ctx()
    sb = lambda n, sh, dt: K.sb(n, sh, dt, pers=True)
    T.bl = sb("bl", [128, 8, NCH], F32)
    T.bmid = sb("bmid", [128, 8, NCH], F32)
    T.e1 = sb("e1", [128, 8, NCH], F32)
    T.e2 = sb("e2", [128, 8, NCH], F32)
    T.em = sb("em", [128, 8, NCH], F32)
    T.D = sb("D", [128, 8], F32)
    return T


def hgrn_P(K, C, d, scr, T):
    P, sb, ps = K.P, K.sb, K.ps
    xT3 = d["xT"].rearrange("(c p) t -> p c t", p=128)
    hT = sb("hT", [128, 8, NT], BF16)
    hTs = [Buf(f"hT{s}", hT.t[:, :, s * SW:(s + 1) * SW]) for s in range(NS)]
    xst = [sb("xst", [128, 8, SW], F32) for _ in range(2)]
    sq = sb("sq", [128, 8, SW], BF16)
    rstd = sb("rstd", [128, SW], F32)
    ps_n = ps(0, [128, SW])
    for s in range(NS):
        xs = xst[s % 2]
        P.dma("sp", xs[:], xT3[:, :, s * SW:(s + 1) * SW], writes=[xs])
        rmsnorm_fm(P, C, xs, C.gmix, hTs[s], hTs[s], sq, ps_n, rstd)

    wi = sb("wi", [128, 8, 1024], BF16)
    for h in range(8):
        P.dma("pool", wi[:, :, h * 128:(h + 1) * 128], d["w_in_r"][16 + h], pwrites=[wi])
    ps_v = [ps(1, [64, 512]), ps(2, [64, 512])]
    vst = [sb("vst", [64, CPS, 1024], BF16) for _ in range(2)]
    v3 = scr["v"].rearrange("(c s) j -> s c j", s=64)
    for s in range(NS):
        vs = vst[s % 2]
        for c in range(CPS):
            cg = s * CPS + c
            for hf in range(2):
                pv = ps_v[hf]
                for m in range(8):
                    P.op("pe", lambda e, m=m, cg=cg, hf=hf, pv=pv: e.matmul(
                        pv[:], lhsT=hT[:, m, cg * CH:(cg + 1) * CH], rhs=wi[:, m, hf * 512:(hf + 1) * 512],
                        start=(m == 0), stop=(m == 7)), reads=[hTs[s], wi], writes=[pv])
                if hf == 0:
                    P.op("act", lambda e, c=c, hf=hf, pv=pv, vs=vs: e.copy(out=vs[:, c, hf * 512:(hf + 1) * 512], in_=pv[:]),
                         reads=[pv], pwrites=[vs])
                else:
                    P.op("dve", lambda e, c=c, hf=hf, pv=pv, vs=vs: e.tensor_copy(out=vs[:, c, hf * 512:(hf + 1) * 512], in_=pv[:]),
                         reads=[pv], pwrites=[vs])
        P.dma("sp", v3[:, s * CPS:(s + 1) * CPS, :], vs[:], reads=[vs], pwrites=[scr["v_tok"]])

    wq = [sb("wq", [128, 8, 128], BF16) for _ in range(2)]
    wf = [sb("wf", [128, 8, 128], BF16) for _ in range(2)]
    wg = [sb("wg", [128, 8, 128], BF16) for _ in range(2)]
    ps_f = ps(3, [128, SW])
    ps_q = ps(4, [128, SW])
    ps_g = ps(5, [128, SW])
    ps_t = ps(0, [64, CPS, 128], BF16)
    sig = sb("sig", [128, SW], F32)
    logf = sb("logf", [128, SW], F32)
    nsig = sb("nsig", [128, SW], F32)
    b3 = sb("b3", [128, CPS, CH], F32)
    bm = sb("bm", [128, CPS, CH], F32)
    Ep = sb("Ep", [128, SW], F32)
    Em = sb("Em", [128, SW], F32)
    sqf = sb("sqf", [128, SW], F32)
    qst = [sb("qst", [128, SW], BF16) for _ in range(2)]
    kst = [sb("kst", [128, SW], BF16) for _ in range(2)]
    gst = [sb("gst", [128, SW], BF16) for _ in range(2)]
    ktst = [sb("ktst", [64, CPS, 128], BF16) for _ in range(2)]
    b_flat = b3.t.rearrange("p c t -> p (c t)")
    bm_flat = bm.t.rearrange("p c t -> p (c t)")
    kt3 = scr["kt"].rearrange("(c s) j -> s c j", s=64)
    it = 0
    for h in range(8):
        wqh, wfh, wgh = wq[h % 2], wf[h % 2], wg[h % 2]
        P.dma("pool", wfh[:], d["w_in_r"][8 + h], writes=[wfh])
        P.dma("pool", wqh[:], d["w_in_r"][0 + h], writes=[wqh])
        P.dma("pool", wgh[:], d["w_in_r"][24 + h], writes=[wgh])
        for s in range(NS):
            hs = hTs[s]
            sl = slice(s * SW, (s + 1) * SW)
            for (pp, ww) in ((ps_f, wfh), (ps_q, wqh), (ps_g, wgh)):
                for m in range(8):
                    P.op("pe", lambda e, m=m, pp=pp, ww=ww, sl=sl: e.matmul(
                        pp[:], lhsT=ww[:, m, :], rhs=hT[:, m, sl], start=(m == 0), stop=(m == 7)),
                        reads=[ww, hs], writes=[pp])
            q_o, k_o, g_o, kt_o = qst[it % 2], kst[it % 2], gst[it % 2], ktst[it % 2]
            it += 1
            P.op("act", lambda e: e.activation(out=sig[:], in_=ps_f[:], func=AF.Sigmoid), reads=[ps_f], writes=[sig])
            P.op("act", lambda e, h=h: e.activation(out=logf[:], in_=sig[:], func=AF.Ln, scale=C.oml[:, h:h + 1],
                                                    bias=C.lb[:, h:h + 1]), reads=[sig, C.oml, C.lb], writes=[logf])
            P.op("dve", lambda e: e.tensor_scalar(out=nsig[:], in0=sig[:], scalar1=-1.0, scalar2=1.0, op0=ALU.mult,
                                                  op1=ALU.add), reads=[sig], writes=[nsig])
            P.op("dve", lambda e: e.tensor_tensor_scan(out=b_flat, data0=C.resetm[:], data1=logf[:], initial=0.0,
                                                       op0=ALU.mult, op1=ALU.add), reads=[C.resetm, logf], writes=[b3])
            P.op("dve", lambda e, h=h, s=s: e.tensor_copy(out=T.bl[:, h, s * CPS:(s + 1) * CPS], in_=b3[:, :, CH - 1]),
                 reads=[b3], pwrites=[T.bl])
            P.op("dve", lambda e, h=h, s=s: e.tensor_copy(out=T.bmid[:, h, s * CPS:(s + 1) * CPS], in_=b3[:, :, CH // 2 - 1]),
                 reads=[b3], pwrites=[T.bmid])
            P.op("dve", lambda e: e.tensor_tensor(out=bm[:], in0=b3[:], in1=b3[:, :, CH // 2 - 1:CH // 2].to_broadcast([128, CPS, CH]),
                                                  op=ALU.subtract), reads=[b3], writes=[bm])
            P.op("act", lambda e: e.activation(out=Ep[:], in_=bm_flat, func=AF.Exp), reads=[bm], writes=[Ep])
            P.op("act", lambda e: e.activation(out=Em[:], in_=bm_flat, func=AF.Exp, scale=-1.0), reads=[bm], writes=[Em])
            P.op("act", lambda e: e.activation(out=sqf[:], in_=ps_q[:], func=AF.Silu), reads=[ps_q], writes=[sqf])
            P.op("act", lambda e, g_o=g_o: e.activation(out=g_o[:], in_=ps_g[:], func=AF.Silu), reads=[ps_g], writes=[g_o])
            P.op("dve", lambda e, q_o=q_o: e.tensor_tensor(out=q_o[:], in0=sqf[:], in1=Ep[:], op=ALU.mult),
                 reads=[sqf, Ep], writes=[q_o])
            P.op("dve", lambda e, k_o=k_o, h=h: e.scalar_tensor_tensor(out=k_o[:], in0=nsig[:], scalar=C.oml[:, h:h + 1],
                                                                       in1=Em[:], op0=ALU.mult, op1=ALU.mult),
                 reads=[nsig, C.oml, Em], writes=[k_o])
            hsl = slice(h * 128, (h + 1) * 128)
            P.dma("sp", scr["qT"][hsl, sl], q_o[:], reads=[q_o], pwrites=[scr["qT_tok"]])
            P.dma("sp", scr["kT"][hsl, sl], k_o[:], reads=[k_o], pwrites=[scr["kT_tok"]])
            P.dma("sp", scr["gT"][hsl, sl], g_o[:], reads=[g_o], pwrites=[scr["gT_tok"]])
            for c in range(CPS):
                P.op("pe", lambda e, c=c, k_o=k_o: e.transpose(out=ps_t[:, c, :], in_=k_o[:, c * CH:(c + 1) * CH],
                                                              identity=C.ident[:]), reads=[k_o, C.ident], writes=[ps_t])
            P.op("act", lambda e, kt_o=kt_o: e.copy(out=kt_o[:], in_=ps_t[:]), reads=[ps_t], writes=[kt_o])
            P.dma("sp", kt3[:, s * CPS:(s + 1) * CPS, hsl], kt_o[:], reads=[kt_o], pwrites=[scr["kt_tok"]])
    P.op("act", lambda e: e.activation(out=T.e1[:], in_=T.bl[:], func=AF.Exp), reads=[T.bl], writes=[T.e1])
    P.op("act", lambda e: e.activation(out=T.em[:], in_=T.bmid[:], func=AF.Exp), reads=[T.bmid], writes=[T.em])
    P.op("dve", lambda e: e.tensor_sub(out=T.e2[:], in0=T.bl[:], in1=T.bmid[:]), reads=[T.bl, T.bmid], writes=[T.e2])
    P.op("act", lambda e: e.activation(out=T.e2[:], in_=T.e2[:], func=AF.Exp), reads=[T.e2], writes=[T.e2])
    P.op("dve", lambda e: e.reduce_sum(out=T.D[:], in_=T.bl[:], axis=AX.X), reads=[T.bl], writes=[T.D])
    P.op("act", lambda e: e.activation(out=T.D[:], in_=T.D[:], func=AF.Exp), reads=[T.D], writes=[T.D])


def hgrn_R(K, C, d, scr, T, full, S):
    P, sb, ps = K.P, K.sb, K.ps
    q3 = scr["qT"].rearrange("(h p) t -> p h t", p=128)
    k3 = scr["kT"].rearrange("(h p) t -> p h t", p=128)
    g3 = scr["gT"].rearrange("(h p) t -> p h t", p=128)
    v3 = scr["v"].rearrange("(c s) j -> s c j", s=64)
    kt3 = scr["kt"].rearrange("(c s) j -> s c j", s=64)
    v_sb = [sb("v_sb", [64, CPS, 1024], BF16) for _ in range(2)]
    kt_sb = [sb("kt_sb", [64, CPS, 1024], BF16) for _ in range(1)]
    ps_dS = ps(6, [128, 8, 128])
    tmp = sb("tmpS", [128, 8, 128], F32)
    if full:
        q_sb = [sb("q_sb", [128, 8, SW], BF16) for _ in range(2)]
        k_sb = [sb("k_sb", [128, 8, SW], BF16) for _ in range(2)]
        g_sb = [sb("g_sb", [128, 8, SW], BF16) for _ in range(1)]
        ps_sc = [ps(0, [64, 8, 64]), ps(1, [64, 8, 64])]
        ps_o = [ps(2, [128, 8, 64]), ps(3, [128, 8, 64])]
        sc_sb = [sb("sc_sb", [64, 8, 64], BF16) for _ in range(2)]
        Sb = [sb("Sb", [128, 8, 128], BF16) for _ in range(2)]
        oT = sb("oT", [128, 8, SW], F32)
        osq = sb("osq", [128, 8, SW], BF16)
        ogT = sb("ogT", [128, 8, SW], BF16)
        t1 = sb("t1", [128, SW], F32)
        rstd = sb("rstd", [128, SW], F32)
        wout = sb("wout", [128, 8, 1024], BF16)
        P.dma("pool", wout[:], d["w_out_r"], writes=[wout])
        ps_y = [ps(4, [128, SW]), ps(5, [128, SW])]
        ps_n = ps(4, [128, SW])
        xT3 = d["xT"].rearrange("(c p) t -> p c t", p=128)
        x1T3 = d["x1T"].rearrange("(c p) t -> p c t", p=128)
        xst = [sb("xst", [128, 8, SW], F32) for _ in range(1)]
    for s in range(NS):
        sl = slice(s * SW, (s + 1) * SW)
        vs, kts = v_sb[s % 2], kt_sb[0]
        P.dma("sp", vs[:], v3[:, s * CPS:(s + 1) * CPS, :], reads=[scr["v_tok"]], writes=[vs])
        P.dma("sp", kts[:], kt3[:, s * CPS:(s + 1) * CPS, :], reads=[scr["kt_tok"]], writes=[kts])
        if full:
            qs, ks, gs = q_sb[s % 2], k_sb[s % 2], g_sb[0]
            P.dma("sp", qs[:], q3[:, :, sl], reads=[scr["qT_tok"]], writes=[qs])
            P.dma("sp", ks[:], k3[:, :, sl], reads=[scr["kT_tok"]], writes=[ks])
            P.dma("sp", gs[:], g3[:, :, sl], reads=[scr["gT_tok"]], writes=[gs])
            xs = xst[0]
            P.dma("sp", xs[:], xT3[:, :, sl], writes=[xs])
        for c in range(CPS):
            cg = s * CPS + c
            csl = slice(c * CH, (c + 1) * CH)
            if full:
                psc, pso, scb, Sbb = ps_sc[cg % 2], ps_o[cg % 2], sc_sb[cg % 2], Sb[cg % 2]
                for h in range(8):
                    P.op("pe", lambda e, h=h, psc=psc, ks=ks, qs=qs, csl=csl: e.matmul(
                        psc[:, h, :], lhsT=ks[:, h, csl], rhs=qs[:, h, csl], start=True, stop=True),
                        reads=[ks, qs], writes=[psc])
                P.op("dve", lambda e, psc=psc, scb=scb: e.tensor_tensor(out=scb[:], in0=psc[:], in1=C.maskc[:], op=ALU.mult),
                     reads=[psc, C.maskc], writes=[scb])
                P.op("dve", lambda e, Sbb=Sbb, cg=cg: e.tensor_tensor(
                    out=Sbb[:], in0=S[:], in1=T.em[:, :, cg:cg + 1].to_broadcast([128, 8, 128]), op=ALU.mult),
                    reads=[S, T.em], writes=[Sbb])
                for h in range(8):
                    hsl = slice(h * 128, (h + 1) * 128)
                    P.op("pe", lambda e, h=h, pso=pso, Sbb=Sbb, qs=qs, csl=csl: e.matmul(
                        pso[:, h, :], lhsT=Sbb[:, h, :], rhs=qs[:, h, csl], start=True, stop=False),
                        reads=[Sbb, qs], writes=[pso])
                    P.op("pe", lambda e, h=h, pso=pso, vs=vs, scb=scb, c=c, hsl=hsl: e.matmul(
                        pso[:, h, :], lhsT=vs[:, c, hsl], rhs=scb[:, h, :], start=False, stop=True),
                        reads=[vs, scb], writes=[pso])
                P.op("act", lambda e, pso=pso, csl=csl: e.copy(out=oT[:, :, csl], in_=pso[:]), reads=[pso], pwrites=[oT])
            for h in range(8):
                hsl = slice(h * 128, (h + 1) * 128)
                P.op("pe", lambda e, h=h, kts=kts, vs=vs, c=c, hsl=hsl: e.matmul(
                    ps_dS[:, h, :], lhsT=kts[:, c, hsl], rhs=vs[:, c, hsl], start=True, stop=True),
                    reads=[kts, vs], writes=[ps_dS])
            P.op("dve", lambda e, cg=cg: e.tensor_tensor(
                out=tmp[:], in0=ps_dS[:], in1=T.e2[:, :, cg:cg + 1].to_broadcast([128, 8, 128]), op=ALU.mult),
                reads=[ps_dS, T.e2], writes=[tmp])
            P.op("pool", lambda e, cg=cg: e.tensor_tensor(
                out=S[:], in0=S[:], in1=T.e1[:, :, cg:cg + 1].to_broadcast([128, 8, 128]), op=ALU.mult),
                reads=[S, T.e1], writes=[S])
            P.op("dve", lambda e: e.tensor_tensor(out=S[:], in0=S[:], in1=tmp[:], op=ALU.add), reads=[S, tmp], writes=[S])
        if full:
            P.op("act", lambda e: e.activation(out=osq[:], in_=oT[:], func=AF.Square), reads=[oT], writes=[osq])
            for h in range(8):
                P.op("pe", lambda e, h=h: e.matmul(ps_n[:], lhsT=C.ones[:], rhs=osq[:, h, :], start=(h == 0), stop=(h == 7)),
                     reads=[C.ones, osq], writes=[ps_n])
            P.op("act", lambda e: e.activation(out=rstd[:], in_=ps_n[:], func=AF.Sqrt, scale=1.0 / 1024, bias=C.eps[:]),
                 reads=[ps_n, C.eps], writes=[rstd])
            P.op("dve", lambda e: e.reciprocal(out=rstd[:], in_=rstd[:]), reads=[rstd], writes=[rstd])
            for h in range(8):
                P.op("dve", lambda e, h=h: e.scalar_tensor_tensor(out=t1[:], in0=oT[:, h, :], scalar=C.gn[:, h:h + 1],
                                                                  in1=rstd[:], op0=ALU.mult, op1=ALU.mult),
                     reads=[oT, C.gn, rstd], writes=[t1])
                P.op("dve", lambda e, h=h, gs=gs: e.tensor_tensor(out=ogT[:, h, :], in0=t1[:], in1=gs[:, h, :], op=ALU.mult),
                     reads=[t1, gs], pwrites=[ogT])
            for j in range(8):
                py = ps_y[j % 2]
                for h in range(8):
                    P.op("pe", lambda e, h=h, j=j, py=py: e.matmul(py[:], lhsT=wout[:, h, j * 128:(j + 1) * 128],
                                                                  rhs=ogT[:, h, :], start=(h == 0), stop=(h == 7)),
                         reads=[wout, ogT], writes=[py])
                P.op("dve", lambda e, j=j, py=py, xs=xs: e.tensor_tensor(out=xs[:, j, :], in0=py[:], in1=xs[:, j, :], op=ALU.add),
                     reads=[py, xs], pwrites=[xs])
            P.dma("sp", x1T3[:, :, sl], xs[:], reads=[xs], pwrites=[d["x1T_tok"]])
            if s == NS - 1 and "halo_out" in d:
                P.dma("sp", d["halo_out"].rearrange("p (c t) -> p c t", c=8), xs[:, :, SW - 2:SW], reads=[xs],
                      writes=[d["halo_out_tok"]])


NFF = 22


def ffn_layer(K, C, d, li, xin, xin_tok, xout, xout_tok, halo_all, halo_tok, final_norm=None):
    P, sb, ps = K.P, K.sb, K.ps
    K.phase()
    xT3 = xin.rearrange("(c p) t -> p c t", p=128)
    gf = sb("gf", [128, 8], F32)
    P.dma("sp", gf[:], d[f"gffn{li}"], writes=[gf])
    convp = sb("convp", [128, 2 * NFF, 4], F32)
    P.dma("sp", convp[:], d[f"convp{li}"], writes=[convp])
    nf = sb("nf", [128, 1], F32)
    P.dma("sp", nf[:], d["notfirst"], writes=[nf])
    aT = sb("aT", [128, NFF, NT], BF16)
    aTs = [Buf(f"aT{s}", aT.t[:, :, s * SW:(s + 1) * SW]) for s in range(NS)]
    mark = K.A.cur
    hT = sb("h2T", [128, 8, NT], BF16)
    hTs = [Buf(f"h2T{s}", hT.t[:, :, s * SW:(s + 1) * SW]) for s in range(NS)]
    xst = [sb("xst", [128, 8, SW], F32) for _ in range(1)]
    sq = sb("sq", [128, 8, SW], BF16)
    rstd = sb("rstd", [128, SW], F32)
    ps_n = ps(0, [128, SW])
    for s in range(NS):
        xs = xst[0]
        P.dma("sp", xs[:], xT3[:, :, s * SW:(s + 1) * SW], reads=[xin_tok], writes=[xs])
        rmsnorm_fm(P, C, xs, gf, hTs[s], hTs[s], sq, ps_n, rstd)
    xh = sb("xh", [128, 8, 2], F32)

    def dyn(e):
        pid = P.pid(e)
        prev = (pid + 7) % 8
        return e.dma_start(out=xh[:], in_=halo_all[bass.ds(prev * 128, 128), :].rearrange("p (c t) -> p c t", c=8))
    P._add("sp", dyn, [halo_tok], [xh], (), True)
    sqh = sb("sqh", [128, 8, 2], BF16)
    rsh = sb("rsh", [128, 2], F32)
    hh = sb("hh", [128, 8, 2], BF16)
    ps_h = ps(1, [128, 2])
    P.op("act", lambda e: e.activation(out=sqh[:], in_=xh[:], func=AF.Square), reads=[xh], writes=[sqh])
    for c in range(8):
        P.op("pe", lambda e, c=c: e.matmul(ps_h[:], lhsT=C.ones[:], rhs=sqh[:, c, :], start=(c == 0), stop=(c == 7)),
             reads=[C.ones, sqh], writes=[ps_h])
    P.op("act", lambda e: e.activation(out=rsh[:], in_=ps_h[:], func=AF.Sqrt, scale=1.0 / 1024, bias=C.eps[:]),
         reads=[ps_h, C.eps], writes=[rsh])
    P.op("dve", lambda e: e.reciprocal(out=rsh[:], in_=rsh[:]), reads=[rsh], writes=[rsh])
    P.op("dve", lambda e: e.tensor_scalar(out=rsh[:], in0=rsh[:], scalar1=nf[:, 0:1], scalar2=None, op0=ALU.mult),
         reads=[rsh, nf], writes=[rsh])
    for c in range(8):
        P.op("dve", lambda e, c=c: e.scalar_tensor_tensor(out=hh[:, c, :], in0=xh[:, c, :], scalar=gf[:, c:c + 1],
                                                          in1=rsh[:], op0=ALU.mult, op1=ALU.mult),
             reads=[xh, gf, rsh], pwrites=[hh])

    wu = [[sb("wu", [128, 8, 128], BF16) for _ in range(2)] for _ in range(2)]
    ug = [sb("ug", [128, SW + 2], F32) for _ in range(2)]
    uv = [sb("uv", [128, SW + 2], F32) for _ in range(2)]
    ag = [sb("ag", [128, SW], F32) for _ in range(2)]
    av = [sb("av", [128, SW], F32) for _ in range(2)]
    sg = [sb("sg", [128, SW], F32) for _ in range(2)]
    ps_g = [ps(2, [128, SW]), ps(3, [128, SW])]
    ps_v = [ps(4, [128, SW]), ps(5, [128, SW])]
    ps_hh = ps(1, [128, 2, 2])
    it = 0
    for c in range(NFF):
        wg_, wv_ = wu[c % 2]
        P.dma("pool", wg_[:], d[f"w_up_r{li}"][c], writes=[wg_])
        P.dma("pool", wv_[:], d[f"w_up_r{li}"][c + NFF], writes=[wv_])
        for s in range(NS):
            sl = slice(s * SW, (s + 1) * SW)
            cur, prv = it % 2, (it + 1) % 2
            it += 1
            pg, pv = ps_g[cur], ps_v[cur]
            ugc, uvc, agc, avc, sgc = ug[cur], uv[cur], ag[cur], av[cur], sg[cur]
            for (pp, ww) in ((pg, wg_), (pv, wv_)):
                for m in range(8):
                    P.op("pe", lambda e, m=m, pp=pp, ww=ww, sl=sl: e.matmul(
                        pp[:], lhsT=ww[:, m, :], rhs=hT[:, m, sl], start=(m == 0), stop=(m == 7)),
                        reads=[ww, hTs[s]], writes=[pp])
            if s == 0:
                for gi, ww in ((0, wg_), (1, wv_)):
                    for m in range(8):
                        P.op("pe", lambda e, m=m, gi=gi, ww=ww: e.matmul(
                            ps_hh[:, gi, :], lhsT=ww[:, m, :], rhs=hh[:, m, :], start=(m == 0), stop=(m == 7)),
                            reads=[ww, hh], writes=[ps_hh])
                P.op("dve", lambda e, ugc=ugc: e.tensor_copy(out=ugc[:, 0:2], in_=ps_hh[:, 0, :]), reads=[ps_hh], pwrites=[ugc])
                P.op("dve", lambda e, uvc=uvc: e.tensor_copy(out=uvc[:, 0:2], in_=ps_hh[:, 1, :]), reads=[ps_hh], pwrites=[uvc])
            else:
                P.op("dve", lambda e, ugc=ugc, p_=ug[prv]: e.tensor_copy(out=ugc[:, 0:2], in_=p_[:, SW:SW + 2]),
                     reads=[ug[prv]], pwrites=[ugc])
                P.op("pool", lambda e, uvc=uvc, p_=uv[prv]: e.tensor_copy(out=uvc[:, 0:2], in_=p_[:, SW:SW + 2]),
                     reads=[uv[prv]], pwrites=[uvc])
            cg, cv = c, c + NFF
            P.op("act", lambda e, ugc=ugc, pg=pg: e.copy(out=ugc[:, 2:SW + 2], in_=pg[:]), reads=[pg], pwrites=[ugc])
            P.op("act", lambda e, agc=agc, pg=pg, cg=cg: e.activation(out=agc[:], in_=pg[:], func=AF.Identity,
                                                                      scale=convp[:, cg, 2:3], bias=convp[:, cg, 3:4]),
                 reads=[pg, convp], writes=[agc])
            P.op("act", lambda e, uvc=uvc, pv=pv: e.copy(out=uvc[:, 2:SW + 2], in_=pv[:]), reads=[pv], pwrites=[uvc])
            P.op("act", lambda e, avc=avc, pv=pv, cv=cv: e.activation(out=avc[:], in_=pv[:], func=AF.Identity,
                                                                      scale=convp[:, cv, 2:3], bias=convp[:, cv, 3:4]),
                 reads=[pv, convp], writes=[avc])
            P.op("dve", lambda e, agc=agc, ugc=ugc, cg=cg: e.scalar_tensor_tensor(
                out=agc[:], in0=ugc[:, 1:SW + 1], scalar=convp[:, cg, 1:2], in1=agc[:], op0=ALU.mult, op1=ALU.add),
                reads=[ugc, convp, agc], writes=[agc])
            P.op("dve", lambda e, agc=agc, ugc=ugc, cg=cg: e.scalar_tensor_tensor(
                out=agc[:], in0=ugc[:, 0:SW], scalar=convp[:, cg, 0:1], in1=agc[:], op0=ALU.mult, op1=ALU.add),
                reads=[ugc, convp, agc], writes=[agc])
            P.op("dve", lambda e, avc=avc, uvc=uvc, cv=cv: e.scalar_tensor_tensor(
                out=avc[:], in0=uvc[:, 1:SW + 1], scalar=convp[:, cv, 1:2], in1=avc[:], op0=ALU.mult, op1=ALU.add),
                reads=[uvc, convp, avc], writes=[avc])
            P.op("dve", lambda e, avc=avc, uvc=uvc, cv=cv: e.scalar_tensor_tensor(
                out=avc[:], in0=uvc[:, 0:SW], scalar=convp[:, cv, 0:1], in1=avc[:], op0=ALU.mult, op1=ALU.add),
                reads=[uvc, convp, avc], writes=[avc])
            P.op("act", lambda e, sgc=sgc, agc=agc: e.activation(out=sgc[:], in_=agc[:], func=AF.Silu), reads=[agc], writes=[sgc])
            P.op("dve", lambda e, sgc=sgc, avc=avc, c=c, sl=sl: e.tensor_tensor(out=aT[:, c, sl], in0=sgc[:], in1=avc[:], op=ALU.mult),
                 reads=[sgc, avc], pwrites=[aTs[s]])

    P.barrier()
    K.A.cur = mark
    wd = [sb("wd", [128, NFF, 128], BF16) for _ in range(2)]
    xj = [sb("xj", [128, SW], F32) for _ in range(3)]
    ps_y = [ps(0, [128, SW]), ps(1, [128, SW])]
    xin3 = xin.rearrange("(c p) t -> p c t", p=128)
    xout3 = xout.rearrange("(c p) t -> p c t", p=128)
    it = 0
    for j in range(8):
        wdj = wd[j % 2]
        P.dma("pool", wdj[:], d[f"w_down_r{li}"][j], writes=[wdj])
        for s in range(NS):
            sl = slice(s * SW, (s + 1) * SW)
            py = ps_y[it % 2]
            xs = xj[it % 3]
            it += 1
            P.dma("sp", xs[:], xin3[:, j, sl], reads=[xin_tok], writes=[xs])
            for c in range(NFF):
                P.op("pe", lambda e, c=c, py=py, wdj=wdj, sl=sl: e.matmul(
                    py[:], lhsT=wdj[:, c, :], rhs=aT[:, c, sl], start=(c == 0), stop=(c == NFF - 1)),
                    reads=[wdj, aTs[s]], writes=[py])
            P.op("dve", lambda e, py=py, xs=xs: e.tensor_tensor(out=xs[:], in0=py[:], in1=xs[:], op=ALU.add),
                 reads=[py, xs], writes=[xs])
            P.dma("sp", xout3[:, j, sl], xs[:], reads=[xs], pwrites=[xout_tok])


def final_norm(K, C, d, xin, xin_tok, out, out_tok):
    P, sb, ps = K.P, K.sb, K.ps
    K.phase()
    gfin = sb("gfin", [128, 8], F32)
    P.dma("sp", gfin[:], d["gfinal"], writes=[gfin])
    xT3 = xin.rearrange("(c p) t -> p c t", p=128)
    o3 = out.rearrange("(c p) t -> p c t", p=128)
    xst = [sb("xst", [128, 8, SW], F32) for _ in range(2)]
    ost = [sb("ost", [128, 8, SW], F32) for _ in range(2)]
    sq = sb("sq", [128, 8, SW], BF16)
    rstd = sb("rstd", [128, SW], F32)
    ps_n = ps(0, [128, SW])
    for s in range(NS):
        xs, os_ = xst[s % 2], ost[s % 2]
        P.dma("sp", xs[:], xT3[:, :, s * SW:(s + 1) * SW], reads=[xin_tok], writes=[xs])
        rmsnorm_fm(P, C, xs, gfin, os_, os_, sq, ps_n, rstd)
        P.dma("sp", o3[:, :, s * SW:(s + 1) * SW], os_[:], reads=[os_], pwrites=[out_tok])


import math

T_ALL = 16384
NB = T_ALL // 128
NQS = T_ALL // SW
LAM_INIT = 0.8 - 0.6 * math.exp(-0.3 * 1)
NEG = -30000.0
GLEN = 1151


def kvq_proj(K, C, d, xin, xin_tok, qkv_in, qkv_tok):
    P, sb, ps = K.P, K.sb, K.ps
    K.phase()
    xT3 = xin.rearrange("(c p) t -> p c t", p=128)
    gkv = sb("gkv", [128, 8], F32)
    gq = sb("gq", [128, 8], F32)
    P.dma("sp", gkv[:], d["gkv"], writes=[gkv])
    P.dma("sp", gq[:], d["gmix1"], writes=[gq])
    hk = sb("hk", [128, 8, NT], BF16)
    hq = sb("hq", [128, 8, NT], BF16)
    hks = [Buf(f"hk{s}", hk.t[:, :, s * SW:(s + 1) * SW]) for s in range(NS)]
    hqs = [Buf(f"hq{s}", hq.t[:, :, s * SW:(s + 1) * SW]) for s in range(NS)]
    xst = sb("xst", [128, 8, SW], F32)
    sq = sb("sq", [128, 8, SW], BF16)
    rstd = sb("rstd", [128, SW], F32)
    ps_n = ps(0, [128, SW])
    for s in range(NS):
        P.dma("sp", xst[:], xT3[:, :, s * SW:(s + 1) * SW], reads=[xin_tok], writes=[xst])
        rmsnorm_fm(P, C, xst, gkv, hks[s], hks[s], sq, ps_n, rstd)
        for c in range(8):
            P.op("dve", lambda e, c=c, s=s: e.scalar_tensor_tensor(out=hqs[s][:, c, :], in0=xst[:, c, :], scalar=gq[:, c:c + 1],
                                                                    in1=rstd[:], op0=ALU.mult, op1=ALU.mult),
                 reads=[xst, gq, rstd], pwrites=[hqs[s]])
    wk = [sb("wk", [128, 8, 128], BF16) for _ in range(2)]
    wq = [sb("wq", [128, 8, 128], BF16) for _ in range(2)]
    kst = [sb("kst", [128, NT], BF16) for _ in range(2)]
    qst = [sb("qst", [128, NT], BF16) for _ in range(2)]
    psk = [ps(1, [128, SW]), ps(2, [128, SW])]
    psq = [ps(3, [128, SW]), ps(4, [128, SW])]
    it = 0
    for h in range(8):
        wkh, wqh, ks_, qs_ = wk[h % 2], wq[h % 2], kst[h % 2], qst[h % 2]
        P.dma("pool", wkh[:], d["w_k_r"][h], writes=[wkh])
        P.dma("pool", wqh[:], d["w_q_r"][h], writes=[wqh])
        for s in range(NS):
            sl = slice(s * SW, (s + 1) * SW)
            pk, pq = psk[it % 2], psq[it % 2]
            it += 1
            for m in range(8):
                P.op("pe", lambda e, m=m, pk=pk, wkh=wkh, sl=sl: e.matmul(pk[:], lhsT=wkh[:, m, :], rhs=hk[:, m, sl],
                                                                          start=(m == 0), stop=(m == 7)),
                     reads=[wkh, hks[s]], writes=[pk])
            for m in range(8):
                P.op("pe", lambda e, m=m, pq=pq, wqh=wqh, sl=sl: e.matmul(pq[:], lhsT=wqh[:, m, :], rhs=hq[:, m, sl],
                                                                          start=(m == 0), stop=(m == 7)),
                     reads=[wqh, hqs[s]], writes=[pq])
            P.op("act", lambda e, pk=pk, ks_=ks_, sl=sl: e.copy(out=ks_[:, sl], in_=pk[:]), reads=[pk], pwrites=[ks_])
            P.op("dve", lambda e, pq=pq, qs_=qs_, sl=sl: e.tensor_scalar(out=qs_[:, sl], in0=pq[:], scalar1=0.125, scalar2=None,
                                                                         op0=ALU.mult), reads=[pq], pwrites=[qs_])
        P.dma("sp", qkv_in[h * 384:h * 384 + 128, :], qs_[:], reads=[qs_], pwrites=[qkv_tok])
        P.dma("sp", qkv_in[h * 384 + 128:h * 384 + 256, :], ks_[:], reads=[ks_], pwrites=[qkv_tok])
    wv = sb("wv", [128, 8, 1024], BF16)
    for h in range(8):
        P.dma("pool", wv[:, :, h * 128:(h + 1) * 128], d["w_v_r"][h], pwrites=[wv])
    vstage = sb("vstage", [128, 8, 16, 128], BF16)
    psv = [ps(1, [128, 4, 128]), ps(2, [128, 4, 128])]
    psv_flat = [ps(1, [128, 512]), ps(2, [128, 512])]
    for tb in range(16):
        s = tb // 4
        for hf in range(2):
            pv = psv_flat[hf]
            for m in range(8):
                P.op("pe", lambda e, m=m, pv=pv, tb=tb, hf=hf: e.matmul(
                    pv[:], lhsT=hk[:, m, tb * 128:(tb + 1) * 128], rhs=wv[:, m, hf * 512:(hf + 1) * 512],
                    start=(m == 0), stop=(m == 7)), reads=[hks[s], wv], writes=[pv])
            if hf == 0:
                P.op("act", lambda e, tb=tb, hf=hf: e.copy(out=vstage[:, hf * 4:(hf + 1) * 4, tb, :], in_=psv[hf][:]),
                     reads=[psv[hf]], pwrites=[vstage])
            else:
                P.op("dve", lambda e, tb=tb, hf=hf: e.tensor_copy(out=vstage[:, hf * 4:(hf + 1) * 4, tb, :], in_=psv[hf][:]),
                     reads=[psv[hf]], pwrites=[vstage])
    for h in range(8):
        P.dma("sp", qkv_in[h * 384 + 256:h * 384 + 384, :], vstage.t[:, h, :, :].rearrange("p b v -> p (b v)"),
              reads=[vstage], pwrites=[qkv_tok])


def attn_core(K, C, d, qkv_all, qkv_all_tok, o_in, o_tok, gvec, gvec_tok):
    P, sb, ps = K.P, K.sb, K.ps
    K.phase()
    QKV = sb("QKV", [128, 3, 8, NT], BF16)
    QT = QKV.alias(QKV.t[:, 0, :, :].rearrange("p r t -> p (r t)"))
    KT = QKV.alias(QKV.t[:, 1, :, :].rearrange("p r t -> p (r t)"))
    VA = sb("VA", [128, NB, 129], BF16)
    P.op("pool", lambda e: e.memset(VA[:], 1.0), writes=[VA])
    q4 = qkv_all.rearrange("(r h x) t -> r h x t", r=8, h=8)
    for r in range(8):
        def fn(e, r=r):
            pid = P.pid(e)
            src = q4[r, bass.ds(pid, 1), :, :].rearrange("o (k p) t -> p (o k) t", k=3)
            return e.dma_start(out=QKV[:, :, r, :], in_=src)
        P._add("act", fn, [qkv_all_tok], (), [QKV], True)
    for r in range(8):
        P.op("dve" if r % 2 == 0 else "pool",
             lambda e, r=r: e.tensor_copy(out=VA[:, r * 16:(r + 1) * 16, 0:128],
                                          in_=QKV[:, 2, r, :].rearrange("p (b v) -> p b v", b=16)),
             reads=[QKV], pwrites=[VA])
    Vreg = QKV.t[:, 2, :, :].rearrange("p r t -> p (r t)")
    P.op("dve", lambda e: e.tensor_copy(out=Vreg[64:128, :], in_=QT[64:128, :]), reads=[QKV], pwrites=[QKV])
    P.op("pool", lambda e: e.memset(Vreg[0:64, :], 0.0), pwrites=[QKV])
    P.op("dve", lambda e: e.memset(QT[64:128, :], 0.0), reads=[QKV], pwrites=[QKV])
    Qz = [QT, QKV.alias(Vreg)]
    lamv = sb("lamv", [128, 4, 64], F32)
    P.dma("sp", lamv[:], d["lamv"].partition_broadcast(128), writes=[lamv])
    lp = sb("lp", [128, 2, 64], F32)
    ls = sb("ls", [128, 2], F32)
    nlam = sb("nlam", [128, 1], F32)
    P.op("dve", lambda e: e.tensor_tensor(out=lp[:, 0, :], in0=lamv[:, 0, :], in1=lamv[:, 1, :], op=ALU.mult), reads=[lamv], pwrites=[lp])
    P.op("dve", lambda e: e.tensor_tensor(out=lp[:, 1, :], in0=lamv[:, 2, :], in1=lamv[:, 3, :], op=ALU.mult), reads=[lamv], pwrites=[lp])
    P.op("dve", lambda e: e.reduce_sum(out=ls[:], in_=lp[:], axis=AX.X), reads=[lp], writes=[ls])
    P.op("act", lambda e: e.activation(out=ls[:], in_=ls[:], func=AF.Exp), reads=[ls], writes=[ls])
    P.op("dve", lambda e: e.tensor_sub(out=nlam[:], in0=ls[:, 1:2], in1=ls[:, 0:1]), reads=[ls], writes=[nlam])
    P.op("dve", lambda e: e.tensor_scalar(out=nlam[:], in0=nlam[:], scalar1=-LAM_INIT, scalar2=None, op0=ALU.add),
         reads=[nlam], writes=[nlam])
    gsub = sb("gsub", [128, 128], F32)
    P.dma("sp", gsub[:], d["subln"].partition_broadcast(128), writes=[gsub])
    P.op("dve", lambda e: e.tensor_scalar(out=gsub[:], in0=gsub[:], scalar1=1.0 - LAM_INIT, scalar2=None, op0=ALU.mult),
         reads=[gsub], writes=[gsub])
    eps128 = C.eps
    relcol = sb("relcol", [32, 1], F32)
    oh = sb("oh", [32, 128], F32)
    P.dma("sp", relcol[:], d["relcol"], writes=[relcol])
    P.dma("sp", oh[:], d["oh"], writes=[oh])
    ohb = sb("ohb", [32, 128], BF16)
    rc_hi = sb("rc_hi", [32, 1], BF16)
    rc_lo = sb("rc_lo", [32, 1], BF16)
    P.op("dve", lambda e: e.tensor_copy(out=ohb[:], in_=oh[:]), reads=[oh], writes=[ohb])
    P.op("dve", lambda e: e.tensor_copy(out=rc_hi[:], in_=relcol[:]), reads=[relcol], writes=[rc_hi])
    P.op("dve", lambda e: e.tensor_tensor(out=rc_lo[:], in0=relcol[:], in1=rc_hi[:], op=ALU.subtract),
         reads=[relcol, rc_hi], writes=[rc_lo])
    ps_g = ps(0, [1, 128])
    gm = sb("gm", [1, 128], F32)
    P.op("pe", lambda e: e.matmul(ps_g[:], lhsT=rc_hi[:], rhs=ohb[:], start=True, stop=False), reads=[rc_hi, ohb], writes=[ps_g])
    P.op("pe", lambda e: e.matmul(ps_g[:], lhsT=rc_lo[:], rhs=ohb[:], start=False, stop=True), reads=[rc_lo, ohb], writes=[ps_g])
    P.op("act", lambda e: e.copy(out=gm[:], in_=ps_g[:]), reads=[ps_g], writes=[gm])
    gv = gvec.ap()
    P.dma("sp", gv, d["gconst"], writes=[gvec_tok])
    P.dma("sp", gv[:, 511:639], gm[:], reads=[gm], writes=[gvec_tok])
    btile = sb("btile", [128, 5, SW], F32)
    antiI = sb("antiI", [128, 128], BF16)
    P.dma("pool", antiI[:], d["antiI"], writes=[antiI])
    hk_t = [sb("hk_t", [128, SW], F32) for _ in range(2)]
    hk_hi = [sb("hk_hi", [128, SW], BF16) for _ in range(2)]
    hk_lo = [sb("hk_lo", [128, SW], BF16) for _ in range(2)]
    for i in range(5):
        src = bass.AP(gvec, 512 - 128 * i, [[1, 128], [1, SW]])
        hkt, hhi, hlo = hk_t[i % 2], hk_hi[i % 2], hk_lo[i % 2]
        P.dma("sp", hkt[:], src, reads=[gvec_tok], writes=[hkt])
        P.op("dve", lambda e, hkt=hkt, hhi=hhi: e.tensor_copy(out=hhi[:], in_=hkt[:]), reads=[hkt], writes=[hhi])
        P.op("dve", lambda e, hkt=hkt, hhi=hhi, hlo=hlo: e.tensor_tensor(out=hlo[:], in0=hkt[:], in1=hhi[:], op=ALU.subtract),
             reads=[hkt, hhi], writes=[hlo])
        pbt = K.pb[i % 2]
        P.op("pe", lambda e, pbt=pbt, hhi=hhi: e.matmul(pbt[:], lhsT=antiI[:], rhs=hhi[:], start=True, stop=False),
             reads=[antiI, hhi], writes=[pbt])
        P.op("pe", lambda e, pbt=pbt, hlo=hlo: e.matmul(pbt[:], lhsT=antiI[:], rhs=hlo[:], start=False, stop=True),
             reads=[antiI, hlo], writes=[pbt])
        P.op("act", lambda e, pbt=pbt, i=i: e.copy(out=btile[:, i, :], in_=pbt[:]), reads=[pbt], pwrites=[btile])

    pT = [[sb("pT", [128, SW], BF16) for _ in range(2)] for _ in range(2)]
    stmp = [sb("stmp", [128, SW], F32) for _ in range(2)]
    psS = [[K.pb[0], K.pb[1]], [K.pb[2], K.pb[3]]]
    accb = [K.pb[4], K.pb[5], K.pb2]

    def acc(m, j):
        i = m * 4 + j
        b = accb[i // 3]
        o = (i % 3) * 129
        return b, b.t[:, o:o + 129]
    ps_tr = K.pb2.alias(K.pb2.t[:, 512:768].bitcast(BF16))
    accS = [sb("accS", [128, 8 * 129], F32) for _ in range(2)]
    o_sb = [sb("o_sb", [128, 128], F32) for _ in range(4)]
    osq = [sb("osq", [128, 128], F32) for _ in range(2)]
    on = [sb("on", [128, 128], BF16) for _ in range(4)]
    sm = [sb("sm", [128, 8], F32) for _ in range(4)]
    oT_st = [sb("oT_st", [128, SW], BF16) for _ in range(2)]
    def emit_qk(qs, kb, maps=(0, 1)):
        i_near = kb - (qs * 4 - 1)
        near = i_near >= 0
        j0 = max(0, kb - qs * 4)
        c0 = j0 * 128
        for m in maps:
            pS = psS[m][kb % 2]
            P.op("pe", lambda e, pS=pS, m=m, kb=kb, qs=qs, c0=c0: e.matmul(
                pS[:, c0:SW], lhsT=KT[:, kb * 128:(kb + 1) * 128], rhs=Qz[m][:, qs * SW + c0:(qs + 1) * SW],
                start=True, stop=True), reads=[KT, QT], writes=[pS])
        for m in maps:
            pS = psS[m][kb % 2]
            pt = pT[m][kb % 2]
            if near:
                st = stmp[m]
                P.op("dve", lambda e, st=st, pS=pS, i_near=i_near, c0=c0: e.tensor_tensor(
                    out=st[:, c0:SW], in0=pS[:, c0:SW], in1=btile[:, i_near, c0:SW], op=ALU.add),
                    reads=[pS, btile], writes=[st])
                P.op("act", lambda e, st=st, pt=pt, c0=c0: e.activation(out=pt[:, c0:SW], in_=st[:, c0:SW], func=AF.Exp),
                     reads=[st], writes=[pt])
            else:
                P.op("act", lambda e, pS=pS, pt=pt: e.activation(out=pt[:], in_=pS[:], func=AF.Exp), reads=[pS], writes=[pt])

    def emit_pv(qs, kb, maps=(0, 1), last=True):
        j0 = max(0, kb - qs * 4)
        for m in maps:
            pt = pT[m][kb % 2]
            for j in range(j0, 4):
                ab, aap = acc(m, j)
                st_ = (kb == 0) and ((m * 4 + j) % 3 == 0)
                P.op("pe", lambda e, aap=aap, pt=pt, j=j, kb=kb, qs=qs, st_=st_: e.matmul(
                    aap, lhsT=pt[:, j * 128:(j + 1) * 128], rhs=VA[:, kb, :], start=st_, stop=(kb == qs * 4 + j)),
                    reads=[pt, VA], pwrites=[ab])
        if last and kb == (qs + 1) * 4 - 1:
            epilogue(qs)

    def epilogue(qs):
        aS = accS[qs % 2]
        P.op("dve", lambda e, aS=aS: e.tensor_copy(out=aS[:, 0:387], in_=accb[0].t[:, 0:387]), reads=[accb[0]], pwrites=[aS])
        P.op("dve", lambda e, aS=aS: e.tensor_copy(out=aS[:, 387:774], in_=accb[1].t[:, 0:387]), reads=[accb[1]], pwrites=[aS])
        P.op("dve", lambda e, aS=aS: e.tensor_copy(out=aS[:, 774:1032], in_=accb[2].t[:, 0:258]), reads=[accb[2]], pwrites=[aS])
        for j in range(4):
            a0 = aS.t[:, j * 129:(j + 1) * 129]
            a1 = aS.t[:, (4 + j) * 129:(5 + j) * 129]
            s_, o_, q_, n_ = sm[j], o_sb[j], osq[j % 2], on[j]
            P.op("dve", lambda e, a0=a0, s_=s_: e.reciprocal(out=s_[:, 0:1], in_=a0[:, 128:129]), reads=[aS], pwrites=[s_])
            P.op("dve", lambda e, a1=a1, s_=s_: e.reciprocal(out=s_[:, 1:2], in_=a1[:, 128:129]), reads=[aS], pwrites=[s_])
            P.op("dve", lambda e, s_=s_: e.tensor_tensor(out=s_[:, 2:3], in0=s_[:, 1:2], in1=nlam[:], op=ALU.mult),
                 reads=[s_, nlam], pwrites=[s_])
            P.op("dve", lambda e, a0=a0, s_=s_, o_=o_: e.tensor_scalar(out=o_[:], in0=a0[:, 0:128], scalar1=s_[:, 0:1], scalar2=None,
                                                                      op0=ALU.mult), reads=[aS, s_], writes=[o_])
            P.op("dve", lambda e, a1=a1, s_=s_, o_=o_: e.scalar_tensor_tensor(out=o_[:], in0=a1[:, 0:128], scalar=s_[:, 2:3], in1=o_[:],
                                                                             op0=ALU.mult, op1=ALU.add), reads=[aS, s_, o_], writes=[o_])
            P.op("pool", lambda e, o_=o_, q_=q_: e.tensor_tensor(out=q_[:], in0=o_[:], in1=o_[:], op=ALU.mult), reads=[o_], writes=[q_])
            P.op("dve", lambda e, s_=s_, q_=q_: e.reduce_sum(out=s_[:, 3:4], in_=q_[:], axis=AX.X), reads=[q_], pwrites=[s_])
            P.op("act", lambda e, s_=s_: e.activation(out=s_[:, 4:5], in_=s_[:, 3:4], func=AF.Ln, scale=1.0 / 128, bias=eps128[:]),
                 reads=[s_, eps128], pwrites=[s_])
            P.op("act", lambda e, s_=s_: e.activation(out=s_[:, 5:6], in_=s_[:, 4:5], func=AF.Exp, scale=-0.5),
                 reads=[s_], pwrites=[s_])
            P.op("dve", lambda e, s_=s_, o_=o_, n_=n_: e.scalar_tensor_tensor(out=n_[:], in0=o_[:], scalar=s_[:, 5:6], in1=gsub[:],
                                                                             op0=ALU.mult, op1=ALU.mult), reads=[o_, s_, gsub], writes=[n_])

    def epilogue_out(qs):
        ost = oT_st[qs % 2]
        for j in range(4):
            P.op("pe", lambda e, j=j: e.transpose(out=ps_tr[:, j * 128:(j + 1) * 128], in_=on[j][:], identity=C.ident[:]),
                 reads=[on[j], C.ident], pwrites=[ps_tr])
        P.op("dve", lambda e, ost=ost: e.tensor_copy(out=ost[:], in_=ps_tr[:]), reads=[ps_tr], writes=[ost])
        P.dma("sp", o_in[:, qs * SW:(qs + 1) * SW], ost[:], reads=[ost], pwrites=[o_tok])

    units = [(qs, kb) for qs in range(NQS) for kb in range((qs + 1) * 4)]
    pending = []
    for idx in range(len(units) + 1):
        for m in range(2):
            if idx < len(units):
                emit_qk(*units[idx], maps=(m,))
            if idx >= 1:
                emit_pv(*units[idx - 1], maps=(m,), last=(m == 1))
        if idx >= 1:
            qs_, kb_ = units[idx - 1]
            if kb_ == (qs_ + 1) * 4 - 1:
                pending.append((idx + 3, qs_))
        while pending and pending[0][0] <= idx:
            epilogue_out(pending.pop(0)[1])
    for _, qs_ in pending:
        epilogue_out(qs_)


def attn_out(K, C, d, o_all, o_all_tok, xin, xin_tok, xout, xout_tok, halo_in, halo_tok):
    P, sb, ps = K.P, K.sb, K.ps
    K.phase()
    og = sb("og", [128, 8, NT], BF16)

    def fn(e):
        pid = P.pid(e)
        return e.dma_start(out=og[:], in_=o_all.rearrange("(h p) t -> p h t", p=128)[:, :, bass.ds(pid * NT, NT)])
    P._add("sp", fn, [o_all_tok], [og], (), True)
    wo = sb("wo", [128, 8, 1024], BF16)
    P.dma("pool", wo[:], d["w_o_r"], writes=[wo])
    xj = [sb("xj", [128, SW], F32) for _ in range(3)]
    ps_y = [ps(0, [128, SW]), ps(1, [128, SW])]
    xin3 = xin.rearrange("(c p) t -> p c t", p=128)
    xout3 = xout.rearrange("(c p) t -> p c t", p=128)
    it = 0
    for j in range(8):
        for s in range(NS):
            sl = slice(s * SW, (s + 1) * SW)
            py = ps_y[it % 2]
            xs = xj[it % 3]
            it += 1
            P.dma("sp", xs[:], xin3[:, j, sl], reads=[xin_tok], writes=[xs])
            for h in range(8):
                P.op("pe", lambda e, h=h, j=j, py=py, sl=sl: e.matmul(py[:], lhsT=wo[:, h, j * 128:(j + 1) * 128], rhs=og[:, h, sl],
                                                                      start=(h == 0), stop=(h == 7)), reads=[wo, og], writes=[py])
            P.op("dve", lambda e, py=py, xs=xs: e.tensor_tensor(out=xs[:], in0=py[:], in1=xs[:], op=ALU.add),
                 reads=[py, xs], writes=[xs])
            P.dma("sp", xout3[:, j, sl], xs[:], reads=[xs], pwrites=[xout_tok])
            if s == NS - 1:
                P.dma("sp", halo_in[:, j * 2:(j + 1) * 2], xs[:, SW - 2:SW], reads=[xs], pwrites=[halo_tok])


import numpy as np
from concourse.bass_utils import run_bass_kernel_spmd

NCORES = 8


def allgather(P, src_h, dst_h, src_tok, dst_tok, rows=None):
    dst = dst_h.ap() if rows is None else dst_h.ap()[0:rows, :]
    P.async_op("pool", lambda e: e.collective_compute("AllGather", ALU.bypass, replica_groups=[list(range(NCORES))],
                                                      ins=[src_h.ap().opt()], outs=[dst.opt()]),
               reads=[src_tok], writes=[dst_tok], inc=1)


IN_SPECS = [
    ("xT", [1024, NT]), ("w_in_r", [32, 128, 8, 128]), ("w_out_r", [128, 8, 1024]), ("ident", [128, 128]),
    ("resetm", [128, SW]), ("maskc", [64, 8, 64]), ("gmix0", [128, 8]), ("gnorm", [128, 8]), ("lbl", [128, 2, 8]),
    ("sel", [128, 8]), ("notfirst", [128, 1]),
    ("gffn0", [128, 8]), ("w_up_r0", [44, 128, 8, 128]), ("convp0", [128, 44, 4]), ("w_down_r0", [8, 128, 22, 128]),
    ("gffn1", [128, 8]), ("w_up_r1", [44, 128, 8, 128]), ("convp1", [128, 44, 4]), ("w_down_r1", [8, 128, 22, 128]),
    ("gkv", [128, 8]), ("gmix1", [128, 8]), ("w_k_r", [8, 128, 8, 128]), ("w_q_r", [8, 128, 8, 128]),
    ("w_v_r", [8, 128, 8, 128]), ("relcol", [32, 1]), ("oh", [32, 128]), ("gconst", [1, GLEN]), ("antiI", [128, 128]), ("lamv", [4, 64]),
    ("subln", [1, 128]), ("w_o_r", [128, 8, 1024]), ("gfinal", [128, 8]),
]


def build(debug=None):
    nc = bass.Bass("TRN2", target_bir_lowering=False)
    K = KB(nc)
    P = K.P
    d = {}
    for name, shape in IN_SPECS:
        d[name] = nc.dram_tensor(name, shape, F32, kind="ExternalInput").ap()
    outT = nc.dram_tensor("outT", [1024, NT], F32, kind="ExternalOutput").ap()
    out_tok = Buf("out")

    def stream(name):
        kind = "ExternalOutput" if debug == name else "Internal"
        return nc.dram_tensor(name, [1024, NT], F32, kind=kind).ap(), Buf(name)
    x1T, x1_tok = stream("x1T")
    x2T, x2_tok = stream("x2T")
    x3T, x3_tok = stream("x3T")
    x4T, x4_tok = stream("x4T")
    scr = {}
    for n in ("qT", "kT", "gT"):
        scr[n] = nc.dram_tensor(n + "_s", [1024, NT], BF16).ap()
        scr[n + "_tok"] = Buf(n)
    for n in ("v", "kt"):
        scr[n] = nc.dram_tensor(n + "_s", [NT, 1024], BF16).ap()
        scr[n + "_tok"] = Buf(n)
    hx_in = nc.dram_tensor("hx_in", [128, 1032], F32)
    hx_all = nc.dram_tensor("hx_all", [NCORES * 128, 1032], F32)
    hx_in_tok, hx_all_tok = Buf("hx_in"), Buf("hx_all")
    halo_in = [nc.dram_tensor(f"halo_in{i}", [128, 16], F32) for i in range(2)]
    halo_all = [nc.dram_tensor(f"halo_all{i}", [NCORES * 128, 16], F32) for i in range(2)]
    halo_in_tok = [Buf("hi0"), Buf("hi1")]
    halo_all_tok = [Buf("ha0"), Buf("ha1")]
    qkv_in = nc.dram_tensor("qkv_in", [3072, NT], BF16)
    qkv_all = nc.dram_tensor("qkv_all", [NCORES * 3072 + 384, NT], BF16)
    qkv_in_tok, qkv_all_tok = Buf("qkv_in"), Buf("qkv_all")
    gvec = nc.dram_tensor("gvec", [1, GLEN], F32)
    gvec_tok = Buf("gvec")
    o_in = nc.dram_tensor("o_in", [128, T_ALL], BF16)
    o_all = nc.dram_tensor("o_all", [NCORES * 128, T_ALL], BF16)
    o_in_tok, o_all_tok = Buf("o_in"), Buf("o_all")

    C = hgrn_consts(K, d)
    T = hgrn_alloc_T(K)
    S = K.sb("S", [128, 8, 128], F32, pers=True)
    Rr = K.sb("Rr", [128, 8, 128], F32, pers=True)
    sel = K.sb("sel", [128, 8], F32, pers=True)
    P.dma("sp", sel[:], d["sel"], writes=[sel])
    P.op("pool", lambda e: e.memset(S[:], 0.0), writes=[S])
    K.A.start_phase()
    hgrn_P(K, C, d, scr, T)
    K.phase()
    hgrn_R(K, C, d, scr, T, False, S)
    P.dma("sp", hx_in.ap()[:, 0:1024], S.t.rearrange("p h v -> p (h v)"), reads=[S], pwrites=[hx_in_tok])
    P.dma("sp", hx_in.ap()[:, 1024:1032], T.D[:], reads=[T.D], pwrites=[hx_in_tok])
    allgather(P, hx_in, hx_all, hx_in_tok, hx_all_tok)
    K.phase()
    Sj = [K.sb("Sj", [128, 1032], F32) for _ in range(2)]
    P.op("pool", lambda e: e.memset(S[:], 0.0), writes=[S])
    P.op("pool", lambda e: e.memset(Rr[:], 0.0), writes=[Rr])
    for j in range(NCORES):
        sj = Sj[j % 2]
        P.dma("sp", sj[:], hx_all.ap()[j * 128:(j + 1) * 128, :], reads=[hx_all_tok], writes=[sj])
        P.op("dve", lambda e, j=j: e.scalar_tensor_tensor(out=S[:], in0=Rr[:], scalar=sel[:, j:j + 1], in1=S[:],
                                                          op0=ALU.mult, op1=ALU.add), reads=[Rr, sel, S], writes=[S])
        if j < NCORES - 1:
            P.op("dve", lambda e, sj=sj: e.tensor_tensor(out=Rr[:], in0=Rr[:],
                                                         in1=sj[:, 1024:1032].unsqueeze(2).to_broadcast([128, 8, 128]),
                                                         op=ALU.mult), reads=[Rr, sj], writes=[Rr])
            P.op("dve", lambda e, sj=sj: e.tensor_tensor(out=Rr[:], in0=Rr[:],
                                                         in1=sj[:, 0:1024].rearrange("p (h v) -> p h v", h=8),
                                                         op=ALU.add), reads=[Rr, sj], writes=[Rr])
    d["x1T"], d["x1T_tok"] = x1T, x1_tok
    d["halo_out"], d["halo_out_tok"] = halo_in[0].ap(), halo_in_tok[0]
    hgrn_R(K, C, d, scr, T, True, S)
    allgather(P, halo_in[0], halo_all[0], halo_in_tok[0], halo_all_tok[0])
    if debug == "x1T":
        P.emit(final_bufs=[x1_tok, halo_all_tok[0]])
        print("nflag", P.nflag, "ndma", P.n_dma, "peak", K.A.peak)
        return nc
    ffn_layer(K, C, d, 0, x1T, x1_tok, x2T, x2_tok, halo_all[0].ap(), halo_all_tok[0])
    if debug == "x2T":
        P.emit(final_bufs=[x2_tok])
        print("nflag", P.nflag, "ndma", P.n_dma, "peak", K.A.peak)
        return nc
    kvq_proj(K, C, d, x2T, x2_tok, qkv_in.ap(), qkv_in_tok)
    allgather(P, qkv_in, qkv_all, qkv_in_tok, qkv_all_tok, rows=NCORES * 3072)
    attn_core(K, C, d, qkv_all.ap()[0:NCORES * 3072, :], qkv_all_tok, o_in.ap(), o_in_tok, gvec, gvec_tok)
    allgather(P, o_in, o_all, o_in_tok, o_all_tok)
    attn_out(K, C, d, o_all.ap(), o_all_tok, x2T, x2_tok, x3T, x3_tok, halo_in[1].ap(), halo_in_tok[1])
    allgather(P, halo_in[1], halo_all[1], halo_in_tok[1], halo_all_tok[1])
    if debug == "x3T":
        P.emit(final_bufs=[x3_tok, halo_all_tok[1]])
        return nc
    ffn_layer(K, C, d, 1, x3T, x3_tok, x4T, x4_tok, halo_all[1].ap(), halo_all_tok[1])
    final_norm(K, C, d, x4T, x4_tok, outT, out_tok)
    P.emit(final_bufs=[out_tok])
    return nc


def t5_bucket_np(rel):
    max_exact = 16
    n = np.maximum(rel, 0)
    log_ratio = (np.log(np.maximum(n, 1).astype(np.float32) / np.float32(max_exact)) / np.float32(math.log(128 / max_exact))).astype(np.float32)
    large = np.minimum(max_exact + (log_ratio * np.float32(32 - max_exact)).astype(np.int32), 31)
    return np.where(n < max_exact, n, large)


def host_inputs(inp, c):
    f = lambda a: np.ascontiguousarray(np.asarray(a, dtype=np.float32))
    pc = lambda v: f(np.asarray(v).reshape(8, 128).T)
    m = {}
    m["xT"] = f(np.asarray(inp["x"])[0, c * NT:(c + 1) * NT, :].T)
    m["w_in_r"] = f(np.asarray(inp["a_w_in"])[0].reshape(8, 128, 32, 128).transpose(2, 1, 0, 3))
    m["w_out_r"] = f(np.asarray(inp["a_w_out"])[0].reshape(8, 128, 1024).transpose(1, 0, 2))
    m["ident"] = np.eye(128, dtype=np.float32)
    r = np.ones((128, SW), np.float32)
    r[:, ::CH] = 0
    m["resetm"] = r
    mk_ = (np.arange(64)[:, None] <= np.arange(64)[None, :]).astype(np.float32)
    m["maskc"] = f(np.broadcast_to(mk_[:, None, :], (64, 8, 64)))
    m["gmix0"] = pc(inp["norm_mix"][0])
    m["gmix1"] = pc(inp["norm_mix"][1])
    m["gnorm"] = pc(inp["a_gnorm"][0])
    m["lbl"] = f(np.asarray(inp["a_lb_logits"]).reshape(2, 8, 128).transpose(2, 0, 1))
    s = np.zeros((128, 8), np.float32)
    s[:, c] = 1.0
    m["sel"] = s
    m["notfirst"] = np.full((128, 1), 0.0 if c == 0 else 1.0, np.float32)
    for li in range(2):
        m[f"gffn{li}"] = pc(inp["norm_ffn"][li])
        m[f"w_up_r{li}"] = f(np.asarray(inp["ffn_w_up"])[li].reshape(8, 128, 44, 128).transpose(2, 1, 0, 3))
        cw = np.asarray(inp["ffn_conv_w"])[li]
        cb = np.asarray(inp["ffn_conv_b"])[li]
        cp = np.concatenate([cw, cb[None]], 0)
        m[f"convp{li}"] = f(cp.reshape(4, 44, 128).transpose(2, 1, 0))
        m[f"w_down_r{li}"] = f(np.asarray(inp["ffn_w_down"])[li].reshape(22, 128, 8, 128).transpose(2, 1, 0, 3))
    m["gkv"] = pc(inp["kv_norm"])
    kvw = np.asarray(inp["kv_w"])
    m["w_k_r"] = f(kvw[:, :1024].reshape(8, 128, 8, 128).transpose(2, 1, 0, 3))
    m["w_v_r"] = f(kvw[:, 1024:].reshape(8, 128, 8, 128).transpose(2, 1, 0, 3))
    m["w_q_r"] = f(np.asarray(inp["b_w_q"])[0].reshape(8, 128, 8, 128).transpose(2, 1, 0, 3))
    m["w_o_r"] = f(np.asarray(inp["b_w_o"])[0].reshape(8, 128, 1024).transpose(1, 0, 2))
    m["relcol"] = f(np.asarray(inp["rel_table"])[:, c:c + 1])
    bk = t5_bucket_np(np.arange(128))
    oh = np.zeros((32, 128), np.float32)
    oh[bk, np.arange(128)] = 1.0
    oh[31, :] -= 1.0
    m["oh"] = oh
    g = np.zeros((1, GLEN), np.float32)
    g[0, :511] = NEG
    m["gconst"] = g
    m["antiI"] = np.ascontiguousarray(np.eye(128, dtype=np.float32)[::-1])
    m["lamv"] = f(np.stack([np.asarray(inp[k])[0] for k in ("b_lam_q1", "b_lam_k1", "b_lam_q2", "b_lam_k2")]))
    m["subln"] = f(np.asarray(inp["b_subln"])[0][None, :])
    m["gfinal"] = pc(inp["final_norm"])
    return m


_NC_CACHE = {}


def kernel(**inputs):
    if "nc" not in _NC_CACHE:
        _NC_CACHE["nc"] = build()
    nc = _NC_CACHE["nc"]
    in_maps = [host_inputs(inputs, c) for c in range(NCORES)]
    res = run_bass_kernel_spmd(nc, in_maps, core_ids=list(range(NCORES)))
    out = np.empty((1, NCORES * NT, 1024), np.float32)
    for c in range(NCORES):
        out[0, c * NT:(c + 1) * NT, :] = res.results[c]["outT"].T
    return out
```
